# Optimizing a Trainium2 kernel written in Bass

```python
import jax
import jax.numpy as jnp
from jax import lax
import numpy as np

D_MODEL = 2048
BATCH = 4
SEQ = 4096
DEPTH = 4

HEAD_DIM = 64
ROPE_THETA = 10000.0
NORM_EPS = 1e-6
MOBA_HEADS = 8
MOBA_W = MOBA_HEADS * HEAD_DIM
MOBA_BLOCK = 256
MOBA_TOPK = 3
MOBA_QBLOCK = 32
SSD_HEADS = 8
SSD_HEAD_DIM = 64
SSD_W = SSD_HEADS * SSD_HEAD_DIM
SSD_GROUPS = 2
SSD_STATE = 128
SSD_CONV = 4
SSD_CHUNK = 256
SSD_CONV_CH = SSD_W + 2 * SSD_GROUPS * SSD_STATE
SWA_HEADS = 8
SWA_KV_HEADS = 2
SWA_W = SWA_HEADS * HEAD_DIM
SWA_KV_W = SWA_KV_HEADS * HEAD_DIM
SWA_WINDOW = 128
S5_W = 512
S5_GROUP = 16
S5_GROUPS = S5_W // S5_GROUP
S5_STATE = 64

MIX_W = MOBA_W + SSD_W + SWA_W + S5_W
IN_SPLITS = (MOBA_W, MOBA_W, MOBA_W, MOBA_W,
             SSD_CONV_CH, SSD_HEADS, SSD_W,
             SWA_W, SWA_KV_W, SWA_KV_W, SWA_W,
             S5_W, S5_W)
IN_W = sum(IN_SPLITS)

kernel_name = 'hybrid_parallel_moba_ssd_swa_s5'

F32 = jnp.float32


def rms_norm(x, g):
    xf = x.astype(F32)
    y = xf * lax.rsqrt(jnp.mean(xf * xf, axis=-1, keepdims=True) + NORM_EPS) * g.astype(F32)
    return y.astype(x.dtype)


def rope(x):
    L, D = x.shape[1], x.shape[-1]
    inv = 1.0 / (ROPE_THETA ** (jnp.arange(0, D, 2, dtype=F32) / D))
    ang = jnp.arange(L, dtype=F32)[:, None] * inv[None, :]
    cos = jnp.cos(ang)[None, :, None, :]
    sin = jnp.sin(ang)[None, :, None, :]
    xf = x.astype(F32)
    x1, x2 = xf[..., : D // 2], xf[..., D // 2:]
    return jnp.concatenate([x1 * cos - x2 * sin, x2 * cos + x1 * sin], axis=-1).astype(x.dtype)


def pad_seq(a, mult):
    pad = (-a.shape[1]) % mult
    return jnp.pad(a, [(0, 0), (0, pad)] + [(0, 0)] * (a.ndim - 2))


def moba_attention(q, k, v):
    B, L, H, D = q.shape
    q, k, v = pad_seq(q, MOBA_BLOCK), pad_seq(k, MOBA_BLOCK), pad_seq(v, MOBA_BLOCK)
    Lp = q.shape[1]
    nb = Lp // MOBA_BLOCK
    nq = Lp // MOBA_QBLOCK
    topk = min(MOBA_TOPK, nb)
    scale = D ** -0.5
    kb = k.reshape(B, nb, MOBA_BLOCK, H, D).transpose(0, 3, 1, 2, 4)
    vb = v.reshape(B, nb, MOBA_BLOCK, H, D).transpose(0, 3, 1, 2, 4)
    k_mean = jnp.mean(kb.astype(F32), axis=3)
    qh = q.transpose(0, 2, 1, 3)
    gate = jnp.einsum('bhld,bhnd->bhln', qh.astype(F32), k_mean)
    q_blk = jnp.arange(Lp) // MOBA_BLOCK
    past = jnp.arange(nb)[None, :] < q_blk[:, None]
    gate = jnp.where(past, gate, -jnp.inf)
    _, sel = lax.top_k(gate, topk)
    valid = sel < q_blk[:, None]

    def to_chunks(t):
        t = t.reshape((B, H, nq, MOBA_QBLOCK) + t.shape[3:])
        return jnp.moveaxis(t, 2, 0)

    b_idx = jnp.arange(B)[:, None, None, None]
    h_idx = jnp.arange(H)[None, :, None, None]

    def query_block(args):
        qc, selc, validc, ci = args
        kg = kb[b_idx, h_idx, selc]
        vg = vb[b_idx, h_idx, selc]
        s_sel = jnp.einsum('bhqd,bhqkjd->bhqkj', qc, kg).astype(F32) * scale
        s_sel = jnp.where(validc[..., None], s_sel, -jnp.inf)
        s_sel = s_sel.reshape(B, H, MOBA_QBLOCK, topk * MOBA_BLOCK)
        own = (ci * MOBA_QBLOCK) // MOBA_BLOCK
        ko = lax.dynamic_index_in_dim(kb, own, axis=2, keepdims=False)
        vo = lax.dynamic_index_in_dim(vb, own, axis=2, keepdims=False)
        s_own = jnp.einsum('bhqd,bhjd->bhqj', qc, ko).astype(F32) * scale
        qpos = ci * MOBA_QBLOCK + jnp.arange(MOBA_QBLOCK)
        kpos = own * MOBA_BLOCK + jnp.arange(MOBA_BLOCK)
        s_own = jnp.where(kpos[None, :] <= qpos[:, None], s_own, -jnp.inf)
        p = jax.nn.softmax(jnp.concatenate([s_sel, s_own], axis=-1), axis=-1).astype(qc.dtype)
        p_sel = p[..., : topk * MOBA_BLOCK].reshape(B, H, MOBA_QBLOCK, topk, MOBA_BLOCK)
        p_own = p[..., topk * MOBA_BLOCK:]
        return (jnp.einsum('bhqkj,bhqkjd->bhqd', p_sel, vg)
                + jnp.einsum('bhqj,bhjd->bhqd', p_own, vo))

    out = lax.map(query_block, (to_chunks(qh), to_chunks(sel), to_chunks(valid),
                                jnp.arange(nq, dtype=jnp.int32)))
    out = jnp.moveaxis(out, 0, 2).reshape(B, H, Lp, D).transpose(0, 2, 1, 3)
    return out[:, :L]


def causal_depthwise_conv(x, w, b):
    K, C = w.shape
    y = lax.conv_general_dilated(x, w[:, None, :].astype(x.dtype), window_strides=(1,),
                                 padding=[(K - 1, 0)],
                                 dimension_numbers=('NWC', 'WIO', 'NWC'),
                                 feature_group_count=C)
    return y + b.astype(x.dtype)


def segsum(a):
    T = a.shape[-1]
    cs = jnp.cumsum(a, axis=-1)
    seg = cs[..., :, None] - cs[..., None, :]
    mask = jnp.tril(jnp.ones((T, T), dtype=bool))
    return jnp.where(mask, seg, -jnp.inf)


def ssd_scan(x, dt, A, Bm, Cm):
    b, l, h, p = x.shape
    n = Bm.shape[-1]
    s = SSD_CHUNK
    c = l // s
    X = (x * dt[..., None]).reshape(b, c, s, h, p)
    a = (dt * A).reshape(b, c, s, h).transpose(0, 3, 1, 2)
    Bc = Bm.reshape(b, c, s, h, n)
    Cc = Cm.reshape(b, c, s, h, n)
    a_cum = jnp.cumsum(a, axis=-1)
    cb = jnp.einsum('bclhn,bcshn->bhcls', Cc, Bc)
    y_diag = jnp.einsum('bhcls,bcshp->bclhp', cb * jnp.exp(segsum(a)), X)
    decay_states = jnp.exp(a_cum[..., -1:] - a_cum).transpose(0, 2, 3, 1)
    chunk_states = jnp.einsum('bclhn,bclhp->bchpn', Bc, X * decay_states[..., None])
    chunk_decay = jnp.exp(a_cum[..., -1])

    def pass_state(state, inp):
        st, dec = inp
        return dec[..., None, None] * state + st, state

    _, prev = lax.scan(pass_state, jnp.zeros((b, h, p, n), F32),
                       (jnp.moveaxis(chunk_states, 1, 0), jnp.moveaxis(chunk_decay, 2, 0)))
    prev = jnp.moveaxis(prev, 0, 1)
    y_off = jnp.einsum('bclhn,bchpn->bclhp', Cc, prev) * jnp.exp(a_cum).transpose(0, 2, 3, 1)[..., None]
    return (y_diag + y_off).reshape(b, l, h, p)


def mamba2_mixer(xbc, dt_raw, z, conv_w, conv_b, dt_bias, a_log, d_skip, norm_w):
    B, L, _ = xbc.shape
    xbc = jax.nn.silu(causal_depthwise_conv(xbc, conv_w, conv_b))
    xs, bm, cm = jnp.split(xbc, [SSD_W, SSD_W + SSD_GROUPS * SSD_STATE], axis=-1)
    rep = SSD_HEADS // SSD_GROUPS
    xs = xs.reshape(B, L, SSD_HEADS, SSD_HEAD_DIM).astype(F32)
    bm = jnp.repeat(bm.reshape(B, L, SSD_GROUPS, SSD_STATE).astype(F32), rep, axis=2)
    cm = jnp.repeat(cm.reshape(B, L, SSD_GROUPS, SSD_STATE).astype(F32), rep, axis=2)
    dt = jax.nn.softplus(dt_raw.astype(F32) + dt_bias.astype(F32))
    A = -jnp.exp(a_log.astype(F32))
    y = ssd_scan(pad_seq(xs, SSD_CHUNK), pad_seq(dt, SSD_CHUNK), A,
                 pad_seq(bm, SSD_CHUNK), pad_seq(cm, SSD_CHUNK))[:, :L]
    y = y + d_skip.astype(F32)[:, None] * xs
    y = y.reshape(B, L, SSD_W) * jax.nn.silu(z.astype(F32))
    yg = y.reshape(B, L, SSD_GROUPS, SSD_W // SSD_GROUPS)
    yg = yg * lax.rsqrt(jnp.mean(yg * yg, axis=-1, keepdims=True) + NORM_EPS)
    return (yg.reshape(B, L, SSD_W) * norm_w.astype(F32)).astype(z.dtype)


def swa_attention(q, k, v, sinks):
    B, L, HQ, D = q.shape
    HKV = k.shape[2]
    G = HQ // HKV
    W = SWA_WINDOW
    nb = L // W
    scale = D ** -0.5
    qb = q.reshape(B, nb, W, HKV, G, D)

    def band(t):
        tb = t.reshape(B, nb, W, HKV, D)
        prev = jnp.pad(tb, ((0, 0), (1, 0), (0, 0), (0, 0), (0, 0)))[:, :-1]
        return jnp.concatenate([prev, tb], axis=2)

    kk, vv = band(k), band(v)
    s = jnp.einsum('bnqhgd,bnkhd->bnhgqk', qb, kk).astype(F32) * scale
    qpos = jnp.arange(nb)[:, None, None] * W + jnp.arange(W)[None, :, None]
    kpos = jnp.arange(nb)[:, None, None] * W - W + jnp.arange(2 * W)[None, None, :]
    diff = qpos - kpos
    mask = (diff >= 0) & (diff < W) & (kpos >= 0)
    s = jnp.where(mask[None, :, None, None], s, -jnp.inf)
    sink = jnp.broadcast_to(sinks.astype(F32).reshape(1, 1, HKV, G, 1, 1), s.shape[:-1] + (1,))
    p = jax.nn.softmax(jnp.concatenate([s, sink], axis=-1), axis=-1)[..., :-1].astype(q.dtype)
    return jnp.einsum('bnhgqk,bnkhd->bnqhgd', p, vv).reshape(B, L, HQ, D)


def s5_mixer(u, a_re, a_im, log_dt, b_re, b_im, c_re, c_im, d_skip, glu_w, glu_b):
    B, L, _ = u.shape
    lam = lax.complex(a_re.astype(F32), a_im.astype(F32))
    step = jnp.exp(log_dt.astype(F32))[:, None]
    a_bar = jnp.exp(lam * step)
    b_bar = ((a_bar - 1.0) / lam)[..., None] * lax.complex(b_re.astype(F32), b_im.astype(F32))
    ug = u.astype(F32).reshape(B, L, S5_GROUPS, S5_GROUP)
    bu = jnp.einsum('blgh,gph->blgp', ug.astype(jnp.complex64), b_bar)

    def combine(e1, e2):
        a1, b1 = e1
        a2, b2 = e2
        return a1 * a2, a2 * b1 + b2

    _, states = lax.associative_scan(combine, (jnp.broadcast_to(a_bar, bu.shape), bu), axis=1)
    c = lax.complex(c_re.astype(F32), c_im.astype(F32))
    y = jnp.real(jnp.einsum('blgp,ghp->blgh', states, c)).reshape(B, L, S5_W)
    y = y + d_skip.astype(F32) * u.astype(F32)
    y = jax.nn.gelu(y)
    y = y * jax.nn.sigmoid(y @ glu_w.astype(F32) + glu_b.astype(F32))
    return y.astype(u.dtype)


def hybrid_layer(x, pre_g, post_g, w_in, w_out, conv_w, conv_b, dt_bias, a_log, ssd_d, ssd_norm,
                 sinks, a_re, a_im, log_dt, b_re, b_im, c_re, c_im, s5_d, glu_w, glu_b):
    B, L, _ = x.shape
    h = rms_norm(x, pre_g)
    proj = h @ w_in
    split_points = [int(v) for v in np.cumsum(IN_SPLITS)[:-1]]
    (mq, mk, mv, mg, xbc, dt_raw, z, sq, sk, sv, sg, su, s5g) = jnp.split(proj, split_points, axis=-1)

    def heads(t):
        return t.reshape(B, L, -1, HEAD_DIM)

    y_moba = moba_attention(rope(heads(mq)), rope(heads(mk)), heads(mv)).reshape(B, L, MOBA_W)
    y_moba = y_moba * jax.nn.silu(mg)
    y_ssd = mamba2_mixer(xbc, dt_raw, z, conv_w, conv_b, dt_bias, a_log, ssd_d, ssd_norm)
    y_swa = swa_attention(rope(heads(sq)), rope(heads(sk)), heads(sv), sinks).reshape(B, L, SWA_W)
    y_swa = y_swa * jax.nn.silu(sg)
    y_s5 = s5_mixer(su, a_re, a_im, log_dt, b_re, b_im, c_re, c_im, s5_d, glu_w, glu_b)
    y_s5 = y_s5 * jax.nn.silu(s5g)
    mix = jnp.concatenate([y_moba, y_ssd.astype(x.dtype), y_swa, y_s5], axis=-1)
    return x + rms_norm(mix @ w_out, post_g)


def setup_inputs(seed: int = 0) -> dict:
    key = jax.random.key(seed)
    ks = jax.random.split(key, 24)

    def nrm(k, shape, scale):
        return scale * jax.random.normal(k, shape, F32)

    x = nrm(ks[0], (BATCH, SEQ, D_MODEL), 1.0)
    pre_norm = 1.0 + nrm(ks[1], (DEPTH, D_MODEL), 0.05)
    post_norm = 1.0 + nrm(ks[2], (DEPTH, D_MODEL), 0.05)
    w_in = nrm(ks[3], (DEPTH, D_MODEL, IN_W), D_MODEL ** -0.5)
    w_out = nrm(ks[4], (DEPTH, MIX_W, D_MODEL), MIX_W ** -0.5)
    ssd_conv_w = nrm(ks[5], (DEPTH, SSD_CONV, SSD_CONV_CH), SSD_CONV ** -0.5)
    ssd_conv_b = nrm(ks[6], (DEPTH, SSD_CONV_CH), 0.02)
    dt0 = jnp.exp(jax.random.uniform(ks[7], (DEPTH, SSD_HEADS), F32,
                                     minval=float(np.log(1e-3)), maxval=float(np.log(1e-1))))
    ssd_dt_bias = dt0 + jnp.log(-jnp.expm1(-dt0))
    ssd_a_log = jnp.log(jax.random.uniform(ks[8], (DEPTH, SSD_HEADS), F32, minval=1.0, maxval=16.0))
    ssd_d = 1.0 + nrm(ks[9], (DEPTH, SSD_HEADS), 0.05)
    ssd_norm = 1.0 + nrm(ks[10], (DEPTH, SSD_W), 0.05)
    swa_sinks = nrm(ks[11], (DEPTH, SWA_HEADS), 1.0)
    n_idx = jnp.arange(S5_STATE, dtype=F32)
    s5_a_re = -0.5 + nrm(ks[12], (DEPTH, S5_GROUPS, S5_STATE), 0.01)
    s5_a_im = jnp.pi * n_idx + nrm(ks[13], (DEPTH, S5_GROUPS, S5_STATE), 0.01)
    s5_log_dt = jax.random.uniform(ks[14], (DEPTH, S5_GROUPS), F32,
                                   minval=float(np.log(1e-3)), maxval=float(np.log(1e-1)))
    s5_b_re = nrm(ks[15], (DEPTH, S5_GROUPS, S5_STATE, S5_GROUP), (2 * S5_GROUP) ** -0.5)
    s5_b_im = nrm(ks[16], (DEPTH, S5_GROUPS, S5_STATE, S5_GROUP), (2 * S5_GROUP) ** -0.5)
    s5_c_re = nrm(ks[17], (DEPTH, S5_GROUPS, S5_GROUP, S5_STATE), S5_STATE ** -0.5)
    s5_c_im = nrm(ks[18], (DEPTH, S5_GROUPS, S5_GROUP, S5_STATE), S5_STATE ** -0.5)
    s5_d = nrm(ks[19], (DEPTH, S5_W), 1.0)
    s5_glu_w = nrm(ks[20], (DEPTH, S5_W, S5_W), S5_W ** -0.5)
    s5_glu_b = nrm(ks[21], (DEPTH, S5_W), 0.02)
    return {'x': x, 'pre_norm': pre_norm, 'post_norm': post_norm, 'w_in': w_in, 'w_out': w_out,
            'ssd_conv_w': ssd_conv_w, 'ssd_conv_b': ssd_conv_b, 'ssd_dt_bias': ssd_dt_bias,
            'ssd_a_log': ssd_a_log, 'ssd_d': ssd_d, 'ssd_norm': ssd_norm, 'swa_sinks': swa_sinks,
            's5_a_re': s5_a_re, 's5_a_im': s5_a_im, 's5_log_dt': s5_log_dt,
            's5_b_re': s5_b_re, 's5_b_im': s5_b_im, 's5_c_re': s5_c_re, 's5_c_im': s5_c_im,
            's5_d': s5_d, 's5_glu_w': s5_glu_w, 's5_glu_b': s5_glu_b}


def reference(x, pre_norm, post_norm, w_in, w_out, ssd_conv_w, ssd_conv_b, ssd_dt_bias, ssd_a_log,
              ssd_d, ssd_norm, swa_sinks, s5_a_re, s5_a_im, s5_log_dt, s5_b_re, s5_b_im,
              s5_c_re, s5_c_im, s5_d, s5_glu_w, s5_glu_b):
    for l in range(DEPTH):
        x = hybrid_layer(x, pre_norm[l], post_norm[l], w_in[l], w_out[l],
                         ssd_conv_w[l], ssd_conv_b[l], ssd_dt_bias[l], ssd_a_log[l], ssd_d[l], ssd_norm[l],
                         swa_sinks[l], s5_a_re[l], s5_a_im[l], s5_log_dt[l], s5_b_re[l], s5_b_im[l],
                         s5_c_re[l], s5_c_im[l], s5_d[l], s5_glu_w[l], s5_glu_b[l])
    return x
```

```python
from contextlib import ExitStack
import numpy as np
import ml_dtypes
import concourse.bass as bass
import concourse.mybir as mybir
from concourse.bass_utils import run_bass_kernel_spmd

F32 = mybir.dt.float32
BF16 = mybir.dt.bfloat16
ALU = mybir.AluOpType
AF = mybir.ActivationFunctionType
AX = mybir.AxisListType

D = 2048
L = 4096
NT = L // 128
DEPTH = 4
EPS = 1e-6
C_MQ, C_MK, C_MV, C_MG = 0, 256, 512, 768
C_SQ, C_SK, C_SV, C_SG = 1024, 1280, 1344, 1408
C_XS, C_BM, C_CM, C_Z = 1664, 1920, 2048, 2176
C_DT, C_SU, C_S5G = 2432, 2436, 2948
NP = 3460
NPH = 2436
MIXH = 1024


class Buf:
    __slots__ = ("t", "last_w", "readers")

    def __init__(self, t):
        self.t = t
        self.last_w = None
        self.readers = {}

    def __getitem__(self, k):
        return self.t[k]


class DramT:
    def __init__(self, nc, name, shape, dt, kind="Internal"):
        self.ap = nc.dram_tensor(name, list(shape), dt, kind=kind).ap()
        self.tiles = [Buf(None) for _ in range((shape[0] + 127) // 128)]

    def bufs(self, r0, r1):
        return self.tiles[r0 // 128:(r1 + 127) // 128]


class Eng:
    def __init__(self, key, h, sem):
        self.key, self.h, self.sem = key, h, sem
        self.cnt = 0
        self.seen = {}


class Sched:
    NSLOT = 8

    def __init__(self, nc):
        self.nc = nc
        self.engs = {}
        for key, h in (("pe", nc.tensor), ("act", nc.scalar), ("dve", nc.vector),
                       ("pool", nc.gpsimd), ("sp", nc.sync)):
            self.engs[key] = Eng(key, h, nc.alloc_semaphore(name=f"prog_{key}"))
        self.dma_sems = {}
        self.dma_rings = {}
        self.n_ins = 0
        self.muted = False
        self.same_engine_raw = True

    def _deps(self, reads, writes):
        deps = {}
        for b in reads:
            if b.last_w is not None:
                k, i = b.last_w
                if deps.get(k, 0) < i:
                    deps[k] = i
        for b in writes:
            if b.last_w is not None:
                k, i = b.last_w
                if deps.get(k, 0) < i:
                    deps[k] = i
            for k, i in b.readers.items():
                if deps.get(k, 0) < i:
                    deps[k] = i
        return deps

    def _emit_waits(self, e, deps, same_ok=True):
        for k, i in deps.items():
            if k == e.key and same_ok:
                continue
            if e.seen.get(k, 0) >= i:
                continue
            sem = self.dma_sems[k] if k.startswith("dma") else self.engs[k].sem
            e.h.wait_ge(sem, i)
            e.seen[k] = i
            self.n_ins += 1

    def _record(self, key, idx, reads, writes):
        for b in reads:
            if b.readers.get(key, 0) < idx:
                b.readers[key] = idx
        for b in writes:
            b.last_w = (key, idx)
            b.readers = {}

    def op(self, ek, fn, reads=(), writes=()):
        if self.muted:
            return None
        e = self.engs[ek]
        self._emit_waits(e, self._deps(reads, writes))
        own = 0
        if self.same_engine_raw and ek != "pe":
            for b in reads:
                if b.last_w is not None and b.last_w[0] == ek and b.last_w[1] > own:
                    own = b.last_w[1]
            for b in writes:
                if b.last_w is not None and b.last_w[0] == ek and b.last_w[1] > own:
                    own = b.last_w[1]
                r = b.readers.get(ek, 0)
                if r > own:
                    own = r
        if own > e.seen.get(ek, 0):
            e.h.wait_ge(e.sem, own)
            e.seen[ek] = own
            self.n_ins += 1
        ins = fn(e.h)
        e.cnt += 1
        ins.then_inc(e.sem, 1)
        self._record(ek, e.cnt, reads, writes)
        self.n_ins += 1
        return ins

    def dma(self, ek, out, in_, reads=(), writes=(), **kw):
        if self.muted:
            return None
        e = self.engs[ek]
        deps = self._deps(reads, writes)
        ring = self.dma_rings.setdefault(ek, {"next": 0, "cnt": [0] * self.NSLOT})
        slot = ring["next"]
        ring["next"] = (slot + 1) % self.NSLOT
        qk = f"dma:{ek}:{slot}"
        if qk not in self.dma_sems:
            self.dma_sems[qk] = self.nc.alloc_semaphore(name=f"dma_{ek}_{slot}")
        if ring["cnt"][slot] > 0:
            deps[qk] = max(deps.get(qk, 0), ring["cnt"][slot])
        self._emit_waits(e, deps, same_ok=False)
        ring["cnt"][slot] += 16
        ins = e.h.dma_start(out=out, in_=in_, **kw)
        ins.then_inc(self.dma_sems[qk], 16)
        self._record(qk, ring["cnt"][slot], reads, writes)
        self.n_ins += 1
        return ins

    def barrier(self):
        if self.muted:
            return
        deps = {}
        for k, e in self.engs.items():
            if e.cnt > 0:
                deps[k] = e.cnt
        for ek, ring in self.dma_rings.items():
            for slot, c in enumerate(ring["cnt"]):
                if c > 0:
                    deps[f"dma:{ek}:{slot}"] = c
        for e in self.engs.values():
            self._emit_waits(e, deps)

    def finish(self, bufs):
        self.muted = False
        e = self.engs["sp"]
        self._emit_waits(e, self._deps(bufs, bufs))
        e.h.nop()


class Ctx:
    def __init__(self, nc):
        self.nc = nc
        self.S = Sched(nc)
        self._rr = {}
        self.nt = NT

    def rr(self, key, choices):
        i = self._rr.get(key, 0)
        self._rr[key] = i + 1
        return choices[i % len(choices)]

    def dbg(self, name, buf, shape, dt):
        if not getattr(self, "debug", False):
            return
        o = DramT(self.nc, name, list(shape), dt, kind="ExternalOutput")
        self.S.dma("sp", o.ap, buf[:], reads=[buf], writes=o.tiles)
        self.dbg_outs = getattr(self, "dbg_outs", []) + o.tiles

    def uid(self, name):
        self._uid = getattr(self, "_uid", 0) + 1
        return f"{name}_u{self._uid}"

    def sb(self, es, name, shape, dt):
        return Buf(es.enter_context(self.nc.sbuf_tensor(self.uid(name), list(shape), dt)))

    def ps(self, es, name, shape, dt=F32):
        return Buf(es.enter_context(self.nc.psum_tensor(self.uid(name), list(shape), dt)))


def load_weight_bf16(cx, es, name, w_ap, K, N, scale_sb=None):
    nc, S = cx.nc, cx.S
    KT = K // 128
    Wb = cx.sb(es, name, [128, KT, N], BF16)
    with ExitStack() as es2:
        stg = [cx.sb(es2, f"{name}_stg{i}", [128, N], F32) for i in range(3)]
        for kt in range(KT):
            st = stg[kt % 3]
            S.dma(cx.rr("wq", ["sp", "pool"]), st[:], w_ap[kt * 128:(kt + 1) * 128, :], writes=[st])
            ek = cx.rr("wcast", ["dve", "act"])
            if scale_sb is not None:
                if ek == "dve":
                    S.op("dve", lambda h: h.tensor_scalar(Wb[:, kt, :], st[:], scale_sb[:, kt:kt + 1], None, ALU.mult),
                         [st, scale_sb], [Wb])
                else:
                    S.op("act", lambda h: h.activation(out=Wb[:, kt, :], in_=st[:], func=AF.Copy, scale=scale_sb[:, kt:kt + 1]),
                         [st, scale_sb], [Wb])
            else:
                if ek == "dve":
                    S.op("dve", lambda h: h.tensor_copy(Wb[:, kt, :], st[:]), [st], [Wb])
                else:
                    S.op("act", lambda h: h.copy(Wb[:, kt, :], st[:]), [st], [Wb])
        S.barrier()
    return Wb


def phase_inproj(cx, x_dt, w_ap, g_ap, proj_dt, ident_bf, npc=NP):
    nc, S = cx.nc, cx.S
    KT = D // 128
    with ExitStack() as es:
        g_sb = cx.sb(es, "g_sb", [128, KT], F32)
        S.dma("sp", g_sb[:], g_ap, writes=[g_sb])
        Wb = load_weight_bf16(cx, es, "Win", w_ap, D, npc, g_sb)
        xt = [cx.sb(es, f"xt{i}", [128, D], F32) for i in range(2)]
        xb = [cx.sb(es, f"xb{i}", [128, D], BF16) for i in range(2)]
        junk = cx.sb(es, "junk", [128, D], BF16)
        ss = [cx.sb(es, f"ss{i}", [128, 1], F32) for i in range(2)]
        rstd = [cx.sb(es, f"rstd{i}", [128, 1], F32) for i in range(2)]
        hT = [cx.sb(es, f"hT{i}", [128, KT, 128], BF16) for i in range(2)]
        stage = [cx.sb(es, f"stage{i}", [128, npc], F32) for i in range(2)]
        nch = (npc + 511) // 512
        NACC = 6
        pmb = [cx.ps(es, f"pm{i}", [128, 512], F32) for i in range(min(nch, NACC))]
        pm = [pmb[c % NACC] for c in range(nch)]
        ptl = [cx.ps(es, f"pt{i}", [128, 4, 128], BF16) for i in range(2)]
        def prep(t):
            i2 = t % 2
            X, XB, SS, RS, HT, ST = xt[i2], xb[i2], ss[i2], rstd[i2], hT[i2], stage[i2]
            S.dma("sp", X[:], x_dt.ap[t * 128:(t + 1) * 128, :], reads=x_dt.bufs(t * 128, t * 128 + 128), writes=[X])
            S.op("act", lambda h: h.activation(out=junk[:], in_=X[:], func=AF.Square, accum_out=SS[:]),
                 [X], [junk, SS])
            S.op("pool", lambda h: h.tensor_copy(XB[:], X[:]), [X], [XB])
            S.op("dve", lambda h: h.tensor_scalar(RS[:], SS[:], 1.0 / D, EPS, ALU.mult, ALU.add), [SS], [RS])
            S.op("act", lambda h: h.sqrt(RS[:], RS[:]), [RS], [RS])
            S.op("dve", lambda h: h.reciprocal(RS[:], RS[:]), [RS], [RS])
            for q in range(KT // 4):
                P = ptl[q % 2]
                pv = P[:]

                def tr(h, q=q, pv=pv):
                    for r in range(4):
                        kt = q * 4 + r
                        ins = h.transpose(pv[:, r, :], XB[:, kt * 128:(kt + 1) * 128], ident_bf[:])
                    return ins
                S.op("pe", tr, [XB, ident_bf], [P])
                ek = "act"
                if ek == "dve":
                    S.op("dve", lambda h: h.tensor_copy(HT[:, q * 4:(q + 1) * 4, :], pv), [P], [HT])
                else:
                    S.op("act", lambda h: h.copy(HT[:, q * 4:(q + 1) * 4, :], pv), [P], [HT])

        def compute(t):
            i2 = t % 2
            X, XB, SS, RS, HT, ST = xt[i2], xb[i2], ss[i2], rstd[i2], hT[i2], stage[i2]

            def evac_chunks(cs, ST=ST, RS=RS):
                for c in cs:
                    n0 = c * 512
                    n1 = min(npc, n0 + 512)
                    PM = pm[c]
                    ek = cx.rr("pjev", ["act", "dve"])
                    if ek == "dve":
                        S.op("dve", lambda h: h.tensor_scalar(ST[:, n0:n1], PM[:, 0:n1 - n0], RS[:, 0:1], None, ALU.mult),
                             [PM, RS], [ST])
                    else:
                        S.op("act", lambda h: h.activation(out=ST[:, n0:n1], in_=PM[:, 0:n1 - n0], func=AF.Copy,
                                                           scale=RS[:, 0:1]), [PM, RS], [ST])

            for g0 in range(0, nch, NACC):
                cs = list(range(g0, min(nch, g0 + NACC)))

                def mm(h, HT=HT, cs=cs):
                    for kt in range(KT):
                        for c in cs:
                            n0 = c * 512
                            n1 = min(npc, n0 + 512)
                            ins = h.matmul(pm[c][:, 0:n1 - n0], HT[:, kt, :], Wb[:, kt, n0:n1],
                                           start=(kt == 0), stop=(kt == KT - 1))
                    return ins
                S.op("pe", mm, [HT, Wb], [pm[c] for c in cs])
                evac_chunks(cs)
            if t == 0:
                cx.dbg("d_rs", RS, [128, 1], F32)
                cx.dbg("d_ss", SS, [128, 1], F32)
                cx.dbg("d_xb", XB, [128, D], BF16)
                cx.dbg("d_hT", HT, [128, KT, 128], BF16)
                cx.dbg("d_Wb", Wb, [128, KT, npc], BF16)
            S.dma("pool", proj_dt.ap[t * 128:(t + 1) * 128, 0:npc], ST[:], reads=[ST], writes=proj_dt.bufs(t * 128, t * 128 + 128))

        prep(0)
        for t in range(cx.nt):
            if t + 1 < cx.nt:
                prep(t + 1)
            compute(t)
        S.barrier()


def phase_outproj(cx, mix_dt, x_dt, w_ap, gbc_ap, out_dt, ident_bf, ntiles):
    nc, S = cx.nc, cx.S
    KT = D // 128
    with ExitStack() as es:
        Wb = load_weight_bf16(cx, es, "Wout", w_ap, D, D, None)
        gbc = cx.sb(es, "gbc", [128, D], F32)
        S.dma("sp", gbc[:], gbc_ap, writes=[gbc])
        mt = [cx.sb(es, f"mt{i}", [128, D], F32) for i in range(2)]
        mb = [cx.sb(es, f"mb{i}", [128, D], BF16) for i in range(2)]
        xt = [cx.sb(es, f"oxt{i}", [128, D], F32) for i in range(2)]
        mT = [cx.sb(es, f"mT{i}", [128, KT, 128], BF16) for i in range(2)]
        o = [cx.sb(es, f"o{i}", [128, D], F32) for i in range(2)]
        o2 = [cx.sb(es, f"o2{i}", [128, D], F32) for i in range(2)]
        junk = cx.sb(es, "ojunk", [128, D], BF16)
        ss = [cx.sb(es, f"oss{i}", [128, 1], F32) for i in range(2)]
        pt = [cx.ps(es, f"opt{i}", [128, 4, 128], BF16) for i in range(2)]
        pm = [cx.ps(es, f"opm{i}", [128, 512], F32) for i in range(4)]
        def prep(t):
            i2 = t % 2
            M, MB, X, MT, O, O2, SS = mt[i2], mb[i2], xt[i2], mT[i2], o[i2], o2[i2], ss[i2]
            r0, r1 = t * 128, (t + 1) * 128
            S.dma("sp", M[:], mix_dt.ap[r0:r1, :], reads=mix_dt.bufs(r0, r1), writes=[M])
            S.dma("sp", X[:], x_dt.ap[r0:r1, :], reads=x_dt.bufs(r0, r1), writes=[X])
            S.op("act", lambda h: h.copy(MB[:], M[:]), [M], [MB])
            for q in range(KT // 4):
                P = pt[q % 2]

                def tr(h, q=q, P=P):
                    for r in range(4):
                        kt = q * 4 + r
                        ins = h.transpose(P[:, r, :], MB[:, kt * 128:(kt + 1) * 128], ident_bf[:])
                    return ins
                S.op("pe", tr, [MB, ident_bf], [P])
                if q % 2 == 0:
                    S.op("dve", lambda h: h.tensor_copy(MT[:, q * 4:(q + 1) * 4, :], P[:]), [P], [MT])
                else:
                    S.op("act", lambda h: h.copy(MT[:, q * 4:(q + 1) * 4, :], P[:]), [P], [MT])
        def compute(t):
            i2 = t % 2
            M, MB, X, MT, O, O2, SS = mt[i2], mb[i2], xt[i2], mT[i2], o[i2], o2[i2], ss[i2]
            r0, r1 = t * 128, (t + 1) * 128

            def mm(h, MT=MT):
                for kt in range(KT):
                    for c in range(4):
                        ins = h.matmul(pm[c][:, :], MT[:, kt, :], Wb[:, kt, c * 512:(c + 1) * 512],
                                       start=(kt == 0), stop=(kt == KT - 1))
                return ins
            S.op("pe", mm, [MT, Wb], pm)
            for c in range(4):
                n0, n1 = c * 512, (c + 1) * 512
                PM = pm[c]
                if c % 2 == 0:
                    S.op("act", lambda h: h.copy(O[:, n0:n1], PM[:, :]), [PM], [O])
                else:
                    S.op("dve", lambda h: h.tensor_copy(O[:, n0:n1], PM[:, :]), [PM], [O])
            S.op("act", lambda h: h.activation(out=junk[:], in_=O[:], func=AF.Square, accum_out=SS[:]), [O], [junk, SS])
            S.op("dve", lambda h: h.tensor_scalar(SS[:], SS[:], 1.0 / D, EPS, ALU.mult, ALU.add), [SS], [SS])
            S.op("act", lambda h: h.sqrt(SS[:], SS[:]), [SS], [SS])
            S.op("dve", lambda h: h.reciprocal(SS[:], SS[:]), [SS], [SS])
            S.op("dve", lambda h: h.scalar_tensor_tensor(O2[:], O[:], SS[:, 0:1], gbc[:], ALU.mult, ALU.mult),
                 [O, SS, gbc], [O2])
            if t == 1:
                cx.dbg("d_o", O, [128, D], F32)
                cx.dbg("d_rs", SS, [128, 1], F32)
                cx.dbg("d_o2", O2, [128, D], F32)
            S.op("dve", lambda h: h.tensor_tensor(O2[:], O2[:], X[:], ALU.add), [O2, X], [O2])
            S.dma("pool", out_dt.ap[r0:r1, :], O2[:], reads=[O2], writes=out_dt.bufs(r0, r1))

        prep(0)
        for t in range(ntiles):
            if t + 1 < ntiles:
                prep(t + 1)
            compute(t)
        S.barrier()


def rope_tiles(cx, S, dst_bf, src, nh, cosb, sinb, tmp):
    s4 = src.rearrange("p (h two d) -> p h two d", two=2, d=32)
    d4 = dst_bf.rearrange("p (h two d) -> p h two d", two=2, d=32)
    x1, x2 = s4[:, :, 0, :], s4[:, :, 1, :]
    cb = cosb.unsqueeze(1).broadcast_to([128, nh, 32])
    sb_ = sinb.unsqueeze(1).broadcast_to([128, nh, 32])
    tv = tmp[:].rearrange("p f (h d) -> p f h d", d=32)
    return x1, x2, cb, sb_, tv, d4


def phase_swa(cx, proj_dt, mix_dt, cos_sb, sin_sb, ident_bf, mask_prev_ap, mask_own_ap, sinks_ap, mcol=0):
    nc, S = cx.nc, cx.S
    Lc, NTc = cx.L, cx.L // 128
    with ExitStack() as es:
        QKT = cx.sb(es, "swa_QKT", [64, 5, Lc], BF16)
        Vaug = cx.sb(es, "swa_V", [128, NTc, 65], BF16)
        mprev = cx.sb(es, "swa_mprev", [128, 4, 128], F32)
        mown = cx.sb(es, "swa_mown", [128, 4, 128], F32)
        esink = cx.sb(es, "swa_esink", [128, 4], F32)
        S.dma("sp", mprev[:], mask_prev_ap, writes=[mprev])
        S.dma("sp", mown[:], mask_own_ap, writes=[mown])
        S.dma("sp", esink[:], sinks_ap, writes=[esink])
        S.op("act", lambda h: h.activation(out=esink[:], in_=esink[:], func=AF.Exp), [esink], [esink])
        S.op("pool", lambda h: h.memset(Vaug[:, :, 64:65], 1.0), [], [Vaug])
        tin = [cx.sb(es, f"swa_tin{i}", [128, 384], F32) for i in range(2)]
        rtmp = [cx.sb(es, f"swa_rtmp{i}", [128, 4, 160], F32) for i in range(2)]
        rb = [cx.sb(es, f"swa_rb{i}", [128, 320], BF16) for i in range(2)]
        ptr = [cx.ps(es, f"swa_ptr{i}", [64, 8, 128], BF16) for i in range(2)]
        for t in range(NTc):
            i2 = t % 2
            T, TMP, RB, PT = tin[i2], rtmp[i2], rb[i2], ptr[i2]
            r0, r1 = t * 128, (t + 1) * 128
            S.dma("sp", T[:], proj_dt.ap[r0:r1, C_SQ:C_SQ + 384],
                  reads=proj_dt.bufs(r0, r1), writes=[T])
            x1, x2, cb, sb_, tv, d4 = rope_tiles(cx, S, RB[:], T[:, 0:320], 5, cos_sb[:, t, :], sin_sb[:, t, :], TMP)
            S.op("dve", lambda h: h.tensor_tensor(tv[:, 0], x1, cb, ALU.mult), [T, cos_sb], [TMP])
            S.op("pool", lambda h: h.tensor_tensor(tv[:, 1], x2, sb_, ALU.mult), [T, sin_sb], [TMP])
            S.op("dve", lambda h: h.tensor_tensor(tv[:, 2], x2, cb, ALU.mult), [T, cos_sb], [TMP])
            S.op("pool", lambda h: h.tensor_tensor(tv[:, 3], x1, sb_, ALU.mult), [T, sin_sb], [TMP])
            S.op("dve", lambda h: h.tensor_tensor(d4[:, :, 0, :], tv[:, 0], tv[:, 1], ALU.subtract), [TMP], [RB])
            S.op("dve", lambda h: h.tensor_tensor(d4[:, :, 1, :], tv[:, 2], tv[:, 3], ALU.add), [TMP], [RB])
            S.op("pool", lambda h: h.tensor_copy(Vaug[:, t, 0:64], T[:, 320:384]), [T], [Vaug])

            def tr(h, RB=RB, PT=PT):
                for hh in range(5):
                    ins = h.transpose(PT[:, hh, :], RB[:, hh * 64:(hh + 1) * 64], ident_bf[:])
                return ins
            S.op("pe", tr, [RB, ident_bf], [PT])
            S.op("act", lambda h: h.copy(QKT[:, :, r0:r1], PT[:, 0:5, :]), [PT], [QKT])
        sg = [cx.sb(es, f"swa_sg{i}", [128, 256], F32) for i in range(2)]
        st = [cx.sb(es, f"swa_st{i}", [128, 4, 128], F32) for i in range(2)]
        pT = [cx.sb(es, f"swa_pT{i}", [128, 4, 128], BF16) for i in range(4)]
        den = [cx.sb(es, f"swa_den{i}", [128, 4], F32) for i in range(2)]
        yb = [cx.sb(es, f"swa_y{i}", [128, 256], F32) for i in range(2)]
        pss = [cx.ps(es, f"swa_pss{i}", [128, 4, 128], F32) for i in range(2)]
        pso = [cx.ps(es, f"swa_pso{i}", [128, 4, 128], F32) for i in range(2)]
        sge = [cx.sb(es, f"swa_sge{i}", [128, 256], F32) for i in range(2)]
        items = [(t, kk) for t in range(NTc) for kk in ([t - 1, t] if t > 0 else [t])]

        def score(i):
            t, kk = items[i]
            r0, r1 = t * 128, (t + 1) * 128
            if kk == max(t - 1, 0):
                SG, SE = sg[t % 2], sge[t % 2]
                S.dma("sp", SG[:], proj_dt.ap[r0:r1, C_SG:C_SG + 256], reads=proj_dt.bufs(r0, r1), writes=[SG])
                S.op("act", lambda h: h.activation(out=SG[:], in_=SG[:], func=AF.Silu), [SG], [SG])
            PS = pss[i % 2]
            S.op("pe", lambda h: h.matmul(PS[:], QKT[:, 4, kk * 128:(kk + 1) * 128], QKT[:, 0:4, r0:r1],
                                          start=True, stop=True), [QKT], [PS])

        def finish_item(i):
            t, kk = items[i]
            r0, r1 = t * 128, (t + 1) * 128
            PS, ST, PTB, PO = pss[i % 2], st[i % 2], pT[i % 4], pso[t % 2]
            M = mown if kk == t else mprev
            S.op("dve", lambda h: h.scalar_tensor_tensor(ST[:], PS[:], 0.125, M[:], ALU.mult, ALU.add), [PS, M], [ST])
            S.op("act", lambda h: h.activation(out=PTB[:], in_=ST[:], func=AF.Exp), [ST], [PTB])
            first = (kk == max(t - 1, 0))

            def pv(h):
                for hh in range(4):
                    ins = h.matmul(PO[:, hh, 0:65], PTB[:, hh, :], Vaug[:, kk, :],
                                   start=(first and hh == 0), stop=(kk == t and hh == 3))
                return ins
            S.op("pe", pv, [PTB, Vaug], [PO])
            if kk == t:
                SG, DEN, Y = sg[t % 2], den[t % 2], yb[t % 2]
                S.op("dve", lambda h: h.tensor_tensor(DEN[:], PO[:, :, 64], esink[:], ALU.add), [PO, esink], [DEN])
                S.op("dve", lambda h: h.reciprocal(DEN[:], DEN[:]), [DEN], [DEN])
                for hh in range(4):
                    S.op("act", lambda h: h.activation(out=Y[:, hh * 64:(hh + 1) * 64], in_=PO[:, hh, 0:64], func=AF.Copy,
                                                       scale=DEN[:, hh:hh + 1]), [PO, DEN], [Y])
                S.op("pool", lambda h: h.tensor_tensor(Y[:], Y[:], SG[:], ALU.mult), [Y, SG], [Y])
                S.dma("pool", mix_dt.ap[r0:r1, mcol + 512:mcol + 768], Y[:], reads=[Y], writes=mix_dt.bufs(r0, r1))

        score(0)
        for i in range(len(items)):
            if i + 1 < len(items):
                score(i + 1)
            finish_item(i)
        S.barrier()


def phase_moba(cx, proj_dt, mix_dt, cos_sb, sin_sb, ident_bf, ident_f, mask_own_ap, gmask_ap, blkind_ap, mcol=0):
    nc, S = cx.nc, cx.S
    Lc, NTc = cx.L, cx.L // 128
    NB = 16
    with ExitStack() as es:
        Q32 = cx.sb(es, "mo_Q32", [128, NTc, 256], F32)
        KaugT = cx.sb(es, "mo_KaugT", [80, 4, Lc], BF16)
        Vaug = cx.sb(es, "mo_V", [128, NTc, 4, 65], BF16)
        kmeanT = cx.sb(es, "mo_kmT", [64, 4, NB], F32)
        mown = cx.sb(es, "mo_mown", [128, 4, 128], F32)
        gmask = cx.sb(es, "mo_gmask", [128, NB, NB], F32)
        c256 = cx.sb(es, "mo_c256", [128, 1], F32)
        S.dma("sp", mown[:], mask_own_ap, writes=[mown])
        S.dma("sp", gmask[:], gmask_ap, writes=[gmask])
        for hh in range(4):
            S.dma("sp", KaugT[64:80, hh, :], blkind_ap[:, 0:Lc], writes=[KaugT])
        S.op("pool", lambda h: h.memset(c256[:], 1.0 / 256.0), [], [c256])
        S.op("pool", lambda h: h.memset(Vaug[:, :, :, 64:65], 1.0), [], [Vaug])
        S.op("pool", lambda h: h.memset(kmeanT[:], 0.0), [], [kmeanT])
        tin = [cx.sb(es, f"mo_tin{i}", [128, 768], F32) for i in range(2)]
        rtmp = [cx.sb(es, f"mo_rtmp{i}", [128, 4, 256], F32) for i in range(2)]
        k32 = [cx.sb(es, f"mo_k32{i}", [128, 256], F32) for i in range(2)]
        kb = [cx.sb(es, f"mo_kb{i}", [128, 256], BF16) for i in range(2)]
        with ExitStack() as es1:
            ptr = [cx.ps(es1, f"mo_ptr{i}", [64, 8, 128], BF16) for i in range(2)]
            kmps = cx.ps(es1, "mo_kmps", [64, 4, 128], F32)
            for t in range(NTc):
                i2 = t % 2
                T, TMP, K32, KB, PT = tin[i2], rtmp[i2], k32[i2], kb[i2], ptr[i2]
                r0, r1 = t * 128, (t + 1) * 128
                S.dma("sp", T[:], proj_dt.ap[r0:r1, C_MQ:C_MQ + 768],
                      reads=proj_dt.bufs(r0, r1), writes=[T])
                s4 = T[:, 0:512].rearrange("p (h two d) -> p h two d", two=2, d=32)
                x1, x2 = s4[:, :, 0, :], s4[:, :, 1, :]
                cb = cos_sb[:, t, :].unsqueeze(1).broadcast_to([128, 8, 32])
                sb_ = sin_sb[:, t, :].unsqueeze(1).broadcast_to([128, 8, 32])
                tv = TMP[:].rearrange("p f (h d) -> p f h d", d=32)
                S.op("dve", lambda h: h.tensor_tensor(tv[:, 0], x1, cb, ALU.mult), [T, cos_sb], [TMP])
                S.op("pool", lambda h: h.tensor_tensor(tv[:, 1], x2, sb_, ALU.mult), [T, sin_sb], [TMP])
                S.op("dve", lambda h: h.tensor_tensor(tv[:, 2], x2, cb, ALU.mult), [T, cos_sb], [TMP])
                S.op("pool", lambda h: h.tensor_tensor(tv[:, 3], x1, sb_, ALU.mult), [T, sin_sb], [TMP])
                q4 = Q32[:, t, :].rearrange("p (h two d) -> p h two d", two=2, d=32)
                k4 = K32[:].rearrange("p (h two d) -> p h two d", two=2, d=32)
                S.op("dve", lambda h: h.tensor_tensor(q4[:, :, 0, :], tv[:, 0, 0:4], tv[:, 1, 0:4], ALU.subtract), [TMP], [Q32])
                S.op("dve", lambda h: h.tensor_tensor(q4[:, :, 1, :], tv[:, 2, 0:4], tv[:, 3, 0:4], ALU.add), [TMP], [Q32])
                S.op("pool", lambda h: h.tensor_tensor(k4[:, :, 0, :], tv[:, 0, 4:8], tv[:, 1, 4:8], ALU.subtract), [TMP], [K32])
                S.op("pool", lambda h: h.tensor_tensor(k4[:, :, 1, :], tv[:, 2, 4:8], tv[:, 3, 4:8], ALU.add), [TMP], [K32])
                S.op("act", lambda h: h.copy(KB[:], K32[:]), [K32], [KB])
                S.op("pool", lambda h: h.tensor_copy(Vaug[:, t, :, 0:64], T[:, 512:768].rearrange("p (h d) -> p h d", d=64)),
                     [T], [Vaug])
                n = t // 2

                def km(h, K32=K32, n=n, t=t):
                    for hh in range(4):
                        ins = h.matmul(kmps[:, hh, n:n + 1], K32[:, hh * 64:(hh + 1) * 64], c256[:, 0:1],
                                       start=(t % 2 == 0 and hh == 0), stop=(t % 2 == 1 and hh == 3))
                    return ins
                S.op("pe", km, [K32, c256], [kmps])

                def tr(h, KB=KB, PT=PT):
                    for hh in range(4):
                        ins = h.transpose(PT[:, hh, :], KB[:, hh * 64:(hh + 1) * 64], ident_bf[:])
                    return ins
                S.op("pe", tr, [KB, ident_bf], [PT])
                S.op("act", lambda h: h.copy(KaugT[0:64, :, r0:r1], PT[:, 0:4, :]), [PT], [KaugT])
            S.op("dve", lambda h: h.tensor_copy(kmeanT[:, :, 0:Lc // 256], kmps[:, :, 0:Lc // 256]), [kmps], [kmeanT])
            S.barrier()
        mg = [cx.sb(es, f"mo_mg{i}", [128, 256], F32) for i in range(2)]
        mge = [cx.sb(es, f"mo_mge{i}", [128, 256], F32) for i in range(2)]
        qT32 = [cx.sb(es, f"mo_qT32{i}", [64, 4, 128], F32) for i in range(2)]
        gm = [cx.sb(es, f"mo_gm{i}", [128, 4, NB], F32) for i in range(2)]
        top8 = [cx.sb(es, f"mo_top8{i}", [128, 4, 8], F32) for i in range(2)]
        qaug = [cx.sb(es, f"mo_qaug{i}", [128, 4, 80], BF16) for i in range(2)]
        QaugT = [cx.sb(es, f"mo_QaugT{i}", [80, 4, 128], BF16) for i in range(2)]
        st = [cx.sb(es, f"mo_st{i}", [128, 4, 128], F32) for i in range(2)]
        pT = [cx.sb(es, f"mo_pT{i}", [128, 4, 128], BF16) for i in range(4)]
        den = [cx.sb(es, f"mo_den{i}", [128, 4], F32) for i in range(2)]
        yb = [cx.sb(es, f"mo_y{i}", [128, 256], F32) for i in range(2)]
        psq = cx.ps(es, "mo_psq", [64, 4, 128], F32)
        psg = cx.ps(es, "mo_psg", [128, 4, 128], F32)
        psa = cx.ps(es, "mo_psa", [80, 8, 128], BF16)
        pss = [cx.ps(es, f"mo_pss{i}", [128, 4, 128], F32) for i in range(2)]
        pso = [cx.ps(es, f"mo_pso{i}", [128, 4, 128], F32) for i in range(2)]
        ctxs = {}

        def prologue(t):
            i2 = t % 2
            r0, r1 = t * 128, (t + 1) * 128
            qb = t // 2
            MG, QT, GM, T8, QA, QAT = mg[i2], qT32[i2], gm[i2], top8[i2], qaug[i2], QaugT[i2]
            S.dma("sp", MG[:], proj_dt.ap[r0:r1, C_MG:C_MG + 256], reads=proj_dt.bufs(r0, r1), writes=[MG])
            S.op("act", lambda h: h.activation(out=MG[:], in_=MG[:], func=AF.Silu), [MG], [MG])

            def trq(h):
                for hh in range(4):
                    ins = h.transpose(psq[:, hh, :], Q32[:, t, hh * 64:(hh + 1) * 64], ident_f[:])
                return ins
            S.op("pe", trq, [Q32, ident_f], [psq])
            S.op("dve", lambda h: h.tensor_copy(QT[:], psq[:]), [psq], [QT])

            def gate(h):
                for hh in range(4):
                    ins = h.matmul(psg[:, hh, 0:NB], QT[:, hh, :], kmeanT[:, hh, :], start=True, stop=True)
                return ins
            S.op("pe", gate, [QT, kmeanT], [psg])
            S.op("dve", lambda h: h.tensor_tensor(GM[:], psg[:, :, 0:NB],
                                                  gmask[:, qb, :].unsqueeze(1).broadcast_to([128, 4, NB]), ALU.add),
                 [psg, gmask], [GM])
            for hh in range(4):
                S.op("dve", lambda h: h.max(T8[:, hh, :], GM[:, hh, :]), [GM], [T8])
            for hh in range(4):
                S.op("dve", lambda h: h.tensor_scalar(QA[:, hh, 64:80], GM[:, hh, :], T8[:, hh, 2:3], -30000.0,
                                                      ALU.is_lt, ALU.mult), [GM, T8], [QA])
            S.op("pool", lambda h: h.memset(QA[:, :, 64 + qb:65 + qb], 0.0), [QA], [QA])
            S.op("act", lambda h: h.copy(QA[:, :, 0:64], Q32[:, t, :].rearrange("p (h d) -> p h d", d=64)), [Q32], [QA])

            def tra(h):
                for hh in range(4):
                    ins = h.transpose(psa[:, hh, :], QA[:, hh, :], ident_bf[:])
                return ins
            S.op("pe", tra, [QA, ident_bf], [psa])
            S.op("dve", lambda h: h.tensor_copy(QAT[:], psa[:, 0:4, :]), [psa], [QAT])

        items = [(t, kk) for t in range(NTc) for kk in range(t + 1)]

        def score(i):
            t, kk = items[i]
            if kk == 0:
                prologue(t)
            PS, QAT = pss[i % 2], QaugT[t % 2]

            def sc(h):
                for hh in range(4):
                    ins = h.matmul(PS[:, hh, :], KaugT[:, hh, kk * 128:(kk + 1) * 128], QAT[:, hh, :],
                                   start=True, stop=True)
                return ins
            S.op("pe", sc, [KaugT, QAT], [PS])

        def finish_item(i):
            t, kk = items[i]
            PS, ST, PTB, PO = pss[i % 2], st[i % 2], pT[i % 4], pso[t % 2]
            if kk == t:
                S.op("dve", lambda h: h.scalar_tensor_tensor(ST[:], PS[:], 0.125, mown[:], ALU.mult, ALU.add),
                     [PS, mown], [ST])
                S.op("act", lambda h: h.activation(out=PTB[:], in_=ST[:], func=AF.Exp), [ST], [PTB])
            else:
                S.op("act", lambda h: h.activation(out=PTB[:], in_=PS[:], func=AF.Exp, scale=0.125), [PS], [PTB])

            def pv(h):
                for hh in range(4):
                    ins = h.matmul(PO[:, hh, 0:65], PTB[:, hh, :], Vaug[:, kk, hh, :],
                                   start=(kk == 0 and hh == 0), stop=(kk == t and hh == 3))
                return ins
            S.op("pe", pv, [PTB, Vaug], [PO])
            if kk == t:
                i2 = t % 2
                r0, r1 = t * 128, (t + 1) * 128
                MG, DEN, Y = mg[i2], den[i2], yb[i2]
                S.op("dve", lambda h: h.reciprocal(DEN[:], PO[:, :, 64]), [PO], [DEN])
                for hh in range(4):
                    S.op("act", lambda h: h.activation(out=Y[:, hh * 64:(hh + 1) * 64], in_=PO[:, hh, 0:64], func=AF.Copy,
                                                       scale=DEN[:, hh:hh + 1]), [PO, DEN], [Y])
                S.op("pool", lambda h: h.tensor_tensor(Y[:], Y[:], MG[:], ALU.mult), [Y, MG], [Y])
                S.dma("pool", mix_dt.ap[r0:r1, mcol:mcol + 256], Y[:], reads=[Y], writes=mix_dt.bufs(r0, r1))

        score(0)
        for i in range(len(items)):
            if i + 1 < len(items):
                score(i + 1)
            finish_item(i)
        S.barrier()


def phase_ssd(cx, proj_dt, mix_dt, ident_bf, cst, mcol=0):
    nc, S = cx.nc, cx.S
    Lc, NTc = cx.L, cx.L // 128
    with ExitStack() as es:
        def ld(name, shape, dt=F32):
            b = cx.sb(es, "ssd_" + name, shape, dt)
            S.dma("sp", b[:], cst[name], writes=[b])
            return b
        convw = ld("convw", [128, 4, 512]); convb = ld("convb", [128, 512]); dtb = ld("dtb", [128, 4])
        Abc = ld("alog", [128, 4]); dskip = ld("dskip", [128, 256]); normw = ld("normw", [128, 256])
        tri = ld("tri", [128, 128]); ones = ld("ones", [128, 128]); maskT = ld("maskT", [128, 128])
        S.op("act", lambda h: h.activation(out=Abc[:], in_=Abc[:], func=AF.Exp), [Abc], [Abc])
        S.op("dve", lambda h: h.tensor_scalar(Abc[:], Abc[:], -1.0, None, ALU.mult), [Abc], [Abc])
        prev32 = cx.sb(es, "ssd_prev32", [128, 256], F32)
        prevb = cx.sb(es, "ssd_prevb", [128, 256], BF16)
        S.op("pool", lambda h: h.memset(prev32[:], 0.0), [], [prev32])
        S.op("pool", lambda h: h.memset(prevb[:], 0.0), [], [prevb])
        Tj = [[cx.sb(es, f"ssd_T{i}_{j}", [128, 512], F32) for j in range(4)] for i in range(2)]
        zt = [cx.sb(es, f"ssd_z{i}", [128, 256], F32) for i in range(2)]
        dtt = [cx.sb(es, f"ssd_dt{i}", [128, 4], F32) for i in range(2)]
        xa = [cx.sb(es, f"ssd_xa{i}", [128, 512], F32) for i in range(2)]
        sm = [cx.sb(es, f"ssd_sm{i}", [128, 8, 4], F32) for i in range(2)]
        Xf = [cx.sb(es, f"ssd_X{i}", [128, 256], F32) for i in range(2)]
        Xb = [cx.sb(es, f"ssd_Xb{i}", [128, 256], BF16) for i in range(2)]
        Xd = [cx.sb(es, f"ssd_Xd{i}", [128, 256], BF16) for i in range(2)]
        BCb = [cx.sb(es, f"ssd_BCb{i}", [128, 256], BF16) for i in range(2)]
        BCT = [cx.sb(es, f"ssd_BCT{i}", [128, 2, 128], BF16) for i in range(2)]
        R4 = [cx.sb(es, f"ssd_R4{i}", [128, 4, 128], F32) for i in range(2)]
        TD4 = [cx.sb(es, f"ssd_TD4{i}", [128, 4, 128], F32) for i in range(2)]
        M4 = [cx.sb(es, f"ssd_M4{i}", [128, 4, 128], BF16) for i in range(2)]
        identf = ld("identf", [128, 128])
        mask4 = cx.sb(es, "ssd_mask4", [128, 4, 128], F32)
        S.op("dve", lambda h: h.tensor_copy(mask4[:], maskT[:].unsqueeze(1).broadcast_to([128, 4, 128])), [maskT], [mask4])
        y1 = [cx.sb(es, f"ssd_y1{i}", [128, 256], F32) for i in range(2)]
        y2 = [cx.sb(es, f"ssd_y2{i}", [128, 256], F32) for i in range(2)]
        junk = cx.sb(es, "ssd_junk", [128, 256], F32)
        csb = [cx.sb(es, f"ssd_cs{i}", [128, 256], F32) for i in range(2)]
        ps_t = cx.ps(es, "ssd_ps_t", [128, 8, 128], BF16)
        ps_s = cx.ps(es, "ssd_ps_s", [128, 512], F32)
        ps_cb = cx.ps(es, "ssd_ps_cb", [128, 512], F32)
        ps_d = cx.ps(es, "ssd_ps_d", [128, 4, 128], F32)
        ps_yd = cx.ps(es, "ssd_ps_yd", [128, 512], F32)
        ps_yo = cx.ps(es, "ssd_ps_yo", [128, 512], F32)
        ps_cs = cx.ps(es, "ssd_ps_cs", [128, 512], F32)
        ndc = [0]

        def front(t):
            nd = ndc[0]
            i2 = t % 2
            r0, r1 = t * 128, (t + 1) * 128
            T, Z, DT, XA, SM, X, XB, XD, BC, BT = Tj[i2], zt[i2], dtt[i2], xa[i2], sm[i2], Xf[i2], Xb[i2], Xd[i2], BCb[i2], BCT[i2]
            for j in range(4):
                sh = 3 - j
                q = "sp"
                if r0 - sh >= 0:
                    S.dma(q, T[j][:], proj_dt.ap[r0 - sh:r1 - sh, C_XS:C_XS + 512],
                          reads=proj_dt.bufs(max(r0 - sh, 0), r1), writes=[T[j]])
                else:
                    S.op("pool", lambda h: h.memset(T[j][0:32, :], 0.0), [], [T[j]])
                    S.dma(q, T[j][sh:128, :], proj_dt.ap[0:128 - sh, C_XS:C_XS + 512],
                          reads=proj_dt.bufs(0, 128), writes=[T[j]])
            S.dma("sp", Z[:], proj_dt.ap[r0:r1, C_Z:C_Z + 256], reads=proj_dt.bufs(r0, r1), writes=[Z])
            S.dma("sp", DT[:], proj_dt.ap[r0:r1, C_DT:C_DT + 4], reads=proj_dt.bufs(r0, r1), writes=[DT])
            for j in range(4):
                ek = "dve" if j % 2 == 0 else "pool"
                S.op(ek, lambda h: h.tensor_tensor(T[j][:], T[j][:], convw[:, j, :], ALU.mult), [T[j], convw], [T[j]])
            S.op("dve", lambda h: h.tensor_tensor(T[0][:], T[0][:], T[2][:], ALU.add), [T[0], T[2]], [T[0]])
            S.op("pool", lambda h: h.tensor_tensor(T[1][:], T[1][:], T[3][:], ALU.add), [T[1], T[3]], [T[1]])
            S.op("dve", lambda h: h.tensor_tensor(T[0][:], T[0][:], T[1][:], ALU.add), [T[0], T[1]], [T[0]])
            S.op("pool", lambda h: h.tensor_tensor(T[0][:], T[0][:], convb[:], ALU.add), [T[0], convb], [T[0]])
            S.op("act", lambda h: h.activation(out=XA[:], in_=T[0][:], func=AF.Silu), [T[0]], [XA])
            S.op("act", lambda h: h.activation(out=Z[:], in_=Z[:], func=AF.Silu), [Z], [Z])
            S.op("dve", lambda h: h.tensor_tensor(DT[:], DT[:], dtb[:], ALU.add), [DT, dtb], [DT])
            S.op("act", lambda h: h.activation(out=DT[:], in_=DT[:], func=AF.Exp), [DT], [DT])
            S.op("act", lambda h: h.activation(out=DT[:], in_=DT[:], func=AF.Ln, bias=1.0), [DT], [DT])
            S.op("dve", lambda h: h.tensor_tensor(SM[:, 0, :], DT[:], Abc[:], ALU.mult), [DT, Abc], [SM])

            def cum(h, SM=SM):
                h.matmul(ps_s[:, 0:4], tri[:], SM[:, 0, :], start=True, stop=False)
                return h.matmul(ps_s[:, 4:8], ones[:], SM[:, 0, :], start=False, stop=True)
            S.op("pe", cum, [tri, ones, SM], [ps_s])
            S.op("dve", lambda h: h.tensor_copy(SM[:, 1, :], ps_s[:, 0:4]), [ps_s], [SM])
            S.op("dve", lambda h: h.tensor_scalar(SM[:, 2, :], ps_s[:, 0:4], -1.0, None, ALU.mult), [ps_s], [SM])
            S.op("dve", lambda h: h.tensor_tensor(SM[:, 5, :], ps_s[:, 4:8], SM[:, 1, :], ALU.subtract), [ps_s, SM], [SM])
            S.op("act", lambda h: h.activation(out=SM[:, 3, :], in_=SM[:, 1, :], func=AF.Exp), [SM], [SM])
            S.op("act", lambda h: h.activation(out=SM[:, 4, :], in_=ps_s[:, 4:8], func=AF.Exp), [ps_s], [SM])
            S.op("act", lambda h: h.activation(out=SM[:, 5, :], in_=SM[:, 5, :], func=AF.Exp), [SM], [SM])
            x3 = X[:].rearrange("p (h d) -> p h d", d=64)
            S.op("dve", lambda h: h.tensor_tensor(x3, XA[:, 0:256].rearrange("p (h d) -> p h d", d=64),
                                                  DT[:].unsqueeze(2).broadcast_to([128, 4, 64]), ALU.mult), [XA, DT], [X])
            S.op("pool", lambda h: h.tensor_copy(XB[:], X[:]), [X], [XB])
            S.op("dve", lambda h: h.tensor_tensor(XD[:].rearrange("p (h d) -> p h d", d=64), x3,
                                                  SM[:, 5, :].unsqueeze(2).broadcast_to([128, 4, 64]), ALU.mult), [X, SM], [XD])
            S.op("pool", lambda h: h.tensor_copy(BC[:], XA[:, 256:512]), [XA], [BC])

            def trbc(h, BC=BC):
                h.transpose(ps_t[:, 0, :], BC[:, 0:128], ident_bf[:])
                return h.transpose(ps_t[:, 1, :], BC[:, 128:256], ident_bf[:])
            S.op("pe", trbc, [BC, ident_bf], [ps_t])
            S.op("act", lambda h: h.copy(BT[:], ps_t[:, 0:2, :]), [ps_t], [BT])
            S.op("pe", lambda h: h.matmul(ps_cb[:, 0:128], BT[:, 0, :], BT[:, 1, :], start=True, stop=True), [BT], [ps_cb])
            S.op("pe", lambda h: h.matmul(ps_cs[:, 0:256], BC[:, 0:128], XD[:], start=True, stop=True), [BC, XD], [ps_cs])
            S.op("act", lambda h: h.copy(csb[i2][:], ps_cs[:, 0:256]), [ps_cs], [csb[i2]])
            RR, TD, MM = R4[i2], TD4[i2], M4[i2]
            S.op("dve", lambda h: h.tensor_tensor(RR[:], tri[:].unsqueeze(1).broadcast_to([128, 4, 128]),
                                                  SM[:, 0, :].unsqueeze(2).broadcast_to([128, 4, 128]), ALU.mult), [tri, SM], [RR])

            def dbc(h):
                h.matmul(ps_d[:], ones[:], RR[:], start=True, stop=False)
                return h.matmul(ps_d[:], identf[:], mask4[:], start=False, stop=True)
            S.op("pe", dbc, [ones, RR, identf, mask4], [ps_d])
            S.op("dve", lambda h: h.tensor_tensor(TD[:], ps_d[:], SM[:, 1, :].unsqueeze(2).broadcast_to([128, 4, 128]), ALU.subtract),
                 [ps_d, SM], [TD])
            S.op("act", lambda h: h.activation(out=TD[:], in_=TD[:], func=AF.Exp), [TD], [TD])
            S.op("dve", lambda h: h.tensor_tensor(MM[:], TD[:], ps_cb[:, 0:128].unsqueeze(1).broadcast_to([128, 4, 128]), ALU.mult),
                 [TD, ps_cb], [MM])

            def ydiag(h):
                for hh in range(4):
                    ins = h.matmul(ps_yd[:, hh * 64:(hh + 1) * 64], MM[:, hh, :], XB[:, hh * 64:(hh + 1) * 64],
                                   start=(hh == 0), stop=(hh == 3))
                return ins
            S.op("pe", ydiag, [MM, XB], [ps_yd])
            ndc[0] = nd
            S.op("act", lambda h: h.copy(y1[i2][:], ps_yd[:, 0:256]), [ps_yd], [y1[i2]])

        def back(t):
            i2 = t % 2
            r0, r1 = t * 128, (t + 1) * 128
            Z, XA, SM, BT = zt[i2], xa[i2], sm[i2], BCT[i2]
            Y1, Y2 = y1[i2], y2[i2]
            S.op("pe", lambda h: h.matmul(ps_yo[:, 0:256], BT[:, 1, :], prevb[:], start=True, stop=True), [BT, prevb], [ps_yo])
            p3 = prev32[:].rearrange("p (h d) -> p h d", d=64)
            S.op("dve", lambda h: h.tensor_tensor(p3, p3, SM[:, 4, :].unsqueeze(2).broadcast_to([128, 4, 64]), ALU.mult),
                 [prev32, SM], [prev32])
            S.op("dve", lambda h: h.tensor_tensor(prev32[:], prev32[:], csb[i2][:], ALU.add), [prev32, csb[i2]], [prev32])
            S.op("act", lambda h: h.copy(prevb[:], prev32[:]), [prev32], [prevb])
            for hh in range(4):
                sl = slice(hh * 64, (hh + 1) * 64)
                S.op("dve", lambda h: h.scalar_tensor_tensor(Y1[:, sl], ps_yo[:, sl], SM[:, 3, hh:hh + 1], Y1[:, sl],
                                                             ALU.mult, ALU.add), [ps_yo, SM, Y1], [Y1])
            S.op("pool", lambda h: h.tensor_tensor(Y2[:], XA[:, 0:256], dskip[:], ALU.mult), [XA, dskip], [Y2])
            S.op("pool", lambda h: h.tensor_tensor(Y1[:], Y1[:], Y2[:], ALU.add), [Y1, Y2], [Y1])
            S.op("pool", lambda h: h.tensor_tensor(Y1[:], Y1[:], Z[:], ALU.mult), [Y1, Z], [Y1])
            S.op("act", lambda h: h.activation(out=junk[:], in_=Y1[:], func=AF.Square, accum_out=SM[:, 6, 0:1]), [Y1], [junk, SM])
            S.op("dve", lambda h: h.tensor_scalar(SM[:, 6, 0:1], SM[:, 6, 0:1], 1.0 / 256.0, EPS, ALU.mult, ALU.add), [SM], [SM])
            S.op("act", lambda h: h.sqrt(SM[:, 6, 0:1], SM[:, 6, 0:1]), [SM], [SM])
            S.op("dve", lambda h: h.reciprocal(SM[:, 6, 0:1], SM[:, 6, 0:1]), [SM], [SM])
            S.op("dve", lambda h: h.scalar_tensor_tensor(Y2[:], Y1[:], SM[:, 6, 0:1], normw[:], ALU.mult, ALU.mult),
                 [Y1, SM, normw], [Y2])
            S.dma("pool", mix_dt.ap[r0:r1, mcol + 256:mcol + 512], Y2[:], reads=[Y2], writes=mix_dt.bufs(r0, r1))

        front(0)
        for t in range(NTc):
            if t + 1 < NTc:
                front(t + 1)
            back(t)
        S.barrier()


class StopPhase(Exception):
    pass


def stage(cx, n):
    if getattr(cx, "stop_stage", None) == n and not cx.S.muted:
        cx.S.barrier()
        cx.S.muted = True


def cmul(S, ek, out_re, out_im, a_re, a_im, b_re, b_im, t1, t2, reads, writes, conj_b=False):
    o1 = ALU.subtract if not conj_b else ALU.add
    o2 = ALU.add if not conj_b else ALU.subtract
    S.op(ek, lambda h: h.tensor_tensor(t1, a_re, b_re, ALU.mult), reads, writes)
    S.op(ek, lambda h: h.tensor_tensor(t2, a_im, b_im, ALU.mult), reads, writes)
    S.op(ek, lambda h: h.tensor_tensor(out_re, t1, t2, o1), reads, writes)
    S.op(ek, lambda h: h.tensor_tensor(t1, a_im, b_re, ALU.mult), reads, writes)
    S.op(ek, lambda h: h.tensor_tensor(t2, a_re, b_im, ALU.mult), reads, writes)
    S.op(ek, lambda h: h.tensor_tensor(out_im, t1, t2, o2), reads, writes)


def phase_s5(cx, proj_dt, mix_dt, ident_bf, ident_f, cst):
    nc, S = cx.nc, cx.S
    Lc = cx.L
    T = 16
    SEG = min(Lc, 2048)
    NSEG = Lc // SEG
    NC = SEG // T
    NCT = NC + 1
    with ExitStack() as es:
        def ld(name, shape, dt=F32, q="sp"):
            b = cx.sb(es, "s5_" + name, shape, dt)
            S.dma(q, b[:], cst[name], writes=[b])
            return b
        are = ld("are", [128, 16]); aim = ld("aim", [128, 16]); ldt = ld("ldt", [128, 16])
        ccre = ld("ccre", [128, 16, 32]); ccim = ld("ccim", [128, 16, 32])
        dfm = ld("dfm", [128, 4]); glub = ld("glub", [128, 4]); kvec = ld("kvec", [128, 256])
        Wg = load_weight_bf16(cx, es, "s5_Wg", cst["gluw"], 512, 512, None)
        BT = [cx.sb(es, f"s5_BT{i}", [128, 16, 128], BF16) for i in range(2)]
        Ere = cx.sb(es, "s5_Ere", [128, 16, T + 1], F32); Eim = cx.sb(es, "s5_Eim", [128, 16, T + 1], F32)
        Rk = cx.sb(es, "s5_Rk", [128, 16, T + 1], F32)
        E2re = cx.sb(es, "s5_E2re", [128, 16, NCT], F32); E2im = cx.sb(es, "s5_E2im", [128, 16, NCT], F32)
        R2 = cx.sb(es, "s5_R2", [128, 16, NCT], F32)
        sm = cx.sb(es, "s5_sm", [128, 12, 16], F32)
        pmax = 1
        while pmax * 2 < max(T + 1, NCT):
            pmax *= 2
        Enim = cx.sb(es, "s5_Enim", [128, 16, T + 1], F32)
        hp = cx.sb(es, "s5_halfpi", [128, 1], F32)
        es_tb = ExitStack()
        tb = cx.sb(es_tb, "s5_tb", [128, 4, 16 + 16 * pmax], F32)
        SMALL = [sm]
        S.op("act", lambda h: h.activation(out=sm[:, 0, :], in_=ldt[:], func=AF.Exp), [ldt], SMALL)
        S.op("dve", lambda h: h.tensor_tensor(sm[:, 1, :], are[:], sm[:, 0, :], ALU.mult), [are] + SMALL, SMALL)
        S.op("dve", lambda h: h.tensor_tensor(sm[:, 2, :], aim[:], sm[:, 0, :], ALU.mult), [aim] + SMALL, SMALL)
        S.op("act", lambda h: h.activation(out=sm[:, 3, :], in_=sm[:, 1, :], func=AF.Exp), SMALL, SMALL)
        S.op("pool", lambda h: h.memset(hp[:], float(np.pi / 2)), [], [hp])
        S.op("act", lambda h: h.activation(out=sm[:, 5, :], in_=sm[:, 2, :], func=AF.Sin, scale=1.0 / 64), SMALL, SMALL)
        S.op("act", lambda h: h.activation(out=sm[:, 4, :], in_=sm[:, 2, :], func=AF.Sin, scale=1.0 / 64, bias=hp[:, 0:1]),
             SMALL + [hp], SMALL)
        for _ in range(6):
            S.op("dve", lambda h: h.tensor_tensor(sm[:, 8, :], sm[:, 4, :], sm[:, 4, :], ALU.mult), SMALL, SMALL)
            S.op("dve", lambda h: h.tensor_tensor(sm[:, 9, :], sm[:, 5, :], sm[:, 5, :], ALU.mult), SMALL, SMALL)
            S.op("dve", lambda h: h.tensor_tensor(sm[:, 10, :], sm[:, 4, :], sm[:, 5, :], ALU.mult), SMALL, SMALL)
            S.op("dve", lambda h: h.tensor_tensor(sm[:, 4, :], sm[:, 8, :], sm[:, 9, :], ALU.subtract), SMALL, SMALL)
            S.op("dve", lambda h: h.tensor_scalar(sm[:, 5, :], sm[:, 10, :], 2.0, None, ALU.mult), SMALL, SMALL)
        S.op("dve", lambda h: h.tensor_tensor(sm[:, 8, :], sm[:, 3, :], sm[:, 4, :], ALU.mult), SMALL, SMALL)
        S.op("dve", lambda h: h.tensor_tensor(sm[:, 9, :], sm[:, 3, :], sm[:, 5, :], ALU.mult), SMALL, SMALL)
        S.op("dve", lambda h: h.tensor_scalar(sm[:, 8, :], sm[:, 8, :], -1.0, None, ALU.add), SMALL, SMALL)
        S.op("dve", lambda h: h.tensor_tensor(sm[:, 10, :], are[:], are[:], ALU.mult), [are], SMALL)
        S.op("dve", lambda h: h.tensor_tensor(sm[:, 11, :], aim[:], aim[:], ALU.mult), [aim], SMALL)
        S.op("dve", lambda h: h.tensor_tensor(sm[:, 10, :], sm[:, 10, :], sm[:, 11, :], ALU.add), SMALL, SMALL)
        S.op("dve", lambda h: h.reciprocal(sm[:, 10, :], sm[:, 10, :]), SMALL, SMALL)
        S.op("dve", lambda h: h.tensor_tensor(sm[:, 6, :], sm[:, 8, :], are[:], ALU.mult), SMALL + [are], SMALL)
        S.op("dve", lambda h: h.tensor_tensor(sm[:, 11, :], sm[:, 9, :], aim[:], ALU.mult), SMALL + [aim], SMALL)
        S.op("dve", lambda h: h.tensor_tensor(sm[:, 6, :], sm[:, 6, :], sm[:, 11, :], ALU.add), SMALL, SMALL)
        S.op("dve", lambda h: h.tensor_tensor(sm[:, 7, :], sm[:, 9, :], are[:], ALU.mult), SMALL + [are], SMALL)
        S.op("dve", lambda h: h.tensor_tensor(sm[:, 11, :], sm[:, 8, :], aim[:], ALU.mult), SMALL + [aim], SMALL)
        S.op("dve", lambda h: h.tensor_tensor(sm[:, 7, :], sm[:, 7, :], sm[:, 11, :], ALU.subtract), SMALL, SMALL)
        S.op("dve", lambda h: h.tensor_tensor(sm[:, 6, :], sm[:, 6, :], sm[:, 10, :], ALU.mult), SMALL, SMALL)
        S.op("dve", lambda h: h.tensor_tensor(sm[:, 7, :], sm[:, 7, :], sm[:, 10, :], ALU.mult), SMALL, SMALL)

        def build_pow_tables(Tre, Tim, n, base_re, base_im):
            TB = [Tre, Tim, tb]
            S.op("pool", lambda h: h.memset(Tre[:, :, 0:1], 1.0), [], [Tre])
            S.op("pool", lambda h: h.memset(Tim[:, :, 0:1], 0.0), [], [Tim])
            S.op("dve", lambda h: h.tensor_copy(Tre[:, :, 1], base_re), SMALL, [Tre])
            S.op("dve", lambda h: h.tensor_copy(Tim[:, :, 1], base_im), SMALL, [Tim])
            m = 2
            while m < n:
                cnt = min(m, n - m)
                pr, pi_ = tb[:, 2, 0:16], tb[:, 3, 0:16]
                cmul(S, "dve", pr, pi_, Tre[:, :, m - 1], Tim[:, :, m - 1], Tre[:, :, 1], Tim[:, :, 1],
                     tb[:, 0, 0:16], tb[:, 1, 0:16], TB, TB)
                prb = pr.unsqueeze(2).broadcast_to([128, 16, cnt])
                pib = pi_.unsqueeze(2).broadcast_to([128, 16, cnt])
                t1 = tb[:, 0, 16:16 + 16 * cnt].rearrange("p (a b) -> p a b", b=cnt)
                t2 = tb[:, 1, 16:16 + 16 * cnt].rearrange("p (a b) -> p a b", b=cnt)
                cmul(S, "dve", Tre[:, :, m:m + cnt], Tim[:, :, m:m + cnt], Tre[:, :, 0:cnt], Tim[:, :, 0:cnt], prb, pib,
                     t1, t2, TB, TB)
                m += cnt
        build_pow_tables(Ere, Eim, T + 1, sm[:, 4, :], sm[:, 5, :])
        S.op("dve", lambda h: h.tensor_scalar(Enim[:], Eim[:], -1.0, None, ALU.mult), [Eim], [Enim])
        S.op("dve", lambda h: h.tensor_copy(sm[:, 8, :], Ere[:, :, T]), [Ere], SMALL)
        S.op("dve", lambda h: h.tensor_copy(sm[:, 9, :], Eim[:, :, T]), [Eim], SMALL)
        build_pow_tables(E2re, E2im, NCT, sm[:, 8, :], sm[:, 9, :])
        S.op("dve", lambda h: h.tensor_tensor(Rk[:], sm[:, 1, :].unsqueeze(2).broadcast_to([128, 16, T + 1]),
                                              kvec[:, 0:T + 1].unsqueeze(1).broadcast_to([128, 16, T + 1]), ALU.mult),
             SMALL + [kvec], [Rk])
        S.op("act", lambda h: h.activation(out=Rk[:], in_=Rk[:], func=AF.Exp), [Rk], [Rk])
        S.op("dve", lambda h: h.tensor_tensor(R2[:], sm[:, 1, :].unsqueeze(2).broadcast_to([128, 16, NCT]),
                                              kvec[:, 0:NCT].unsqueeze(1).broadcast_to([128, 16, NCT]), ALU.mult),
             SMALL + [kvec], [R2])
        S.op("act", lambda h: h.activation(out=R2[:], in_=R2[:], func=AF.Exp, scale=float(T)), [R2], [R2])
        S.barrier()
        es_tb.close()
        stage(cx, 1)
        with ExitStack() as es1:
            bpre = cx.sb(es1, "s5_bpre", [128, 16, 128], F32); bpim = cx.sb(es1, "s5_bpim", [128, 16, 128], F32)
            S.dma("sp", bpre[:], cst["bpre"], writes=[bpre]); S.dma("pool", bpim[:], cst["bpim"], writes=[bpim])
            t1 = cx.sb(es1, "s5_bt1", [128, 16, 128], F32); t2 = cx.sb(es1, "s5_bt2", [128, 16, 128], F32)
            bbre = cx.sb(es1, "s5_bbre", [128, 16, 128], BF16); bbim = cx.sb(es1, "s5_bbim", [128, 16, 128], BF16)
            kr = sm[:, 6, :].unsqueeze(2).broadcast_to([128, 16, 128])
            ki = sm[:, 7, :].unsqueeze(2).broadcast_to([128, 16, 128])
            cmul(S, "dve", bbre[:], bbim[:], bpre[:], bpim[:], kr, ki, t1[:], t2[:], [bpre, bpim, t1, t2] + SMALL, [bbre, bbim, t1, t2])
            pst = cx.ps(es1, "s5_pst", [128, 8, 128], BF16)
            for k in range(16):
                for ri, src in enumerate((bbre, bbim)):
                    S.op("pe", lambda h: h.transpose(pst[:, ri, :], src[:, k, :], ident_bf[:]), [src, ident_bf], [pst])
                    S.op("act", lambda h: h.copy(BT[ri][:, k, :], pst[:, ri, :]), [pst], [BT[ri]])
            S.barrier()
        stage(cx, 2)
        Send = cx.sb(es, "s5_Send", [128, 2, 16], F32)
        S.op("pool", lambda h: h.memset(Send[:], 0.0), [], [Send])
        for seg in range(NSEG):
            t00 = seg * SEG
            with ExitStack() as es2:
                y = [cx.sb(es2, f"s5_y{q}", [128, SEG], F32) for q in range(4)]
                with ExitStack() as es3:
                    uTb = [cx.sb(es3, f"s5_uTb{q}", [128, SEG], BF16) for q in range(4)]
                    with ExitStack() as es4:
                        sut = [cx.sb(es4, f"s5_sut{i}", [128, 512], F32) for i in range(2)]
                        sub = [cx.sb(es4, f"s5_sub{i}", [128, 512], BF16) for i in range(2)]
                        psu = [cx.ps(es4, f"s5_psu{i}", [128, 8, 128], BF16) for i in range(2)]
                        for tt in range(SEG // 128):
                            i2 = tt % 2
                            r0 = t00 + tt * 128
                            S.dma("sp", sut[i2][:], proj_dt.ap[r0:r0 + 128, C_SU:C_SU + 512],
                                  reads=proj_dt.bufs(r0, r0 + 128), writes=[sut[i2]])
                            S.op("pool", lambda h: h.tensor_copy(sub[i2][:], sut[i2][:]), [sut[i2]], [sub[i2]])
                            stage(cx, 21)

                            def tru(h, i2=i2):
                                for q in range(4):
                                    ins = h.transpose(psu[i2][:, q, :], sub[i2][:, q * 128:(q + 1) * 128], ident_bf[:])
                                return ins
                            S.op("pe", tru, [sub[i2], ident_bf], [psu[i2]])
                            stage(cx, 22)
                            for q in range(4):
                                ek = "act"
                                if ek == "act":
                                    S.op("act", lambda h: h.copy(uTb[q][:, tt * 128:(tt + 1) * 128], psu[i2][:, q, :]), [psu[i2]], [uTb[q]])
                                else:
                                    S.op("dve", lambda h: h.tensor_copy(uTb[q][:, tt * 128:(tt + 1) * 128], psu[i2][:, q, :]), [psu[i2]], [uTb[q]])
                                stage(cx, 230 + q)
                            stage(cx, 240 + tt)
                        S.barrier()
                    stage(cx, 3)
                    xre = cx.sb(es3, "s5_xre", [128, SEG], F32); xim = cx.sb(es3, "s5_xim", [128, SEG], F32)
                    vre2 = [cx.sb(es3, f"s5_vre{i}", [128, SEG], BF16) for i in range(2)]
                    vim2 = [cx.sb(es3, f"s5_vim{i}", [128, SEG], BF16) for i in range(2)]
                    rmask = cx.sb(es3, "s5_rmask", [128, SEG], F32)
                    ta = cx.sb(es3, "s5_ta", [128, 512], F32); tbb = cx.sb(es3, "s5_tbb", [128, 512], F32)
                    ctabs = [[cx.sb(es3, f"s5_ctab{par}_{i}", [128, T + 1, 64], BF16) for i in range(4)] for par in range(2)]
                    for par in range(2):
                        for i in range(4):
                            S.op("pool", lambda h: h.memset(ctabs[par][i][:], 0.0), [], [ctabs[par][i]])
                    ct1 = cx.sb(es3, "s5_ct1", [128, T + 1, 32], F32); ct2 = cx.sb(es3, "s5_ct2", [128, T + 1, 32], F32)
                    lv = cx.sb(es3, "s5_lv", [128, 12, NCT], F32)
                    Sp2 = [[cx.sb(es3, f"s5_Sp{par}_{i}", [128, NC], BF16) for i in range(2)] for par in range(2)]
                    R2m = cx.sb(es3, "s5_R2m", [128, NC], F32)
                    psb = [cx.ps(es3, f"s5_psb{i}", [128, 512], F32) for i in range(4)]
                    psy = [cx.ps(es3, f"s5_psy{i}", [128, 4, 128], F32) for i in range(2)]
                    npyc = [0]

                    def front_mid(k):
                        q, j = k // 4, k % 4
                        vre, vim, Sp = vre2[k % 2], vim2[k % 2], Sp2[k % 2]
                        rm3 = rmask[:].rearrange("p (c k) -> p c k", k=T)
                        S.op("act", lambda h: h.copy(rm3[:, :, 1:T], sm[:, 3, k:k + 1].unsqueeze(2).broadcast_to([128, NC, T - 1])),
                             SMALL, [rmask])
                        S.op("pool", lambda h: h.memset(rm3[:, :, 0:1], 0.0), [], [rmask])
                        for blk in range(SEG // 512):
                            c0 = blk * 512
                            PR, PI = psb[(2 * blk) % 4], psb[(2 * blk + 1) % 4]
                            S.op("pe", lambda h: h.matmul(PR[:], BT[0][:, k, :], uTb[q][:, c0:c0 + 512], start=True, stop=True),
                                 [BT[0], uTb[q]], [PR])
                            S.op("pe", lambda h: h.matmul(PI[:], BT[1][:, k, :], uTb[q][:, c0:c0 + 512], start=True, stop=True),
                                 [BT[1], uTb[q]], [PI])
                            cb = Ere[:, k, 0:T].unsqueeze(1).broadcast_to([128, 512 // T, T])
                            sb_ = Eim[:, k, 0:T].unsqueeze(1).broadcast_to([128, 512 // T, T])
                            v3 = lambda ap: ap.rearrange("p (c k) -> p c k", k=T)
                            S.op("dve", lambda h: h.tensor_tensor(v3(ta[:]), v3(PR[:]), cb, ALU.mult), [PR, Ere], [ta])
                            S.op("dve", lambda h: h.tensor_tensor(v3(tbb[:]), v3(PI[:]), sb_, ALU.mult), [PI, Eim], [tbb])
                            S.op("dve", lambda h: h.tensor_tensor(xre[:, c0:c0 + 512], ta[:], tbb[:], ALU.add), [ta, tbb], [xre])
                            S.op("dve", lambda h: h.tensor_tensor(v3(ta[:]), v3(PI[:]), cb, ALU.mult), [PI, Ere], [ta])
                            S.op("dve", lambda h: h.tensor_tensor(v3(tbb[:]), v3(PR[:]), sb_, ALU.mult), [PR, Eim], [tbb])
                            S.op("dve", lambda h: h.tensor_tensor(xim[:, c0:c0 + 512], ta[:], tbb[:], ALU.subtract), [ta, tbb], [xim])
                        stage(cx, 4)
                        S.op("dve", lambda h: h.tensor_tensor_scan(vre[:], rmask[:], xre[:], 0.0, ALU.mult, ALU.add), [rmask, xre], [vre])
                        S.op("dve", lambda h: h.tensor_tensor_scan(vim[:], rmask[:], xim[:], 0.0, ALU.mult, ALU.add), [rmask, xim], [vim])
                        stage(cx, 5)
                        LV = [lv]
                        vr3 = vre[:].rearrange("p (c k) -> p c k", k=T)
                        vi3 = vim[:].rearrange("p (c k) -> p c k", k=T)
                        S.op("dve", lambda h: h.tensor_copy(lv[:, 0, 0:NC], vr3[:, :, T - 1]), [vre], LV)
                        S.op("dve", lambda h: h.tensor_copy(lv[:, 1, 0:NC], vi3[:, :, T - 1]), [vim], LV)
                        er, ei = Ere[:, k, T - 1:T], Eim[:, k, T - 1:T]
                        S.op("dve", lambda h: h.tensor_scalar(lv[:, 10, 0:NC], lv[:, 1, 0:NC], ei, None, ALU.mult), LV + [Eim], LV)
                        S.op("dve", lambda h: h.scalar_tensor_tensor(lv[:, 2, 0:NC], lv[:, 0, 0:NC], er, lv[:, 10, 0:NC], ALU.mult, ALU.subtract), LV + [Ere], LV)
                        S.op("dve", lambda h: h.tensor_scalar(lv[:, 10, 0:NC], lv[:, 0, 0:NC], ei, None, ALU.mult), LV + [Eim], LV)
                        S.op("dve", lambda h: h.scalar_tensor_tensor(lv[:, 3, 0:NC], lv[:, 1, 0:NC], er, lv[:, 10, 0:NC], ALU.mult, ALU.add), LV + [Ere], LV)
                        cmul(S, "dve", lv[:, 4, 0:NC], lv[:, 5, 0:NC], lv[:, 2, 0:NC], lv[:, 3, 0:NC], E2re[:, k, 0:NC], E2im[:, k, 0:NC],
                             lv[:, 10, 0:NC], lv[:, 11, 0:NC], LV + [E2re, E2im], LV, conj_b=True)
                        S.op("pool", lambda h: h.tensor_copy(R2m[:], R2[:, k, 1:2].broadcast_to([128, NC])), [R2], [R2m])
                        S.op("dve", lambda h: h.tensor_tensor_scan(lv[:, 6, 0:NC], R2m[:], lv[:, 4, 0:NC], 0.0, ALU.mult, ALU.add), [R2m] + LV, LV)
                        S.op("dve", lambda h: h.tensor_tensor_scan(lv[:, 7, 0:NC], R2m[:], lv[:, 5, 0:NC], 0.0, ALU.mult, ALU.add), [R2m] + LV, LV)
                        cmul(S, "dve", lv[:, 8, 1:NCT], lv[:, 9, 1:NCT], lv[:, 6, 0:NC], lv[:, 7, 0:NC], E2re[:, k, 0:NC], E2im[:, k, 0:NC],
                             lv[:, 10, 0:NC], lv[:, 11, 0:NC], LV + [E2re, E2im], LV)
                        S.op("dve", lambda h: h.tensor_copy(lv[:, 8, 0:1], Send[:, 0, k:k + 1]), [Send], LV)
                        S.op("dve", lambda h: h.tensor_copy(lv[:, 9, 0:1], Send[:, 1, k:k + 1]), [Send], LV)
                        if seg > 0:
                            S.op("dve", lambda h: h.tensor_tensor(lv[:, 4, 0:NC], R2[:, k, 1:NCT], E2re[:, k, 1:NCT], ALU.mult), [R2, E2re], LV)
                            S.op("dve", lambda h: h.tensor_tensor(lv[:, 5, 0:NC], R2[:, k, 1:NCT], E2im[:, k, 1:NCT], ALU.mult), [R2, E2im], LV)
                            sr, si = Send[:, 0, k:k + 1], Send[:, 1, k:k + 1]
                            S.op("dve", lambda h: h.scalar_tensor_tensor(lv[:, 8, 1:NCT], lv[:, 4, 0:NC], sr, lv[:, 8, 1:NCT], ALU.mult, ALU.add), LV + [Send], LV)
                            S.op("dve", lambda h: h.tensor_scalar(lv[:, 10, 0:NC], lv[:, 5, 0:NC], si, None, ALU.mult), LV + [Send], LV)
                            S.op("dve", lambda h: h.tensor_tensor(lv[:, 8, 1:NCT], lv[:, 8, 1:NCT], lv[:, 10, 0:NC], ALU.subtract), LV, LV)
                            S.op("dve", lambda h: h.scalar_tensor_tensor(lv[:, 9, 1:NCT], lv[:, 5, 0:NC], sr, lv[:, 9, 1:NCT], ALU.mult, ALU.add), LV + [Send], LV)
                            S.op("dve", lambda h: h.tensor_scalar(lv[:, 10, 0:NC], lv[:, 4, 0:NC], si, None, ALU.mult), LV + [Send], LV)
                            S.op("dve", lambda h: h.tensor_tensor(lv[:, 9, 1:NCT], lv[:, 9, 1:NCT], lv[:, 10, 0:NC], ALU.add), LV, LV)
                        S.op("dve", lambda h: h.tensor_copy(Send[:, 0, k:k + 1], lv[:, 8, NC:NCT]), LV, [Send])
                        S.op("dve", lambda h: h.tensor_copy(Send[:, 1, k:k + 1], lv[:, 9, NC:NCT]), LV, [Send])
                        S.op("pool", lambda h: h.tensor_copy(Sp[0][:], lv[:, 8, 0:NC]), LV, [Sp[0]])
                        S.op("pool", lambda h: h.tensor_copy(Sp[1][:], lv[:, 9, 0:NC]), LV, [Sp[1]])
                        stage(cx, 6)
                        cr = ccre[:, k, :].unsqueeze(1).broadcast_to([128, T + 1, 32])
                        ci = ccim[:, k, :].unsqueeze(1).broadcast_to([128, T + 1, 32])
                        ctabf = ctabs[j % 2]
                        hs = slice(32 * (j % 2), 32 * (j % 2) + 32)
                        CT = ctabf + [ct1, ct2]
                        e_r = Ere[:, k, 0:T + 1].unsqueeze(2).broadcast_to([128, T + 1, 32])
                        e_i = Eim[:, k, 0:T + 1].unsqueeze(2).broadcast_to([128, T + 1, 32])
                        e_ni = Enim[:, k, 0:T + 1].unsqueeze(2).broadcast_to([128, T + 1, 32])
                        rb = Rk[:, k, 1:T + 1].unsqueeze(2).broadcast_to([128, T, 32])
                        S.op("dve", lambda h: h.tensor_tensor(ct1[:], cr, e_r, ALU.mult), [ccre, Ere], CT)
                        S.op("dve", lambda h: h.tensor_tensor(ct2[:], ci, e_i, ALU.mult), [ccim, Eim], CT)
                        S.op("dve", lambda h: h.tensor_tensor(ctabf[0][:, :, hs], ct1[:], ct2[:], ALU.subtract), CT, CT)
                        S.op("dve", lambda h: h.tensor_tensor(ct1[:], cr, e_ni, ALU.mult), [ccre, Enim], CT)
                        S.op("dve", lambda h: h.tensor_tensor(ct2[:], ci, e_r, ALU.mult), [ccim, Ere], CT)
                        S.op("dve", lambda h: h.tensor_tensor(ctabf[1][:, :, hs], ct1[:], ct2[:], ALU.subtract), CT, CT)
                        S.op("dve", lambda h: h.tensor_tensor(ctabf[2][:, 0:T, hs], ctabf[0][:, 1:T + 1, hs], rb, ALU.mult), CT + [Rk], CT)
                        S.op("dve", lambda h: h.tensor_tensor(ctabf[3][:, 0:T, hs], ctabf[1][:, 1:T + 1, hs], rb, ALU.mult), CT + [Rk], CT)

                    def back(k):
                        q, j = k // 4, k % 4
                        vre, vim, Sp = vre2[k % 2], vim2[k % 2], Sp2[k % 2]
                        ctabf = ctabs[j % 2]
                        npy = npyc[0]
                        vrb = vre[:].rearrange("p (c k) -> p k c", k=T)
                        vib = vim[:].rearrange("p (c k) -> p k c", k=T)
                        jj = j // 2
                        y3 = y[q][64 * jj:64 * jj + 64, :].rearrange("p (c k) -> p k c", k=T)
                        for kb in range(T // 4):
                            PY = psy[npy % 2]
                            npy += 1

                            def ymm(h, kb=kb, PY=PY):
                                for kk in range(4):
                                    kx = kb * 4 + kk
                                    o = PY[64 * jj:64 * jj + 64, kk, 0:NC]
                                    h.matmul(o, ctabf[0][:, kx, :], vrb[:, kx, :], start=True, stop=False)
                                    h.matmul(o, ctabf[1][:, kx, :], vib[:, kx, :], start=False, stop=False)
                                    h.matmul(o, ctabf[2][:, kx, :], Sp[0][:], start=False, stop=False)
                                    ins = h.matmul(o, ctabf[3][:, kx, :], Sp[1][:], start=False, stop=True)
                                return ins
                            S.op("pe", ymm, ctabf + [vre, vim] + Sp, [PY])
                            if j % 2 == 0:
                                S.op("act", lambda h: h.copy(y3[:, kb * 4:(kb + 1) * 4, :], PY[64 * jj:64 * jj + 64, :, 0:NC]), [PY], [y[q]])
                            else:
                                S.op("dve", lambda h: h.tensor_tensor(y3[:, kb * 4:(kb + 1) * 4, :], PY[64 * jj:64 * jj + 64, :, 0:NC],
                                                                      y3[:, kb * 4:(kb + 1) * 4, :], ALU.add), [PY, y[q]], [y[q]])
                        npyc[0] = npy

                    front_mid(0)
                    for k in range(16):
                        if k + 1 < 16:
                            front_mid(k + 1)
                        back(k)
                    S.barrier()
                stage(cx, 8)
                with ExitStack() as es5:
                    sut = [cx.sb(es5, f"s5_tsut{i}", [128, 4, 512], F32) for i in range(2)]
                    yy = [cx.sb(es5, f"s5_yy{q}", [128, 512], F32) for q in range(4)]
                    w1 = cx.sb(es5, "s5_w1", [128, 512], F32); w2 = cx.sb(es5, "s5_w2", [128, 512], F32)
                    ygb = [cx.sb(es5, f"s5_ygb{q}", [128, 512], BF16) for q in range(4)]
                    og = [cx.sb(es5, f"s5_og{i}", [128, 512], F32) for i in range(4)]
                    g5 = [cx.sb(es5, f"s5_g5{i}", [128, 512], F32) for i in range(2)]
                    yo = [cx.sb(es5, f"s5_yo{i}", [128, 512], F32) for i in range(2)]
                    psT = [cx.ps(es5, f"s5_psT{q}", [128, 512], F32) for q in range(4)]
                    psG = [cx.ps(es5, f"s5_psG{i}", [128, 512], F32) for i in range(2)]
                    psO = [cx.ps(es5, f"s5_psO{i}", [128, 512], F32) for i in range(2)]
                    for blk in range(SEG // 512):
                        c0 = blk * 512
                        SU = sut[blk % 2]
                        for tt in range(4):
                            r0 = t00 + c0 + tt * 128
                            S.dma("sp", SU[:, tt, :], proj_dt.ap[r0:r0 + 128, C_SU:C_SU + 512],
                                  reads=proj_dt.bufs(r0, r0 + 128), writes=[SU])
                        for q in range(4):
                            def tq(h, q=q):
                                for tt in range(4):
                                    ins = h.transpose(psT[q][:, tt * 128:(tt + 1) * 128], SU[:, tt, q * 128:(q + 1) * 128], ident_f[:])
                                return ins
                            S.op("pe", tq, [SU, ident_f], [psT[q]])
                            S.op("dve", lambda h: h.scalar_tensor_tensor(yy[q][:], psT[q][:], dfm[:, q:q + 1], y[q][:, c0:c0 + 512],
                                                                         ALU.mult, ALU.add), [psT[q], dfm, y[q]], [yy[q]])
                            S.op("act", lambda h: h.activation(out=w1[:], in_=yy[q][:], func=AF.Square), [yy[q]], [w1])
                            S.op("dve", lambda h: h.tensor_scalar(w1[:], w1[:], 0.044715, 1.0, ALU.mult, ALU.add), [w1], [w1])
                            S.op("dve", lambda h: h.tensor_tensor(w1[:], w1[:], yy[q][:], ALU.mult), [w1, yy[q]], [w1])
                            S.op("act", lambda h: h.activation(out=w2[:], in_=w1[:], func=AF.Sigmoid, scale=1.5957691216057308), [w1], [w2])
                            S.op("dve", lambda h: h.tensor_tensor(yy[q][:], yy[q][:], w2[:], ALU.mult), [yy[q], w2], [yy[q]])
                            S.op("pool", lambda h: h.tensor_copy(ygb[q][:], yy[q][:]), [yy[q]], [ygb[q]])
                        for nt in range(4):
                            def glu(h, nt=nt):
                                for q in range(4):
                                    ins = h.matmul(psG[nt % 2][:], Wg[:, q, nt * 128:(nt + 1) * 128], ygb[q][:], start=(q == 0), stop=(q == 3))
                                return ins
                            S.op("pe", glu, [Wg] + ygb, [psG[nt % 2]])
                            S.op("act", lambda h: h.activation(out=og[nt][:], in_=psG[nt % 2][:], func=AF.Sigmoid, bias=glub[:, nt:nt + 1]),
                                 [psG[nt % 2], glub], [og[nt]])
                            S.op("dve", lambda h: h.tensor_tensor(og[nt][:], og[nt][:], yy[nt][:], ALU.mult), [og[nt], yy[nt]], [og[nt]])
                        for tt in range(4):
                            i2 = tt % 2
                            r0 = t00 + c0 + tt * 128
                            S.dma("sp", g5[i2][:], proj_dt.ap[r0:r0 + 128, C_S5G:C_S5G + 512], reads=proj_dt.bufs(r0, r0 + 128), writes=[g5[i2]])
                            S.op("act", lambda h: h.activation(out=g5[i2][:], in_=g5[i2][:], func=AF.Silu), [g5[i2]], [g5[i2]])

                            def tro(h, tt=tt, i2=i2):
                                for nt in range(4):
                                    ins = h.transpose(psO[i2][:, nt * 128:(nt + 1) * 128], og[nt][:, tt * 128:(tt + 1) * 128], ident_f[:])
                                return ins
                            S.op("pe", tro, og + [ident_f], [psO[i2]])
                            S.op("dve", lambda h: h.tensor_tensor(yo[i2][:], psO[i2][:, 0:512], g5[i2][:], ALU.mult), [psO[i2], g5[i2]], [yo[i2]])
                            S.dma("pool", mix_dt.ap[r0:r0 + 128, 768:1024], yo[i2][:, 0:256], reads=[yo[i2]], writes=mix_dt.bufs(r0, r0 + 128))
                            S.dma("pool", mix_dt.ap[r0:r0 + 128, 1024 + 768:2048], yo[i2][:, 256:512], reads=[yo[i2]], writes=mix_dt.bufs(r0, r0 + 128))
                    S.barrier()


def s5_layouts(a_re, a_im, log_dt, b_re, b_im, c_re, c_im, d, glu_w, glu_b):
    G = np.arange(32).reshape(16, 2)
    f = np.float32
    are = a_re[G].transpose(1, 2, 0).reshape(128, 16).astype(f)
    aim = a_im[G].transpose(1, 2, 0).reshape(128, 16).astype(f)
    ldt = np.broadcast_to(log_dt[G].transpose(1, 0)[:, None, :], (2, 64, 16)).reshape(128, 16).astype(f)
    ccre = np.zeros((2, 64, 16, 2, 16), f); ccim = np.zeros((2, 64, 16, 2, 16), f)
    bpre = np.zeros((2, 64, 16, 4, 2, 16), f); bpim = np.zeros((2, 64, 16, 4, 2, 16), f)
    for k in range(16):
        for g2 in range(2):
            g = G[k, g2]
            ccre[g2, :, k, g2, :] = c_re[g].T
            ccim[g2, :, k, g2, :] = c_im[g].T
            bpre[g2, :, k, k % 4, g2, :] = b_re[g]
            bpim[g2, :, k, k % 4, g2, :] = b_im[g]
    return dict(are=are, aim=aim, ldt=ldt, ccre=ccre.reshape(128, 16, 32), ccim=ccim.reshape(128, 16, 32),
                bpre=bpre.reshape(128, 16, 128), bpim=bpim.reshape(128, 16, 128),
                dfm=np.ascontiguousarray(d.reshape(4, 128).T).astype(f),
                glub=np.ascontiguousarray(glu_b.reshape(4, 128).T).astype(f),
                gluw=np.ascontiguousarray(glu_w).astype(f))


def bc128(a):
    a = np.asarray(a, np.float32)
    return np.ascontiguousarray(np.broadcast_to(a[None], (128,) + a.shape))


def static_consts():
    f = np.float32
    pos = np.arange(L, dtype=f)
    inv = (1.0 / (np.float32(10000.0) ** (np.arange(0, 64, 2, dtype=f) / np.float32(64)))).astype(f)
    ang = (pos[:, None] * inv[None, :]).astype(f)
    cos = np.cos(ang).astype(f); sin = np.sin(ang).astype(f)
    k = np.arange(128)[:, None]; q = np.arange(128)[None, :]
    mown = np.where(k <= q, 0.0, -30000.0).astype(f)
    mprev = np.where(k > q, 0.0, -30000.0).astype(f)
    return dict(
        ident=np.eye(128).astype(ml_dtypes.bfloat16), identf=np.eye(128, dtype=f),
        cosT=np.ascontiguousarray(cos.reshape(NT, 128, 32).transpose(1, 0, 2)),
        sinT=np.ascontiguousarray(sin.reshape(NT, 128, 32).transpose(1, 0, 2)),
        mown=np.ascontiguousarray(np.tile(mown[:, None, :], (1, 4, 1))),
        mprev=np.ascontiguousarray(np.tile(mprev[:, None, :], (1, 4, 1))),
        gmask=bc128(np.where(np.arange(16)[None, :] < np.arange(16)[:, None], 0.0, -1e30).astype(f)),
        blkind=(np.arange(L)[None, :] // 256 == np.arange(16)[:, None]).astype(ml_dtypes.bfloat16),
        tri=(k <= q).astype(f), ones=np.ones((128, 128), f), maskT=mown.copy(),
        kvec=bc128(np.arange(256, dtype=f)),
    )


CONST_SHAPES = dict(ident=([128, 128], BF16), identf=([128, 128], F32), cosT=([128, NT, 32], F32), sinT=([128, NT, 32], F32),
                    mown=([128, 4, 128], F32), mprev=([128, 4, 128], F32), gmask=([128, 16, 16], F32), blkind=([16, L], BF16),
                    tri=([128, 128], F32), ones=([128, 128], F32), maskT=([128, 128], F32), kvec=([128, 256], F32))
HALF_SHAPES = dict(sinks=[128, 4], convw=[128, 4, 512], convb=[128, 512], dtb=[128, 4], alog=[128, 4],
                   dskip=[128, 256], normw=[128, 256])
LAYER_SHAPES = dict(pre_g=[128, 16], w_out=[D, D], post_g=[128, D],
                    are=[128, 16], aim=[128, 16], ldt=[128, 16], ccre=[128, 16, 32], ccim=[128, 16, 32],
                    bpre=[128, 16, 128], bpim=[128, 16, 128], dfm=[128, 4], glub=[128, 4], gluw=[512, 512])


def in_cols(jh):
    r = lambda a, n: list(range(a, a + n))
    c = (r(0 + jh * 256, 256) + r(512 + jh * 256, 256) + r(1024 + jh * 256, 256) + r(1536 + jh * 256, 256)
         + r(3592 + jh * 256, 256) + r(4104 + jh * 64, 64) + r(4232 + jh * 64, 64) + r(4360 + jh * 256, 256)
         + r(2048 + jh * 256, 256) + r(2560 + jh * 128, 128) + r(2816 + jh * 128, 128) + r(3080 + jh * 256, 256)
         + r(3072 + jh * 4, 4))
    if jh == 0:
        c = c + r(4872, 512) + r(5384, 512)
    return np.array(c)


def half_inputs(inp, l, jh):
    cols = in_cols(jh)
    assert len(cols) == (NP if jh == 0 else NPH)
    cch = np.array(list(range(jh * 256, jh * 256 + 256)) + list(range(512 + jh * 128, 512 + jh * 128 + 128))
                   + list(range(768 + jh * 128, 768 + jh * 128 + 128)))
    return dict(
        w_in=np.ascontiguousarray(inp["w_in"][l][:, cols]),
        sinks=bc128(inp["swa_sinks"][l][4 * jh:4 * jh + 4]),
        convw=bc128(inp["ssd_conv_w"][l][:, cch]), convb=bc128(inp["ssd_conv_b"][l][cch]),
        dtb=bc128(inp["ssd_dt_bias"][l][4 * jh:4 * jh + 4]), alog=bc128(inp["ssd_a_log"][l][4 * jh:4 * jh + 4]),
        dskip=bc128(np.repeat(inp["ssd_d"][l][4 * jh:4 * jh + 4], 64)), normw=bc128(inp["ssd_norm"][l][jh * 256:jh * 256 + 256]),
    )


WOUT_ROWS = np.array([b + jh * 256 + i for jh in range(2) for b in (0, 512, 1024, 1536) for i in range(256)])


def layer_inputs(inp, l):
    d = s5_layouts(inp["s5_a_re"][l], inp["s5_a_im"][l], inp["s5_log_dt"][l], inp["s5_b_re"][l], inp["s5_b_im"][l],
                   inp["s5_c_re"][l], inp["s5_c_im"][l], inp["s5_d"][l], inp["s5_glu_w"][l], inp["s5_glu_b"][l])
    d.update(pre_g=np.ascontiguousarray(inp["pre_norm"][l].reshape(16, 128).T),
             w_out=np.ascontiguousarray(inp["w_out"][l][WOUT_ROWS]), post_g=bc128(inp["post_norm"][l]))
    return d


def load_consts(cx, es, cap):
    S = cx.S
    C = {}
    for nm, key in (("ident", "ident"), ("identf", "identf"), ("cos", "cosT"), ("sin", "sinT")):
        shp, dt = CONST_SHAPES[key]
        C[nm] = cx.sb(es, "c_" + nm, shp, dt)
        S.dma("sp", C[nm][:], cap[key], writes=[C[nm]])
    for key in ("mprev", "mown", "gmask", "blkind", "tri", "ones", "maskT", "kvec", "identf"):
        C["ap_" + key] = cap[key]
    return C


def build_fused(depth=DEPTH):
    nc = bass.Bass("TRN2", target_bir_lowering=False)
    cx = Ctx(nc)
    cx.L = L
    A = lambda n, s, d=F32: nc.dram_tensor(n, list(s), d, kind="ExternalInput").ap()
    x_in = DramT(nc, "x", [L, D], F32, kind="ExternalInput")
    cap = {k: A("k_" + k, s, d) for k, (s, d) in CONST_SHAPES.items()}
    Lw = [{k: A(f"l{l}_{k}", s) for k, s in LAYER_SHAPES.items()} for l in range(depth)]
    Hw = [[dict({k: A(f"l{l}h{jh}_{k}", s) for k, s in HALF_SHAPES.items()},
                w_in=A(f"l{l}h{jh}_w_in", [D, NP if jh == 0 else NPH])) for jh in range(2)] for l in range(depth)]
    proj = DramT(nc, "proj", [L, NP], F32)
    mixc = DramT(nc, "mixc", [L, D], F32)
    xbuf = [DramT(nc, f"xbuf{i}", [L, D], F32) for i in range(2)]
    out = DramT(nc, "out", [L, D], F32, kind="ExternalOutput")
    with ExitStack() as es:
        C = load_consts(cx, es, cap)
        x_cur = x_in
        for l in range(depth):
            x_next = out if l == depth - 1 else xbuf[l % 2]
            for jh in range(2):
                H = Hw[l][jh]
                mcol = jh * MIXH
                phase_inproj(cx, x_cur, H["w_in"], Lw[l]["pre_g"], proj, C["ident"], npc=(NP if jh == 0 else NPH))
                phase_swa(cx, proj, mixc, C["cos"], C["sin"], C["ident"], C["ap_mprev"], C["ap_mown"], H["sinks"], mcol=mcol)
                phase_moba(cx, proj, mixc, C["cos"], C["sin"], C["ident"], C["identf"], C["ap_mown"], C["ap_gmask"],
                           C["ap_blkind"], mcol=mcol)
                ssd_c = {k: H[k] for k in ("convw", "convb", "dtb", "alog", "dskip", "normw")}
                ssd_c.update(tri=C["ap_tri"], ones=C["ap_ones"], maskT=C["ap_maskT"], identf=C["ap_identf"])
                phase_ssd(cx, proj, mixc, C["ident"], ssd_c, mcol=mcol)
                if jh == 0:
                    s5_c = {k: Lw[l][k] for k in ("are", "aim", "ldt", "ccre", "ccim", "bpre", "bpim", "dfm", "glub", "gluw")}
                    s5_c["kvec"] = C["ap_kvec"]
                    phase_s5(cx, proj, mixc, C["ident"], C["identf"], s5_c)
            phase_outproj(cx, mixc, x_cur, Lw[l]["w_out"], Lw[l]["post_g"], x_next, C["ident"], NT)
            x_cur = x_next
        cx.S.finish(out.tiles)
    cx.n_ins = cx.S.n_ins
    return nc


def kernel(**inp):
    inp = {k: np.asarray(v) for k, v in inp.items()}
    x = np.ascontiguousarray(inp["x"], dtype=np.float32)
    nc = build_fused()
    shared = {"k_" + k: v for k, v in static_consts().items()}
    for l in range(DEPTH):
        shared.update({f"l{l}_{k}": v for k, v in layer_inputs(inp, l).items()})
        for jh in range(2):
            shared.update({f"l{l}h{jh}_{k}": v for k, v in half_inputs(inp, l, jh).items()})
    in_maps = []
    for b in range(4):
        m = dict(shared)
        m["x"] = x[b]
        in_maps.append(m)
    res = run_bass_kernel_spmd(nc, in_maps, core_ids=list(range(4)))
    return np.stack([res.results[b]["out"] for b in range(4)]).astype(np.float32)
```

```python
from contextlib import ExitStack
import numpy as np
import ml_dtypes
import concourse.bass as bass
import concourse.mybir as mybir
from concourse.bass_utils import run_bass_kernel_spmd

F32 = mybir.dt.float32
BF16 = mybir.dt.bfloat16
ALU = mybir.AluOpType
AF = mybir.ActivationFunctionType
AX = mybir.AxisListType

D = 2048
L = 4096
NT = L // 128
DEPTH = 4
EPS = 1e-6
C_MQ, C_MK, C_MV, C_MG = 0, 256, 512, 768
C_SQ, C_SK, C_SV, C_SG = 1024, 1280, 1344, 1408
C_XS, C_BM, C_CM, C_Z = 1664, 1920, 2048, 2176
C_DT, C_SU, C_S5G = 2432, 2436, 2948
NP = 3460
NPH = 2436
MIXH = 1024


class Buf:
    __slots__ = ("t", "last_w", "readers")

    def __init__(self, t):
        self.t = t
        self.last_w = None
        self.readers = {}

    def __getitem__(self, k):
        return self.t[k]


class DramT:
    def __init__(self, nc, name, shape, dt, kind="Internal"):
        self.ap = nc.dram_tensor(name, list(shape), dt, kind=kind).ap()
        self.tiles = [Buf(None) for _ in range((shape[0] + 127) // 128)]

    def bufs(self, r0, r1):
        return self.tiles[r0 // 128:(r1 + 127) // 128]


class Eng:
    def __init__(self, key, h, sem):
        self.key, self.h, self.sem = key, h, sem
        self.cnt = 0
        self.seen = {}


class Sched:
    NSLOT = 8

    def __init__(self, nc):
        self.nc = nc
        self.engs = {}
        for key, h in (("pe", nc.tensor), ("act", nc.scalar), ("dve", nc.vector),
                       ("pool", nc.gpsimd), ("sp", nc.sync)):
            self.engs[key] = Eng(key, h, nc.alloc_semaphore(name=f"prog_{key}"))
        self.dma_sems = {}
        self.dma_rings = {}
        self.n_ins = 0
        self.muted = False
        self.same_engine_raw = True

    def _deps(self, reads, writes):
        deps = {}
        for b in reads:
            if b.last_w is not None:
                k, i = b.last_w
                if deps.get(k, 0) < i:
                    deps[k] = i
        for b in writes:
            if b.last_w is not None:
                k, i = b.last_w
                if deps.get(k, 0) < i:
                    deps[k] = i
            for k, i in b.readers.items():
                if deps.get(k, 0) < i:
                    deps[k] = i
        return deps

    def _emit_waits(self, e, deps, same_ok=True):
        for k, i in deps.items():
            if k == e.key and same_ok:
                continue
            if e.seen.get(k, 0) >= i:
                continue
            sem = self.dma_sems[k] if k.startswith("dma") else self.engs[k].sem
            e.h.wait_ge(sem, i)
            e.seen[k] = i
            self.n_ins += 1

    def _record(self, key, idx, reads, writes):
        for b in reads:
            if b.readers.get(key, 0) < idx:
                b.readers[key] = idx
        for b in writes:
            b.last_w = (key, idx)
            b.readers = {}

    def op(self, ek, fn, reads=(), writes=()):
        if self.muted:
            return None
        e = self.engs[ek]
        self._emit_waits(e, self._deps(reads, writes))
        own = 0
        if self.same_engine_raw and ek != "pe":
            for b in reads:
                if b.last_w is not None and b.last_w[0] == ek and b.last_w[1] > own:
                    own = b.last_w[1]
            for b in writes:
                if b.last_w is not None and b.last_w[0] == ek and b.last_w[1] > own:
                    own = b.last_w[1]
                r = b.readers.get(ek, 0)
                if r > own:
                    own = r
        if own > e.seen.get(ek, 0):
            e.h.wait_ge(e.sem, own)
            e.seen[ek] = own
            self.n_ins += 1
        ins = fn(e.h)
        e.cnt += 1
        ins.then_inc(e.sem, 1)
        self._record(ek, e.cnt, reads, writes)
        self.n_ins += 1
        return ins

    def dma(self, ek, out, in_, reads=(), writes=(), **kw):
        if self.muted:
            return None
        e = self.engs[ek]
        deps = self._deps(reads, writes)
        ring = self.dma_rings.setdefault(ek, {"next": 0, "cnt": [0] * self.NSLOT})
        slot = ring["next"]
        ring["next"] = (slot + 1) % self.NSLOT
        qk = f"dma:{ek}:{slot}"
        if qk not in self.dma_sems:
            self.dma_sems[qk] = self.nc.alloc_semaphore(name=f"dma_{ek}_{slot}")
        if ring["cnt"][slot] > 0:
            deps[qk] = max(deps.get(qk, 0), ring["cnt"][slot])
        self._emit_waits(e, deps, same_ok=False)
        ring["cnt"][slot] += 16
        ins = e.h.dma_start(out=out, in_=in_, **kw)
        ins.then_inc(self.dma_sems[qk], 16)
        self._record(qk, ring["cnt"][slot], reads, writes)
        self.n_ins += 1
        return ins

    def barrier(self):
        if self.muted:
            return
        deps = {}
        for k, e in self.engs.items():
            if e.cnt > 0:
                deps[k] = e.cnt
        for ek, ring in self.dma_rings.items():
            for slot, c in enumerate(ring["cnt"]):
                if c > 0:
                    deps[f"dma:{ek}:{slot}"] = c
        for e in self.engs.values():
            self._emit_waits(e, deps)

    def finish(self, bufs):
        self.muted = False
        e = self.engs["sp"]
        self._emit_waits(e, self._deps(bufs, bufs))
        e.h.nop()


class Ctx:
    def __init__(self, nc):
        self.nc = nc
        self.S = Sched(nc)
        self._rr = {}
        self.nt = NT

    def rr(self, key, choices):
        i = self._rr.get(key, 0)
        self._rr[key] = i + 1
        return choices[i % len(choices)]

    def dbg(self, name, buf, shape, dt):
        if not getattr(self, "debug", False):
            return
        o = DramT(self.nc, name, list(shape), dt, kind="ExternalOutput")
        self.S.dma("sp", o.ap, buf[:], reads=[buf], writes=o.tiles)
        self.dbg_outs = getattr(self, "dbg_outs", []) + o.tiles

    def uid(self, name):
        self._uid = getattr(self, "_uid", 0) + 1
        return f"{name}_u{self._uid}"

    def sb(self, es, name, shape, dt):
        return Buf(es.enter_context(self.nc.sbuf_tensor(self.uid(name), list(shape), dt)))

    def ps(self, es, name, shape, dt=F32):
        return Buf(es.enter_context(self.nc.psum_tensor(self.uid(name), list(shape), dt)))


def load_weight_bf16(cx, es, name, w_ap, K, N, scale_sb=None):
    nc, S = cx.nc, cx.S
    KT = K // 128
    Wb = cx.sb(es, name, [128, KT, N], BF16)
    with ExitStack() as es2:
        stg = [cx.sb(es2, f"{name}_stg{i}", [128, N], F32) for i in range(3)]
        for kt in range(KT):
            st = stg[kt % 3]
            S.dma(cx.rr("wq", ["sp", "pool"]), st[:], w_ap[kt * 128:(kt + 1) * 128, :], writes=[st])
            ek = cx.rr("wcast", ["dve", "act"])
            if scale_sb is not None:
                if ek == "dve":
                    S.op("dve", lambda h: h.tensor_scalar(Wb[:, kt, :], st[:], scale_sb[:, kt:kt + 1], None, ALU.mult),
                         [st, scale_sb], [Wb])
                else:
                    S.op("act", lambda h: h.activation(out=Wb[:, kt, :], in_=st[:], func=AF.Copy, scale=scale_sb[:, kt:kt + 1]),
                         [st, scale_sb], [Wb])
            else:
                if ek == "dve":
                    S.op("dve", lambda h: h.tensor_copy(Wb[:, kt, :], st[:]), [st], [Wb])
                else:
                    S.op("act", lambda h: h.copy(Wb[:, kt, :], st[:]), [st], [Wb])
        S.barrier()
    return Wb


def phase_inproj(cx, x_dt, w_ap, g_ap, proj_dt, ident_bf, npc=NP):
    nc, S = cx.nc, cx.S
    KT = D // 128
    with ExitStack() as es:
        g_sb = cx.sb(es, "g_sb", [128, KT], F32)
        S.dma("sp", g_sb[:], g_ap, writes=[g_sb])
        Wb = load_weight_bf16(cx, es, "Win", w_ap, D, npc, g_sb)
        xt = [cx.sb(es, f"xt{i}", [128, D], F32) for i in range(2)]
        xb = [cx.sb(es, f"xb{i}", [128, D], BF16) for i in range(2)]
        junk = cx.sb(es, "junk", [128, D], BF16)
        ss = [cx.sb(es, f"ss{i}", [128, 1], F32) for i in range(2)]
        rstd = [cx.sb(es, f"rstd{i}", [128, 1], F32) for i in range(2)]
        hT = [cx.sb(es, f"hT{i}", [128, KT, 128], BF16) for i in range(2)]
        stage = [cx.sb(es, f"stage{i}", [128, npc], F32) for i in range(2)]
        nch = (npc + 511) // 512
        NACC = 6
        pmb = [cx.ps(es, f"pm{i}", [128, 512], F32) for i in range(min(nch, NACC))]
        pm = [pmb[c % NACC] for c in range(nch)]
        ptl = [cx.ps(es, f"pt{i}", [128, 4, 128], BF16) for i in range(2)]
        def prep(t):
            i2 = t % 2
            X, XB, SS, RS, HT, ST = xt[i2], xb[i2], ss[i2], rstd[i2], hT[i2], stage[i2]
            S.dma("sp", X[:], x_dt.ap[t * 128:(t + 1) * 128, :], reads=x_dt.bufs(t * 128, t * 128 + 128), writes=[X])
            S.op("act", lambda h: h.activation(out=junk[:], in_=X[:], func=AF.Square, accum_out=SS[:]),
                 [X], [junk, SS])
            S.op("pool", lambda h: h.tensor_copy(XB[:], X[:]), [X], [XB])
            S.op("dve", lambda h: h.tensor_scalar(RS[:], SS[:], 1.0 / D, EPS, ALU.mult, ALU.add), [SS], [RS])
            S.op("act", lambda h: h.sqrt(RS[:], RS[:]), [RS], [RS])
            S.op("dve", lambda h: h.reciprocal(RS[:], RS[:]), [RS], [RS])
            for q in range(KT // 4):
                P = ptl[q % 2]
                pv = P[:]

                def tr(h, q=q, pv=pv):
                    for r in range(4):
                        kt = q * 4 + r
                        ins = h.transpose(pv[:, r, :], XB[:, kt * 128:(kt + 1) * 128], ident_bf[:])
                    return ins
                S.op("pe", tr, [XB, ident_bf], [P])
                ek = "act"
                if ek == "dve":
                    S.op("dve", lambda h: h.tensor_copy(HT[:, q * 4:(q + 1) * 4, :], pv), [P], [HT])
                else:
                    S.op("act", lambda h: h.copy(HT[:, q * 4:(q + 1) * 4, :], pv), [P], [HT])

        def compute(t):
            i2 = t % 2
            X, XB, SS, RS, HT, ST = xt[i2], xb[i2], ss[i2], rstd[i2], hT[i2], stage[i2]

            def evac_chunks(cs, ST=ST, RS=RS):
                for c in cs:
                    n0 = c * 512
                    n1 = min(npc, n0 + 512)
                    PM = pm[c]
                    ek = cx.rr("pjev", ["act", "dve"])
                    if ek == "dve":
                        S.op("dve", lambda h: h.tensor_scalar(ST[:, n0:n1], PM[:, 0:n1 - n0], RS[:, 0:1], None, ALU.mult),
                             [PM, RS], [ST])
                    else:
                        S.op("act", lambda h: h.activation(out=ST[:, n0:n1], in_=PM[:, 0:n1 - n0], func=AF.Copy,
                                                           scale=RS[:, 0:1]), [PM, RS], [ST])

            for g0 in range(0, nch, NACC):
                cs = list(range(g0, min(nch, g0 + NACC)))

                def mm(h, HT=HT, cs=cs):
                    for kt in range(KT):
                        for c in cs:
                            n0 = c * 512
                            n1 = min(npc, n0 + 512)
                            ins = h.matmul(pm[c][:, 0:n1 - n0], HT[:, kt, :], Wb[:, kt, n0:n1],
                                           start=(kt == 0), stop=(kt == KT - 1))
                    return ins
                S.op("pe", mm, [HT, Wb], [pm[c] for c in cs])
                evac_chunks(cs)
            if t == 0:
                cx.dbg("d_rs", RS, [128, 1], F32)
                cx.dbg("d_ss", SS, [128, 1], F32)
                cx.dbg("d_xb", XB, [128, D], BF16)
                cx.dbg("d_hT", HT, [128, KT, 128], BF16)
                cx.dbg("d_Wb", Wb, [128, KT, npc], BF16)
            S.dma("pool", proj_dt.ap[t * 128:(t + 1) * 128, 0:npc], ST[:], reads=[ST], writes=proj_dt.bufs(t * 128, t * 128 + 128))

        prep(0)
        for t in range(cx.nt):
            if t + 1 < cx.nt:
                prep(t + 1)
            compute(t)
        S.barrier()


def phase_outproj(cx, mix_dt, x_dt, w_ap, gbc_ap, out_dt, ident_bf, ntiles):
    nc, S = cx.nc, cx.S
    KT = D // 128
    with ExitStack() as es:
        Wb = load_weight_bf16(cx, es, "Wout", w_ap, D, D, None)
        gbc = cx.sb(es, "gbc", [128, D], F32)
        S.dma("sp", gbc[:], gbc_ap, writes=[gbc])
        mt = [cx.sb(es, f"mt{i}", [128, D], F32) for i in range(2)]
        mb = [cx.sb(es, f"mb{i}", [128, D], BF16) for i in range(2)]
        xt = [cx.sb(es, f"oxt{i}", [128, D], F32) for i in range(2)]
        mT = [cx.sb(es, f"mT{i}", [128, KT, 128], BF16) for i in range(2)]
        o = [cx.sb(es, f"o{i}", [128, D], F32) for i in range(2)]
        o2 = [cx.sb(es, f"o2{i}", [128, D], F32) for i in range(2)]
        junk = cx.sb(es, "ojunk", [128, D], BF16)
        ss = [cx.sb(es, f"oss{i}", [128, 1], F32) for i in range(2)]
        pt = [cx.ps(es, f"opt{i}", [128, 4, 128], BF16) for i in range(2)]
        pm = [cx.ps(es, f"opm{i}", [128, 512], F32) for i in range(4)]
        def prep(t):
            i2 = t % 2
            M, MB, X, MT, O, O2, SS = mt[i2], mb[i2], xt[i2], mT[i2], o[i2], o2[i2], ss[i2]
            r0, r1 = t * 128, (t + 1) * 128
            S.dma("sp", M[:], mix_dt.ap[r0:r1, :], reads=mix_dt.bufs(r0, r1), writes=[M])
            S.dma("sp", X[:], x_dt.ap[r0:r1, :], reads=x_dt.bufs(r0, r1), writes=[X])
            S.op("act", lambda h: h.copy(MB[:], M[:]), [M], [MB])
            for q in range(KT // 4):
                P = pt[q % 2]

                def tr(h, q=q, P=P):
                    for r in range(4):
                        kt = q * 4 + r
                        ins = h.transpose(P[:, r, :], MB[:, kt * 128:(kt + 1) * 128], ident_bf[:])
                    return ins
                S.op("pe", tr, [MB, ident_bf], [P])
                if q % 2 == 0:
                    S.op("dve", lambda h: h.tensor_copy(MT[:, q * 4:(q + 1) * 4, :], P[:]), [P], [MT])
                else:
                    S.op("act", lambda h: h.copy(MT[:, q * 4:(q + 1) * 4, :], P[:]), [P], [MT])
        def compute(t):
            i2 = t % 2
            M, MB, X, MT, O, O2, SS = mt[i2], mb[i2], xt[i2], mT[i2], o[i2], o2[i2], ss[i2]
            r0, r1 = t * 128, (t + 1) * 128

            def mm(h, MT=MT):
                for kt in range(KT):
                    for c in range(4):
                        ins = h.matmul(pm[c][:, :], MT[:, kt, :], Wb[:, kt, c * 512:(c + 1) * 512],
                                       start=(kt == 0), stop=(kt == KT - 1))
                return ins
            S.op("pe", mm, [MT, Wb], pm)
            for c in range(4):
                n0, n1 = c * 512, (c + 1) * 512
                PM = pm[c]
                if c % 2 == 0:
                    S.op("act", lambda h: h.copy(O[:, n0:n1], PM[:, :]), [PM], [O])
                else:
                    S.op("dve", lambda h: h.tensor_copy(O[:, n0:n1], PM[:, :]), [PM], [O])
            S.op("act", lambda h: h.activation(out=junk[:], in_=O[:], func=AF.Square, accum_out=SS[:]), [O], [junk, SS])
            S.op("dve", lambda h: h.tensor_scalar(SS[:], SS[:], 1.0 / D, EPS, ALU.mult, ALU.add), [SS], [SS])
            S.op("act", lambda h: h.sqrt(SS[:], SS[:]), [SS], [SS])
            S.op("dve", lambda h: h.reciprocal(SS[:], SS[:]), [SS], [SS])
            S.op("dve", lambda h: h.scalar_tensor_tensor(O2[:], O[:], SS[:, 0:1], gbc[:], ALU.mult, ALU.mult),
                 [O, SS, gbc], [O2])
            if t == 1:
                cx.dbg("d_o", O, [128, D], F32)
                cx.dbg("d_rs", SS, [128, 1], F32)
                cx.dbg("d_o2", O2, [128, D], F32)
            S.op("dve", lambda h: h.tensor_tensor(O2[:], O2[:], X[:], ALU.add), [O2, X], [O2])
            S.dma("pool", out_dt.ap[r0:r1, :], O2[:], reads=[O2], writes=out_dt.bufs(r0, r1))

        prep(0)
        for t in range(ntiles):
            if t + 1 < ntiles:
                prep(t + 1)
            compute(t)
        S.barrier()


def rope_tiles(cx, S, dst_bf, src, nh, cosb, sinb, tmp):
    s4 = src.rearrange("p (h two d) -> p h two d", two=2, d=32)
    d4 = dst_bf.rearrange("p (h two d) -> p h two d", two=2, d=32)
    x1, x2 = s4[:, :, 0, :], s4[:, :, 1, :]
    cb = cosb.unsqueeze(1).broadcast_to([128, nh, 32])
    sb_ = sinb.unsqueeze(1).broadcast_to([128, nh, 32])
    tv = tmp[:].rearrange("p f (h d) -> p f h d", d=32)
    return x1, x2, cb, sb_, tv, d4


def phase_swa(cx, proj_dt, mix_dt, cos_sb, sin_sb, ident_bf, mask_prev_ap, mask_own_ap, sinks_ap, mcol=0):
    nc, S = cx.nc, cx.S
    Lc, NTc = cx.L, cx.L // 128
    with ExitStack() as es:
        QKT = cx.sb(es, "swa_QKT", [64, 5, Lc], BF16)
        Vaug = cx.sb(es, "swa_V", [128, NTc, 65], BF16)
        mprev = cx.sb(es, "swa_mprev", [128, 4, 128], F32)
        mown = cx.sb(es, "swa_mown", [128, 4, 128], F32)
        esink = cx.sb(es, "swa_esink", [128, 4], F32)
        S.dma("sp", mprev[:], mask_prev_ap, writes=[mprev])
        S.dma("sp", mown[:], mask_own_ap, writes=[mown])
        S.dma("sp", esink[:], sinks_ap, writes=[esink])
        S.op("act", lambda h: h.activation(out=esink[:], in_=esink[:], func=AF.Exp), [esink], [esink])
        S.op("pool", lambda h: h.memset(Vaug[:, :, 64:65], 1.0), [], [Vaug])
        tin = [cx.sb(es, f"swa_tin{i}", [128, 384], F32) for i in range(2)]
        rtmp = [cx.sb(es, f"swa_rtmp{i}", [128, 4, 160], F32) for i in range(2)]
        rb = [cx.sb(es, f"swa_rb{i}", [128, 320], BF16) for i in range(2)]
        ptr = [cx.ps(es, f"swa_ptr{i}", [64, 8, 128], BF16) for i in range(2)]
        for t in range(NTc):
            i2 = t % 2
            T, TMP, RB, PT = tin[i2], rtmp[i2], rb[i2], ptr[i2]
            r0, r1 = t * 128, (t + 1) * 128
            S.dma("sp", T[:], proj_dt.ap[r0:r1, C_SQ:C_SQ + 384],
                  reads=proj_dt.bufs(r0, r1), writes=[T])
            x1, x2, cb, sb_, tv, d4 = rope_tiles(cx, S, RB[:], T[:, 0:320], 5, cos_sb[:, t, :], sin_sb[:, t, :], TMP)
            S.op("dve", lambda h: h.tensor_tensor(tv[:, 0], x1, cb, ALU.mult), [T, cos_sb], [TMP])
            S.op("dve", lambda h: h.tensor_tensor(tv[:, 1], x2, sb_, ALU.mult), [T, sin_sb], [TMP])
            S.op("dve", lambda h: h.tensor_tensor(tv[:, 2], x2, cb, ALU.mult), [T, cos_sb], [TMP])
            S.op("dve", lambda h: h.tensor_tensor(tv[:, 3], x1, sb_, ALU.mult), [T, sin_sb], [TMP])
            S.op("dve", lambda h: h.tensor_tensor(d4[:, :, 0, :], tv[:, 0], tv[:, 1], ALU.subtract), [TMP], [RB])
            S.op("dve", lambda h: h.tensor_tensor(d4[:, :, 1, :], tv[:, 2], tv[:, 3], ALU.add), [TMP], [RB])
            S.op("pool", lambda h: h.tensor_copy(Vaug[:, t, 0:64], T[:, 320:384]), [T], [Vaug])

            def tr(h, RB=RB, PT=PT):
                for hh in range(5):
                    ins = h.transpose(PT[:, hh, :], RB[:, hh * 64:(hh + 1) * 64], ident_bf[:])
                return ins
            S.op("pe", tr, [RB, ident_bf], [PT])
            S.op("act", lambda h: h.copy(QKT[:, :, r0:r1], PT[:, 0:5, :]), [PT], [QKT])
        sg = [cx.sb(es, f"swa_sg{i}", [128, 256], F32) for i in range(2)]
        st = [cx.sb(es, f"swa_st{i}", [128, 4, 128], F32) for i in range(2)]
        pT = [cx.sb(es, f"swa_pT{i}", [128, 4, 128], BF16) for i in range(4)]
        den = [cx.sb(es, f"swa_den{i}", [128, 4], F32) for i in range(2)]
        yb = [cx.sb(es, f"swa_y{i}", [128, 256], F32) for i in range(2)]
        pss = [cx.ps(es, f"swa_pss{i}", [128, 4, 128], F32) for i in range(2)]
        pso = [cx.ps(es, f"swa_pso{i}", [128, 4, 128], F32) for i in range(2)]
        sge = [cx.sb(es, f"swa_sge{i}", [128, 256], F32) for i in range(2)]
        items = [(t, kk) for t in range(NTc) for kk in ([t - 1, t] if t > 0 else [t])]

        def score(i):
            t, kk = items[i]
            r0, r1 = t * 128, (t + 1) * 128
            if kk == max(t - 1, 0):
                SG, SE = sg[t % 2], sge[t % 2]
                S.dma("sp", SG[:], proj_dt.ap[r0:r1, C_SG:C_SG + 256], reads=proj_dt.bufs(r0, r1), writes=[SG])
                S.op("act", lambda h: h.activation(out=SG[:], in_=SG[:], func=AF.Silu), [SG], [SG])
            PS = pss[i % 2]
            S.op("pe", lambda h: h.matmul(PS[:], QKT[:, 4, kk * 128:(kk + 1) * 128], QKT[:, 0:4, r0:r1],
                                          start=True, stop=True), [QKT], [PS])

        def finish_item(i):
            t, kk = items[i]
            r0, r1 = t * 128, (t + 1) * 128
            PS, ST, PTB, PO = pss[i % 2], st[i % 2], pT[i % 4], pso[t % 2]
            M = mown if kk == t else mprev
            S.op("dve", lambda h: h.scalar_tensor_tensor(ST[:], PS[:], 0.125, M[:], ALU.mult, ALU.add), [PS, M], [ST])
            S.op("act", lambda h: h.activation(out=PTB[:], in_=ST[:], func=AF.Exp), [ST], [PTB])
            first = (kk == max(t - 1, 0))

            def pv(h):
                for hh in range(4):
                    ins = h.matmul(PO[:, hh, 0:65], PTB[:, hh, :], Vaug[:, kk, :],
                                   start=(first and hh == 0), stop=(kk == t and hh == 3))
                return ins
            S.op("pe", pv, [PTB, Vaug], [PO])
            if kk == t:
                SG, DEN, Y = sg[t % 2], den[t % 2], yb[t % 2]
                S.op("dve", lambda h: h.tensor_tensor(DEN[:], PO[:, :, 64], esink[:], ALU.add), [PO, esink], [DEN])
                S.op("dve", lambda h: h.reciprocal(DEN[:], DEN[:]), [DEN], [DEN])
                for hh in range(4):
                    S.op("act", lambda h: h.activation(out=Y[:, hh * 64:(hh + 1) * 64], in_=PO[:, hh, 0:64], func=AF.Copy,
                                                       scale=DEN[:, hh:hh + 1]), [PO, DEN], [Y])
                S.op("pool", lambda h: h.tensor_tensor(Y[:], Y[:], SG[:], ALU.mult), [Y, SG], [Y])
                S.dma("pool", mix_dt.ap[r0:r1, mcol + 512:mcol + 768], Y[:], reads=[Y], writes=mix_dt.bufs(r0, r1))

        score(0)
        for i in range(len(items)):
            if i + 1 < len(items):
                score(i + 1)
            finish_item(i)
        S.barrier()


def phase_moba(cx, proj_dt, mix_dt, cos_sb, sin_sb, ident_bf, ident_f, mask_own_ap, gmask_ap, blkind_ap, mcol=0):
    nc, S = cx.nc, cx.S
    Lc, NTc = cx.L, cx.L // 128
    NB = 16
    with ExitStack() as es:
        Q32 = cx.sb(es, "mo_Q32", [128, NTc, 256], F32)
        KaugT = cx.sb(es, "mo_KaugT", [80, 4, Lc], BF16)
        Vaug = cx.sb(es, "mo_V", [128, NTc, 4, 65], BF16)
        kmeanT = cx.sb(es, "mo_kmT", [64, 4, NB], F32)
        mown = cx.sb(es, "mo_mown", [128, 4, 128], F32)
        gmask = cx.sb(es, "mo_gmask", [128, NB, NB], F32)
        c256 = cx.sb(es, "mo_c256", [128, 1], F32)
        S.dma("sp", mown[:], mask_own_ap, writes=[mown])
        S.dma("sp", gmask[:], gmask_ap, writes=[gmask])
        for hh in range(4):
            S.dma("sp", KaugT[64:80, hh, :], blkind_ap[:, 0:Lc], writes=[KaugT])
        S.op("pool", lambda h: h.memset(c256[:], 1.0 / 256.0), [], [c256])
        S.op("pool", lambda h: h.memset(Vaug[:, :, :, 64:65], 1.0), [], [Vaug])
        S.op("pool", lambda h: h.memset(kmeanT[:], 0.0), [], [kmeanT])
        tin = [cx.sb(es, f"mo_tin{i}", [128, 768], F32) for i in range(2)]
        rtmp = [cx.sb(es, f"mo_rtmp{i}", [128, 4, 256], F32) for i in range(2)]
        k32 = [cx.sb(es, f"mo_k32{i}", [128, 256], F32) for i in range(2)]
        kb = [cx.sb(es, f"mo_kb{i}", [128, 256], BF16) for i in range(2)]
        with ExitStack() as es1:
            ptr = [cx.ps(es1, f"mo_ptr{i}", [64, 8, 128], BF16) for i in range(2)]
            kmps = cx.ps(es1, "mo_kmps", [64, 4, 128], F32)
            def prepA(t):
                i2 = t % 2
                T, TMP, K32, KB, PT = tin[i2], rtmp[i2], k32[i2], kb[i2], ptr[i2]
                r0, r1 = t * 128, (t + 1) * 128
                S.dma("sp", T[:], proj_dt.ap[r0:r1, C_MQ:C_MQ + 768],
                      reads=proj_dt.bufs(r0, r1), writes=[T])
                s4 = T[:, 0:512].rearrange("p (h two d) -> p h two d", two=2, d=32)
                x1, x2 = s4[:, :, 0, :], s4[:, :, 1, :]
                cb = cos_sb[:, t, :].unsqueeze(1).broadcast_to([128, 8, 32])
                sb_ = sin_sb[:, t, :].unsqueeze(1).broadcast_to([128, 8, 32])
                tv = TMP[:].rearrange("p f (h d) -> p f h d", d=32)
                S.op("dve", lambda h: h.tensor_tensor(tv[:, 0], x1, cb, ALU.mult), [T, cos_sb], [TMP])
                S.op("dve", lambda h: h.tensor_tensor(tv[:, 1], x2, sb_, ALU.mult), [T, sin_sb], [TMP])
                S.op("dve", lambda h: h.tensor_tensor(tv[:, 2], x2, cb, ALU.mult), [T, cos_sb], [TMP])
                S.op("dve", lambda h: h.tensor_tensor(tv[:, 3], x1, sb_, ALU.mult), [T, sin_sb], [TMP])
                q4 = Q32[:, t, :].rearrange("p (h two d) -> p h two d", two=2, d=32)
                k4 = K32[:].rearrange("p (h two d) -> p h two d", two=2, d=32)
                S.op("dve", lambda h: h.tensor_tensor(q4[:, :, 0, :], tv[:, 0, 0:4], tv[:, 1, 0:4], ALU.subtract), [TMP], [Q32])
                S.op("dve", lambda h: h.tensor_tensor(q4[:, :, 1, :], tv[:, 2, 0:4], tv[:, 3, 0:4], ALU.add), [TMP], [Q32])
                S.op("dve", lambda h: h.tensor_tensor(k4[:, :, 0, :], tv[:, 0, 4:8], tv[:, 1, 4:8], ALU.subtract), [TMP], [K32])
                S.op("dve", lambda h: h.tensor_tensor(k4[:, :, 1, :], tv[:, 2, 4:8], tv[:, 3, 4:8], ALU.add), [TMP], [K32])

            def prepB(t):
                i2 = t % 2
                T, TMP, K32, KB, PT = tin[i2], rtmp[i2], k32[i2], kb[i2], ptr[i2]
                r0, r1 = t * 128, (t + 1) * 128
                S.op("act", lambda h: h.copy(KB[:], K32[:]), [K32], [KB])
                S.op("pool", lambda h: h.tensor_copy(Vaug[:, t, :, 0:64], T[:, 512:768].rearrange("p (h d) -> p h d", d=64)),
                     [T], [Vaug])
                n = t // 2

                def km(h, K32=K32, n=n, t=t):
                    for hh in range(4):
                        ins = h.matmul(kmps[:, hh, n:n + 1], K32[:, hh * 64:(hh + 1) * 64], c256[:, 0:1],
                                       start=(t % 2 == 0 and hh == 0), stop=(t % 2 == 1 and hh == 3))
                    return ins
                S.op("pe", km, [K32, c256], [kmps])

                def tr(h, KB=KB, PT=PT):
                    for hh in range(4):
                        ins = h.transpose(PT[:, hh, :], KB[:, hh * 64:(hh + 1) * 64], ident_bf[:])
                    return ins
                S.op("pe", tr, [KB, ident_bf], [PT])
                S.op("act", lambda h: h.copy(KaugT[0:64, :, r0:r1], PT[:, 0:4, :]), [PT], [KaugT])
            prepA(0)
            for t in range(NTc):
                if t + 1 < NTc:
                    prepA(t + 1)
                prepB(t)
            S.op("dve", lambda h: h.tensor_copy(kmeanT[:, :, 0:Lc // 256], kmps[:, :, 0:Lc // 256]), [kmps], [kmeanT])
            S.barrier()
        mg = [cx.sb(es, f"mo_mg{i}", [128, 256], F32) for i in range(2)]
        mge = [cx.sb(es, f"mo_mge{i}", [128, 256], F32) for i in range(2)]
        qT32 = [cx.sb(es, f"mo_qT32{i}", [64, 4, 128], F32) for i in range(2)]
        gm = [cx.sb(es, f"mo_gm{i}", [128, 4, NB], F32) for i in range(2)]
        top8 = [cx.sb(es, f"mo_top8{i}", [128, 4, 8], F32) for i in range(2)]
        qaug = [cx.sb(es, f"mo_qaug{i}", [128, 4, 80], BF16) for i in range(2)]
        QaugT = [cx.sb(es, f"mo_QaugT{i}", [80, 4, 128], BF16) for i in range(2)]
        st = [cx.sb(es, f"mo_st{i}", [128, 4, 128], F32) for i in range(2)]
        pT = [cx.sb(es, f"mo_pT{i}", [128, 4, 128], BF16) for i in range(4)]
        den = [cx.sb(es, f"mo_den{i}", [128, 4], F32) for i in range(2)]
        yb = [cx.sb(es, f"mo_y{i}", [128, 256], F32) for i in range(2)]
        psq = cx.ps(es, "mo_psq", [64, 4, 128], F32)
        psg = cx.ps(es, "mo_psg", [128, 4, 128], F32)
        psa = cx.ps(es, "mo_psa", [80, 8, 128], BF16)
        pss = [cx.ps(es, f"mo_pss{i}", [128, 4, 128], F32) for i in range(2)]
        pso = [cx.ps(es, f"mo_pso{i}", [128, 4, 128], F32) for i in range(2)]
        ctxs = {}

        def prologue(t):
            i2 = t % 2
            r0, r1 = t * 128, (t + 1) * 128
            qb = t // 2
            MG, QT, GM, T8, QA, QAT = mg[i2], qT32[i2], gm[i2], top8[i2], qaug[i2], QaugT[i2]
            S.dma("sp", MG[:], proj_dt.ap[r0:r1, C_MG:C_MG + 256], reads=proj_dt.bufs(r0, r1), writes=[MG])
            S.op("act", lambda h: h.activation(out=MG[:], in_=MG[:], func=AF.Silu), [MG], [MG])

            def trq(h):
                for hh in range(4):
                    ins = h.transpose(psq[:, hh, :], Q32[:, t, hh * 64:(hh + 1) * 64], ident_f[:])
                return ins
            S.op("pe", trq, [Q32, ident_f], [psq])
            S.op("dve", lambda h: h.tensor_copy(QT[:], psq[:]), [psq], [QT])

            def gate(h):
                for hh in range(4):
                    ins = h.matmul(psg[:, hh, 0:NB], QT[:, hh, :], kmeanT[:, hh, :], start=True, stop=True)
                return ins
            S.op("pe", gate, [QT, kmeanT], [psg])
            S.op("dve", lambda h: h.tensor_tensor(GM[:], psg[:, :, 0:NB],
                                                  gmask[:, qb, :].unsqueeze(1).broadcast_to([128, 4, NB]), ALU.add),
                 [psg, gmask], [GM])
            for hh in range(4):
                S.op("dve", lambda h: h.max(T8[:, hh, :], GM[:, hh, :]), [GM], [T8])
            for hh in range(4):
                S.op("dve", lambda h: h.tensor_scalar(QA[:, hh, 64:80], GM[:, hh, :], T8[:, hh, 2:3], -30000.0,
                                                      ALU.is_lt, ALU.mult), [GM, T8], [QA])
            S.op("pool", lambda h: h.memset(QA[:, :, 64 + qb:65 + qb], 0.0), [QA], [QA])
            S.op("act", lambda h: h.copy(QA[:, :, 0:64], Q32[:, t, :].rearrange("p (h d) -> p h d", d=64)), [Q32], [QA])

            def tra(h):
                for hh in range(4):
                    ins = h.transpose(psa[:, hh, :], QA[:, hh, :], ident_bf[:])
                return ins
            S.op("pe", tra, [QA, ident_bf], [psa])
            S.op("dve", lambda h: h.tensor_copy(QAT[:], psa[:, 0:4, :]), [psa], [QAT])

        items = [(t, kk) for t in range(NTc) for kk in range(t + 1)]

        def score(i):
            t, kk = items[i]
            if kk == 0:
                prologue(t)
            PS, QAT = pss[i % 2], QaugT[t % 2]

            def sc(h):
                for hh in range(4):
                    ins = h.matmul(PS[:, hh, :], KaugT[:, hh, kk * 128:(kk + 1) * 128], QAT[:, hh, :],
                                   start=True, stop=True)
                return ins
            S.op("pe", sc, [KaugT, QAT], [PS])

        def finish_item(i):
            t, kk = items[i]
            PS, ST, PTB, PO = pss[i % 2], st[i % 2], pT[i % 4], pso[t % 2]
            if kk == t:
                S.op("dve", lambda h: h.scalar_tensor_tensor(ST[:], PS[:], 0.125, mown[:], ALU.mult, ALU.add),
                     [PS, mown], [ST])
                S.op("act", lambda h: h.activation(out=PTB[:], in_=ST[:], func=AF.Exp), [ST], [PTB])
            else:
                S.op("act", lambda h: h.activation(out=PTB[:], in_=PS[:], func=AF.Exp, scale=0.125), [PS], [PTB])

            def pv(h):
                for hh in range(4):
                    ins = h.matmul(PO[:, hh, 0:65], PTB[:, hh, :], Vaug[:, kk, hh, :],
                                   start=(kk == 0 and hh == 0), stop=(kk == t and hh == 3))
                return ins
            S.op("pe", pv, [PTB, Vaug], [PO])
            if kk == t:
                i2 = t % 2
                r0, r1 = t * 128, (t + 1) * 128
                MG, DEN, Y = mg[i2], den[i2], yb[i2]
                S.op("dve", lambda h: h.reciprocal(DEN[:], PO[:, :, 64]), [PO], [DEN])
                for hh in range(4):
                    S.op("act", lambda h: h.activation(out=Y[:, hh * 64:(hh + 1) * 64], in_=PO[:, hh, 0:64], func=AF.Copy,
                                                       scale=DEN[:, hh:hh + 1]), [PO, DEN], [Y])
                S.op("pool", lambda h: h.tensor_tensor(Y[:], Y[:], MG[:], ALU.mult), [Y, MG], [Y])
                S.dma("pool", mix_dt.ap[r0:r1, mcol:mcol + 256], Y[:], reads=[Y], writes=mix_dt.bufs(r0, r1))

        score(0)
        for i in range(len(items)):
            if i + 1 < len(items):
                score(i + 1)
            finish_item(i)
        S.barrier()


def phase_ssd(cx, proj_dt, mix_dt, ident_bf, cst, mcol=0):
    nc, S = cx.nc, cx.S
    Lc, NTc = cx.L, cx.L // 128
    with ExitStack() as es:
        def ld(name, shape, dt=F32):
            b = cx.sb(es, "ssd_" + name, shape, dt)
            S.dma("sp", b[:], cst[name], writes=[b])
            return b
        convw = ld("convw", [128, 4, 512]); convb = ld("convb", [128, 512]); dtb = ld("dtb", [128, 4])
        Abc = ld("alog", [128, 4]); dskip = ld("dskip", [128, 256]); normw = ld("normw", [128, 256])
        tri = ld("tri", [128, 128]); ones = ld("ones", [128, 128]); maskT = ld("maskT", [128, 128])
        S.op("act", lambda h: h.activation(out=Abc[:], in_=Abc[:], func=AF.Exp), [Abc], [Abc])
        S.op("dve", lambda h: h.tensor_scalar(Abc[:], Abc[:], -1.0, None, ALU.mult), [Abc], [Abc])
        prev32 = cx.sb(es, "ssd_prev32", [128, 256], F32)
        prevb = cx.sb(es, "ssd_prevb", [128, 256], BF16)
        S.op("pool", lambda h: h.memset(prev32[:], 0.0), [], [prev32])
        S.op("pool", lambda h: h.memset(prevb[:], 0.0), [], [prevb])
        Tj = [[cx.sb(es, f"ssd_T{i}_{j}", [128, 512], F32) for j in range(4)] for i in range(2)]
        zt = [cx.sb(es, f"ssd_z{i}", [128, 256], F32) for i in range(2)]
        dtt = [cx.sb(es, f"ssd_dt{i}", [128, 4], F32) for i in range(2)]
        xa = [cx.sb(es, f"ssd_xa{i}", [128, 512], F32) for i in range(2)]
        sm = [cx.sb(es, f"ssd_sm{i}", [128, 8, 4], F32) for i in range(2)]
        Xf = [cx.sb(es, f"ssd_X{i}", [128, 256], F32) for i in range(2)]
        Xb = [cx.sb(es, f"ssd_Xb{i}", [128, 256], BF16) for i in range(2)]
        Xd = [cx.sb(es, f"ssd_Xd{i}", [128, 256], BF16) for i in range(2)]
        BCb = [cx.sb(es, f"ssd_BCb{i}", [128, 256], BF16) for i in range(2)]
        BCT = [cx.sb(es, f"ssd_BCT{i}", [128, 2, 128], BF16) for i in range(2)]
        R4 = [cx.sb(es, f"ssd_R4{i}", [128, 4, 128], F32) for i in range(2)]
        TD4 = [cx.sb(es, f"ssd_TD4{i}", [128, 4, 128], F32) for i in range(2)]
        M4 = [cx.sb(es, f"ssd_M4{i}", [128, 4, 128], BF16) for i in range(2)]
        identf = ld("identf", [128, 128])
        mask4 = cx.sb(es, "ssd_mask4", [128, 4, 128], F32)
        S.op("dve", lambda h: h.tensor_copy(mask4[:], maskT[:].unsqueeze(1).broadcast_to([128, 4, 128])), [maskT], [mask4])
        y1 = [cx.sb(es, f"ssd_y1{i}", [128, 256], F32) for i in range(2)]
        y2 = [cx.sb(es, f"ssd_y2{i}", [128, 256], F32) for i in range(2)]
        junk = cx.sb(es, "ssd_junk", [128, 256], F32)
        csb = [cx.sb(es, f"ssd_cs{i}", [128, 256], F32) for i in range(2)]
        ps_t = cx.ps(es, "ssd_ps_t", [128, 8, 128], BF16)
        ps_s = cx.ps(es, "ssd_ps_s", [128, 512], F32)
        ps_cb = cx.ps(es, "ssd_ps_cb", [128, 512], F32)
        ps_d = cx.ps(es, "ssd_ps_d", [128, 4, 128], F32)
        ps_yd = cx.ps(es, "ssd_ps_yd", [128, 512], F32)
        ps_yo = cx.ps(es, "ssd_ps_yo", [128, 512], F32)
        ps_cs = cx.ps(es, "ssd_ps_cs", [128, 512], F32)
        ndc = [0]

        def front(t):
            nd = ndc[0]
            i2 = t % 2
            r0, r1 = t * 128, (t + 1) * 128
            T, Z, DT, XA, SM, X, XB, XD, BC, BT = Tj[i2], zt[i2], dtt[i2], xa[i2], sm[i2], Xf[i2], Xb[i2], Xd[i2], BCb[i2], BCT[i2]
            for j in range(4):
                sh = 3 - j
                q = "sp"
                if r0 - sh >= 0:
                    S.dma(q, T[j][:], proj_dt.ap[r0 - sh:r1 - sh, C_XS:C_XS + 512],
                          reads=proj_dt.bufs(max(r0 - sh, 0), r1), writes=[T[j]])
                else:
                    S.op("pool", lambda h: h.memset(T[j][0:32, :], 0.0), [], [T[j]])
                    S.dma(q, T[j][sh:128, :], proj_dt.ap[0:128 - sh, C_XS:C_XS + 512],
                          reads=proj_dt.bufs(0, 128), writes=[T[j]])
            S.dma("sp", Z[:], proj_dt.ap[r0:r1, C_Z:C_Z + 256], reads=proj_dt.bufs(r0, r1), writes=[Z])
            S.dma("sp", DT[:], proj_dt.ap[r0:r1, C_DT:C_DT + 4], reads=proj_dt.bufs(r0, r1), writes=[DT])
            for j in range(4):
                ek = "dve" if j % 2 == 0 else "pool"
                S.op(ek, lambda h: h.tensor_tensor(T[j][:], T[j][:], convw[:, j, :], ALU.mult), [T[j], convw], [T[j]])
            S.op("dve", lambda h: h.tensor_tensor(T[0][:], T[0][:], T[2][:], ALU.add), [T[0], T[2]], [T[0]])
            S.op("pool", lambda h: h.tensor_tensor(T[1][:], T[1][:], T[3][:], ALU.add), [T[1], T[3]], [T[1]])
            S.op("dve", lambda h: h.tensor_tensor(T[0][:], T[0][:], T[1][:], ALU.add), [T[0], T[1]], [T[0]])
            S.op("pool", lambda h: h.tensor_tensor(T[0][:], T[0][:], convb[:], ALU.add), [T[0], convb], [T[0]])
            S.op("act", lambda h: h.activation(out=XA[:], in_=T[0][:], func=AF.Silu), [T[0]], [XA])
            S.op("act", lambda h: h.activation(out=Z[:], in_=Z[:], func=AF.Silu), [Z], [Z])
            S.op("dve", lambda h: h.tensor_tensor(DT[:], DT[:], dtb[:], ALU.add), [DT, dtb], [DT])
            S.op("act", lambda h: h.activation(out=DT[:], in_=DT[:], func=AF.Exp), [DT], [DT])
            S.op("act", lambda h: h.activation(out=DT[:], in_=DT[:], func=AF.Ln, bias=1.0), [DT], [DT])
            S.op("dve", lambda h: h.tensor_tensor(SM[:, 0, :], DT[:], Abc[:], ALU.mult), [DT, Abc], [SM])

            def cum(h, SM=SM):
                h.matmul(ps_s[:, 0:4], tri[:], SM[:, 0, :], start=True, stop=False)
                return h.matmul(ps_s[:, 4:8], ones[:], SM[:, 0, :], start=False, stop=True)
            S.op("pe", cum, [tri, ones, SM], [ps_s])
            S.op("dve", lambda h: h.tensor_copy(SM[:, 1, :], ps_s[:, 0:4]), [ps_s], [SM])
            S.op("dve", lambda h: h.tensor_scalar(SM[:, 2, :], ps_s[:, 0:4], -1.0, None, ALU.mult), [ps_s], [SM])
            S.op("dve", lambda h: h.tensor_tensor(SM[:, 5, :], ps_s[:, 4:8], SM[:, 1, :], ALU.subtract), [ps_s, SM], [SM])
            S.op("act", lambda h: h.activation(out=SM[:, 3, :], in_=SM[:, 1, :], func=AF.Exp), [SM], [SM])
            S.op("act", lambda h: h.activation(out=SM[:, 4, :], in_=ps_s[:, 4:8], func=AF.Exp), [ps_s], [SM])
            S.op("act", lambda h: h.activation(out=SM[:, 5, :], in_=SM[:, 5, :], func=AF.Exp), [SM], [SM])
            x3 = X[:].rearrange("p (h d) -> p h d", d=64)
            S.op("dve", lambda h: h.tensor_tensor(x3, XA[:, 0:256].rearrange("p (h d) -> p h d", d=64),
                                                  DT[:].unsqueeze(2).broadcast_to([128, 4, 64]), ALU.mult), [XA, DT], [X])
            S.op("pool", lambda h: h.tensor_copy(XB[:], X[:]), [X], [XB])
            S.op("dve", lambda h: h.tensor_tensor(XD[:].rearrange("p (h d) -> p h d", d=64), x3,
                                                  SM[:, 5, :].unsqueeze(2).broadcast_to([128, 4, 64]), ALU.mult), [X, SM], [XD])
            S.op("pool", lambda h: h.tensor_copy(BC[:], XA[:, 256:512]), [XA], [BC])

            def trbc(h, BC=BC):
                h.transpose(ps_t[:, 0, :], BC[:, 0:128], ident_bf[:])
                return h.transpose(ps_t[:, 1, :], BC[:, 128:256], ident_bf[:])
            S.op("pe", trbc, [BC, ident_bf], [ps_t])
            S.op("act", lambda h: h.copy(BT[:], ps_t[:, 0:2, :]), [ps_t], [BT])
            S.op("pe", lambda h: h.matmul(ps_cb[:, 0:128], BT[:, 0, :], BT[:, 1, :], start=True, stop=True), [BT], [ps_cb])
            S.op("pe", lambda h: h.matmul(ps_cs[:, 0:256], BC[:, 0:128], XD[:], start=True, stop=True), [BC, XD], [ps_cs])
            S.op("act", lambda h: h.copy(csb[i2][:], ps_cs[:, 0:256]), [ps_cs], [csb[i2]])
            RR, TD, MM = R4[i2], TD4[i2], M4[i2]
            S.op("dve", lambda h: h.tensor_tensor(RR[:], tri[:].unsqueeze(1).broadcast_to([128, 4, 128]),
                                                  SM[:, 0, :].unsqueeze(2).broadcast_to([128, 4, 128]), ALU.mult), [tri, SM], [RR])

            def dbc(h):
                h.matmul(ps_d[:], ones[:], RR[:], start=True, stop=False)
                return h.matmul(ps_d[:], identf[:], mask4[:], start=False, stop=True)
            S.op("pe", dbc, [ones, RR, identf, mask4], [ps_d])
            S.op("dve", lambda h: h.tensor_tensor(TD[:], ps_d[:], SM[:, 1, :].unsqueeze(2).broadcast_to([128, 4, 128]), ALU.subtract),
                 [ps_d, SM], [TD])
            S.op("act", lambda h: h.activation(out=TD[:], in_=TD[:], func=AF.Exp), [TD], [TD])
            S.op("dve", lambda h: h.tensor_tensor(MM[:], TD[:], ps_cb[:, 0:128].unsqueeze(1).broadcast_to([128, 4, 128]), ALU.mult),
                 [TD, ps_cb], [MM])

            def ydiag(h):
                for hh in range(4):
                    ins = h.matmul(ps_yd[:, hh * 64:(hh + 1) * 64], MM[:, hh, :], XB[:, hh * 64:(hh + 1) * 64],
                                   start=(hh == 0), stop=(hh == 3))
                return ins
            S.op("pe", ydiag, [MM, XB], [ps_yd])
            ndc[0] = nd
            S.op("act", lambda h: h.copy(y1[i2][:], ps_yd[:, 0:256]), [ps_yd], [y1[i2]])

        def back(t):
            i2 = t % 2
            r0, r1 = t * 128, (t + 1) * 128
            Z, XA, SM, BT = zt[i2], xa[i2], sm[i2], BCT[i2]
            Y1, Y2 = y1[i2], y2[i2]
            S.op("pe", lambda h: h.matmul(ps_yo[:, 0:256], BT[:, 1, :], prevb[:], start=True, stop=True), [BT, prevb], [ps_yo])
            p3 = prev32[:].rearrange("p (h d) -> p h d", d=64)
            S.op("dve", lambda h: h.tensor_tensor(p3, p3, SM[:, 4, :].unsqueeze(2).broadcast_to([128, 4, 64]), ALU.mult),
                 [prev32, SM], [prev32])
            S.op("dve", lambda h: h.tensor_tensor(prev32[:], prev32[:], csb[i2][:], ALU.add), [prev32, csb[i2]], [prev32])
            S.op("act", lambda h: h.copy(prevb[:], prev32[:]), [prev32], [prevb])
            for hh in range(4):
                sl = slice(hh * 64, (hh + 1) * 64)
                S.op("dve", lambda h: h.scalar_tensor_tensor(Y1[:, sl], ps_yo[:, sl], SM[:, 3, hh:hh + 1], Y1[:, sl],
                                                             ALU.mult, ALU.add), [ps_yo, SM, Y1], [Y1])
            S.op("pool", lambda h: h.tensor_tensor(Y2[:], XA[:, 0:256], dskip[:], ALU.mult), [XA, dskip], [Y2])
            S.op("pool", lambda h: h.tensor_tensor(Y1[:], Y1[:], Y2[:], ALU.add), [Y1, Y2], [Y1])
            S.op("pool", lambda h: h.tensor_tensor(Y1[:], Y1[:], Z[:], ALU.mult), [Y1, Z], [Y1])
            S.op("act", lambda h: h.activation(out=junk[:], in_=Y1[:], func=AF.Square, accum_out=SM[:, 6, 0:1]), [Y1], [junk, SM])
            S.op("dve", lambda h: h.tensor_scalar(SM[:, 6, 0:1], SM[:, 6, 0:1], 1.0 / 256.0, EPS, ALU.mult, ALU.add), [SM], [SM])
            S.op("act", lambda h: h.sqrt(SM[:, 6, 0:1], SM[:, 6, 0:1]), [SM], [SM])
            S.op("dve", lambda h: h.reciprocal(SM[:, 6, 0:1], SM[:, 6, 0:1]), [SM], [SM])
            S.op("dve", lambda h: h.scalar_tensor_tensor(Y2[:], Y1[:], SM[:, 6, 0:1], normw[:], ALU.mult, ALU.mult),
                 [Y1, SM, normw], [Y2])
            S.dma("pool", mix_dt.ap[r0:r1, mcol + 256:mcol + 512], Y2[:], reads=[Y2], writes=mix_dt.bufs(r0, r1))

        front(0)
        for t in range(NTc):
            if t + 1 < NTc:
                front(t + 1)
            back(t)
        S.barrier()


class StopPhase(Exception):
    pass


def stage(cx, n):
    if getattr(cx, "stop_stage", None) == n and not cx.S.muted:
        cx.S.barrier()
        cx.S.muted = True


def cmul(S, ek, out_re, out_im, a_re, a_im, b_re, b_im, t1, t2, reads, writes, conj_b=False):
    o1 = ALU.subtract if not conj_b else ALU.add
    o2 = ALU.add if not conj_b else ALU.subtract
    S.op(ek, lambda h: h.tensor_tensor(t1, a_re, b_re, ALU.mult), reads, writes)
    S.op(ek, lambda h: h.tensor_tensor(t2, a_im, b_im, ALU.mult), reads, writes)
    S.op(ek, lambda h: h.tensor_tensor(out_re, t1, t2, o1), reads, writes)
    S.op(ek, lambda h: h.tensor_tensor(t1, a_im, b_re, ALU.mult), reads, writes)
    S.op(ek, lambda h: h.tensor_tensor(t2, a_re, b_im, ALU.mult), reads, writes)
    S.op(ek, lambda h: h.tensor_tensor(out_im, t1, t2, o2), reads, writes)


def phase_s5(cx, proj_dt, mix_dt, ident_bf, ident_f, cst):
    nc, S = cx.nc, cx.S
    Lc = cx.L
    T = 16
    SEG = min(Lc, 2048)
    NSEG = Lc // SEG
    NC = SEG // T
    NCT = NC + 1
    with ExitStack() as es:
        def ld(name, shape, dt=F32, q="sp"):
            b = cx.sb(es, "s5_" + name, shape, dt)
            S.dma(q, b[:], cst[name], writes=[b])
            return b
        are = ld("are", [128, 16]); aim = ld("aim", [128, 16]); ldt = ld("ldt", [128, 16])
        ccre = ld("ccre", [128, 16, 32]); ccim = ld("ccim", [128, 16, 32])
        dfm = ld("dfm", [128, 4]); glub = ld("glub", [128, 4]); kvec = ld("kvec", [128, 256])
        Wg = load_weight_bf16(cx, es, "s5_Wg", cst["gluw"], 512, 512, None)
        BT = [cx.sb(es, f"s5_BT{i}", [128, 16, 128], BF16) for i in range(2)]
        Ere = cx.sb(es, "s5_Ere", [128, 16, T + 1], F32); Eim = cx.sb(es, "s5_Eim", [128, 16, T + 1], F32)
        Rk = cx.sb(es, "s5_Rk", [128, 16, T + 1], F32)
        E2re = cx.sb(es, "s5_E2re", [128, 16, NCT], F32); E2im = cx.sb(es, "s5_E2im", [128, 16, NCT], F32)
        R2 = cx.sb(es, "s5_R2", [128, 16, NCT], F32)
        sm = cx.sb(es, "s5_sm", [128, 12, 16], F32)
        pmax = 1
        while pmax * 2 < max(T + 1, NCT):
            pmax *= 2
        Enim = cx.sb(es, "s5_Enim", [128, 16, T + 1], F32)
        hp = cx.sb(es, "s5_halfpi", [128, 1], F32)
        es_tb = ExitStack()
        tb = cx.sb(es_tb, "s5_tb", [128, 4, 16 + 16 * pmax], F32)
        SMALL = [sm]
        S.op("act", lambda h: h.activation(out=sm[:, 0, :], in_=ldt[:], func=AF.Exp), [ldt], SMALL)
        S.op("dve", lambda h: h.tensor_tensor(sm[:, 1, :], are[:], sm[:, 0, :], ALU.mult), [are] + SMALL, SMALL)
        S.op("dve", lambda h: h.tensor_tensor(sm[:, 2, :], aim[:], sm[:, 0, :], ALU.mult), [aim] + SMALL, SMALL)
        S.op("act", lambda h: h.activation(out=sm[:, 3, :], in_=sm[:, 1, :], func=AF.Exp), SMALL, SMALL)
        S.op("pool", lambda h: h.memset(hp[:], float(np.pi / 2)), [], [hp])
        S.op("act", lambda h: h.activation(out=sm[:, 5, :], in_=sm[:, 2, :], func=AF.Sin, scale=1.0 / 64), SMALL, SMALL)
        S.op("act", lambda h: h.activation(out=sm[:, 4, :], in_=sm[:, 2, :], func=AF.Sin, scale=1.0 / 64, bias=hp[:, 0:1]),
             SMALL + [hp], SMALL)
        for _ in range(6):
            S.op("dve", lambda h: h.tensor_tensor(sm[:, 8, :], sm[:, 4, :], sm[:, 4, :], ALU.mult), SMALL, SMALL)
            S.op("dve", lambda h: h.tensor_tensor(sm[:, 9, :], sm[:, 5, :], sm[:, 5, :], ALU.mult), SMALL, SMALL)
            S.op("dve", lambda h: h.tensor_tensor(sm[:, 10, :], sm[:, 4, :], sm[:, 5, :], ALU.mult), SMALL, SMALL)
            S.op("dve", lambda h: h.tensor_tensor(sm[:, 4, :], sm[:, 8, :], sm[:, 9, :], ALU.subtract), SMALL, SMALL)
            S.op("dve", lambda h: h.tensor_scalar(sm[:, 5, :], sm[:, 10, :], 2.0, None, ALU.mult), SMALL, SMALL)
        S.op("dve", lambda h: h.tensor_tensor(sm[:, 8, :], sm[:, 3, :], sm[:, 4, :], ALU.mult), SMALL, SMALL)
        S.op("dve", lambda h: h.tensor_tensor(sm[:, 9, :], sm[:, 3, :], sm[:, 5, :], ALU.mult), SMALL, SMALL)
        S.op("dve", lambda h: h.tensor_scalar(sm[:, 8, :], sm[:, 8, :], -1.0, None, ALU.add), SMALL, SMALL)
        S.op("dve", lambda h: h.tensor_tensor(sm[:, 10, :], are[:], are[:], ALU.mult), [are], SMALL)
        S.op("dve", lambda h: h.tensor_tensor(sm[:, 11, :], aim[:], aim[:], ALU.mult), [aim], SMALL)
        S.op("dve", lambda h: h.tensor_tensor(sm[:, 10, :], sm[:, 10, :], sm[:, 11, :], ALU.add), SMALL, SMALL)
        S.op("dve", lambda h: h.reciprocal(sm[:, 10, :], sm[:, 10, :]), SMALL, SMALL)
        S.op("dve", lambda h: h.tensor_tensor(sm[:, 6, :], sm[:, 8, :], are[:], ALU.mult), SMALL + [are], SMALL)
        S.op("dve", lambda h: h.tensor_tensor(sm[:, 11, :], sm[:, 9, :], aim[:], ALU.mult), SMALL + [aim], SMALL)
        S.op("dve", lambda h: h.tensor_tensor(sm[:, 6, :], sm[:, 6, :], sm[:, 11, :], ALU.add), SMALL, SMALL)
        S.op("dve", lambda h: h.tensor_tensor(sm[:, 7, :], sm[:, 9, :], are[:], ALU.mult), SMALL + [are], SMALL)
        S.op("dve", lambda h: h.tensor_tensor(sm[:, 11, :], sm[:, 8, :], aim[:], ALU.mult), SMALL + [aim], SMALL)
        S.op("dve", lambda h: h.tensor_tensor(sm[:, 7, :], sm[:, 7, :], sm[:, 11, :], ALU.subtract), SMALL, SMALL)
        S.op("dve", lambda h: h.tensor_tensor(sm[:, 6, :], sm[:, 6, :], sm[:, 10, :], ALU.mult), SMALL, SMALL)
        S.op("dve", lambda h: h.tensor_tensor(sm[:, 7, :], sm[:, 7, :], sm[:, 10, :], ALU.mult), SMALL, SMALL)

        def build_pow_tables(Tre, Tim, n, base_re, base_im):
            TB = [Tre, Tim, tb]
            S.op("pool", lambda h: h.memset(Tre[:, :, 0:1], 1.0), [], [Tre])
            S.op("pool", lambda h: h.memset(Tim[:, :, 0:1], 0.0), [], [Tim])
            S.op("dve", lambda h: h.tensor_copy(Tre[:, :, 1], base_re), SMALL, [Tre])
            S.op("dve", lambda h: h.tensor_copy(Tim[:, :, 1], base_im), SMALL, [Tim])
            m = 2
            while m < n:
                cnt = min(m, n - m)
                pr, pi_ = tb[:, 2, 0:16], tb[:, 3, 0:16]
                cmul(S, "dve", pr, pi_, Tre[:, :, m - 1], Tim[:, :, m - 1], Tre[:, :, 1], Tim[:, :, 1],
                     tb[:, 0, 0:16], tb[:, 1, 0:16], TB, TB)
                prb = pr.unsqueeze(2).broadcast_to([128, 16, cnt])
                pib = pi_.unsqueeze(2).broadcast_to([128, 16, cnt])
                t1 = tb[:, 0, 16:16 + 16 * cnt].rearrange("p (a b) -> p a b", b=cnt)
                t2 = tb[:, 1, 16:16 + 16 * cnt].rearrange("p (a b) -> p a b", b=cnt)
                cmul(S, "dve", Tre[:, :, m:m + cnt], Tim[:, :, m:m + cnt], Tre[:, :, 0:cnt], Tim[:, :, 0:cnt], prb, pib,
                     t1, t2, TB, TB)
                m += cnt
        build_pow_tables(Ere, Eim, T + 1, sm[:, 4, :], sm[:, 5, :])
        S.op("dve", lambda h: h.tensor_scalar(Enim[:], Eim[:], -1.0, None, ALU.mult), [Eim], [Enim])
        S.op("dve", lambda h: h.tensor_copy(sm[:, 8, :], Ere[:, :, T]), [Ere], SMALL)
        S.op("dve", lambda h: h.tensor_copy(sm[:, 9, :], Eim[:, :, T]), [Eim], SMALL)
        build_pow_tables(E2re, E2im, NCT, sm[:, 8, :], sm[:, 9, :])
        S.op("dve", lambda h: h.tensor_tensor(Rk[:], sm[:, 1, :].unsqueeze(2).broadcast_to([128, 16, T + 1]),
                                              kvec[:, 0:T + 1].unsqueeze(1).broadcast_to([128, 16, T + 1]), ALU.mult),
             SMALL + [kvec], [Rk])
        S.op("act", lambda h: h.activation(out=Rk[:], in_=Rk[:], func=AF.Exp), [Rk], [Rk])
        S.op("dve", lambda h: h.tensor_tensor(R2[:], sm[:, 1, :].unsqueeze(2).broadcast_to([128, 16, NCT]),
                                              kvec[:, 0:NCT].unsqueeze(1).broadcast_to([128, 16, NCT]), ALU.mult),
             SMALL + [kvec], [R2])
        S.op("act", lambda h: h.activation(out=R2[:], in_=R2[:], func=AF.Exp, scale=float(T)), [R2], [R2])
        S.barrier()
        es_tb.close()
        stage(cx, 1)
        with ExitStack() as es1:
            bpre = cx.sb(es1, "s5_bpre", [128, 16, 128], F32); bpim = cx.sb(es1, "s5_bpim", [128, 16, 128], F32)
            S.dma("sp", bpre[:], cst["bpre"], writes=[bpre]); S.dma("pool", bpim[:], cst["bpim"], writes=[bpim])
            t1 = cx.sb(es1, "s5_bt1", [128, 16, 128], F32); t2 = cx.sb(es1, "s5_bt2", [128, 16, 128], F32)
            bbre = cx.sb(es1, "s5_bbre", [128, 16, 128], BF16); bbim = cx.sb(es1, "s5_bbim", [128, 16, 128], BF16)
            kr = sm[:, 6, :].unsqueeze(2).broadcast_to([128, 16, 128])
            ki = sm[:, 7, :].unsqueeze(2).broadcast_to([128, 16, 128])
            cmul(S, "dve", bbre[:], bbim[:], bpre[:], bpim[:], kr, ki, t1[:], t2[:], [bpre, bpim, t1, t2] + SMALL, [bbre, bbim, t1, t2])
            pst = cx.ps(es1, "s5_pst", [128, 8, 128], BF16)
            for k in range(16):
                for ri, src in enumerate((bbre, bbim)):
                    S.op("pe", lambda h: h.transpose(pst[:, ri, :], src[:, k, :], ident_bf[:]), [src, ident_bf], [pst])
                    S.op("act", lambda h: h.copy(BT[ri][:, k, :], pst[:, ri, :]), [pst], [BT[ri]])
            S.barrier()
        stage(cx, 2)
        Send = cx.sb(es, "s5_Send", [128, 2, 16], F32)
        S.op("pool", lambda h: h.memset(Send[:], 0.0), [], [Send])
        for seg in range(NSEG):
            t00 = seg * SEG
            with ExitStack() as es2:
                y = [cx.sb(es2, f"s5_y{q}", [128, SEG], F32) for q in range(4)]
                with ExitStack() as es3:
                    uTb = [cx.sb(es3, f"s5_uTb{q}", [128, SEG], BF16) for q in range(4)]
                    with ExitStack() as es4:
                        sut = [cx.sb(es4, f"s5_sut{i}", [128, 512], F32) for i in range(2)]
                        sub = [cx.sb(es4, f"s5_sub{i}", [128, 512], BF16) for i in range(2)]
                        psu = [cx.ps(es4, f"s5_psu{i}", [128, 8, 128], BF16) for i in range(2)]
                        for tt in range(SEG // 128):
                            i2 = tt % 2
                            r0 = t00 + tt * 128
                            S.dma("sp", sut[i2][:], proj_dt.ap[r0:r0 + 128, C_SU:C_SU + 512],
                                  reads=proj_dt.bufs(r0, r0 + 128), writes=[sut[i2]])
                            S.op("pool", lambda h: h.tensor_copy(sub[i2][:], sut[i2][:]), [sut[i2]], [sub[i2]])
                            stage(cx, 21)

                            def tru(h, i2=i2):
                                for q in range(4):
                                    ins = h.transpose(psu[i2][:, q, :], sub[i2][:, q * 128:(q + 1) * 128], ident_bf[:])
                                return ins
                            S.op("pe", tru, [sub[i2], ident_bf], [psu[i2]])
                            stage(cx, 22)
                            for q in range(4):
                                ek = "act"
                                if ek == "act":
                                    S.op("act", lambda h: h.copy(uTb[q][:, tt * 128:(tt + 1) * 128], psu[i2][:, q, :]), [psu[i2]], [uTb[q]])
                                else:
                                    S.op("dve", lambda h: h.tensor_copy(uTb[q][:, tt * 128:(tt + 1) * 128], psu[i2][:, q, :]), [psu[i2]], [uTb[q]])
                                stage(cx, 230 + q)
                            stage(cx, 240 + tt)
                        S.barrier()
                    stage(cx, 3)
                    xre = cx.sb(es3, "s5_xre", [128, SEG], F32); xim = cx.sb(es3, "s5_xim", [128, SEG], F32)
                    vre2 = [cx.sb(es3, f"s5_vre{i}", [128, SEG], BF16) for i in range(2)]
                    vim2 = [cx.sb(es3, f"s5_vim{i}", [128, SEG], BF16) for i in range(2)]
                    rmask = cx.sb(es3, "s5_rmask", [128, SEG], F32)
                    tas = [cx.sb(es3, f"s5_ta{i}", [128, 512], F32) for i in range(4)]
                    tbs = [cx.sb(es3, f"s5_tbb{i}", [128, 512], F32) for i in range(4)]
                    nrot = [0]
                    ctabs = [[cx.sb(es3, f"s5_ctab{par}_{i}", [128, T + 1, 64], BF16) for i in range(4)] for par in range(2)]
                    for par in range(2):
                        for i in range(4):
                            S.op("pool", lambda h: h.memset(ctabs[par][i][:], 0.0), [], [ctabs[par][i]])
                    ct1 = cx.sb(es3, "s5_ct1", [128, T + 1, 32], F32); ct2 = cx.sb(es3, "s5_ct2", [128, T + 1, 32], F32)
                    lv = cx.sb(es3, "s5_lv", [128, 12, NCT], F32)
                    Sp2 = [[cx.sb(es3, f"s5_Sp{par}_{i}", [128, NC], BF16) for i in range(2)] for par in range(2)]
                    R2m = cx.sb(es3, "s5_R2m", [128, NC], F32)
                    psb = [cx.ps(es3, f"s5_psb{i}", [128, 512], F32) for i in range(4)]
                    psy = [cx.ps(es3, f"s5_psy{i}", [128, 4, 128], F32) for i in range(2)]
                    npyc = [0]

                    def front_mid(k):
                        q, j = k // 4, k % 4
                        vre, vim, Sp = vre2[k % 2], vim2[k % 2], Sp2[k % 2]
                        rm3 = rmask[:].rearrange("p (c k) -> p c k", k=T)
                        S.op("act", lambda h: h.copy(rm3[:, :, 1:T], sm[:, 3, k:k + 1].unsqueeze(2).broadcast_to([128, NC, T - 1])),
                             SMALL, [rmask])
                        S.op("pool", lambda h: h.memset(rm3[:, :, 0:1], 0.0), [], [rmask])
                        for blk in range(SEG // 512):
                            c0 = blk * 512
                            PR, PI = psb[(2 * blk) % 4], psb[(2 * blk + 1) % 4]
                            S.op("pe", lambda h: h.matmul(PR[:], BT[0][:, k, :], uTb[q][:, c0:c0 + 512], start=True, stop=True),
                                 [BT[0], uTb[q]], [PR])
                            S.op("pe", lambda h: h.matmul(PI[:], BT[1][:, k, :], uTb[q][:, c0:c0 + 512], start=True, stop=True),
                                 [BT[1], uTb[q]], [PI])
                            cb = Ere[:, k, 0:T].unsqueeze(1).broadcast_to([128, 512 // T, T])
                            sb_ = Eim[:, k, 0:T].unsqueeze(1).broadcast_to([128, 512 // T, T])
                            v3 = lambda ap: ap.rearrange("p (c k) -> p c k", k=T)
                            ta, tbb = tas[nrot[0] % 4], tbs[nrot[0] % 4]
                            nrot[0] += 1
                            S.op("dve", lambda h: h.tensor_tensor(v3(ta[:]), v3(PR[:]), cb, ALU.mult), [PR, Ere], [ta])
                            S.op("dve", lambda h: h.tensor_tensor(v3(tbb[:]), v3(PI[:]), sb_, ALU.mult), [PI, Eim], [tbb])
                            S.op("pool", lambda h: h.tensor_tensor(xre[:, c0:c0 + 512], ta[:], tbb[:], ALU.add), [ta, tbb], [xre])
                            ta, tbb = tas[nrot[0] % 4], tbs[nrot[0] % 4]
                            nrot[0] += 1
                            S.op("dve", lambda h: h.tensor_tensor(v3(ta[:]), v3(PI[:]), cb, ALU.mult), [PI, Ere], [ta])
                            S.op("dve", lambda h: h.tensor_tensor(v3(tbb[:]), v3(PR[:]), sb_, ALU.mult), [PR, Eim], [tbb])
                            S.op("pool", lambda h: h.tensor_tensor(xim[:, c0:c0 + 512], ta[:], tbb[:], ALU.subtract), [ta, tbb], [xim])
                        stage(cx, 4)
                        S.op("dve", lambda h: h.tensor_tensor_scan(vre[:], rmask[:], xre[:], 0.0, ALU.mult, ALU.add), [rmask, xre], [vre])
                        S.op("dve", lambda h: h.tensor_tensor_scan(vim[:], rmask[:], xim[:], 0.0, ALU.mult, ALU.add), [rmask, xim], [vim])
                        stage(cx, 5)
                        LV = [lv]
                        vr3 = vre[:].rearrange("p (c k) -> p c k", k=T)
                        vi3 = vim[:].rearrange("p (c k) -> p c k", k=T)
                        S.op("dve", lambda h: h.tensor_copy(lv[:, 0, 0:NC], vr3[:, :, T - 1]), [vre], LV)
                        S.op("dve", lambda h: h.tensor_copy(lv[:, 1, 0:NC], vi3[:, :, T - 1]), [vim], LV)
                        er, ei = Ere[:, k, T - 1:T], Eim[:, k, T - 1:T]
                        S.op("dve", lambda h: h.tensor_scalar(lv[:, 10, 0:NC], lv[:, 1, 0:NC], ei, None, ALU.mult), LV + [Eim], LV)
                        S.op("dve", lambda h: h.scalar_tensor_tensor(lv[:, 2, 0:NC], lv[:, 0, 0:NC], er, lv[:, 10, 0:NC], ALU.mult, ALU.subtract), LV + [Ere], LV)
                        S.op("dve", lambda h: h.tensor_scalar(lv[:, 10, 0:NC], lv[:, 0, 0:NC], ei, None, ALU.mult), LV + [Eim], LV)
                        S.op("dve", lambda h: h.scalar_tensor_tensor(lv[:, 3, 0:NC], lv[:, 1, 0:NC], er, lv[:, 10, 0:NC], ALU.mult, ALU.add), LV + [Ere], LV)
                        cmul(S, "dve", lv[:, 4, 0:NC], lv[:, 5, 0:NC], lv[:, 2, 0:NC], lv[:, 3, 0:NC], E2re[:, k, 0:NC], E2im[:, k, 0:NC],
                             lv[:, 10, 0:NC], lv[:, 11, 0:NC], LV + [E2re, E2im], LV, conj_b=True)
                        S.op("pool", lambda h: h.tensor_copy(R2m[:], R2[:, k, 1:2].broadcast_to([128, NC])), [R2], [R2m])
                        S.op("dve", lambda h: h.tensor_tensor_scan(lv[:, 6, 0:NC], R2m[:], lv[:, 4, 0:NC], 0.0, ALU.mult, ALU.add), [R2m] + LV, LV)
                        S.op("dve", lambda h: h.tensor_tensor_scan(lv[:, 7, 0:NC], R2m[:], lv[:, 5, 0:NC], 0.0, ALU.mult, ALU.add), [R2m] + LV, LV)
                        cmul(S, "dve", lv[:, 8, 1:NCT], lv[:, 9, 1:NCT], lv[:, 6, 0:NC], lv[:, 7, 0:NC], E2re[:, k, 0:NC], E2im[:, k, 0:NC],
                             lv[:, 10, 0:NC], lv[:, 11, 0:NC], LV + [E2re, E2im], LV)
                        S.op("dve", lambda h: h.tensor_copy(lv[:, 8, 0:1], Send[:, 0, k:k + 1]), [Send], LV)
                        S.op("dve", lambda h: h.tensor_copy(lv[:, 9, 0:1], Send[:, 1, k:k + 1]), [Send], LV)
                        if seg > 0:
                            S.op("dve", lambda h: h.tensor_tensor(lv[:, 4, 0:NC], R2[:, k, 1:NCT], E2re[:, k, 1:NCT], ALU.mult), [R2, E2re], LV)
                            S.op("dve", lambda h: h.tensor_tensor(lv[:, 5, 0:NC], R2[:, k, 1:NCT], E2im[:, k, 1:NCT], ALU.mult), [R2, E2im], LV)
                            sr, si = Send[:, 0, k:k + 1], Send[:, 1, k:k + 1]
                            S.op("dve", lambda h: h.scalar_tensor_tensor(lv[:, 8, 1:NCT], lv[:, 4, 0:NC], sr, lv[:, 8, 1:NCT], ALU.mult, ALU.add), LV + [Send], LV)
                            S.op("dve", lambda h: h.tensor_scalar(lv[:, 10, 0:NC], lv[:, 5, 0:NC], si, None, ALU.mult), LV + [Send], LV)
                            S.op("dve", lambda h: h.tensor_tensor(lv[:, 8, 1:NCT], lv[:, 8, 1:NCT], lv[:, 10, 0:NC], ALU.subtract), LV, LV)
                            S.op("dve", lambda h: h.scalar_tensor_tensor(lv[:, 9, 1:NCT], lv[:, 5, 0:NC], sr, lv[:, 9, 1:NCT], ALU.mult, ALU.add), LV + [Send], LV)
                            S.op("dve", lambda h: h.tensor_scalar(lv[:, 10, 0:NC], lv[:, 4, 0:NC], si, None, ALU.mult), LV + [Send], LV)
                            S.op("dve", lambda h: h.tensor_tensor(lv[:, 9, 1:NCT], lv[:, 9, 1:NCT], lv[:, 10, 0:NC], ALU.add), LV, LV)
                        S.op("dve", lambda h: h.tensor_copy(Send[:, 0, k:k + 1], lv[:, 8, NC:NCT]), LV, [Send])
                        S.op("dve", lambda h: h.tensor_copy(Send[:, 1, k:k + 1], lv[:, 9, NC:NCT]), LV, [Send])
                        S.op("pool", lambda h: h.tensor_copy(Sp[0][:], lv[:, 8, 0:NC]), LV, [Sp[0]])
                        S.op("pool", lambda h: h.tensor_copy(Sp[1][:], lv[:, 9, 0:NC]), LV, [Sp[1]])
                        stage(cx, 6)
                        cr = ccre[:, k, :].unsqueeze(1).broadcast_to([128, T + 1, 32])
                        ci = ccim[:, k, :].unsqueeze(1).broadcast_to([128, T + 1, 32])
                        ctabf = ctabs[j % 2]
                        hs = slice(32 * (j % 2), 32 * (j % 2) + 32)
                        CT = ctabf + [ct1, ct2]
                        e_r = Ere[:, k, 0:T + 1].unsqueeze(2).broadcast_to([128, T + 1, 32])
                        e_i = Eim[:, k, 0:T + 1].unsqueeze(2).broadcast_to([128, T + 1, 32])
                        e_ni = Enim[:, k, 0:T + 1].unsqueeze(2).broadcast_to([128, T + 1, 32])
                        rb = Rk[:, k, 1:T + 1].unsqueeze(2).broadcast_to([128, T, 32])
                        S.op("dve", lambda h: h.tensor_tensor(ct1[:], cr, e_r, ALU.mult), [ccre, Ere], CT)
                        S.op("dve", lambda h: h.tensor_tensor(ct2[:], ci, e_i, ALU.mult), [ccim, Eim], CT)
                        S.op("dve", lambda h: h.tensor_tensor(ctabf[0][:, :, hs], ct1[:], ct2[:], ALU.subtract), CT, CT)
                        S.op("dve", lambda h: h.tensor_tensor(ct1[:], cr, e_ni, ALU.mult), [ccre, Enim], CT)
                        S.op("dve", lambda h: h.tensor_tensor(ct2[:], ci, e_r, ALU.mult), [ccim, Ere], CT)
                        S.op("dve", lambda h: h.tensor_tensor(ctabf[1][:, :, hs], ct1[:], ct2[:], ALU.subtract), CT, CT)
                        S.op("dve", lambda h: h.tensor_tensor(ctabf[2][:, 0:T, hs], ctabf[0][:, 1:T + 1, hs], rb, ALU.mult), CT + [Rk], CT)
                        S.op("dve", lambda h: h.tensor_tensor(ctabf[3][:, 0:T, hs], ctabf[1][:, 1:T + 1, hs], rb, ALU.mult), CT + [Rk], CT)

                    def back(k):
                        q, j = k // 4, k % 4
                        vre, vim, Sp = vre2[k % 2], vim2[k % 2], Sp2[k % 2]
                        ctabf = ctabs[j % 2]
                        npy = npyc[0]
                        vrb = vre[:].rearrange("p (c k) -> p k c", k=T)
                        vib = vim[:].rearrange("p (c k) -> p k c", k=T)
                        jj = j // 2
                        y3 = y[q][64 * jj:64 * jj + 64, :].rearrange("p (c k) -> p k c", k=T)
                        for kb in range(T // 4):
                            PY = psy[npy % 2]
                            npy += 1

                            def ymm(h, kb=kb, PY=PY):
                                for kk in range(4):
                                    kx = kb * 4 + kk
                                    o = PY[64 * jj:64 * jj + 64, kk, 0:NC]
                                    h.matmul(o, ctabf[0][:, kx, :], vrb[:, kx, :], start=True, stop=False)
                                    h.matmul(o, ctabf[1][:, kx, :], vib[:, kx, :], start=False, stop=False)
                                    h.matmul(o, ctabf[2][:, kx, :], Sp[0][:], start=False, stop=False)
                                    ins = h.matmul(o, ctabf[3][:, kx, :], Sp[1][:], start=False, stop=True)
                                return ins
                            S.op("pe", ymm, ctabf + [vre, vim] + Sp, [PY])
                            if j % 2 == 0:
                                S.op("act", lambda h: h.copy(y3[:, kb * 4:(kb + 1) * 4, :], PY[64 * jj:64 * jj + 64, :, 0:NC]), [PY], [y[q]])
                            else:
                                S.op("dve", lambda h: h.tensor_tensor(y3[:, kb * 4:(kb + 1) * 4, :], PY[64 * jj:64 * jj + 64, :, 0:NC],
                                                                      y3[:, kb * 4:(kb + 1) * 4, :], ALU.add), [PY, y[q]], [y[q]])
                        npyc[0] = npy

                    front_mid(0)
                    for k in range(16):
                        if k + 1 < 16:
                            front_mid(k + 1)
                        back(k)
                    S.barrier()
                stage(cx, 8)
                with ExitStack() as es5:
                    sut = [cx.sb(es5, f"s5_tsut{i}", [128, 4, 512], F32) for i in range(2)]
                    yy = [cx.sb(es5, f"s5_yy{q}", [128, 512], F32) for q in range(4)]
                    w1 = cx.sb(es5, "s5_w1", [128, 512], F32); w2 = cx.sb(es5, "s5_w2", [128, 512], F32)
                    ygb = [cx.sb(es5, f"s5_ygb{q}", [128, 512], BF16) for q in range(4)]
                    og = [cx.sb(es5, f"s5_og{i}", [128, 512], F32) for i in range(4)]
                    g5 = [cx.sb(es5, f"s5_g5{i}", [128, 512], F32) for i in range(2)]
                    yo = [cx.sb(es5, f"s5_yo{i}", [128, 512], F32) for i in range(2)]
                    psT = [cx.ps(es5, f"s5_psT{q}", [128, 512], F32) for q in range(4)]
                    psG = [cx.ps(es5, f"s5_psG{i}", [128, 512], F32) for i in range(2)]
                    psO = [cx.ps(es5, f"s5_psO{i}", [128, 512], F32) for i in range(2)]
                    for blk in range(SEG // 512):
                        c0 = blk * 512
                        SU = sut[blk % 2]
                        for tt in range(4):
                            r0 = t00 + c0 + tt * 128
                            S.dma("sp", SU[:, tt, :], proj_dt.ap[r0:r0 + 128, C_SU:C_SU + 512],
                                  reads=proj_dt.bufs(r0, r0 + 128), writes=[SU])
                        for q in range(4):
                            def tq(h, q=q):
                                for tt in range(4):
                                    ins = h.transpose(psT[q][:, tt * 128:(tt + 1) * 128], SU[:, tt, q * 128:(q + 1) * 128], ident_f[:])
                                return ins
                            S.op("pe", tq, [SU, ident_f], [psT[q]])
                            S.op("dve", lambda h: h.scalar_tensor_tensor(yy[q][:], psT[q][:], dfm[:, q:q + 1], y[q][:, c0:c0 + 512],
                                                                         ALU.mult, ALU.add), [psT[q], dfm, y[q]], [yy[q]])
                            S.op("act", lambda h: h.activation(out=w1[:], in_=yy[q][:], func=AF.Square), [yy[q]], [w1])
                            S.op("dve", lambda h: h.tensor_scalar(w1[:], w1[:], 0.044715, 1.0, ALU.mult, ALU.add), [w1], [w1])
                            S.op("dve", lambda h: h.tensor_tensor(w1[:], w1[:], yy[q][:], ALU.mult), [w1, yy[q]], [w1])
                            S.op("act", lambda h: h.activation(out=w2[:], in_=w1[:], func=AF.Sigmoid, scale=1.5957691216057308), [w1], [w2])
                            S.op("dve", lambda h: h.tensor_tensor(yy[q][:], yy[q][:], w2[:], ALU.mult), [yy[q], w2], [yy[q]])
                            S.op("pool", lambda h: h.tensor_copy(ygb[q][:], yy[q][:]), [yy[q]], [ygb[q]])
                        for nt in range(4):
                            def glu(h, nt=nt):
                                for q in range(4):
                                    ins = h.matmul(psG[nt % 2][:], Wg[:, q, nt * 128:(nt + 1) * 128], ygb[q][:], start=(q == 0), stop=(q == 3))
                                return ins
                            S.op("pe", glu, [Wg] + ygb, [psG[nt % 2]])
                            S.op("act", lambda h: h.activation(out=og[nt][:], in_=psG[nt % 2][:], func=AF.Sigmoid, bias=glub[:, nt:nt + 1]),
                                 [psG[nt % 2], glub], [og[nt]])
                            S.op("dve", lambda h: h.tensor_tensor(og[nt][:], og[nt][:], yy[nt][:], ALU.mult), [og[nt], yy[nt]], [og[nt]])
                        for tt in range(4):
                            i2 = tt % 2
                            r0 = t00 + c0 + tt * 128
                            S.dma("sp", g5[i2][:], proj_dt.ap[r0:r0 + 128, C_S5G:C_S5G + 512], reads=proj_dt.bufs(r0, r0 + 128), writes=[g5[i2]])
                            S.op("act", lambda h: h.activation(out=g5[i2][:], in_=g5[i2][:], func=AF.Silu), [g5[i2]], [g5[i2]])

                            def tro(h, tt=tt, i2=i2):
                                for nt in range(4):
                                    ins = h.transpose(psO[i2][:, nt * 128:(nt + 1) * 128], og[nt][:, tt * 128:(tt + 1) * 128], ident_f[:])
                                return ins
                            S.op("pe", tro, og + [ident_f], [psO[i2]])
                            S.op("dve", lambda h: h.tensor_tensor(yo[i2][:], psO[i2][:, 0:512], g5[i2][:], ALU.mult), [psO[i2], g5[i2]], [yo[i2]])
                            S.dma("pool", mix_dt.ap[r0:r0 + 128, 768:1024], yo[i2][:, 0:256], reads=[yo[i2]], writes=mix_dt.bufs(r0, r0 + 128))
                            S.dma("pool", mix_dt.ap[r0:r0 + 128, 1024 + 768:2048], yo[i2][:, 256:512], reads=[yo[i2]], writes=mix_dt.bufs(r0, r0 + 128))
                    S.barrier()


def s5_layouts(a_re, a_im, log_dt, b_re, b_im, c_re, c_im, d, glu_w, glu_b):
    G = np.arange(32).reshape(16, 2)
    f = np.float32
    are = a_re[G].transpose(1, 2, 0).reshape(128, 16).astype(f)
    aim = a_im[G].transpose(1, 2, 0).reshape(128, 16).astype(f)
    ldt = np.broadcast_to(log_dt[G].transpose(1, 0)[:, None, :], (2, 64, 16)).reshape(128, 16).astype(f)
    ccre = np.zeros((2, 64, 16, 2, 16), f); ccim = np.zeros((2, 64, 16, 2, 16), f)
    bpre = np.zeros((2, 64, 16, 4, 2, 16), f); bpim = np.zeros((2, 64, 16, 4, 2, 16), f)
    for k in range(16):
        for g2 in range(2):
            g = G[k, g2]
            ccre[g2, :, k, g2, :] = c_re[g].T
            ccim[g2, :, k, g2, :] = c_im[g].T
            bpre[g2, :, k, k % 4, g2, :] = b_re[g]
            bpim[g2, :, k, k % 4, g2, :] = b_im[g]
    return dict(are=are, aim=aim, ldt=ldt, ccre=ccre.reshape(128, 16, 32), ccim=ccim.reshape(128, 16, 32),
                bpre=bpre.reshape(128, 16, 128), bpim=bpim.reshape(128, 16, 128),
                dfm=np.ascontiguousarray(d.reshape(4, 128).T).astype(f),
                glub=np.ascontiguousarray(glu_b.reshape(4, 128).T).astype(f),
                gluw=np.ascontiguousarray(glu_w).astype(f))


def bc128(a):
    a = np.asarray(a, np.float32)
    return np.ascontiguousarray(np.broadcast_to(a[None], (128,) + a.shape))


def static_consts():
    f = np.float32
    pos = np.arange(L, dtype=f)
    inv = (1.0 / (np.float32(10000.0) ** (np.arange(0, 64, 2, dtype=f) / np.float32(64)))).astype(f)
    ang = (pos[:, None] * inv[None, :]).astype(f)
    cos = np.cos(ang).astype(f); sin = np.sin(ang).astype(f)
    k = np.arange(128)[:, None]; q = np.arange(128)[None, :]
    mown = np.where(k <= q, 0.0, -30000.0).astype(f)
    mprev = np.where(k > q, 0.0, -30000.0).astype(f)
    return dict(
        ident=np.eye(128).astype(ml_dtypes.bfloat16), identf=np.eye(128, dtype=f),
        cosT=np.ascontiguousarray(cos.reshape(NT, 128, 32).transpose(1, 0, 2)),
        sinT=np.ascontiguousarray(sin.reshape(NT, 128, 32).transpose(1, 0, 2)),
        mown=np.ascontiguousarray(np.tile(mown[:, None, :], (1, 4, 1))),
        mprev=np.ascontiguousarray(np.tile(mprev[:, None, :], (1, 4, 1))),
        gmask=bc128(np.where(np.arange(16)[None, :] < np.arange(16)[:, None], 0.0, -1e30).astype(f)),
        blkind=(np.arange(L)[None, :] // 256 == np.arange(16)[:, None]).astype(ml_dtypes.bfloat16),
        tri=(k <= q).astype(f), ones=np.ones((128, 128), f), maskT=mown.copy(),
        kvec=bc128(np.arange(256, dtype=f)),
    )


CONST_SHAPES = dict(ident=([128, 128], BF16), identf=([128, 128], F32), cosT=([128, NT, 32], F32), sinT=([128, NT, 32], F32),
                    mown=([128, 4, 128], F32), mprev=([128, 4, 128], F32), gmask=([128, 16, 16], F32), blkind=([16, L], BF16),
                    tri=([128, 128], F32), ones=([128, 128], F32), maskT=([128, 128], F32), kvec=([128, 256], F32))
HALF_SHAPES = dict(sinks=[128, 4], convw=[128, 4, 512], convb=[128, 512], dtb=[128, 4], alog=[128, 4],
                   dskip=[128, 256], normw=[128, 256])
LAYER_SHAPES = dict(pre_g=[128, 16], w_out=[D, D], post_g=[128, D],
                    are=[128, 16], aim=[128, 16], ldt=[128, 16], ccre=[128, 16, 32], ccim=[128, 16, 32],
                    bpre=[128, 16, 128], bpim=[128, 16, 128], dfm=[128, 4], glub=[128, 4], gluw=[512, 512])


def in_cols(jh):
    r = lambda a, n: list(range(a, a + n))
    c = (r(0 + jh * 256, 256) + r(512 + jh * 256, 256) + r(1024 + jh * 256, 256) + r(1536 + jh * 256, 256)
         + r(3592 + jh * 256, 256) + r(4104 + jh * 64, 64) + r(4232 + jh * 64, 64) + r(4360 + jh * 256, 256)
         + r(2048 + jh * 256, 256) + r(2560 + jh * 128, 128) + r(2816 + jh * 128, 128) + r(3080 + jh * 256, 256)
         + r(3072 + jh * 4, 4))
    if jh == 0:
        c = c + r(4872, 512) + r(5384, 512)
    return np.array(c)


def half_inputs(inp, l, jh):
    cols = in_cols(jh)
    assert len(cols) == (NP if jh == 0 else NPH)
    cch = np.array(list(range(jh * 256, jh * 256 + 256)) + list(range(512 + jh * 128, 512 + jh * 128 + 128))
                   + list(range(768 + jh * 128, 768 + jh * 128 + 128)))
    return dict(
        w_in=np.ascontiguousarray(inp["w_in"][l][:, cols]),
        sinks=bc128(inp["swa_sinks"][l][4 * jh:4 * jh + 4]),
        convw=bc128(inp["ssd_conv_w"][l][:, cch]), convb=bc128(inp["ssd_conv_b"][l][cch]),
        dtb=bc128(inp["ssd_dt_bias"][l][4 * jh:4 * jh + 4]), alog=bc128(inp["ssd_a_log"][l][4 * jh:4 * jh + 4]),
        dskip=bc128(np.repeat(inp["ssd_d"][l][4 * jh:4 * jh + 4], 64)), normw=bc128(inp["ssd_norm"][l][jh * 256:jh * 256 + 256]),
    )


WOUT_ROWS = np.array([b + jh * 256 + i for jh in range(2) for b in (0, 512, 1024, 1536) for i in range(256)])


def layer_inputs(inp, l):
    d = s5_layouts(inp["s5_a_re"][l], inp["s5_a_im"][l], inp["s5_log_dt"][l], inp["s5_b_re"][l], inp["s5_b_im"][l],
                   inp["s5_c_re"][l], inp["s5_c_im"][l], inp["s5_d"][l], inp["s5_glu_w"][l], inp["s5_glu_b"][l])
    d.update(pre_g=np.ascontiguousarray(inp["pre_norm"][l].reshape(16, 128).T),
             w_out=np.ascontiguousarray(inp["w_out"][l][WOUT_ROWS]), post_g=bc128(inp["post_norm"][l]))
    return d


def load_consts(cx, es, cap):
    S = cx.S
    C = {}
    for nm, key in (("ident", "ident"), ("identf", "identf"), ("cos", "cosT"), ("sin", "sinT")):
        shp, dt = CONST_SHAPES[key]
        C[nm] = cx.sb(es, "c_" + nm, shp, dt)
        S.dma("sp", C[nm][:], cap[key], writes=[C[nm]])
    for key in ("mprev", "mown", "gmask", "blkind", "tri", "ones", "maskT", "kvec", "identf"):
        C["ap_" + key] = cap[key]
    return C


def build_fused(depth=DEPTH):
    nc = bass.Bass("TRN2", target_bir_lowering=False)
    cx = Ctx(nc)
    cx.L = L
    A = lambda n, s, d=F32: nc.dram_tensor(n, list(s), d, kind="ExternalInput").ap()
    x_in = DramT(nc, "x", [L, D], F32, kind="ExternalInput")
    cap = {k: A("k_" + k, s, d) for k, (s, d) in CONST_SHAPES.items()}
    Lw = [{k: A(f"l{l}_{k}", s) for k, s in LAYER_SHAPES.items()} for l in range(depth)]
    Hw = [[dict({k: A(f"l{l}h{jh}_{k}", s) for k, s in HALF_SHAPES.items()},
                w_in=A(f"l{l}h{jh}_w_in", [D, NP if jh == 0 else NPH])) for jh in range(2)] for l in range(depth)]
    proj = DramT(nc, "proj", [L, NP], F32)
    mixc = DramT(nc, "mixc", [L, D], F32)
    xbuf = [DramT(nc, f"xbuf{i}", [L, D], F32) for i in range(2)]
    out = DramT(nc, "out", [L, D], F32, kind="ExternalOutput")
    with ExitStack() as es:
        C = load_consts(cx, es, cap)
        x_cur = x_in
        for l in range(depth):
            x_next = out if l == depth - 1 else xbuf[l % 2]
            for jh in range(2):
                H = Hw[l][jh]
                mcol = jh * MIXH
                phase_inproj(cx, x_cur, H["w_in"], Lw[l]["pre_g"], proj, C["ident"], npc=(NP if jh == 0 else NPH))
                phase_swa(cx, proj, mixc, C["cos"], C["sin"], C["ident"], C["ap_mprev"], C["ap_mown"], H["sinks"], mcol=mcol)
                phase_moba(cx, proj, mixc, C["cos"], C["sin"], C["ident"], C["identf"], C["ap_mown"], C["ap_gmask"],
                           C["ap_blkind"], mcol=mcol)
                ssd_c = {k: H[k] for k in ("convw", "convb", "dtb", "alog", "dskip", "normw")}
                ssd_c.update(tri=C["ap_tri"], ones=C["ap_ones"], maskT=C["ap_maskT"], identf=C["ap_identf"])
                phase_ssd(cx, proj, mixc, C["ident"], ssd_c, mcol=mcol)
                if jh == 0:
                    s5_c = {k: Lw[l][k] for k in ("are", "aim", "ldt", "ccre", "ccim", "bpre", "bpim", "dfm", "glub", "gluw")}
                    s5_c["kvec"] = C["ap_kvec"]
                    phase_s5(cx, proj, mixc, C["ident"], C["identf"], s5_c)
            phase_outproj(cx, mixc, x_cur, Lw[l]["w_out"], Lw[l]["post_g"], x_next, C["ident"], NT)
            x_cur = x_next
        cx.S.finish(out.tiles)
    cx.n_ins = cx.S.n_ins
    return nc


def kernel(**inp):
    inp = {k: np.asarray(v) for k, v in inp.items()}
    x = np.ascontiguousarray(inp["x"], dtype=np.float32)
    nc = build_fused()
    shared = {"k_" + k: v for k, v in static_consts().items()}
    for l in range(DEPTH):
        shared.update({f"l{l}_{k}": v for k, v in layer_inputs(inp, l).items()})
        for jh in range(2):
            shared.update({f"l{l}h{jh}_{k}": v for k, v in half_inputs(inp, l, jh).items()})
    in_maps = []
    for b in range(4):
        m = dict(shared)
        m["x"] = x[b]
        in_maps.append(m)
    res = run_bass_kernel_spmd(nc, in_maps, core_ids=list(range(4)))
    return np.stack([res.results[b]["out"] for b in range(4)]).astype(np.float32)
```

```python
from contextlib import ExitStack
import numpy as np
import ml_dtypes
import concourse.bass as bass
import concourse.mybir as mybir
from concourse.bass_utils import run_bass_kernel_spmd

F32 = mybir.dt.float32
BF16 = mybir.dt.bfloat16
ALU = mybir.AluOpType
AF = mybir.ActivationFunctionType
AX = mybir.AxisListType

D = 2048
L = 4096
NT = L // 128
DEPTH = 4
EPS = 1e-6
C_MQ, C_MK, C_MV, C_MG = 0, 256, 512, 768
C_SQ, C_SK, C_SV, C_SG = 1024, 1280, 1344, 1408
C_XS, C_BM, C_CM, C_Z = 1664, 1920, 2048, 2176
C_DT, C_SU, C_S5G = 2432, 2436, 2948
NP = 3460
NPH = 2436
MIXH = 1024


class Buf:
    __slots__ = ("t", "last_w", "readers")

    def __init__(self, t):
        self.t = t
        self.last_w = None
        self.readers = {}

    def __getitem__(self, k):
        return self.t[k]


class DramT:
    def __init__(self, nc, name, shape, dt, kind="Internal"):
        self.ap = nc.dram_tensor(name, list(shape), dt, kind=kind).ap()
        self.tiles = [Buf(None) for _ in range((shape[0] + 127) // 128)]

    def bufs(self, r0, r1):
        return self.tiles[r0 // 128:(r1 + 127) // 128]


class Eng:
    def __init__(self, key, h, sem):
        self.key, self.h, self.sem = key, h, sem
        self.cnt = 0
        self.seen = {}


class Sched:
    NSLOT = 8

    def __init__(self, nc):
        self.nc = nc
        self.engs = {}
        for key, h in (("pe", nc.tensor), ("act", nc.scalar), ("dve", nc.vector),
                       ("pool", nc.gpsimd), ("sp", nc.sync)):
            self.engs[key] = Eng(key, h, nc.alloc_semaphore(name=f"prog_{key}"))
        self.dma_sems = {}
        self.dma_rings = {}
        self.n_ins = 0
        self.muted = False
        self.same_engine_raw = True

    def _deps(self, reads, writes):
        deps = {}
        for b in reads:
            if b.last_w is not None:
                k, i = b.last_w
                if deps.get(k, 0) < i:
                    deps[k] = i
        for b in writes:
            if b.last_w is not None:
                k, i = b.last_w
                if deps.get(k, 0) < i:
                    deps[k] = i
            for k, i in b.readers.items():
                if deps.get(k, 0) < i:
                    deps[k] = i
        return deps

    def _emit_waits(self, e, deps, same_ok=True):
        for k, i in deps.items():
            if k == e.key and same_ok:
                continue
            if e.seen.get(k, 0) >= i:
                continue
            sem = self.dma_sems[k] if k.startswith("dma") else self.engs[k].sem
            e.h.wait_ge(sem, i)
            e.seen[k] = i
            self.n_ins += 1

    def _record(self, key, idx, reads, writes):
        for b in reads:
            if b.readers.get(key, 0) < idx:
                b.readers[key] = idx
        for b in writes:
            b.last_w = (key, idx)
            b.readers = {}

    def op(self, ek, fn, reads=(), writes=()):
        if self.muted:
            return None
        e = self.engs[ek]
        self._emit_waits(e, self._deps(reads, writes))
        own = 0
        if self.same_engine_raw and ek != "pe":
            for b in reads:
                if b.last_w is not None and b.last_w[0] == ek and b.last_w[1] > own:
                    own = b.last_w[1]
            for b in writes:
                if b.last_w is not None and b.last_w[0] == ek and b.last_w[1] > own:
                    own = b.last_w[1]
                r = b.readers.get(ek, 0)
                if r > own:
                    own = r
        if own > e.seen.get(ek, 0):
            e.h.wait_ge(e.sem, own)
            e.seen[ek] = own
            self.n_ins += 1
        ins = fn(e.h)
        e.cnt += 1
        ins.then_inc(e.sem, 1)
        self._record(ek, e.cnt, reads, writes)
        self.n_ins += 1
        return ins

    def dma(self, ek, out, in_, reads=(), writes=(), **kw):
        if self.muted:
            return None
        e = self.engs[ek]
        deps = self._deps(reads, writes)
        ring = self.dma_rings.setdefault(ek, {"next": 0, "cnt": [0] * self.NSLOT})
        slot = ring["next"]
        ring["next"] = (slot + 1) % self.NSLOT
        qk = f"dma:{ek}:{slot}"
        if qk not in self.dma_sems:
            self.dma_sems[qk] = self.nc.alloc_semaphore(name=f"dma_{ek}_{slot}")
        if ring["cnt"][slot] > 0:
            deps[qk] = max(deps.get(qk, 0), ring["cnt"][slot])
        self._emit_waits(e, deps, same_ok=False)
        ring["cnt"][slot] += 16
        ins = e.h.dma_start(out=out, in_=in_, **kw)
        ins.then_inc(self.dma_sems[qk], 16)
        self._record(qk, ring["cnt"][slot], reads, writes)
        self.n_ins += 1
        return ins

    def barrier(self):
        if self.muted:
            return
        deps = {}
        for k, e in self.engs.items():
            if e.cnt > 0:
                deps[k] = e.cnt
        for ek, ring in self.dma_rings.items():
            for slot, c in enumerate(ring["cnt"]):
                if c > 0:
                    deps[f"dma:{ek}:{slot}"] = c
        for e in self.engs.values():
            self._emit_waits(e, deps)

    def finish(self, bufs):
        self.muted = False
        e = self.engs["sp"]
        self._emit_waits(e, self._deps(bufs, bufs))
        e.h.nop()


class Ctx:
    def __init__(self, nc):
        self.nc = nc
        self.S = Sched(nc)
        self._rr = {}
        self.nt = NT

    def rr(self, key, choices):
        i = self._rr.get(key, 0)
        self._rr[key] = i + 1
        return choices[i % len(choices)]

    def dbg(self, name, buf, shape, dt):
        if not getattr(self, "debug", False):
            return
        o = DramT(self.nc, name, list(shape), dt, kind="ExternalOutput")
        self.S.dma("sp", o.ap, buf[:], reads=[buf], writes=o.tiles)
        self.dbg_outs = getattr(self, "dbg_outs", []) + o.tiles

    def uid(self, name):
        self._uid = getattr(self, "_uid", 0) + 1
        return f"{name}_u{self._uid}"

    def sb(self, es, name, shape, dt):
        return Buf(es.enter_context(self.nc.sbuf_tensor(self.uid(name), list(shape), dt)))

    def ps(self, es, name, shape, dt=F32):
        return Buf(es.enter_context(self.nc.psum_tensor(self.uid(name), list(shape), dt)))


def load_weight_bf16(cx, es, name, w_ap, K, N, scale_sb=None):
    nc, S = cx.nc, cx.S
    KT = K // 128
    Wb = cx.sb(es, name, [128, KT, N], BF16)
    with ExitStack() as es2:
        stg = [cx.sb(es2, f"{name}_stg{i}", [128, N], F32) for i in range(3)]
        for kt in range(KT):
            st = stg[kt % 3]
            S.dma(cx.rr("wq", ["sp", "pool"]), st[:], w_ap[kt * 128:(kt + 1) * 128, :], writes=[st])
            ek = cx.rr("wcast", ["dve", "act"])
            if scale_sb is not None:
                if ek == "dve":
                    S.op("dve", lambda h: h.tensor_scalar(Wb[:, kt, :], st[:], scale_sb[:, kt:kt + 1], None, ALU.mult),
                         [st, scale_sb], [Wb])
                else:
                    S.op("act", lambda h: h.activation(out=Wb[:, kt, :], in_=st[:], func=AF.Copy, scale=scale_sb[:, kt:kt + 1]),
                         [st, scale_sb], [Wb])
            else:
                if ek == "dve":
                    S.op("dve", lambda h: h.tensor_copy(Wb[:, kt, :], st[:]), [st], [Wb])
                else:
                    S.op("act", lambda h: h.copy(Wb[:, kt, :], st[:]), [st], [Wb])
        S.barrier()
    return Wb


def phase_inproj(cx, x_dt, w_ap, g_ap, proj_dt, ident_bf, npc=NP):
    nc, S = cx.nc, cx.S
    KT = D // 128
    with ExitStack() as es:
        g_sb = cx.sb(es, "g_sb", [128, KT], F32)
        S.dma("sp", g_sb[:], g_ap, writes=[g_sb])
        Wb = load_weight_bf16(cx, es, "Win", w_ap, D, npc, g_sb)
        xt = [cx.sb(es, f"xt{i}", [128, D], F32) for i in range(2)]
        xb = [cx.sb(es, f"xb{i}", [128, D], BF16) for i in range(2)]
        junk = cx.sb(es, "junk", [128, D], BF16)
        ss = [cx.sb(es, f"ss{i}", [128, 1], F32) for i in range(2)]
        rstd = [cx.sb(es, f"rstd{i}", [128, 1], F32) for i in range(2)]
        hT = [cx.sb(es, f"hT{i}", [128, KT, 128], BF16) for i in range(2)]
        stage = [cx.sb(es, f"stage{i}", [128, npc], F32) for i in range(2)]
        nch = (npc + 511) // 512
        NACC = 6
        pmb = [cx.ps(es, f"pm{i}", [128, 512], F32) for i in range(min(nch, NACC))]
        pm = [pmb[c % NACC] for c in range(nch)]
        ptl = [cx.ps(es, f"pt{i}", [128, 4, 128], BF16) for i in range(2)]
        def prep(t):
            i2 = t % 2
            X, XB, SS, RS, HT, ST = xt[i2], xb[i2], ss[i2], rstd[i2], hT[i2], stage[i2]
            S.dma("sp", X[:], x_dt.ap[t * 128:(t + 1) * 128, :], reads=x_dt.bufs(t * 128, t * 128 + 128), writes=[X])
            S.op("act", lambda h: h.activation(out=junk[:], in_=X[:], func=AF.Square, accum_out=SS[:]),
                 [X], [junk, SS])
            S.op("pool", lambda h: h.tensor_copy(XB[:], X[:]), [X], [XB])
            S.op("dve", lambda h: h.tensor_scalar(RS[:], SS[:], 1.0 / D, EPS, ALU.mult, ALU.add), [SS], [RS])
            S.op("act", lambda h: h.sqrt(RS[:], RS[:]), [RS], [RS])
            S.op("dve", lambda h: h.reciprocal(RS[:], RS[:]), [RS], [RS])
            for q in range(KT // 4):
                P = ptl[q % 2]
                pv = P[:]

                def tr(h, q=q, pv=pv):
                    for r in range(4):
                        kt = q * 4 + r
                        ins = h.transpose(pv[:, r, :], XB[:, kt * 128:(kt + 1) * 128], ident_bf[:])
                    return ins
                S.op("pe", tr, [XB, ident_bf], [P])
                ek = "act"
                if ek == "dve":
                    S.op("dve", lambda h: h.tensor_copy(HT[:, q * 4:(q + 1) * 4, :], pv), [P], [HT])
                else:
                    S.op("act", lambda h: h.copy(HT[:, q * 4:(q + 1) * 4, :], pv), [P], [HT])

        def compute(t):
            i2 = t % 2
            X, XB, SS, RS, HT, ST = xt[i2], xb[i2], ss[i2], rstd[i2], hT[i2], stage[i2]

            def evac_chunks(cs, ST=ST, RS=RS):
                for c in cs:
                    n0 = c * 512
                    n1 = min(npc, n0 + 512)
                    PM = pm[c]
                    ek = cx.rr("pjev", ["act", "dve"])
                    if ek == "dve":
                        S.op("dve", lambda h: h.tensor_scalar(ST[:, n0:n1], PM[:, 0:n1 - n0], RS[:, 0:1], None, ALU.mult),
                             [PM, RS], [ST])
                    else:
                        S.op("act", lambda h: h.activation(out=ST[:, n0:n1], in_=PM[:, 0:n1 - n0], func=AF.Copy,
                                                           scale=RS[:, 0:1]), [PM, RS], [ST])

            for g0 in range(0, nch, NACC):
                cs = list(range(g0, min(nch, g0 + NACC)))

                def mm(h, HT=HT, cs=cs):
                    for kt in range(KT):
                        for c in cs:
                            n0 = c * 512
                            n1 = min(npc, n0 + 512)
                            ins = h.matmul(pm[c][:, 0:n1 - n0], HT[:, kt, :], Wb[:, kt, n0:n1],
                                           start=(kt == 0), stop=(kt == KT - 1))
                    return ins
                S.op("pe", mm, [HT, Wb], [pm[c] for c in cs])
                evac_chunks(cs)
            if t == 0:
                cx.dbg("d_rs", RS, [128, 1], F32)
                cx.dbg("d_ss", SS, [128, 1], F32)
                cx.dbg("d_xb", XB, [128, D], BF16)
                cx.dbg("d_hT", HT, [128, KT, 128], BF16)
                cx.dbg("d_Wb", Wb, [128, KT, npc], BF16)
            S.dma("pool", proj_dt.ap[t * 128:(t + 1) * 128, 0:npc], ST[:], reads=[ST], writes=proj_dt.bufs(t * 128, t * 128 + 128))

        prep(0)
        for t in range(cx.nt):
            if t + 1 < cx.nt:
                prep(t + 1)
            compute(t)
        S.barrier()


def phase_outproj(cx, mix_dt, x_dt, w_ap, gbc_ap, out_dt, ident_bf, ntiles):
    nc, S = cx.nc, cx.S
    KT = D // 128
    with ExitStack() as es:
        Wb = load_weight_bf16(cx, es, "Wout", w_ap, D, D, None)
        gbc = cx.sb(es, "gbc", [128, D], F32)
        S.dma("sp", gbc[:], gbc_ap, writes=[gbc])
        mt = [cx.sb(es, f"mt{i}", [128, D], F32) for i in range(2)]
        mb = [cx.sb(es, f"mb{i}", [128, D], BF16) for i in range(2)]
        xt = [cx.sb(es, f"oxt{i}", [128, D], F32) for i in range(2)]
        mT = [cx.sb(es, f"mT{i}", [128, KT, 128], BF16) for i in range(2)]
        o = [cx.sb(es, f"o{i}", [128, D], F32) for i in range(2)]
        o2 = [cx.sb(es, f"o2{i}", [128, D], F32) for i in range(2)]
        junk = cx.sb(es, "ojunk", [128, D], BF16)
        ss = [cx.sb(es, f"oss{i}", [128, 1], F32) for i in range(2)]
        pt = [cx.ps(es, f"opt{i}", [128, 4, 128], BF16) for i in range(2)]
        pm = [cx.ps(es, f"opm{i}", [128, 512], F32) for i in range(4)]
        def prep(t):
            i2 = t % 2
            M, MB, X, MT, O, O2, SS = mt[i2], mb[i2], xt[i2], mT[i2], o[i2], o2[i2], ss[i2]
            r0, r1 = t * 128, (t + 1) * 128
            S.dma("sp", M[:], mix_dt.ap[r0:r1, :], reads=mix_dt.bufs(r0, r1), writes=[M])
            S.dma("sp", X[:], x_dt.ap[r0:r1, :], reads=x_dt.bufs(r0, r1), writes=[X])
            S.op("act", lambda h: h.copy(MB[:], M[:]), [M], [MB])
            for q in range(KT // 4):
                P = pt[q % 2]

                def tr(h, q=q, P=P):
                    for r in range(4):
                        kt = q * 4 + r
                        ins = h.transpose(P[:, r, :], MB[:, kt * 128:(kt + 1) * 128], ident_bf[:])
                    return ins
                S.op("pe", tr, [MB, ident_bf], [P])
                if q % 2 == 0:
                    S.op("dve", lambda h: h.tensor_copy(MT[:, q * 4:(q + 1) * 4, :], P[:]), [P], [MT])
                else:
                    S.op("act", lambda h: h.copy(MT[:, q * 4:(q + 1) * 4, :], P[:]), [P], [MT])
        def compute(t):
            i2 = t % 2
            M, MB, X, MT, O, O2, SS = mt[i2], mb[i2], xt[i2], mT[i2], o[i2], o2[i2], ss[i2]
            r0, r1 = t * 128, (t + 1) * 128

            def mm(h, MT=MT):
                for kt in range(KT):
                    for c in range(4):
                        ins = h.matmul(pm[c][:, :], MT[:, kt, :], Wb[:, kt, c * 512:(c + 1) * 512],
                                       start=(kt == 0), stop=(kt == KT - 1))
                return ins
            S.op("pe", mm, [MT, Wb], pm)
            for c in range(4):
                n0, n1 = c * 512, (c + 1) * 512
                PM = pm[c]
                if c % 2 == 0:
                    S.op("act", lambda h: h.copy(O[:, n0:n1], PM[:, :]), [PM], [O])
                else:
                    S.op("dve", lambda h: h.tensor_copy(O[:, n0:n1], PM[:, :]), [PM], [O])
            S.op("act", lambda h: h.activation(out=junk[:], in_=O[:], func=AF.Square, accum_out=SS[:]), [O], [junk, SS])
            S.op("dve", lambda h: h.tensor_scalar(SS[:], SS[:], 1.0 / D, EPS, ALU.mult, ALU.add), [SS], [SS])
            S.op("act", lambda h: h.sqrt(SS[:], SS[:]), [SS], [SS])
            S.op("dve", lambda h: h.reciprocal(SS[:], SS[:]), [SS], [SS])
            S.op("dve", lambda h: h.scalar_tensor_tensor(O2[:], O[:], SS[:, 0:1], gbc[:], ALU.mult, ALU.mult),
                 [O, SS, gbc], [O2])
            if t == 1:
                cx.dbg("d_o", O, [128, D], F32)
                cx.dbg("d_rs", SS, [128, 1], F32)
                cx.dbg("d_o2", O2, [128, D], F32)
            S.op("dve", lambda h: h.tensor_tensor(O2[:], O2[:], X[:], ALU.add), [O2, X], [O2])
            S.dma("pool", out_dt.ap[r0:r1, :], O2[:], reads=[O2], writes=out_dt.bufs(r0, r1))

        prep(0)
        for t in range(ntiles):
            if t + 1 < ntiles:
                prep(t + 1)
            compute(t)
        S.barrier()


def rope_tiles(cx, S, dst_bf, src, nh, cosb, sinb, tmp):
    s4 = src.rearrange("p (h two d) -> p h two d", two=2, d=32)
    d4 = dst_bf.rearrange("p (h two d) -> p h two d", two=2, d=32)
    x1, x2 = s4[:, :, 0, :], s4[:, :, 1, :]
    cb = cosb.unsqueeze(1).broadcast_to([128, nh, 32])
    sb_ = sinb.unsqueeze(1).broadcast_to([128, nh, 32])
    tv = tmp[:].rearrange("p f (h d) -> p f h d", d=32)
    return x1, x2, cb, sb_, tv, d4


def phase_swa(cx, proj_dt, mix_dt, cos_sb, sin_sb, ident_bf, mask_prev_ap, mask_own_ap, sinks_ap, mcol=0):
    nc, S = cx.nc, cx.S
    Lc, NTc = cx.L, cx.L // 128
    with ExitStack() as es:
        QKT = cx.sb(es, "swa_QKT", [64, 5, Lc], BF16)
        Vaug = cx.sb(es, "swa_V", [128, NTc, 65], BF16)
        mprev = cx.sb(es, "swa_mprev", [128, 4, 128], F32)
        mown = cx.sb(es, "swa_mown", [128, 4, 128], F32)
        esink = cx.sb(es, "swa_esink", [128, 4], F32)
        S.dma("sp", mprev[:], mask_prev_ap, writes=[mprev])
        S.dma("sp", mown[:], mask_own_ap, writes=[mown])
        S.dma("sp", esink[:], sinks_ap, writes=[esink])
        S.op("act", lambda h: h.activation(out=esink[:], in_=esink[:], func=AF.Exp), [esink], [esink])
        S.op("pool", lambda h: h.memset(Vaug[:, :, 64:65], 1.0), [], [Vaug])
        tin = [cx.sb(es, f"swa_tin{i}", [128, 384], F32) for i in range(2)]
        rtmp = [cx.sb(es, f"swa_rtmp{i}", [128, 4, 160], F32) for i in range(2)]
        rb = [cx.sb(es, f"swa_rb{i}", [128, 320], BF16) for i in range(2)]
        ptr = [cx.ps(es, f"swa_ptr{i}", [64, 8, 128], BF16) for i in range(2)]
        for t in range(NTc):
            i2 = t % 2
            T, TMP, RB, PT = tin[i2], rtmp[i2], rb[i2], ptr[i2]
            r0, r1 = t * 128, (t + 1) * 128
            S.dma("sp", T[:], proj_dt.ap[r0:r1, C_SQ:C_SQ + 384],
                  reads=proj_dt.bufs(r0, r1), writes=[T])
            x1, x2, cb, sb_, tv, d4 = rope_tiles(cx, S, RB[:], T[:, 0:320], 5, cos_sb[:, t, :], sin_sb[:, t, :], TMP)
            S.op("dve", lambda h: h.tensor_tensor(tv[:, 0], x1, cb, ALU.mult), [T, cos_sb], [TMP])
            S.op("dve", lambda h: h.tensor_tensor(tv[:, 1], x2, sb_, ALU.mult), [T, sin_sb], [TMP])
            S.op("dve", lambda h: h.tensor_tensor(tv[:, 2], x2, cb, ALU.mult), [T, cos_sb], [TMP])
            S.op("dve", lambda h: h.tensor_tensor(tv[:, 3], x1, sb_, ALU.mult), [T, sin_sb], [TMP])
            S.op("dve", lambda h: h.tensor_tensor(d4[:, :, 0, :], tv[:, 0], tv[:, 1], ALU.subtract), [TMP], [RB])
            S.op("dve", lambda h: h.tensor_tensor(d4[:, :, 1, :], tv[:, 2], tv[:, 3], ALU.add), [TMP], [RB])
            S.op("pool", lambda h: h.tensor_copy(Vaug[:, t, 0:64], T[:, 320:384]), [T], [Vaug])

            def tr(h, RB=RB, PT=PT):
                for hh in range(5):
                    ins = h.transpose(PT[:, hh, :], RB[:, hh * 64:(hh + 1) * 64], ident_bf[:])
                return ins
            S.op("pe", tr, [RB, ident_bf], [PT])
            S.op("act", lambda h: h.copy(QKT[:, :, r0:r1], PT[:, 0:5, :]), [PT], [QKT])
        sg = [cx.sb(es, f"swa_sg{i}", [128, 256], F32) for i in range(2)]
        st = [cx.sb(es, f"swa_st{i}", [128, 4, 128], F32) for i in range(2)]
        pT = [cx.sb(es, f"swa_pT{i}", [128, 4, 128], BF16) for i in range(4)]
        den = [cx.sb(es, f"swa_den{i}", [128, 4], F32) for i in range(2)]
        yb = [cx.sb(es, f"swa_y{i}", [128, 256], F32) for i in range(2)]
        pss = [cx.ps(es, f"swa_pss{i}", [128, 4, 128], F32) for i in range(2)]
        pso = [cx.ps(es, f"swa_pso{i}", [128, 4, 128], F32) for i in range(2)]
        sge = [cx.sb(es, f"swa_sge{i}", [128, 256], F32) for i in range(2)]
        items = [(t, kk) for t in range(NTc) for kk in ([t - 1, t] if t > 0 else [t])]

        def score(i):
            t, kk = items[i]
            r0, r1 = t * 128, (t + 1) * 128
            if kk == max(t - 1, 0):
                SG, SE = sg[t % 2], sge[t % 2]
                S.dma("sp", SG[:], proj_dt.ap[r0:r1, C_SG:C_SG + 256], reads=proj_dt.bufs(r0, r1), writes=[SG])
                S.op("act", lambda h: h.activation(out=SG[:], in_=SG[:], func=AF.Silu), [SG], [SG])
            PS = pss[i % 2]
            S.op("pe", lambda h: h.matmul(PS[:], QKT[:, 4, kk * 128:(kk + 1) * 128], QKT[:, 0:4, r0:r1],
                                          start=True, stop=True), [QKT], [PS])

        def finish_item(i):
            t, kk = items[i]
            r0, r1 = t * 128, (t + 1) * 128
            PS, ST, PTB, PO = pss[i % 2], st[i % 2], pT[i % 4], pso[t % 2]
            M = mown if kk == t else mprev
            S.op("dve", lambda h: h.scalar_tensor_tensor(ST[:], PS[:], 0.125, M[:], ALU.mult, ALU.add), [PS, M], [ST])
            S.op("act", lambda h: h.activation(out=PTB[:], in_=ST[:], func=AF.Exp), [ST], [PTB])
            first = (kk == max(t - 1, 0))

            def pv(h):
                for hh in range(4):
                    ins = h.matmul(PO[:, hh, 0:65], PTB[:, hh, :], Vaug[:, kk, :],
                                   start=(first and hh == 0), stop=(kk == t and hh == 3))
                return ins
            S.op("pe", pv, [PTB, Vaug], [PO])
            if kk == t:
                SG, DEN, Y = sg[t % 2], den[t % 2], yb[t % 2]
                S.op("dve", lambda h: h.tensor_tensor(DEN[:], PO[:, :, 64], esink[:], ALU.add), [PO, esink], [DEN])
                S.op("dve", lambda h: h.reciprocal(DEN[:], DEN[:]), [DEN], [DEN])
                for hh in range(4):
                    S.op("act", lambda h: h.activation(out=Y[:, hh * 64:(hh + 1) * 64], in_=PO[:, hh, 0:64], func=AF.Copy,
                                                       scale=DEN[:, hh:hh + 1]), [PO, DEN], [Y])
                S.op("dve", lambda h: h.tensor_tensor(Y[:], Y[:], SG[:], ALU.mult), [Y, SG], [Y])
                S.dma("pool", mix_dt.ap[r0:r1, mcol + 512:mcol + 768], Y[:], reads=[Y], writes=mix_dt.bufs(r0, r1))

        score(0)
        for i in range(len(items)):
            if i + 1 < len(items):
                score(i + 1)
            finish_item(i)
        S.barrier()


def phase_moba(cx, proj_dt, mix_dt, cos_sb, sin_sb, ident_bf, ident_f, mask_own_ap, gmask_ap, blkind_ap, mcol=0):
    nc, S = cx.nc, cx.S
    Lc, NTc = cx.L, cx.L // 128
    NB = 16
    with ExitStack() as es:
        Q32 = cx.sb(es, "mo_Q32", [128, NTc, 256], F32)
        KaugT = cx.sb(es, "mo_KaugT", [80, 4, Lc], BF16)
        Vaug = cx.sb(es, "mo_V", [128, NTc, 4, 65], BF16)
        kmeanT = cx.sb(es, "mo_kmT", [64, 4, NB], F32)
        mown = cx.sb(es, "mo_mown", [128, 4, 128], F32)
        gmask = cx.sb(es, "mo_gmask", [128, NB, NB], F32)
        c256 = cx.sb(es, "mo_c256", [128, 1], F32)
        S.dma("sp", mown[:], mask_own_ap, writes=[mown])
        S.dma("sp", gmask[:], gmask_ap, writes=[gmask])
        for hh in range(4):
            S.dma("sp", KaugT[64:80, hh, :], blkind_ap[:, 0:Lc], writes=[KaugT])
        S.op("pool", lambda h: h.memset(c256[:], 1.0 / 256.0), [], [c256])
        S.op("pool", lambda h: h.memset(Vaug[:, :, :, 64:65], 1.0), [], [Vaug])
        S.op("pool", lambda h: h.memset(kmeanT[:], 0.0), [], [kmeanT])
        tin = [cx.sb(es, f"mo_tin{i}", [128, 768], F32) for i in range(2)]
        rtmp = [cx.sb(es, f"mo_rtmp{i}", [128, 4, 256], F32) for i in range(2)]
        k32 = [cx.sb(es, f"mo_k32{i}", [128, 256], F32) for i in range(2)]
        kb = [cx.sb(es, f"mo_kb{i}", [128, 256], BF16) for i in range(2)]
        with ExitStack() as es1:
            ptr = [cx.ps(es1, f"mo_ptr{i}", [64, 8, 128], BF16) for i in range(2)]
            kmps = cx.ps(es1, "mo_kmps", [64, 4, 128], F32)
            def prepA(t):
                i2 = t % 2
                T, TMP, K32, KB, PT = tin[i2], rtmp[i2], k32[i2], kb[i2], ptr[i2]
                r0, r1 = t * 128, (t + 1) * 128
                S.dma("sp", T[:], proj_dt.ap[r0:r1, C_MQ:C_MQ + 768],
                      reads=proj_dt.bufs(r0, r1), writes=[T])
                s4 = T[:, 0:512].rearrange("p (h two d) -> p h two d", two=2, d=32)
                x1, x2 = s4[:, :, 0, :], s4[:, :, 1, :]
                cb = cos_sb[:, t, :].unsqueeze(1).broadcast_to([128, 8, 32])
                sb_ = sin_sb[:, t, :].unsqueeze(1).broadcast_to([128, 8, 32])
                tv = TMP[:].rearrange("p f (h d) -> p f h d", d=32)
                S.op("dve", lambda h: h.tensor_tensor(tv[:, 0], x1, cb, ALU.mult), [T, cos_sb], [TMP])
                S.op("dve", lambda h: h.tensor_tensor(tv[:, 1], x2, sb_, ALU.mult), [T, sin_sb], [TMP])
                S.op("dve", lambda h: h.tensor_tensor(tv[:, 2], x2, cb, ALU.mult), [T, cos_sb], [TMP])
                S.op("dve", lambda h: h.tensor_tensor(tv[:, 3], x1, sb_, ALU.mult), [T, sin_sb], [TMP])
                q4 = Q32[:, t, :].rearrange("p (h two d) -> p h two d", two=2, d=32)
                k4 = K32[:].rearrange("p (h two d) -> p h two d", two=2, d=32)
                S.op("dve", lambda h: h.tensor_tensor(q4[:, :, 0, :], tv[:, 0, 0:4], tv[:, 1, 0:4], ALU.subtract), [TMP], [Q32])
                S.op("dve", lambda h: h.tensor_tensor(q4[:, :, 1, :], tv[:, 2, 0:4], tv[:, 3, 0:4], ALU.add), [TMP], [Q32])
                S.op("dve", lambda h: h.tensor_tensor(k4[:, :, 0, :], tv[:, 0, 4:8], tv[:, 1, 4:8], ALU.subtract), [TMP], [K32])
                S.op("dve", lambda h: h.tensor_tensor(k4[:, :, 1, :], tv[:, 2, 4:8], tv[:, 3, 4:8], ALU.add), [TMP], [K32])

            def prepB(t):
                i2 = t % 2
                T, TMP, K32, KB, PT = tin[i2], rtmp[i2], k32[i2], kb[i2], ptr[i2]
                r0, r1 = t * 128, (t + 1) * 128
                S.op("act", lambda h: h.copy(KB[:], K32[:]), [K32], [KB])
                S.op("pool", lambda h: h.tensor_copy(Vaug[:, t, :, 0:64], T[:, 512:768].rearrange("p (h d) -> p h d", d=64)),
                     [T], [Vaug])
                n = t // 2

                def km(h, K32=K32, n=n, t=t):
                    for hh in range(4):
                        ins = h.matmul(kmps[:, hh, n:n + 1], K32[:, hh * 64:(hh + 1) * 64], c256[:, 0:1],
                                       start=(t % 2 == 0 and hh == 0), stop=(t % 2 == 1 and hh == 3))
                    return ins
                S.op("pe", km, [K32, c256], [kmps])

                def tr(h, KB=KB, PT=PT):
                    for hh in range(4):
                        ins = h.transpose(PT[:, hh, :], KB[:, hh * 64:(hh + 1) * 64], ident_bf[:])
                    return ins
                S.op("pe", tr, [KB, ident_bf], [PT])
                S.op("act", lambda h: h.copy(KaugT[0:64, :, r0:r1], PT[:, 0:4, :]), [PT], [KaugT])
            prepA(0)
            for t in range(NTc):
                if t + 1 < NTc:
                    prepA(t + 1)
                prepB(t)
            S.op("dve", lambda h: h.tensor_copy(kmeanT[:, :, 0:Lc // 256], kmps[:, :, 0:Lc // 256]), [kmps], [kmeanT])
            S.barrier()
        mg = [cx.sb(es, f"mo_mg{i}", [128, 256], F32) for i in range(2)]
        mge = [cx.sb(es, f"mo_mge{i}", [128, 256], F32) for i in range(2)]
        qT32 = [cx.sb(es, f"mo_qT32{i}", [64, 4, 128], F32) for i in range(2)]
        gm = [cx.sb(es, f"mo_gm{i}", [128, 4, NB], F32) for i in range(2)]
        top8 = [cx.sb(es, f"mo_top8{i}", [128, 4, 8], F32) for i in range(2)]
        qaug = [cx.sb(es, f"mo_qaug{i}", [128, 4, 80], BF16) for i in range(2)]
        QaugT = [cx.sb(es, f"mo_QaugT{i}", [80, 4, 128], BF16) for i in range(2)]
        st = [cx.sb(es, f"mo_st{i}", [128, 4, 128], F32) for i in range(2)]
        pT = [cx.sb(es, f"mo_pT{i}", [128, 4, 128], BF16) for i in range(4)]
        den = [cx.sb(es, f"mo_den{i}", [128, 4], F32) for i in range(2)]
        yb = [cx.sb(es, f"mo_y{i}", [128, 256], F32) for i in range(2)]
        psq = cx.ps(es, "mo_psq", [64, 4, 128], F32)
        psg = cx.ps(es, "mo_psg", [128, 4, 128], F32)
        psa = cx.ps(es, "mo_psa", [80, 8, 128], BF16)
        pss = [cx.ps(es, f"mo_pss{i}", [128, 4, 128], F32) for i in range(2)]
        pso = [cx.ps(es, f"mo_pso{i}", [128, 4, 128], F32) for i in range(2)]
        ctxs = {}

        def prologue(t):
            i2 = t % 2
            r0, r1 = t * 128, (t + 1) * 128
            qb = t // 2
            MG, QT, GM, T8, QA, QAT = mg[i2], qT32[i2], gm[i2], top8[i2], qaug[i2], QaugT[i2]
            S.dma("sp", MG[:], proj_dt.ap[r0:r1, C_MG:C_MG + 256], reads=proj_dt.bufs(r0, r1), writes=[MG])
            S.op("act", lambda h: h.activation(out=MG[:], in_=MG[:], func=AF.Silu), [MG], [MG])

            def trq(h):
                for hh in range(4):
                    ins = h.transpose(psq[:, hh, :], Q32[:, t, hh * 64:(hh + 1) * 64], ident_f[:])
                return ins
            S.op("pe", trq, [Q32, ident_f], [psq])
            S.op("dve", lambda h: h.tensor_copy(QT[:], psq[:]), [psq], [QT])

            def gate(h):
                for hh in range(4):
                    ins = h.matmul(psg[:, hh, 0:NB], QT[:, hh, :], kmeanT[:, hh, :], start=True, stop=True)
                return ins
            S.op("pe", gate, [QT, kmeanT], [psg])
            S.op("dve", lambda h: h.tensor_tensor(GM[:], psg[:, :, 0:NB],
                                                  gmask[:, qb, :].unsqueeze(1).broadcast_to([128, 4, NB]), ALU.add),
                 [psg, gmask], [GM])
            for hh in range(4):
                S.op("dve", lambda h: h.max(T8[:, hh, :], GM[:, hh, :]), [GM], [T8])
            for hh in range(4):
                S.op("dve", lambda h: h.tensor_scalar(QA[:, hh, 64:80], GM[:, hh, :], T8[:, hh, 2:3], -30000.0,
                                                      ALU.is_lt, ALU.mult), [GM, T8], [QA])
            S.op("pool", lambda h: h.memset(QA[:, :, 64 + qb:65 + qb], 0.0), [QA], [QA])
            S.op("act", lambda h: h.copy(QA[:, :, 0:64], Q32[:, t, :].rearrange("p (h d) -> p h d", d=64)), [Q32], [QA])

            def tra(h):
                for hh in range(4):
                    ins = h.transpose(psa[:, hh, :], QA[:, hh, :], ident_bf[:])
                return ins
            S.op("pe", tra, [QA, ident_bf], [psa])
            S.op("dve", lambda h: h.tensor_copy(QAT[:], psa[:, 0:4, :]), [psa], [QAT])

        items = [(t, kk) for t in range(NTc) for kk in range(t + 1)]

        def score(i):
            t, kk = items[i]
            if kk == 0:
                prologue(t)
            PS, QAT = pss[i % 2], QaugT[t % 2]

            def sc(h):
                for hh in range(4):
                    ins = h.matmul(PS[:, hh, :], KaugT[:, hh, kk * 128:(kk + 1) * 128], QAT[:, hh, :],
                                   start=True, stop=True)
                return ins
            S.op("pe", sc, [KaugT, QAT], [PS])

        def finish_item(i):
            t, kk = items[i]
            PS, ST, PTB, PO = pss[i % 2], st[i % 2], pT[i % 4], pso[t % 2]
            if kk == t:
                S.op("dve", lambda h: h.scalar_tensor_tensor(ST[:], PS[:], 0.125, mown[:], ALU.mult, ALU.add),
                     [PS, mown], [ST])
                S.op("act", lambda h: h.activation(out=PTB[:], in_=ST[:], func=AF.Exp), [ST], [PTB])
            else:
                S.op("act", lambda h: h.activation(out=PTB[:], in_=PS[:], func=AF.Exp, scale=0.125), [PS], [PTB])

            def pv(h):
                for hh in range(4):
                    ins = h.matmul(PO[:, hh, 0:65], PTB[:, hh, :], Vaug[:, kk, hh, :],
                                   start=(kk == 0 and hh == 0), stop=(kk == t and hh == 3))
                return ins
            S.op("pe", pv, [PTB, Vaug], [PO])
            if kk == t:
                i2 = t % 2
                r0, r1 = t * 128, (t + 1) * 128
                MG, DEN, Y = mg[i2], den[i2], yb[i2]
                S.op("dve", lambda h: h.reciprocal(DEN[:], PO[:, :, 64]), [PO], [DEN])
                for hh in range(4):
                    S.op("act", lambda h: h.activation(out=Y[:, hh * 64:(hh + 1) * 64], in_=PO[:, hh, 0:64], func=AF.Copy,
                                                       scale=DEN[:, hh:hh + 1]), [PO, DEN], [Y])
                S.op("dve", lambda h: h.tensor_tensor(Y[:], Y[:], MG[:], ALU.mult), [Y, MG], [Y])
                S.dma("pool", mix_dt.ap[r0:r1, mcol:mcol + 256], Y[:], reads=[Y], writes=mix_dt.bufs(r0, r1))

        score(0)
        for i in range(len(items)):
            if i + 1 < len(items):
                score(i + 1)
            finish_item(i)
        S.barrier()


def phase_ssd(cx, proj_dt, mix_dt, ident_bf, cst, mcol=0):
    nc, S = cx.nc, cx.S
    Lc, NTc = cx.L, cx.L // 128
    with ExitStack() as es:
        def ld(name, shape, dt=F32):
            b = cx.sb(es, "ssd_" + name, shape, dt)
            S.dma("sp", b[:], cst[name], writes=[b])
            return b
        convw = ld("convw", [128, 4, 512]); convb = ld("convb", [128, 512]); dtb = ld("dtb", [128, 4])
        Abc = ld("alog", [128, 4]); dskip = ld("dskip", [128, 256]); normw = ld("normw", [128, 256])
        tri = ld("tri", [128, 128]); ones = ld("ones", [128, 128]); maskT = ld("maskT", [128, 128])
        S.op("act", lambda h: h.activation(out=Abc[:], in_=Abc[:], func=AF.Exp), [Abc], [Abc])
        S.op("dve", lambda h: h.tensor_scalar(Abc[:], Abc[:], -1.0, None, ALU.mult), [Abc], [Abc])
        prev32 = cx.sb(es, "ssd_prev32", [128, 256], F32)
        prevb = cx.sb(es, "ssd_prevb", [128, 256], BF16)
        S.op("pool", lambda h: h.memset(prev32[:], 0.0), [], [prev32])
        S.op("pool", lambda h: h.memset(prevb[:], 0.0), [], [prevb])
        Tj = [[cx.sb(es, f"ssd_T{i}_{j}", [128, 512], F32) for j in range(4)] for i in range(2)]
        zt = [cx.sb(es, f"ssd_z{i}", [128, 256], F32) for i in range(2)]
        dtt = [cx.sb(es, f"ssd_dt{i}", [128, 4], F32) for i in range(2)]
        xa = [cx.sb(es, f"ssd_xa{i}", [128, 512], F32) for i in range(2)]
        sm = [cx.sb(es, f"ssd_sm{i}", [128, 8, 4], F32) for i in range(2)]
        Xf = [cx.sb(es, f"ssd_X{i}", [128, 256], F32) for i in range(2)]
        Xb = [cx.sb(es, f"ssd_Xb{i}", [128, 256], BF16) for i in range(2)]
        Xd = [cx.sb(es, f"ssd_Xd{i}", [128, 256], BF16) for i in range(2)]
        BCb = [cx.sb(es, f"ssd_BCb{i}", [128, 256], BF16) for i in range(2)]
        BCT = [cx.sb(es, f"ssd_BCT{i}", [128, 2, 128], BF16) for i in range(2)]
        R4 = [cx.sb(es, f"ssd_R4{i}", [128, 4, 128], F32) for i in range(2)]
        TD4 = [cx.sb(es, f"ssd_TD4{i}", [128, 4, 128], F32) for i in range(2)]
        M4 = [cx.sb(es, f"ssd_M4{i}", [128, 4, 128], BF16) for i in range(2)]
        identf = ld("identf", [128, 128])
        mask4 = cx.sb(es, "ssd_mask4", [128, 4, 128], F32)
        S.op("dve", lambda h: h.tensor_copy(mask4[:], maskT[:].unsqueeze(1).broadcast_to([128, 4, 128])), [maskT], [mask4])
        y1 = [cx.sb(es, f"ssd_y1{i}", [128, 256], F32) for i in range(2)]
        y2 = [cx.sb(es, f"ssd_y2{i}", [128, 256], F32) for i in range(2)]
        junk = cx.sb(es, "ssd_junk", [128, 256], F32)
        csb = [cx.sb(es, f"ssd_cs{i}", [128, 256], F32) for i in range(2)]
        ps_t = cx.ps(es, "ssd_ps_t", [128, 8, 128], BF16)
        ps_s = cx.ps(es, "ssd_ps_s", [128, 512], F32)
        ps_cb = cx.ps(es, "ssd_ps_cb", [128, 512], F32)
        ps_d = cx.ps(es, "ssd_ps_d", [128, 4, 128], F32)
        ps_yd = cx.ps(es, "ssd_ps_yd", [128, 512], F32)
        ps_yo = cx.ps(es, "ssd_ps_yo", [128, 512], F32)
        ps_cs = cx.ps(es, "ssd_ps_cs", [128, 512], F32)
        ndc = [0]

        def front(t):
            nd = ndc[0]
            i2 = t % 2
            r0, r1 = t * 128, (t + 1) * 128
            T, Z, DT, XA, SM, X, XB, XD, BC, BT = Tj[i2], zt[i2], dtt[i2], xa[i2], sm[i2], Xf[i2], Xb[i2], Xd[i2], BCb[i2], BCT[i2]
            for j in range(4):
                sh = 3 - j
                q = "sp"
                if r0 - sh >= 0:
                    S.dma(q, T[j][:], proj_dt.ap[r0 - sh:r1 - sh, C_XS:C_XS + 512],
                          reads=proj_dt.bufs(max(r0 - sh, 0), r1), writes=[T[j]])
                else:
                    S.op("pool", lambda h: h.memset(T[j][0:32, :], 0.0), [], [T[j]])
                    S.dma(q, T[j][sh:128, :], proj_dt.ap[0:128 - sh, C_XS:C_XS + 512],
                          reads=proj_dt.bufs(0, 128), writes=[T[j]])
            S.dma("sp", Z[:], proj_dt.ap[r0:r1, C_Z:C_Z + 256], reads=proj_dt.bufs(r0, r1), writes=[Z])
            S.dma("sp", DT[:], proj_dt.ap[r0:r1, C_DT:C_DT + 4], reads=proj_dt.bufs(r0, r1), writes=[DT])
            for j in range(4):
                ek = "dve"
                S.op(ek, lambda h: h.tensor_tensor(T[j][:], T[j][:], convw[:, j, :], ALU.mult), [T[j], convw], [T[j]])
            S.op("dve", lambda h: h.tensor_tensor(T[0][:], T[0][:], T[2][:], ALU.add), [T[0], T[2]], [T[0]])
            S.op("dve", lambda h: h.tensor_tensor(T[1][:], T[1][:], T[3][:], ALU.add), [T[1], T[3]], [T[1]])
            S.op("dve", lambda h: h.tensor_tensor(T[0][:], T[0][:], T[1][:], ALU.add), [T[0], T[1]], [T[0]])
            S.op("dve", lambda h: h.tensor_tensor(T[0][:], T[0][:], convb[:], ALU.add), [T[0], convb], [T[0]])
            S.op("act", lambda h: h.activation(out=XA[:], in_=T[0][:], func=AF.Silu), [T[0]], [XA])
            S.op("act", lambda h: h.activation(out=Z[:], in_=Z[:], func=AF.Silu), [Z], [Z])
            S.op("dve", lambda h: h.tensor_tensor(DT[:], DT[:], dtb[:], ALU.add), [DT, dtb], [DT])
            S.op("act", lambda h: h.activation(out=DT[:], in_=DT[:], func=AF.Exp), [DT], [DT])
            S.op("act", lambda h: h.activation(out=DT[:], in_=DT[:], func=AF.Ln, bias=1.0), [DT], [DT])
            S.op("dve", lambda h: h.tensor_tensor(SM[:, 0, :], DT[:], Abc[:], ALU.mult), [DT, Abc], [SM])

            def cum(h, SM=SM):
                h.matmul(ps_s[:, 0:4], tri[:], SM[:, 0, :], start=True, stop=False)
                return h.matmul(ps_s[:, 4:8], ones[:], SM[:, 0, :], start=False, stop=True)
            S.op("pe", cum, [tri, ones, SM], [ps_s])
            S.op("dve", lambda h: h.tensor_copy(SM[:, 1, :], ps_s[:, 0:4]), [ps_s], [SM])
            S.op("dve", lambda h: h.tensor_scalar(SM[:, 2, :], ps_s[:, 0:4], -1.0, None, ALU.mult), [ps_s], [SM])
            S.op("dve", lambda h: h.tensor_tensor(SM[:, 5, :], ps_s[:, 4:8], SM[:, 1, :], ALU.subtract), [ps_s, SM], [SM])
            S.op("act", lambda h: h.activation(out=SM[:, 3, :], in_=SM[:, 1, :], func=AF.Exp), [SM], [SM])
            S.op("act", lambda h: h.activation(out=SM[:, 4, :], in_=ps_s[:, 4:8], func=AF.Exp), [ps_s], [SM])
            S.op("act", lambda h: h.activation(out=SM[:, 5, :], in_=SM[:, 5, :], func=AF.Exp), [SM], [SM])
            x3 = X[:].rearrange("p (h d) -> p h d", d=64)
            S.op("dve", lambda h: h.tensor_tensor(x3, XA[:, 0:256].rearrange("p (h d) -> p h d", d=64),
                                                  DT[:].unsqueeze(2).broadcast_to([128, 4, 64]), ALU.mult), [XA, DT], [X])
            S.op("act", lambda h: h.copy(XB[:], X[:]), [X], [XB])
            S.op("dve", lambda h: h.tensor_tensor(XD[:].rearrange("p (h d) -> p h d", d=64), x3,
                                                  SM[:, 5, :].unsqueeze(2).broadcast_to([128, 4, 64]), ALU.mult), [X, SM], [XD])
            S.op("act", lambda h: h.copy(BC[:], XA[:, 256:512]), [XA], [BC])

            def trbc(h, BC=BC):
                h.transpose(ps_t[:, 0, :], BC[:, 0:128], ident_bf[:])
                return h.transpose(ps_t[:, 1, :], BC[:, 128:256], ident_bf[:])
            S.op("pe", trbc, [BC, ident_bf], [ps_t])
            S.op("act", lambda h: h.copy(BT[:], ps_t[:, 0:2, :]), [ps_t], [BT])
            S.op("pe", lambda h: h.matmul(ps_cb[:, 0:128], BT[:, 0, :], BT[:, 1, :], start=True, stop=True), [BT], [ps_cb])
            S.op("pe", lambda h: h.matmul(ps_cs[:, 0:256], BC[:, 0:128], XD[:], start=True, stop=True), [BC, XD], [ps_cs])
            S.op("act", lambda h: h.copy(csb[i2][:], ps_cs[:, 0:256]), [ps_cs], [csb[i2]])
            RR, TD, MM = R4[i2], TD4[i2], M4[i2]
            S.op("dve", lambda h: h.tensor_tensor(RR[:], tri[:].unsqueeze(1).broadcast_to([128, 4, 128]),
                                                  SM[:, 0, :].unsqueeze(2).broadcast_to([128, 4, 128]), ALU.mult), [tri, SM], [RR])

            def dbc(h):
                h.matmul(ps_d[:], ones[:], RR[:], start=True, stop=False)
                return h.matmul(ps_d[:], identf[:], mask4[:], start=False, stop=True)
            S.op("pe", dbc, [ones, RR, identf, mask4], [ps_d])
            S.op("dve", lambda h: h.tensor_tensor(TD[:], ps_d[:], SM[:, 1, :].unsqueeze(2).broadcast_to([128, 4, 128]), ALU.subtract),
                 [ps_d, SM], [TD])
            S.op("act", lambda h: h.activation(out=TD[:], in_=TD[:], func=AF.Exp), [TD], [TD])
            S.op("dve", lambda h: h.tensor_tensor(MM[:], TD[:], ps_cb[:, 0:128].unsqueeze(1).broadcast_to([128, 4, 128]), ALU.mult),
                 [TD, ps_cb], [MM])

            def ydiag(h):
                for hh in range(4):
                    ins = h.matmul(ps_yd[:, hh * 64:(hh + 1) * 64], MM[:, hh, :], XB[:, hh * 64:(hh + 1) * 64],
                                   start=(hh == 0), stop=(hh == 3))
                return ins
            S.op("pe", ydiag, [MM, XB], [ps_yd])
            ndc[0] = nd
            S.op("act", lambda h: h.copy(y1[i2][:], ps_yd[:, 0:256]), [ps_yd], [y1[i2]])

        def back(t):
            i2 = t % 2
            r0, r1 = t * 128, (t + 1) * 128
            Z, XA, SM, BT = zt[i2], xa[i2], sm[i2], BCT[i2]
            Y1, Y2 = y1[i2], y2[i2]
            S.op("pe", lambda h: h.matmul(ps_yo[:, 0:256], BT[:, 1, :], prevb[:], start=True, stop=True), [BT, prevb], [ps_yo])
            p3 = prev32[:].rearrange("p (h d) -> p h d", d=64)
            S.op("dve", lambda h: h.tensor_tensor(p3, p3, SM[:, 4, :].unsqueeze(2).broadcast_to([128, 4, 64]), ALU.mult),
                 [prev32, SM], [prev32])
            S.op("dve", lambda h: h.tensor_tensor(prev32[:], prev32[:], csb[i2][:], ALU.add), [prev32, csb[i2]], [prev32])
            S.op("act", lambda h: h.copy(prevb[:], prev32[:]), [prev32], [prevb])
            for hh in range(4):
                sl = slice(hh * 64, (hh + 1) * 64)
                S.op("dve", lambda h: h.scalar_tensor_tensor(Y1[:, sl], ps_yo[:, sl], SM[:, 3, hh:hh + 1], Y1[:, sl],
                                                             ALU.mult, ALU.add), [ps_yo, SM, Y1], [Y1])
            S.op("pool", lambda h: h.tensor_tensor(Y2[:], XA[:, 0:256], dskip[:], ALU.mult), [XA, dskip], [Y2])
            S.op("dve", lambda h: h.tensor_tensor(Y1[:], Y1[:], Y2[:], ALU.add), [Y1, Y2], [Y1])
            S.op("dve", lambda h: h.tensor_tensor(Y1[:], Y1[:], Z[:], ALU.mult), [Y1, Z], [Y1])
            S.op("act", lambda h: h.activation(out=junk[:], in_=Y1[:], func=AF.Square, accum_out=SM[:, 6, 0:1]), [Y1], [junk, SM])
            S.op("dve", lambda h: h.tensor_scalar(SM[:, 6, 0:1], SM[:, 6, 0:1], 1.0 / 256.0, EPS, ALU.mult, ALU.add), [SM], [SM])
            S.op("act", lambda h: h.sqrt(SM[:, 6, 0:1], SM[:, 6, 0:1]), [SM], [SM])
            S.op("dve", lambda h: h.reciprocal(SM[:, 6, 0:1], SM[:, 6, 0:1]), [SM], [SM])
            S.op("dve", lambda h: h.scalar_tensor_tensor(Y2[:], Y1[:], SM[:, 6, 0:1], normw[:], ALU.mult, ALU.mult),
                 [Y1, SM, normw], [Y2])
            S.dma("pool", mix_dt.ap[r0:r1, mcol + 256:mcol + 512], Y2[:], reads=[Y2], writes=mix_dt.bufs(r0, r1))

        front(0)
        for t in range(NTc):
            if t + 1 < NTc:
                front(t + 1)
            back(t)
        S.barrier()


class StopPhase(Exception):
    pass


def stage(cx, n):
    if getattr(cx, "stop_stage", None) == n and not cx.S.muted:
        cx.S.barrier()
        cx.S.muted = True


def cmul(S, ek, out_re, out_im, a_re, a_im, b_re, b_im, t1, t2, reads, writes, conj_b=False):
    o1 = ALU.subtract if not conj_b else ALU.add
    o2 = ALU.add if not conj_b else ALU.subtract
    S.op(ek, lambda h: h.tensor_tensor(t1, a_re, b_re, ALU.mult), reads, writes)
    S.op(ek, lambda h: h.tensor_tensor(t2, a_im, b_im, ALU.mult), reads, writes)
    S.op(ek, lambda h: h.tensor_tensor(out_re, t1, t2, o1), reads, writes)
    S.op(ek, lambda h: h.tensor_tensor(t1, a_im, b_re, ALU.mult), reads, writes)
    S.op(ek, lambda h: h.tensor_tensor(t2, a_re, b_im, ALU.mult), reads, writes)
    S.op(ek, lambda h: h.tensor_tensor(out_im, t1, t2, o2), reads, writes)


def phase_s5(cx, proj_dt, mix_dt, ident_bf, ident_f, cst):
    nc, S = cx.nc, cx.S
    Lc = cx.L
    T = 16
    SEG = min(Lc, 2048)
    NSEG = Lc // SEG
    NC = SEG // T
    NCT = NC + 1
    with ExitStack() as es:
        def ld(name, shape, dt=F32, q="sp"):
            b = cx.sb(es, "s5_" + name, shape, dt)
            S.dma(q, b[:], cst[name], writes=[b])
            return b
        are = ld("are", [128, 16]); aim = ld("aim", [128, 16]); ldt = ld("ldt", [128, 16])
        ccre = ld("ccre", [128, 16, 32]); ccim = ld("ccim", [128, 16, 32])
        dfm = ld("dfm", [128, 4]); glub = ld("glub", [128, 4]); kvec = ld("kvec", [128, 256])
        Wg = load_weight_bf16(cx, es, "s5_Wg", cst["gluw"], 512, 512, None)
        BT = [cx.sb(es, f"s5_BT{i}", [128, 16, 128], BF16) for i in range(2)]
        Ere = cx.sb(es, "s5_Ere", [128, 16, T + 1], F32); Eim = cx.sb(es, "s5_Eim", [128, 16, T + 1], F32)
        Rk = cx.sb(es, "s5_Rk", [128, 16, T + 1], F32)
        E2re = cx.sb(es, "s5_E2re", [128, 16, NCT], F32); E2im = cx.sb(es, "s5_E2im", [128, 16, NCT], F32)
        R2 = cx.sb(es, "s5_R2", [128, 16, NCT], F32)
        sm = cx.sb(es, "s5_sm", [128, 12, 16], F32)
        pmax = 1
        while pmax * 2 < max(T + 1, NCT):
            pmax *= 2
        Enim = cx.sb(es, "s5_Enim", [128, 16, T + 1], F32)
        hp = cx.sb(es, "s5_halfpi", [128, 1], F32)
        es_tb = ExitStack()
        tb = cx.sb(es_tb, "s5_tb", [128, 4, 16 + 16 * pmax], F32)
        SMALL = [sm]
        S.op("act", lambda h: h.activation(out=sm[:, 0, :], in_=ldt[:], func=AF.Exp), [ldt], SMALL)
        S.op("dve", lambda h: h.tensor_tensor(sm[:, 1, :], are[:], sm[:, 0, :], ALU.mult), [are] + SMALL, SMALL)
        S.op("dve", lambda h: h.tensor_tensor(sm[:, 2, :], aim[:], sm[:, 0, :], ALU.mult), [aim] + SMALL, SMALL)
        S.op("act", lambda h: h.activation(out=sm[:, 3, :], in_=sm[:, 1, :], func=AF.Exp), SMALL, SMALL)
        S.op("pool", lambda h: h.memset(hp[:], float(np.pi / 2)), [], [hp])
        S.op("act", lambda h: h.activation(out=sm[:, 5, :], in_=sm[:, 2, :], func=AF.Sin, scale=1.0 / 64), SMALL, SMALL)
        S.op("act", lambda h: h.activation(out=sm[:, 4, :], in_=sm[:, 2, :], func=AF.Sin, scale=1.0 / 64, bias=hp[:, 0:1]),
             SMALL + [hp], SMALL)
        for _ in range(6):
            S.op("dve", lambda h: h.tensor_tensor(sm[:, 8, :], sm[:, 4, :], sm[:, 4, :], ALU.mult), SMALL, SMALL)
            S.op("dve", lambda h: h.tensor_tensor(sm[:, 9, :], sm[:, 5, :], sm[:, 5, :], ALU.mult), SMALL, SMALL)
            S.op("dve", lambda h: h.tensor_tensor(sm[:, 10, :], sm[:, 4, :], sm[:, 5, :], ALU.mult), SMALL, SMALL)
            S.op("dve", lambda h: h.tensor_tensor(sm[:, 4, :], sm[:, 8, :], sm[:, 9, :], ALU.subtract), SMALL, SMALL)
            S.op("dve", lambda h: h.tensor_scalar(sm[:, 5, :], sm[:, 10, :], 2.0, None, ALU.mult), SMALL, SMALL)
        S.op("dve", lambda h: h.tensor_tensor(sm[:, 8, :], sm[:, 3, :], sm[:, 4, :], ALU.mult), SMALL, SMALL)
        S.op("dve", lambda h: h.tensor_tensor(sm[:, 9, :], sm[:, 3, :], sm[:, 5, :], ALU.mult), SMALL, SMALL)
        S.op("dve", lambda h: h.tensor_scalar(sm[:, 8, :], sm[:, 8, :], -1.0, None, ALU.add), SMALL, SMALL)
        S.op("dve", lambda h: h.tensor_tensor(sm[:, 10, :], are[:], are[:], ALU.mult), [are], SMALL)
        S.op("dve", lambda h: h.tensor_tensor(sm[:, 11, :], aim[:], aim[:], ALU.mult), [aim], SMALL)
        S.op("dve", lambda h: h.tensor_tensor(sm[:, 10, :], sm[:, 10, :], sm[:, 11, :], ALU.add), SMALL, SMALL)
        S.op("dve", lambda h: h.reciprocal(sm[:, 10, :], sm[:, 10, :]), SMALL, SMALL)
        S.op("dve", lambda h: h.tensor_tensor(sm[:, 6, :], sm[:, 8, :], are[:], ALU.mult), SMALL + [are], SMALL)
        S.op("dve", lambda h: h.tensor_tensor(sm[:, 11, :], sm[:, 9, :], aim[:], ALU.mult), SMALL + [aim], SMALL)
        S.op("dve", lambda h: h.tensor_tensor(sm[:, 6, :], sm[:, 6, :], sm[:, 11, :], ALU.add), SMALL, SMALL)
        S.op("dve", lambda h: h.tensor_tensor(sm[:, 7, :], sm[:, 9, :], are[:], ALU.mult), SMALL + [are], SMALL)
        S.op("dve", lambda h: h.tensor_tensor(sm[:, 11, :], sm[:, 8, :], aim[:], ALU.mult), SMALL + [aim], SMALL)
        S.op("dve", lambda h: h.tensor_tensor(sm[:, 7, :], sm[:, 7, :], sm[:, 11, :], ALU.subtract), SMALL, SMALL)
        S.op("dve", lambda h: h.tensor_tensor(sm[:, 6, :], sm[:, 6, :], sm[:, 10, :], ALU.mult), SMALL, SMALL)
        S.op("dve", lambda h: h.tensor_tensor(sm[:, 7, :], sm[:, 7, :], sm[:, 10, :], ALU.mult), SMALL, SMALL)

        def build_pow_tables(Tre, Tim, n, base_re, base_im):
            TB = [Tre, Tim, tb]
            S.op("pool", lambda h: h.memset(Tre[:, :, 0:1], 1.0), [], [Tre])
            S.op("pool", lambda h: h.memset(Tim[:, :, 0:1], 0.0), [], [Tim])
            S.op("dve", lambda h: h.tensor_copy(Tre[:, :, 1], base_re), SMALL, [Tre])
            S.op("dve", lambda h: h.tensor_copy(Tim[:, :, 1], base_im), SMALL, [Tim])
            m = 2
            while m < n:
                cnt = min(m, n - m)
                pr, pi_ = tb[:, 2, 0:16], tb[:, 3, 0:16]
                cmul(S, "dve", pr, pi_, Tre[:, :, m - 1], Tim[:, :, m - 1], Tre[:, :, 1], Tim[:, :, 1],
                     tb[:, 0, 0:16], tb[:, 1, 0:16], TB, TB)
                prb = pr.unsqueeze(2).broadcast_to([128, 16, cnt])
                pib = pi_.unsqueeze(2).broadcast_to([128, 16, cnt])
                t1 = tb[:, 0, 16:16 + 16 * cnt].rearrange("p (a b) -> p a b", b=cnt)
                t2 = tb[:, 1, 16:16 + 16 * cnt].rearrange("p (a b) -> p a b", b=cnt)
                cmul(S, "dve", Tre[:, :, m:m + cnt], Tim[:, :, m:m + cnt], Tre[:, :, 0:cnt], Tim[:, :, 0:cnt], prb, pib,
                     t1, t2, TB, TB)
                m += cnt
        build_pow_tables(Ere, Eim, T + 1, sm[:, 4, :], sm[:, 5, :])
        S.op("dve", lambda h: h.tensor_scalar(Enim[:], Eim[:], -1.0, None, ALU.mult), [Eim], [Enim])
        S.op("dve", lambda h: h.tensor_copy(sm[:, 8, :], Ere[:, :, T]), [Ere], SMALL)
        S.op("dve", lambda h: h.tensor_copy(sm[:, 9, :], Eim[:, :, T]), [Eim], SMALL)
        build_pow_tables(E2re, E2im, NCT, sm[:, 8, :], sm[:, 9, :])
        S.op("dve", lambda h: h.tensor_tensor(Rk[:], sm[:, 1, :].unsqueeze(2).broadcast_to([128, 16, T + 1]),
                                              kvec[:, 0:T + 1].unsqueeze(1).broadcast_to([128, 16, T + 1]), ALU.mult),
             SMALL + [kvec], [Rk])
        S.op("act", lambda h: h.activation(out=Rk[:], in_=Rk[:], func=AF.Exp), [Rk], [Rk])
        S.op("dve", lambda h: h.tensor_tensor(R2[:], sm[:, 1, :].unsqueeze(2).broadcast_to([128, 16, NCT]),
                                              kvec[:, 0:NCT].unsqueeze(1).broadcast_to([128, 16, NCT]), ALU.mult),
             SMALL + [kvec], [R2])
        S.op("act", lambda h: h.activation(out=R2[:], in_=R2[:], func=AF.Exp, scale=float(T)), [R2], [R2])
        S.barrier()
        es_tb.close()
        stage(cx, 1)
        with ExitStack() as es1:
            bpre = cx.sb(es1, "s5_bpre", [128, 16, 128], F32); bpim = cx.sb(es1, "s5_bpim", [128, 16, 128], F32)
            S.dma("sp", bpre[:], cst["bpre"], writes=[bpre]); S.dma("pool", bpim[:], cst["bpim"], writes=[bpim])
            t1 = cx.sb(es1, "s5_bt1", [128, 16, 128], F32); t2 = cx.sb(es1, "s5_bt2", [128, 16, 128], F32)
            bbre = cx.sb(es1, "s5_bbre", [128, 16, 128], BF16); bbim = cx.sb(es1, "s5_bbim", [128, 16, 128], BF16)
            kr = sm[:, 6, :].unsqueeze(2).broadcast_to([128, 16, 128])
            ki = sm[:, 7, :].unsqueeze(2).broadcast_to([128, 16, 128])
            cmul(S, "dve", bbre[:], bbim[:], bpre[:], bpim[:], kr, ki, t1[:], t2[:], [bpre, bpim, t1, t2] + SMALL, [bbre, bbim, t1, t2])
            pst = cx.ps(es1, "s5_pst", [128, 8, 128], BF16)
            for k in range(16):
                for ri, src in enumerate((bbre, bbim)):
                    S.op("pe", lambda h: h.transpose(pst[:, ri, :], src[:, k, :], ident_bf[:]), [src, ident_bf], [pst])
                    S.op("act", lambda h: h.copy(BT[ri][:, k, :], pst[:, ri, :]), [pst], [BT[ri]])
            S.barrier()
        stage(cx, 2)
        Send = cx.sb(es, "s5_Send", [128, 2, 16], F32)
        S.op("pool", lambda h: h.memset(Send[:], 0.0), [], [Send])
        for seg in range(NSEG):
            t00 = seg * SEG
            with ExitStack() as es2:
                y = [cx.sb(es2, f"s5_y{q}", [128, SEG], F32) for q in range(4)]
                with ExitStack() as es3:
                    uTb = [cx.sb(es3, f"s5_uTb{q}", [128, SEG], BF16) for q in range(4)]
                    with ExitStack() as es4:
                        sut = [cx.sb(es4, f"s5_sut{i}", [128, 512], F32) for i in range(2)]
                        sub = [cx.sb(es4, f"s5_sub{i}", [128, 512], BF16) for i in range(2)]
                        psu = [cx.ps(es4, f"s5_psu{i}", [128, 8, 128], BF16) for i in range(2)]
                        for tt in range(SEG // 128):
                            i2 = tt % 2
                            r0 = t00 + tt * 128
                            S.dma("sp", sut[i2][:], proj_dt.ap[r0:r0 + 128, C_SU:C_SU + 512],
                                  reads=proj_dt.bufs(r0, r0 + 128), writes=[sut[i2]])
                            S.op("pool", lambda h: h.tensor_copy(sub[i2][:], sut[i2][:]), [sut[i2]], [sub[i2]])
                            stage(cx, 21)

                            def tru(h, i2=i2):
                                for q in range(4):
                                    ins = h.transpose(psu[i2][:, q, :], sub[i2][:, q * 128:(q + 1) * 128], ident_bf[:])
                                return ins
                            S.op("pe", tru, [sub[i2], ident_bf], [psu[i2]])
                            stage(cx, 22)
                            for q in range(4):
                                ek = "act"
                                if ek == "act":
                                    S.op("act", lambda h: h.copy(uTb[q][:, tt * 128:(tt + 1) * 128], psu[i2][:, q, :]), [psu[i2]], [uTb[q]])
                                else:
                                    S.op("dve", lambda h: h.tensor_copy(uTb[q][:, tt * 128:(tt + 1) * 128], psu[i2][:, q, :]), [psu[i2]], [uTb[q]])
                                stage(cx, 230 + q)
                            stage(cx, 240 + tt)
                        S.barrier()
                    stage(cx, 3)
                    xre = cx.sb(es3, "s5_xre", [128, SEG], F32); xim = cx.sb(es3, "s5_xim", [128, SEG], F32)
                    vre2 = [cx.sb(es3, f"s5_vre{i}", [128, SEG], BF16) for i in range(2)]
                    vim2 = [cx.sb(es3, f"s5_vim{i}", [128, SEG], BF16) for i in range(2)]
                    rmask = cx.sb(es3, "s5_rmask", [128, SEG], F32)
                    tas = [cx.sb(es3, f"s5_ta{i}", [128, 512], F32) for i in range(4)]
                    tbs = [cx.sb(es3, f"s5_tbb{i}", [128, 512], F32) for i in range(4)]
                    nrot = [0]
                    ctabs = [[cx.sb(es3, f"s5_ctab{par}_{i}", [128, T + 1, 64], BF16) for i in range(4)] for par in range(2)]
                    for par in range(2):
                        for i in range(4):
                            S.op("pool", lambda h: h.memset(ctabs[par][i][:], 0.0), [], [ctabs[par][i]])
                    ct1 = cx.sb(es3, "s5_ct1", [128, T + 1, 32], F32); ct2 = cx.sb(es3, "s5_ct2", [128, T + 1, 32], F32)
                    lv = cx.sb(es3, "s5_lv", [128, 12, NCT], F32)
                    Sp2 = [[cx.sb(es3, f"s5_Sp{par}_{i}", [128, NC], BF16) for i in range(2)] for par in range(2)]
                    R2m = cx.sb(es3, "s5_R2m", [128, NC], F32)
                    psb = [cx.ps(es3, f"s5_psb{i}", [128, 512], F32) for i in range(4)]
                    psy = [cx.ps(es3, f"s5_psy{i}", [128, 4, 128], F32) for i in range(2)]
                    npyc = [0]

                    def front_mid(k):
                        q, j = k // 4, k % 4
                        vre, vim, Sp = vre2[k % 2], vim2[k % 2], Sp2[k % 2]
                        rm3 = rmask[:].rearrange("p (c k) -> p c k", k=T)
                        S.op("act", lambda h: h.copy(rm3[:, :, 1:T], sm[:, 3, k:k + 1].unsqueeze(2).broadcast_to([128, NC, T - 1])),
                             SMALL, [rmask])
                        S.op("pool", lambda h: h.memset(rm3[:, :, 0:1], 0.0), [], [rmask])
                        for blk in range(SEG // 512):
                            c0 = blk * 512
                            PR, PI = psb[(2 * blk) % 4], psb[(2 * blk + 1) % 4]
                            S.op("pe", lambda h: h.matmul(PR[:], BT[0][:, k, :], uTb[q][:, c0:c0 + 512], start=True, stop=True),
                                 [BT[0], uTb[q]], [PR])
                            S.op("pe", lambda h: h.matmul(PI[:], BT[1][:, k, :], uTb[q][:, c0:c0 + 512], start=True, stop=True),
                                 [BT[1], uTb[q]], [PI])
                            cb = Ere[:, k, 0:T].unsqueeze(1).broadcast_to([128, 512 // T, T])
                            sb_ = Eim[:, k, 0:T].unsqueeze(1).broadcast_to([128, 512 // T, T])
                            v3 = lambda ap: ap.rearrange("p (c k) -> p c k", k=T)
                            ta, tbb = tas[nrot[0] % 4], tbs[nrot[0] % 4]
                            nrot[0] += 1
                            S.op("dve", lambda h: h.tensor_tensor(v3(ta[:]), v3(PR[:]), cb, ALU.mult), [PR, Ere], [ta])
                            S.op("dve", lambda h: h.tensor_tensor(v3(tbb[:]), v3(PI[:]), sb_, ALU.mult), [PI, Eim], [tbb])
                            S.op("pool", lambda h: h.tensor_tensor(xre[:, c0:c0 + 512], ta[:], tbb[:], ALU.add), [ta, tbb], [xre])
                            ta, tbb = tas[nrot[0] % 4], tbs[nrot[0] % 4]
                            nrot[0] += 1
                            S.op("dve", lambda h: h.tensor_tensor(v3(ta[:]), v3(PI[:]), cb, ALU.mult), [PI, Ere], [ta])
                            S.op("dve", lambda h: h.tensor_tensor(v3(tbb[:]), v3(PR[:]), sb_, ALU.mult), [PR, Eim], [tbb])
                            S.op("pool", lambda h: h.tensor_tensor(xim[:, c0:c0 + 512], ta[:], tbb[:], ALU.subtract), [ta, tbb], [xim])
                        stage(cx, 4)
                        S.op("dve", lambda h: h.tensor_tensor_scan(vre[:], rmask[:], xre[:], 0.0, ALU.mult, ALU.add), [rmask, xre], [vre])
                        S.op("dve", lambda h: h.tensor_tensor_scan(vim[:], rmask[:], xim[:], 0.0, ALU.mult, ALU.add), [rmask, xim], [vim])
                        stage(cx, 5)
                        LV = [lv]
                        vr3 = vre[:].rearrange("p (c k) -> p c k", k=T)
                        vi3 = vim[:].rearrange("p (c k) -> p c k", k=T)
                        S.op("dve", lambda h: h.tensor_copy(lv[:, 0, 0:NC], vr3[:, :, T - 1]), [vre], LV)
                        S.op("dve", lambda h: h.tensor_copy(lv[:, 1, 0:NC], vi3[:, :, T - 1]), [vim], LV)
                        er, ei = Ere[:, k, T - 1:T], Eim[:, k, T - 1:T]
                        S.op("dve", lambda h: h.tensor_scalar(lv[:, 10, 0:NC], lv[:, 1, 0:NC], ei, None, ALU.mult), LV + [Eim], LV)
                        S.op("dve", lambda h: h.scalar_tensor_tensor(lv[:, 2, 0:NC], lv[:, 0, 0:NC], er, lv[:, 10, 0:NC], ALU.mult, ALU.subtract), LV + [Ere], LV)
                        S.op("dve", lambda h: h.tensor_scalar(lv[:, 10, 0:NC], lv[:, 0, 0:NC], ei, None, ALU.mult), LV + [Eim], LV)
                        S.op("dve", lambda h: h.scalar_tensor_tensor(lv[:, 3, 0:NC], lv[:, 1, 0:NC], er, lv[:, 10, 0:NC], ALU.mult, ALU.add), LV + [Ere], LV)
                        cmul(S, "dve", lv[:, 4, 0:NC], lv[:, 5, 0:NC], lv[:, 2, 0:NC], lv[:, 3, 0:NC], E2re[:, k, 0:NC], E2im[:, k, 0:NC],
                             lv[:, 10, 0:NC], lv[:, 11, 0:NC], LV + [E2re, E2im], LV, conj_b=True)
                        S.op("pool", lambda h: h.tensor_copy(R2m[:], R2[:, k, 1:2].broadcast_to([128, NC])), [R2], [R2m])
                        S.op("dve", lambda h: h.tensor_tensor_scan(lv[:, 6, 0:NC], R2m[:], lv[:, 4, 0:NC], 0.0, ALU.mult, ALU.add), [R2m] + LV, LV)
                        S.op("dve", lambda h: h.tensor_tensor_scan(lv[:, 7, 0:NC], R2m[:], lv[:, 5, 0:NC], 0.0, ALU.mult, ALU.add), [R2m] + LV, LV)
                        cmul(S, "dve", lv[:, 8, 1:NCT], lv[:, 9, 1:NCT], lv[:, 6, 0:NC], lv[:, 7, 0:NC], E2re[:, k, 0:NC], E2im[:, k, 0:NC],
                             lv[:, 10, 0:NC], lv[:, 11, 0:NC], LV + [E2re, E2im], LV)
                        S.op("dve", lambda h: h.tensor_copy(lv[:, 8, 0:1], Send[:, 0, k:k + 1]), [Send], LV)
                        S.op("dve", lambda h: h.tensor_copy(lv[:, 9, 0:1], Send[:, 1, k:k + 1]), [Send], LV)
                        if seg > 0:
                            S.op("dve", lambda h: h.tensor_tensor(lv[:, 4, 0:NC], R2[:, k, 1:NCT], E2re[:, k, 1:NCT], ALU.mult), [R2, E2re], LV)
                            S.op("dve", lambda h: h.tensor_tensor(lv[:, 5, 0:NC], R2[:, k, 1:NCT], E2im[:, k, 1:NCT], ALU.mult), [R2, E2im], LV)
                            sr, si = Send[:, 0, k:k + 1], Send[:, 1, k:k + 1]
                            S.op("dve", lambda h: h.scalar_tensor_tensor(lv[:, 8, 1:NCT], lv[:, 4, 0:NC], sr, lv[:, 8, 1:NCT], ALU.mult, ALU.add), LV + [Send], LV)
                            S.op("dve", lambda h: h.tensor_scalar(lv[:, 10, 0:NC], lv[:, 5, 0:NC], si, None, ALU.mult), LV + [Send], LV)
                            S.op("dve", lambda h: h.tensor_tensor(lv[:, 8, 1:NCT], lv[:, 8, 1:NCT], lv[:, 10, 0:NC], ALU.subtract), LV, LV)
                            S.op("dve", lambda h: h.scalar_tensor_tensor(lv[:, 9, 1:NCT], lv[:, 5, 0:NC], sr, lv[:, 9, 1:NCT], ALU.mult, ALU.add), LV + [Send], LV)
                            S.op("dve", lambda h: h.tensor_scalar(lv[:, 10, 0:NC], lv[:, 4, 0:NC], si, None, ALU.mult), LV + [Send], LV)
                            S.op("dve", lambda h: h.tensor_tensor(lv[:, 9, 1:NCT], lv[:, 9, 1:NCT], lv[:, 10, 0:NC], ALU.add), LV, LV)
                        S.op("dve", lambda h: h.tensor_copy(Send[:, 0, k:k + 1], lv[:, 8, NC:NCT]), LV, [Send])
                        S.op("dve", lambda h: h.tensor_copy(Send[:, 1, k:k + 1], lv[:, 9, NC:NCT]), LV, [Send])
                        S.op("pool", lambda h: h.tensor_copy(Sp[0][:], lv[:, 8, 0:NC]), LV, [Sp[0]])
                        S.op("pool", lambda h: h.tensor_copy(Sp[1][:], lv[:, 9, 0:NC]), LV, [Sp[1]])
                        stage(cx, 6)
                        cr = ccre[:, k, :].unsqueeze(1).broadcast_to([128, T + 1, 32])
                        ci = ccim[:, k, :].unsqueeze(1).broadcast_to([128, T + 1, 32])
                        ctabf = ctabs[j % 2]
                        hs = slice(32 * (j % 2), 32 * (j % 2) + 32)
                        CT = ctabf + [ct1, ct2]
                        e_r = Ere[:, k, 0:T + 1].unsqueeze(2).broadcast_to([128, T + 1, 32])
                        e_i = Eim[:, k, 0:T + 1].unsqueeze(2).broadcast_to([128, T + 1, 32])
                        e_ni = Enim[:, k, 0:T + 1].unsqueeze(2).broadcast_to([128, T + 1, 32])
                        rb = Rk[:, k, 1:T + 1].unsqueeze(2).broadcast_to([128, T, 32])
                        S.op("dve", lambda h: h.tensor_tensor(ct1[:], cr, e_r, ALU.mult), [ccre, Ere], CT)
                        S.op("dve", lambda h: h.tensor_tensor(ct2[:], ci, e_i, ALU.mult), [ccim, Eim], CT)
                        S.op("dve", lambda h: h.tensor_tensor(ctabf[0][:, :, hs], ct1[:], ct2[:], ALU.subtract), CT, CT)
                        S.op("dve", lambda h: h.tensor_tensor(ct1[:], cr, e_ni, ALU.mult), [ccre, Enim], CT)
                        S.op("dve", lambda h: h.tensor_tensor(ct2[:], ci, e_r, ALU.mult), [ccim, Ere], CT)
                        S.op("dve", lambda h: h.tensor_tensor(ctabf[1][:, :, hs], ct1[:], ct2[:], ALU.subtract), CT, CT)
                        S.op("dve", lambda h: h.tensor_tensor(ctabf[2][:, 0:T, hs], ctabf[0][:, 1:T + 1, hs], rb, ALU.mult), CT + [Rk], CT)
                        S.op("dve", lambda h: h.tensor_tensor(ctabf[3][:, 0:T, hs], ctabf[1][:, 1:T + 1, hs], rb, ALU.mult), CT + [Rk], CT)

                    def back(k):
                        q, j = k // 4, k % 4
                        vre, vim, Sp = vre2[k % 2], vim2[k % 2], Sp2[k % 2]
                        ctabf = ctabs[j % 2]
                        npy = npyc[0]
                        vrb = vre[:].rearrange("p (c k) -> p k c", k=T)
                        vib = vim[:].rearrange("p (c k) -> p k c", k=T)
                        jj = j // 2
                        y3 = y[q][64 * jj:64 * jj + 64, :].rearrange("p (c k) -> p k c", k=T)
                        for kb in range(T // 4):
                            PY = psy[npy % 2]
                            npy += 1

                            def ymm(h, kb=kb, PY=PY):
                                for kk in range(4):
                                    kx = kb * 4 + kk
                                    o = PY[64 * jj:64 * jj + 64, kk, 0:NC]
                                    h.matmul(o, ctabf[0][:, kx, :], vrb[:, kx, :], start=True, stop=False)
                                    h.matmul(o, ctabf[1][:, kx, :], vib[:, kx, :], start=False, stop=False)
                                    h.matmul(o, ctabf[2][:, kx, :], Sp[0][:], start=False, stop=False)
                                    ins = h.matmul(o, ctabf[3][:, kx, :], Sp[1][:], start=False, stop=True)
                                return ins
                            S.op("pe", ymm, ctabf + [vre, vim] + Sp, [PY])
                            if j % 2 == 0:
                                S.op("act", lambda h: h.copy(y3[:, kb * 4:(kb + 1) * 4, :], PY[64 * jj:64 * jj + 64, :, 0:NC]), [PY], [y[q]])
                            else:
                                S.op("dve", lambda h: h.tensor_tensor(y3[:, kb * 4:(kb + 1) * 4, :], PY[64 * jj:64 * jj + 64, :, 0:NC],
                                                                      y3[:, kb * 4:(kb + 1) * 4, :], ALU.add), [PY, y[q]], [y[q]])
                        npyc[0] = npy

                    front_mid(0)
                    for k in range(16):
                        if k + 1 < 16:
                            front_mid(k + 1)
                        back(k)
                    S.barrier()
                stage(cx, 8)
                with ExitStack() as es5:
                    sut = [cx.sb(es5, f"s5_tsut{i}", [128, 4, 512], F32) for i in range(2)]
                    yy = [cx.sb(es5, f"s5_yy{q}", [128, 512], F32) for q in range(4)]
                    w1s = [cx.sb(es5, f"s5_w1_{q}", [128, 512], F32) for q in range(4)]
                    w2s = [cx.sb(es5, f"s5_w2_{q}", [128, 512], F32) for q in range(4)]
                    ygb = [cx.sb(es5, f"s5_ygb{q}", [128, 512], BF16) for q in range(4)]
                    og = [cx.sb(es5, f"s5_og{i}", [128, 512], F32) for i in range(4)]
                    g5 = [cx.sb(es5, f"s5_g5{i}", [128, 512], F32) for i in range(2)]
                    yo = [cx.sb(es5, f"s5_yo{i}", [128, 512], F32) for i in range(2)]
                    psT = [cx.ps(es5, f"s5_psT{q}", [128, 512], F32) for q in range(4)]
                    psG = [cx.ps(es5, f"s5_psG{i}", [128, 512], F32) for i in range(2)]
                    psO = [cx.ps(es5, f"s5_psO{i}", [128, 512], F32) for i in range(2)]
                    for blk in range(SEG // 512):
                        c0 = blk * 512
                        SU = sut[blk % 2]
                        for tt in range(4):
                            r0 = t00 + c0 + tt * 128
                            S.dma("sp", SU[:, tt, :], proj_dt.ap[r0:r0 + 128, C_SU:C_SU + 512],
                                  reads=proj_dt.bufs(r0, r0 + 128), writes=[SU])
                        for q in range(4):
                            def tq(h, q=q):
                                for tt in range(4):
                                    ins = h.transpose(psT[q][:, tt * 128:(tt + 1) * 128], SU[:, tt, q * 128:(q + 1) * 128], ident_f[:])
                                return ins
                            S.op("pe", tq, [SU, ident_f], [psT[q]])
                            S.op("dve", lambda h: h.scalar_tensor_tensor(yy[q][:], psT[q][:], dfm[:, q:q + 1], y[q][:, c0:c0 + 512],
                                                                         ALU.mult, ALU.add), [psT[q], dfm, y[q]], [yy[q]])
                            w1, w2 = w1s[q], w2s[q]
                            S.op("act", lambda h: h.activation(out=w1[:], in_=yy[q][:], func=AF.Square), [yy[q]], [w1])
                            S.op("dve", lambda h: h.tensor_scalar(w1[:], w1[:], 0.044715, 1.0, ALU.mult, ALU.add), [w1], [w1])
                            S.op("dve", lambda h: h.tensor_tensor(w1[:], w1[:], yy[q][:], ALU.mult), [w1, yy[q]], [w1])
                            S.op("act", lambda h: h.activation(out=w2[:], in_=w1[:], func=AF.Sigmoid, scale=1.5957691216057308), [w1], [w2])
                            S.op("dve", lambda h: h.tensor_tensor(yy[q][:], yy[q][:], w2[:], ALU.mult), [yy[q], w2], [yy[q]])
                            S.op("pool", lambda h: h.tensor_copy(ygb[q][:], yy[q][:]), [yy[q]], [ygb[q]])
                        for nt in range(4):
                            def glu(h, nt=nt):
                                for q in range(4):
                                    ins = h.matmul(psG[nt % 2][:], Wg[:, q, nt * 128:(nt + 1) * 128], ygb[q][:], start=(q == 0), stop=(q == 3))
                                return ins
                            S.op("pe", glu, [Wg] + ygb, [psG[nt % 2]])
                            S.op("act", lambda h: h.activation(out=og[nt][:], in_=psG[nt % 2][:], func=AF.Sigmoid, bias=glub[:, nt:nt + 1]),
                                 [psG[nt % 2], glub], [og[nt]])
                            S.op("dve", lambda h: h.tensor_tensor(og[nt][:], og[nt][:], yy[nt][:], ALU.mult), [og[nt], yy[nt]], [og[nt]])
                        for tt in range(4):
                            i2 = tt % 2
                            r0 = t00 + c0 + tt * 128
                            S.dma("sp", g5[i2][:], proj_dt.ap[r0:r0 + 128, C_S5G:C_S5G + 512], reads=proj_dt.bufs(r0, r0 + 128), writes=[g5[i2]])
                            S.op("act", lambda h: h.activation(out=g5[i2][:], in_=g5[i2][:], func=AF.Silu), [g5[i2]], [g5[i2]])

                            def tro(h, tt=tt, i2=i2):
                                for nt in range(4):
                                    ins = h.transpose(psO[i2][:, nt * 128:(nt + 1) * 128], og[nt][:, tt * 128:(tt + 1) * 128], ident_f[:])
                                return ins
                            S.op("pe", tro, og + [ident_f], [psO[i2]])
                            S.op("dve", lambda h: h.tensor_tensor(yo[i2][:], psO[i2][:, 0:512], g5[i2][:], ALU.mult), [psO[i2], g5[i2]], [yo[i2]])
                            S.dma("pool", mix_dt.ap[r0:r0 + 128, 768:1024], yo[i2][:, 0:256], reads=[yo[i2]], writes=mix_dt.bufs(r0, r0 + 128))
                            S.dma("pool", mix_dt.ap[r0:r0 + 128, 1024 + 768:2048], yo[i2][:, 256:512], reads=[yo[i2]], writes=mix_dt.bufs(r0, r0 + 128))
                    S.barrier()


def s5_layouts(a_re, a_im, log_dt, b_re, b_im, c_re, c_im, d, glu_w, glu_b):
    G = np.arange(32).reshape(16, 2)
    f = np.float32
    are = a_re[G].transpose(1, 2, 0).reshape(128, 16).astype(f)
    aim = a_im[G].transpose(1, 2, 0).reshape(128, 16).astype(f)
    ldt = np.broadcast_to(log_dt[G].transpose(1, 0)[:, None, :], (2, 64, 16)).reshape(128, 16).astype(f)
    ccre = np.zeros((2, 64, 16, 2, 16), f); ccim = np.zeros((2, 64, 16, 2, 16), f)
    bpre = np.zeros((2, 64, 16, 4, 2, 16), f); bpim = np.zeros((2, 64, 16, 4, 2, 16), f)
    for k in range(16):
        for g2 in range(2):
            g = G[k, g2]
            ccre[g2, :, k, g2, :] = c_re[g].T
            ccim[g2, :, k, g2, :] = c_im[g].T
            bpre[g2, :, k, k % 4, g2, :] = b_re[g]
            bpim[g2, :, k, k % 4, g2, :] = b_im[g]
    return dict(are=are, aim=aim, ldt=ldt, ccre=ccre.reshape(128, 16, 32), ccim=ccim.reshape(128, 16, 32),
                bpre=bpre.reshape(128, 16, 128), bpim=bpim.reshape(128, 16, 128),
                dfm=np.ascontiguousarray(d.reshape(4, 128).T).astype(f),
                glub=np.ascontiguousarray(glu_b.reshape(4, 128).T).astype(f),
                gluw=np.ascontiguousarray(glu_w).astype(f))


def bc128(a):
    a = np.asarray(a, np.float32)
    return np.ascontiguousarray(np.broadcast_to(a[None], (128,) + a.shape))


def static_consts():
    f = np.float32
    pos = np.arange(L, dtype=f)
    inv = (1.0 / (np.float32(10000.0) ** (np.arange(0, 64, 2, dtype=f) / np.float32(64)))).astype(f)
    ang = (pos[:, None] * inv[None, :]).astype(f)
    cos = np.cos(ang).astype(f); sin = np.sin(ang).astype(f)
    k = np.arange(128)[:, None]; q = np.arange(128)[None, :]
    mown = np.where(k <= q, 0.0, -30000.0).astype(f)
    mprev = np.where(k > q, 0.0, -30000.0).astype(f)
    return dict(
        ident=np.eye(128).astype(ml_dtypes.bfloat16), identf=np.eye(128, dtype=f),
        cosT=np.ascontiguousarray(cos.reshape(NT, 128, 32).transpose(1, 0, 2)),
        sinT=np.ascontiguousarray(sin.reshape(NT, 128, 32).transpose(1, 0, 2)),
        mown=np.ascontiguousarray(np.tile(mown[:, None, :], (1, 4, 1))),
        mprev=np.ascontiguousarray(np.tile(mprev[:, None, :], (1, 4, 1))),
        gmask=bc128(np.where(np.arange(16)[None, :] < np.arange(16)[:, None], 0.0, -1e30).astype(f)),
        blkind=(np.arange(L)[None, :] // 256 == np.arange(16)[:, None]).astype(ml_dtypes.bfloat16),
        tri=(k <= q).astype(f), ones=np.ones((128, 128), f), maskT=mown.copy(),
        kvec=bc128(np.arange(256, dtype=f)),
    )


CONST_SHAPES = dict(ident=([128, 128], BF16), identf=([128, 128], F32), cosT=([128, NT, 32], F32), sinT=([128, NT, 32], F32),
                    mown=([128, 4, 128], F32), mprev=([128, 4, 128], F32), gmask=([128, 16, 16], F32), blkind=([16, L], BF16),
                    tri=([128, 128], F32), ones=([128, 128], F32), maskT=([128, 128], F32), kvec=([128, 256], F32))
HALF_SHAPES = dict(sinks=[128, 4], convw=[128, 4, 512], convb=[128, 512], dtb=[128, 4], alog=[128, 4],
                   dskip=[128, 256], normw=[128, 256])
LAYER_SHAPES = dict(pre_g=[128, 16], w_out=[D, D], post_g=[128, D],
                    are=[128, 16], aim=[128, 16], ldt=[128, 16], ccre=[128, 16, 32], ccim=[128, 16, 32],
                    bpre=[128, 16, 128], bpim=[128, 16, 128], dfm=[128, 4], glub=[128, 4], gluw=[512, 512])


def in_cols(jh):
    r = lambda a, n: list(range(a, a + n))
    c = (r(0 + jh * 256, 256) + r(512 + jh * 256, 256) + r(1024 + jh * 256, 256) + r(1536 + jh * 256, 256)
         + r(3592 + jh * 256, 256) + r(4104 + jh * 64, 64) + r(4232 + jh * 64, 64) + r(4360 + jh * 256, 256)
         + r(2048 + jh * 256, 256) + r(2560 + jh * 128, 128) + r(2816 + jh * 128, 128) + r(3080 + jh * 256, 256)
         + r(3072 + jh * 4, 4))
    if jh == 0:
        c = c + r(4872, 512) + r(5384, 512)
    return np.array(c)


def half_inputs(inp, l, jh):
    cols = in_cols(jh)
    assert len(cols) == (NP if jh == 0 else NPH)
    cch = np.array(list(range(jh * 256, jh * 256 + 256)) + list(range(512 + jh * 128, 512 + jh * 128 + 128))
                   + list(range(768 + jh * 128, 768 + jh * 128 + 128)))
    return dict(
        w_in=np.ascontiguousarray(inp["w_in"][l][:, cols]),
        sinks=bc128(inp["swa_sinks"][l][4 * jh:4 * jh + 4]),
        convw=bc128(inp["ssd_conv_w"][l][:, cch]), convb=bc128(inp["ssd_conv_b"][l][cch]),
        dtb=bc128(inp["ssd_dt_bias"][l][4 * jh:4 * jh + 4]), alog=bc128(inp["ssd_a_log"][l][4 * jh:4 * jh + 4]),
        dskip=bc128(np.repeat(inp["ssd_d"][l][4 * jh:4 * jh + 4], 64)), normw=bc128(inp["ssd_norm"][l][jh * 256:jh * 256 + 256]),
    )


WOUT_ROWS = np.array([b + jh * 256 + i for jh in range(2) for b in (0, 512, 1024, 1536) for i in range(256)])


def layer_inputs(inp, l):
    d = s5_layouts(inp["s5_a_re"][l], inp["s5_a_im"][l], inp["s5_log_dt"][l], inp["s5_b_re"][l], inp["s5_b_im"][l],
                   inp["s5_c_re"][l], inp["s5_c_im"][l], inp["s5_d"][l], inp["s5_glu_w"][l], inp["s5_glu_b"][l])
    d.update(pre_g=np.ascontiguousarray(inp["pre_norm"][l].reshape(16, 128).T),
             w_out=np.ascontiguousarray(inp["w_out"][l][WOUT_ROWS]), post_g=bc128(inp["post_norm"][l]))
    return d


def load_consts(cx, es, cap):
    S = cx.S
    C = {}
    for nm, key in (("ident", "ident"), ("identf", "identf"), ("cos", "cosT"), ("sin", "sinT")):
        shp, dt = CONST_SHAPES[key]
        C[nm] = cx.sb(es, "c_" + nm, shp, dt)
        S.dma("sp", C[nm][:], cap[key], writes=[C[nm]])
    for key in ("mprev", "mown", "gmask", "blkind", "tri", "ones", "maskT", "kvec", "identf"):
        C["ap_" + key] = cap[key]
    return C


def build_fused(depth=DEPTH):
    nc = bass.Bass("TRN2", target_bir_lowering=False)
    cx = Ctx(nc)
    cx.L = L
    A = lambda n, s, d=F32: nc.dram_tensor(n, list(s), d, kind="ExternalInput").ap()
    x_in = DramT(nc, "x", [L, D], F32, kind="ExternalInput")
    cap = {k: A("k_" + k, s, d) for k, (s, d) in CONST_SHAPES.items()}
    Lw = [{k: A(f"l{l}_{k}", s) for k, s in LAYER_SHAPES.items()} for l in range(depth)]
    Hw = [[dict({k: A(f"l{l}h{jh}_{k}", s) for k, s in HALF_SHAPES.items()},
                w_in=A(f"l{l}h{jh}_w_in", [D, NP if jh == 0 else NPH])) for jh in range(2)] for l in range(depth)]
    proj = DramT(nc, "proj", [L, NP], F32)
    mixc = DramT(nc, "mixc", [L, D], F32)
    xbuf = [DramT(nc, f"xbuf{i}", [L, D], F32) for i in range(2)]
    out = DramT(nc, "out", [L, D], F32, kind="ExternalOutput")
    with ExitStack() as es:
        C = load_consts(cx, es, cap)
        x_cur = x_in
        for l in range(depth):
            x_next = out if l == depth - 1 else xbuf[l % 2]
            for jh in range(2):
                H = Hw[l][jh]
                mcol = jh * MIXH
                phase_inproj(cx, x_cur, H["w_in"], Lw[l]["pre_g"], proj, C["ident"], npc=(NP if jh == 0 else NPH))
                phase_swa(cx, proj, mixc, C["cos"], C["sin"], C["ident"], C["ap_mprev"], C["ap_mown"], H["sinks"], mcol=mcol)
                phase_moba(cx, proj, mixc, C["cos"], C["sin"], C["ident"], C["identf"], C["ap_mown"], C["ap_gmask"],
                           C["ap_blkind"], mcol=mcol)
                ssd_c = {k: H[k] for k in ("convw", "convb", "dtb", "alog", "dskip", "normw")}
                ssd_c.update(tri=C["ap_tri"], ones=C["ap_ones"], maskT=C["ap_maskT"], identf=C["ap_identf"])
                phase_ssd(cx, proj, mixc, C["ident"], ssd_c, mcol=mcol)
                if jh == 0:
                    s5_c = {k: Lw[l][k] for k in ("are", "aim", "ldt", "ccre", "ccim", "bpre", "bpim", "dfm", "glub", "gluw")}
                    s5_c["kvec"] = C["ap_kvec"]
                    phase_s5(cx, proj, mixc, C["ident"], C["identf"], s5_c)
            phase_outproj(cx, mixc, x_cur, Lw[l]["w_out"], Lw[l]["post_g"], x_next, C["ident"], NT)
            x_cur = x_next
        cx.S.finish(out.tiles)
    cx.n_ins = cx.S.n_ins
    return nc


def kernel(**inp):
    inp = {k: np.asarray(v) for k, v in inp.items()}
    x = np.ascontiguousarray(inp["x"], dtype=np.float32)
    nc = build_fused()
    shared = {"k_" + k: v for k, v in static_consts().items()}
    for l in range(DEPTH):
        shared.update({f"l{l}_{k}": v for k, v in layer_inputs(inp, l).items()})
        for jh in range(2):
            shared.update({f"l{l}h{jh}_{k}": v for k, v in half_inputs(inp, l, jh).items()})
    in_maps = []
    for b in range(4):
        m = dict(shared)
        m["x"] = x[b]
        in_maps.append(m)
    res = run_bass_kernel_spmd(nc, in_maps, core_ids=list(range(4)))
    return np.stack([res.results[b]["out"] for b in range(4)]).astype(np.float32)
```

```python
from contextlib import ExitStack
import numpy as np
import ml_dtypes
import concourse.bass as bass
import concourse.mybir as mybir
from concourse.bass_utils import run_bass_kernel_spmd

F32 = mybir.dt.float32
BF16 = mybir.dt.bfloat16
ALU = mybir.AluOpType
AF = mybir.ActivationFunctionType
AX = mybir.AxisListType

D = 2048
L = 4096
NT = L // 128
DEPTH = 4
EPS = 1e-6
C_MQ, C_MK, C_MV, C_MG = 0, 256, 512, 768
C_SQ, C_SK, C_SV, C_SG = 1024, 1280, 1344, 1408
C_XS, C_BM, C_CM, C_Z = 1664, 1920, 2048, 2176
C_DT, C_SU, C_S5G = 2432, 2436, 2948
NP = 3460
NPH = 2436
MIXH = 1024


class Buf:
    __slots__ = ("t", "last_w", "readers")

    def __init__(self, t):
        self.t = t
        self.last_w = None
        self.readers = {}

    def __getitem__(self, k):
        return self.t[k]


class DramT:
    def __init__(self, nc, name, shape, dt, kind="Internal"):
        self.ap = nc.dram_tensor(name, list(shape), dt, kind=kind).ap()
        self.tiles = [Buf(None) for _ in range((shape[0] + 127) // 128)]

    def bufs(self, r0, r1):
        return self.tiles[r0 // 128:(r1 + 127) // 128]


class Eng:
    def __init__(self, key, h, sem):
        self.key, self.h, self.sem = key, h, sem
        self.cnt = 0
        self.seen = {}


class Sched:
    NSLOT = 8

    def __init__(self, nc):
        self.nc = nc
        self.engs = {}
        for key, h in (("pe", nc.tensor), ("act", nc.scalar), ("dve", nc.vector),
                       ("pool", nc.gpsimd), ("sp", nc.sync)):
            self.engs[key] = Eng(key, h, nc.alloc_semaphore(name=f"prog_{key}"))
        self.dma_sems = {}
        self.dma_rings = {}
        self.n_ins = 0
        self.muted = False
        self.same_engine_raw = True

    def _deps(self, reads, writes):
        deps = {}
        for b in reads:
            if b.last_w is not None:
                k, i = b.last_w
                if deps.get(k, 0) < i:
                    deps[k] = i
        for b in writes:
            if b.last_w is not None:
                k, i = b.last_w
                if deps.get(k, 0) < i:
                    deps[k] = i
            for k, i in b.readers.items():
                if deps.get(k, 0) < i:
                    deps[k] = i
        return deps

    def _emit_waits(self, e, deps, same_ok=True):
        for k, i in deps.items():
            if k == e.key and same_ok:
                continue
            if e.seen.get(k, 0) >= i:
                continue
            sem = self.dma_sems[k] if k.startswith("dma") else self.engs[k].sem
            e.h.wait_ge(sem, i)
            e.seen[k] = i
            self.n_ins += 1

    def _record(self, key, idx, reads, writes):
        for b in reads:
            if b.readers.get(key, 0) < idx:
                b.readers[key] = idx
        for b in writes:
            b.last_w = (key, idx)
            b.readers = {}

    def op(self, ek, fn, reads=(), writes=()):
        if self.muted:
            return None
        e = self.engs[ek]
        self._emit_waits(e, self._deps(reads, writes))
        own = 0
        if self.same_engine_raw and ek != "pe":
            for b in reads:
                if b.last_w is not None and b.last_w[0] == ek and b.last_w[1] > own:
                    own = b.last_w[1]
            for b in writes:
                if b.last_w is not None and b.last_w[0] == ek and b.last_w[1] > own:
                    own = b.last_w[1]
                r = b.readers.get(ek, 0)
                if r > own:
                    own = r
        if own > e.seen.get(ek, 0):
            e.h.wait_ge(e.sem, own)
            e.seen[ek] = own
            self.n_ins += 1
        ins = fn(e.h)
        e.cnt += 1
        ins.then_inc(e.sem, 1)
        self._record(ek, e.cnt, reads, writes)
        self.n_ins += 1
        return ins

    def dma(self, ek, out, in_, reads=(), writes=(), **kw):
        if self.muted:
            return None
        e = self.engs[ek]
        deps = self._deps(reads, writes)
        ring = self.dma_rings.setdefault(ek, {"next": 0, "cnt": [0] * self.NSLOT})
        slot = ring["next"]
        ring["next"] = (slot + 1) % self.NSLOT
        qk = f"dma:{ek}:{slot}"
        if qk not in self.dma_sems:
            self.dma_sems[qk] = self.nc.alloc_semaphore(name=f"dma_{ek}_{slot}")
        if ring["cnt"][slot] > 0:
            deps[qk] = max(deps.get(qk, 0), ring["cnt"][slot])
        self._emit_waits(e, deps, same_ok=False)
        ring["cnt"][slot] += 16
        ins = e.h.dma_start(out=out, in_=in_, **kw)
        ins.then_inc(self.dma_sems[qk], 16)
        self._record(qk, ring["cnt"][slot], reads, writes)
        self.n_ins += 1
        return ins

    def barrier(self):
        if self.muted:
            return
        deps = {}
        for k, e in self.engs.items():
            if e.cnt > 0:
                deps[k] = e.cnt
        for ek, ring in self.dma_rings.items():
            for slot, c in enumerate(ring["cnt"]):
                if c > 0:
                    deps[f"dma:{ek}:{slot}"] = c
        for e in self.engs.values():
            self._emit_waits(e, deps)

    def finish(self, bufs):
        self.muted = False
        e = self.engs["sp"]
        self._emit_waits(e, self._deps(bufs, bufs))
        e.h.nop()


class Ctx:
    def __init__(self, nc):
        self.nc = nc
        self.S = Sched(nc)
        self._rr = {}
        self.nt = NT

    def rr(self, key, choices):
        i = self._rr.get(key, 0)
        self._rr[key] = i + 1
        return choices[i % len(choices)]

    def dbg(self, name, buf, shape, dt):
        if not getattr(self, "debug", False):
            return
        o = DramT(self.nc, name, list(shape), dt, kind="ExternalOutput")
        self.S.dma("sp", o.ap, buf[:], reads=[buf], writes=o.tiles)
        self.dbg_outs = getattr(self, "dbg_outs", []) + o.tiles

    def uid(self, name):
        self._uid = getattr(self, "_uid", 0) + 1
        return f"{name}_u{self._uid}"

    def sb(self, es, name, shape, dt):
        return Buf(es.enter_context(self.nc.sbuf_tensor(self.uid(name), list(shape), dt)))

    def ps(self, es, name, shape, dt=F32):
        return Buf(es.enter_context(self.nc.psum_tensor(self.uid(name), list(shape), dt)))


def load_weight_bf16(cx, es, name, w_ap, K, N, scale_sb=None):
    nc, S = cx.nc, cx.S
    KT = K // 128
    Wb = cx.sb(es, name, [128, KT, N], BF16)
    with ExitStack() as es2:
        stg = [cx.sb(es2, f"{name}_stg{i}", [128, N], F32) for i in range(3)]
        for kt in range(KT):
            st = stg[kt % 3]
            S.dma(cx.rr("wq", ["sp", "pool"]), st[:], w_ap[kt * 128:(kt + 1) * 128, :], writes=[st])
            ek = cx.rr("wcast", ["dve", "act"])
            if scale_sb is not None:
                if ek == "dve":
                    S.op("dve", lambda h: h.tensor_scalar(Wb[:, kt, :], st[:], scale_sb[:, kt:kt + 1], None, ALU.mult),
                         [st, scale_sb], [Wb])
                else:
                    S.op("act", lambda h: h.activation(out=Wb[:, kt, :], in_=st[:], func=AF.Copy, scale=scale_sb[:, kt:kt + 1]),
                         [st, scale_sb], [Wb])
            else:
                if ek == "dve":
                    S.op("dve", lambda h: h.tensor_copy(Wb[:, kt, :], st[:]), [st], [Wb])
                else:
                    S.op("act", lambda h: h.copy(Wb[:, kt, :], st[:]), [st], [Wb])
        S.barrier()
    return Wb


def phase_inproj(cx, x_dt, w_ap, g_ap, proj_dt, ident_bf, npc=NP):
    nc, S = cx.nc, cx.S
    KT = D // 128
    with ExitStack() as es:
        g_sb = cx.sb(es, "g_sb", [128, KT], F32)
        S.dma("sp", g_sb[:], g_ap, writes=[g_sb])
        Wb = load_weight_bf16(cx, es, "Win", w_ap, D, npc, g_sb)
        xt = [cx.sb(es, f"xt{i}", [128, D], F32) for i in range(2)]
        xb = [cx.sb(es, f"xb{i}", [128, D], BF16) for i in range(2)]
        junk = cx.sb(es, "junk", [128, D], BF16)
        ss = [cx.sb(es, f"ss{i}", [128, 1], F32) for i in range(2)]
        rstd = [cx.sb(es, f"rstd{i}", [128, 1], F32) for i in range(2)]
        hT = [cx.sb(es, f"hT{i}", [128, KT, 128], BF16) for i in range(2)]
        stage = [cx.sb(es, f"stage{i}", [128, npc], F32) for i in range(2)]
        nch = (npc + 511) // 512
        NACC = 6
        pmb = [cx.ps(es, f"pm{i}", [128, 512], F32) for i in range(min(nch, NACC))]
        pm = [pmb[c % NACC] for c in range(nch)]
        ptl = [cx.ps(es, f"pt{i}", [128, 4, 128], BF16) for i in range(2)]
        def prep(t):
            i2 = t % 2
            X, XB, SS, RS, HT, ST = xt[i2], xb[i2], ss[i2], rstd[i2], hT[i2], stage[i2]
            S.dma("sp", X[:], x_dt.ap[t * 128:(t + 1) * 128, :], reads=x_dt.bufs(t * 128, t * 128 + 128), writes=[X])
            S.op("act", lambda h: h.activation(out=junk[:], in_=X[:], func=AF.Square, accum_out=SS[:]),
                 [X], [junk, SS])
            S.op("pool", lambda h: h.tensor_copy(XB[:], X[:]), [X], [XB])
            S.op("dve", lambda h: h.tensor_scalar(RS[:], SS[:], 1.0 / D, EPS, ALU.mult, ALU.add), [SS], [RS])
            S.op("act", lambda h: h.sqrt(RS[:], RS[:]), [RS], [RS])
            S.op("dve", lambda h: h.reciprocal(RS[:], RS[:]), [RS], [RS])
            for q in range(KT // 4):
                P = ptl[q % 2]
                pv = P[:]

                def tr(h, q=q, pv=pv):
                    for r in range(4):
                        kt = q * 4 + r
                        ins = h.transpose(pv[:, r, :], XB[:, kt * 128:(kt + 1) * 128], ident_bf[:])
                    return ins
                S.op("pe", tr, [XB, ident_bf], [P])
                ek = "act"
                if ek == "dve":
                    S.op("dve", lambda h: h.tensor_copy(HT[:, q * 4:(q + 1) * 4, :], pv), [P], [HT])
                else:
                    S.op("act", lambda h: h.copy(HT[:, q * 4:(q + 1) * 4, :], pv), [P], [HT])

        def compute(t):
            i2 = t % 2
            X, XB, SS, RS, HT, ST = xt[i2], xb[i2], ss[i2], rstd[i2], hT[i2], stage[i2]

            def evac_chunks(cs, ST=ST, RS=RS):
                for c in cs:
                    n0 = c * 512
                    n1 = min(npc, n0 + 512)
                    PM = pm[c]
                    ek = cx.rr("pjev", ["act", "dve"])
                    if ek == "dve":
                        S.op("dve", lambda h: h.tensor_scalar(ST[:, n0:n1], PM[:, 0:n1 - n0], RS[:, 0:1], None, ALU.mult),
                             [PM, RS], [ST])
                    else:
                        S.op("act", lambda h: h.activation(out=ST[:, n0:n1], in_=PM[:, 0:n1 - n0], func=AF.Copy,
                                                           scale=RS[:, 0:1]), [PM, RS], [ST])

            for g0 in range(0, nch, NACC):
                cs = list(range(g0, min(nch, g0 + NACC)))

                def mm(h, HT=HT, cs=cs):
                    for kt in range(KT):
                        for c in cs:
                            n0 = c * 512
                            n1 = min(npc, n0 + 512)
                            ins = h.matmul(pm[c][:, 0:n1 - n0], HT[:, kt, :], Wb[:, kt, n0:n1],
                                           start=(kt == 0), stop=(kt == KT - 1))
                    return ins
                S.op("pe", mm, [HT, Wb], [pm[c] for c in cs])
                evac_chunks(cs)
            if t == 0:
                cx.dbg("d_rs", RS, [128, 1], F32)
                cx.dbg("d_ss", SS, [128, 1], F32)
                cx.dbg("d_xb", XB, [128, D], BF16)
                cx.dbg("d_hT", HT, [128, KT, 128], BF16)
                cx.dbg("d_Wb", Wb, [128, KT, npc], BF16)
            S.dma("pool", proj_dt.ap[t * 128:(t + 1) * 128, 0:npc], ST[:], reads=[ST], writes=proj_dt.bufs(t * 128, t * 128 + 128))

        prep(0)
        for t in range(cx.nt):
            if t + 1 < cx.nt:
                prep(t + 1)
            compute(t)
        S.barrier()


def phase_outproj(cx, mix_dt, x_dt, w_ap, gbc_ap, out_dt, ident_bf, ntiles):
    nc, S = cx.nc, cx.S
    KT = D // 128
    with ExitStack() as es:
        Wb = load_weight_bf16(cx, es, "Wout", w_ap, D, D, None)
        gbc = cx.sb(es, "gbc", [128, D], F32)
        S.dma("sp", gbc[:], gbc_ap, writes=[gbc])
        mt = [cx.sb(es, f"mt{i}", [128, D], F32) for i in range(2)]
        mb = [cx.sb(es, f"mb{i}", [128, D], BF16) for i in range(2)]
        xt = [cx.sb(es, f"oxt{i}", [128, D], F32) for i in range(2)]
        mT = [cx.sb(es, f"mT{i}", [128, KT, 128], BF16) for i in range(2)]
        o = [cx.sb(es, f"o{i}", [128, D], F32) for i in range(2)]
        o2 = [cx.sb(es, f"o2{i}", [128, D], F32) for i in range(2)]
        junk = cx.sb(es, "ojunk", [128, D], BF16)
        ss = [cx.sb(es, f"oss{i}", [128, 1], F32) for i in range(2)]
        pt = [cx.ps(es, f"opt{i}", [128, 4, 128], BF16) for i in range(2)]
        pm = [cx.ps(es, f"opm{i}", [128, 512], F32) for i in range(4)]
        def prep(t):
            i2 = t % 2
            M, MB, X, MT, O, O2, SS = mt[i2], mb[i2], xt[i2], mT[i2], o[i2], o2[i2], ss[i2]
            r0, r1 = t * 128, (t + 1) * 128
            S.dma("sp", M[:], mix_dt.ap[r0:r1, :], reads=mix_dt.bufs(r0, r1), writes=[M])
            S.dma("sp", X[:], x_dt.ap[r0:r1, :], reads=x_dt.bufs(r0, r1), writes=[X])
            S.op("act", lambda h: h.copy(MB[:], M[:]), [M], [MB])
            for q in range(KT // 4):
                P = pt[q % 2]

                def tr(h, q=q, P=P):
                    for r in range(4):
                        kt = q * 4 + r
                        ins = h.transpose(P[:, r, :], MB[:, kt * 128:(kt + 1) * 128], ident_bf[:])
                    return ins
                S.op("pe", tr, [MB, ident_bf], [P])
                if q % 2 == 0:
                    S.op("dve", lambda h: h.tensor_copy(MT[:, q * 4:(q + 1) * 4, :], P[:]), [P], [MT])
                else:
                    S.op("act", lambda h: h.copy(MT[:, q * 4:(q + 1) * 4, :], P[:]), [P], [MT])
        def compute(t):
            i2 = t % 2
            M, MB, X, MT, O, O2, SS = mt[i2], mb[i2], xt[i2], mT[i2], o[i2], o2[i2], ss[i2]
            r0, r1 = t * 128, (t + 1) * 128

            def mm(h, MT=MT):
                for kt in range(KT):
                    for c in range(4):
                        ins = h.matmul(pm[c][:, :], MT[:, kt, :], Wb[:, kt, c * 512:(c + 1) * 512],
                                       start=(kt == 0), stop=(kt == KT - 1))
                return ins
            S.op("pe", mm, [MT, Wb], pm)
            for c in range(4):
                n0, n1 = c * 512, (c + 1) * 512
                PM = pm[c]
                if c % 2 == 0:
                    S.op("act", lambda h: h.copy(O[:, n0:n1], PM[:, :]), [PM], [O])
                else:
                    S.op("dve", lambda h: h.tensor_copy(O[:, n0:n1], PM[:, :]), [PM], [O])
            S.op("act", lambda h: h.activation(out=junk[:], in_=O[:], func=AF.Square, accum_out=SS[:]), [O], [junk, SS])
            S.op("dve", lambda h: h.tensor_scalar(SS[:], SS[:], 1.0 / D, EPS, ALU.mult, ALU.add), [SS], [SS])
            S.op("act", lambda h: h.sqrt(SS[:], SS[:]), [SS], [SS])
            S.op("dve", lambda h: h.reciprocal(SS[:], SS[:]), [SS], [SS])
            S.op("dve", lambda h: h.scalar_tensor_tensor(O2[:], O[:], SS[:, 0:1], gbc[:], ALU.mult, ALU.mult),
                 [O, SS, gbc], [O2])
            if t == 1:
                cx.dbg("d_o", O, [128, D], F32)
                cx.dbg("d_rs", SS, [128, 1], F32)
                cx.dbg("d_o2", O2, [128, D], F32)
            S.op("dve", lambda h: h.tensor_tensor(O2[:], O2[:], X[:], ALU.add), [O2, X], [O2])
            S.dma("pool", out_dt.ap[r0:r1, :], O2[:], reads=[O2], writes=out_dt.bufs(r0, r1))

        prep(0)
        for t in range(ntiles):
            if t + 1 < ntiles:
                prep(t + 1)
            compute(t)
        S.barrier()


def rope_tiles(cx, S, dst_bf, src, nh, cosb, sinb, tmp):
    s4 = src.rearrange("p (h two d) -> p h two d", two=2, d=32)
    d4 = dst_bf.rearrange("p (h two d) -> p h two d", two=2, d=32)
    x1, x2 = s4[:, :, 0, :], s4[:, :, 1, :]
    cb = cosb.unsqueeze(1).broadcast_to([128, nh, 32])
    sb_ = sinb.unsqueeze(1).broadcast_to([128, nh, 32])
    tv = tmp[:].rearrange("p f (h d) -> p f h d", d=32)
    return x1, x2, cb, sb_, tv, d4


def phase_swa(cx, proj_dt, mix_dt, cos_sb, sin_sb, ident_bf, mask_prev_ap, mask_own_ap, sinks_ap, mcol=0):
    nc, S = cx.nc, cx.S
    Lc, NTc = cx.L, cx.L // 128
    with ExitStack() as es:
        QKT = cx.sb(es, "swa_QKT", [64, 5, Lc], BF16)
        Vaug = cx.sb(es, "swa_V", [128, NTc, 65], BF16)
        mprev = cx.sb(es, "swa_mprev", [128, 4, 128], F32)
        mown = cx.sb(es, "swa_mown", [128, 4, 128], F32)
        esink = cx.sb(es, "swa_esink", [128, 4], F32)
        S.dma("sp", mprev[:], mask_prev_ap, writes=[mprev])
        S.dma("sp", mown[:], mask_own_ap, writes=[mown])
        S.dma("sp", esink[:], sinks_ap, writes=[esink])
        S.op("act", lambda h: h.activation(out=esink[:], in_=esink[:], func=AF.Exp), [esink], [esink])
        S.op("pool", lambda h: h.memset(Vaug[:, :, 64:65], 1.0), [], [Vaug])
        tin = [cx.sb(es, f"swa_tin{i}", [128, 640], F32) for i in range(2)]
        Gs = cx.sb(es, "swa_G", [128, NTc, 256], F32)
        rtmp = [cx.sb(es, f"swa_rtmp{i}", [128, 4, 160], F32) for i in range(2)]
        rb = [cx.sb(es, f"swa_rb{i}", [128, 320], BF16) for i in range(2)]
        ptr = [cx.ps(es, f"swa_ptr{i}", [64, 8, 128], BF16) for i in range(2)]
        for t in range(NTc):
            i2 = t % 2
            T, TMP, RB, PT = tin[i2], rtmp[i2], rb[i2], ptr[i2]
            r0, r1 = t * 128, (t + 1) * 128
            S.dma("sp", T[:], proj_dt.ap[r0:r1, C_SQ:C_SQ + 640],
                  reads=proj_dt.bufs(r0, r1), writes=[T])
            S.op("act", lambda h: h.activation(out=Gs[:, t, :], in_=T[:, 384:640], func=AF.Silu), [T], [Gs])
            x1, x2, cb, sb_, tv, d4 = rope_tiles(cx, S, RB[:], T[:, 0:320], 5, cos_sb[:, t, :], sin_sb[:, t, :], TMP)
            S.op("dve", lambda h: h.tensor_tensor(tv[:, 0], x1, cb, ALU.mult), [T, cos_sb], [TMP])
            S.op("dve", lambda h: h.tensor_tensor(tv[:, 1], x2, sb_, ALU.mult), [T, sin_sb], [TMP])
            S.op("dve", lambda h: h.tensor_tensor(tv[:, 2], x2, cb, ALU.mult), [T, cos_sb], [TMP])
            S.op("dve", lambda h: h.tensor_tensor(tv[:, 3], x1, sb_, ALU.mult), [T, sin_sb], [TMP])
            S.op("dve", lambda h: h.tensor_tensor(d4[:, :, 0, :], tv[:, 0], tv[:, 1], ALU.subtract), [TMP], [RB])
            S.op("dve", lambda h: h.tensor_tensor(d4[:, :, 1, :], tv[:, 2], tv[:, 3], ALU.add), [TMP], [RB])
            S.op("pool", lambda h: h.tensor_copy(Vaug[:, t, 0:64], T[:, 320:384]), [T], [Vaug])

            def tr(h, RB=RB, PT=PT):
                for hh in range(5):
                    ins = h.transpose(PT[:, hh, :], RB[:, hh * 64:(hh + 1) * 64], ident_bf[:])
                return ins
            S.op("pe", tr, [RB, ident_bf], [PT])
            S.op("act", lambda h: h.copy(QKT[:, :, r0:r1], PT[:, 0:5, :]), [PT], [QKT])
        sg = [cx.sb(es, f"swa_sg{i}", [128, 256], F32) for i in range(2)]
        st = [cx.sb(es, f"swa_st{i}", [128, 4, 128], F32) for i in range(2)]
        pT = [cx.sb(es, f"swa_pT{i}", [128, 4, 128], BF16) for i in range(4)]
        den = [cx.sb(es, f"swa_den{i}", [128, 4], F32) for i in range(2)]
        yb = [cx.sb(es, f"swa_y{i}", [128, 256], F32) for i in range(2)]
        pss = [cx.ps(es, f"swa_pss{i}", [128, 4, 128], F32) for i in range(2)]
        pso = [cx.ps(es, f"swa_pso{i}", [128, 4, 128], F32) for i in range(2)]
        sge = [cx.sb(es, f"swa_sge{i}", [128, 256], F32) for i in range(2)]
        items = [(t, kk) for t in range(NTc) for kk in ([t - 1, t] if t > 0 else [t])]

        def score(i):
            t, kk = items[i]
            r0, r1 = t * 128, (t + 1) * 128
            PS = pss[i % 2]
            S.op("pe", lambda h: h.matmul(PS[:], QKT[:, 4, kk * 128:(kk + 1) * 128], QKT[:, 0:4, r0:r1],
                                          start=True, stop=True), [QKT], [PS])

        def finish_item(i):
            t, kk = items[i]
            r0, r1 = t * 128, (t + 1) * 128
            PS, ST, PTB, PO = pss[i % 2], st[i % 2], pT[i % 4], pso[t % 2]
            M = mown if kk == t else mprev
            S.op("dve", lambda h: h.scalar_tensor_tensor(ST[:], PS[:], 0.125, M[:], ALU.mult, ALU.add), [PS, M], [ST])
            S.op("act", lambda h: h.activation(out=PTB[:], in_=ST[:], func=AF.Exp), [ST], [PTB])
            first = (kk == max(t - 1, 0))

            def pv(h):
                for hh in range(4):
                    ins = h.matmul(PO[:, hh, 0:65], PTB[:, hh, :], Vaug[:, kk, :],
                                   start=(first and hh == 0), stop=(kk == t and hh == 3))
                return ins
            S.op("pe", pv, [PTB, Vaug], [PO])
            if kk == t:
                DEN, Y = den[t % 2], yb[t % 2]
                S.op("dve", lambda h: h.tensor_tensor(DEN[:], PO[:, :, 64], esink[:], ALU.add), [PO, esink], [DEN])
                S.op("dve", lambda h: h.reciprocal(DEN[:], DEN[:]), [DEN], [DEN])
                for hh in range(4):
                    S.op("act", lambda h: h.activation(out=Y[:, hh * 64:(hh + 1) * 64], in_=PO[:, hh, 0:64], func=AF.Copy,
                                                       scale=DEN[:, hh:hh + 1]), [PO, DEN], [Y])
                S.op("dve", lambda h: h.tensor_tensor(Y[:], Y[:], Gs[:, t, :], ALU.mult), [Y, Gs], [Y])
                S.dma("pool", mix_dt.ap[r0:r1, mcol + 512:mcol + 768], Y[:], reads=[Y], writes=mix_dt.bufs(r0, r1))

        score(0)
        for i in range(len(items)):
            if i + 1 < len(items):
                score(i + 1)
            finish_item(i)
        S.barrier()


def phase_moba(cx, proj_dt, mix_dt, cos_sb, sin_sb, ident_bf, ident_f, mask_own_ap, gmask_ap, blkind_ap, mcol=0):
    nc, S = cx.nc, cx.S
    Lc, NTc = cx.L, cx.L // 128
    NB = 16
    with ExitStack() as es:
        Q32 = cx.sb(es, "mo_Q32", [128, NTc, 256], F32)
        KaugT = cx.sb(es, "mo_KaugT", [80, 4, Lc], BF16)
        Vaug = cx.sb(es, "mo_V", [128, NTc, 4, 65], BF16)
        kmeanT = cx.sb(es, "mo_kmT", [64, 4, NB], F32)
        mown = cx.sb(es, "mo_mown", [128, 4, 128], F32)
        gmask = cx.sb(es, "mo_gmask", [128, NB, NB], F32)
        c256 = cx.sb(es, "mo_c256", [128, 1], F32)
        S.dma("sp", mown[:], mask_own_ap, writes=[mown])
        S.dma("sp", gmask[:], gmask_ap, writes=[gmask])
        for hh in range(4):
            S.dma("sp", KaugT[64:80, hh, :], blkind_ap[:, 0:Lc], writes=[KaugT])
        S.op("pool", lambda h: h.memset(c256[:], 1.0 / 256.0), [], [c256])
        S.op("pool", lambda h: h.memset(Vaug[:, :, :, 64:65], 1.0), [], [Vaug])
        S.op("pool", lambda h: h.memset(kmeanT[:], 0.0), [], [kmeanT])
        tin = [cx.sb(es, f"mo_tin{i}", [128, 1024], F32) for i in range(2)]
        Gm = cx.sb(es, "mo_G", [128, NTc, 256], F32)
        rtmp = [cx.sb(es, f"mo_rtmp{i}", [128, 4, 256], F32) for i in range(2)]
        k32 = [cx.sb(es, f"mo_k32{i}", [128, 256], F32) for i in range(2)]
        kb = [cx.sb(es, f"mo_kb{i}", [128, 256], BF16) for i in range(2)]
        with ExitStack() as es1:
            ptr = [cx.ps(es1, f"mo_ptr{i}", [64, 8, 128], BF16) for i in range(2)]
            kmps = cx.ps(es1, "mo_kmps", [64, 4, 128], F32)
            def prepA(t):
                i2 = t % 2
                T, TMP, K32, KB, PT = tin[i2], rtmp[i2], k32[i2], kb[i2], ptr[i2]
                r0, r1 = t * 128, (t + 1) * 128
                S.dma("sp", T[:], proj_dt.ap[r0:r1, C_MQ:C_MQ + 1024],
                      reads=proj_dt.bufs(r0, r1), writes=[T])
                S.op("act", lambda h: h.activation(out=Gm[:, t, :], in_=T[:, 768:1024], func=AF.Silu), [T], [Gm])
                s4 = T[:, 0:512].rearrange("p (h two d) -> p h two d", two=2, d=32)
                x1, x2 = s4[:, :, 0, :], s4[:, :, 1, :]
                cb = cos_sb[:, t, :].unsqueeze(1).broadcast_to([128, 8, 32])
                sb_ = sin_sb[:, t, :].unsqueeze(1).broadcast_to([128, 8, 32])
                tv = TMP[:].rearrange("p f (h d) -> p f h d", d=32)
                S.op("dve", lambda h: h.tensor_tensor(tv[:, 0], x1, cb, ALU.mult), [T, cos_sb], [TMP])
                S.op("dve", lambda h: h.tensor_tensor(tv[:, 1], x2, sb_, ALU.mult), [T, sin_sb], [TMP])
                S.op("dve", lambda h: h.tensor_tensor(tv[:, 2], x2, cb, ALU.mult), [T, cos_sb], [TMP])
                S.op("dve", lambda h: h.tensor_tensor(tv[:, 3], x1, sb_, ALU.mult), [T, sin_sb], [TMP])
                q4 = Q32[:, t, :].rearrange("p (h two d) -> p h two d", two=2, d=32)
                k4 = K32[:].rearrange("p (h two d) -> p h two d", two=2, d=32)
                S.op("dve", lambda h: h.tensor_tensor(q4[:, :, 0, :], tv[:, 0, 0:4], tv[:, 1, 0:4], ALU.subtract), [TMP], [Q32])
                S.op("dve", lambda h: h.tensor_tensor(q4[:, :, 1, :], tv[:, 2, 0:4], tv[:, 3, 0:4], ALU.add), [TMP], [Q32])
                S.op("dve", lambda h: h.tensor_tensor(k4[:, :, 0, :], tv[:, 0, 4:8], tv[:, 1, 4:8], ALU.subtract), [TMP], [K32])
                S.op("dve", lambda h: h.tensor_tensor(k4[:, :, 1, :], tv[:, 2, 4:8], tv[:, 3, 4:8], ALU.add), [TMP], [K32])

            def prepB(t):
                i2 = t % 2
                T, TMP, K32, KB, PT = tin[i2], rtmp[i2], k32[i2], kb[i2], ptr[i2]
                r0, r1 = t * 128, (t + 1) * 128
                S.op("act", lambda h: h.copy(KB[:], K32[:]), [K32], [KB])
                S.op("pool", lambda h: h.tensor_copy(Vaug[:, t, :, 0:64], T[:, 512:768].rearrange("p (h d) -> p h d", d=64)),
                     [T], [Vaug])
                n = t // 2

                def km(h, K32=K32, n=n, t=t):
                    for hh in range(4):
                        ins = h.matmul(kmps[:, hh, n:n + 1], K32[:, hh * 64:(hh + 1) * 64], c256[:, 0:1],
                                       start=(t % 2 == 0 and hh == 0), stop=(t % 2 == 1 and hh == 3))
                    return ins
                S.op("pe", km, [K32, c256], [kmps])

                def tr(h, KB=KB, PT=PT):
                    for hh in range(4):
                        ins = h.transpose(PT[:, hh, :], KB[:, hh * 64:(hh + 1) * 64], ident_bf[:])
                    return ins
                S.op("pe", tr, [KB, ident_bf], [PT])
                S.op("act", lambda h: h.copy(KaugT[0:64, :, r0:r1], PT[:, 0:4, :]), [PT], [KaugT])
            prepA(0)
            for t in range(NTc):
                if t + 1 < NTc:
                    prepA(t + 1)
                prepB(t)
            S.op("dve", lambda h: h.tensor_copy(kmeanT[:, :, 0:Lc // 256], kmps[:, :, 0:Lc // 256]), [kmps], [kmeanT])
            S.barrier()
        mg = [cx.sb(es, f"mo_mg{i}", [128, 256], F32) for i in range(2)]
        mge = [cx.sb(es, f"mo_mge{i}", [128, 256], F32) for i in range(2)]
        qT32 = [cx.sb(es, f"mo_qT32{i}", [64, 4, 128], F32) for i in range(2)]
        gm = [cx.sb(es, f"mo_gm{i}", [128, 4, NB], F32) for i in range(2)]
        top8 = [cx.sb(es, f"mo_top8{i}", [128, 4, 8], F32) for i in range(2)]
        qaug = [cx.sb(es, f"mo_qaug{i}", [128, 4, 80], BF16) for i in range(2)]
        QaugT = [cx.sb(es, f"mo_QaugT{i}", [80, 4, 128], BF16) for i in range(2)]
        st = [cx.sb(es, f"mo_st{i}", [128, 4, 128], F32) for i in range(2)]
        pT = [cx.sb(es, f"mo_pT{i}", [128, 4, 128], BF16) for i in range(4)]
        den = [cx.sb(es, f"mo_den{i}", [128, 4], F32) for i in range(2)]
        yb = [cx.sb(es, f"mo_y{i}", [128, 256], F32) for i in range(2)]
        psq = cx.ps(es, "mo_psq", [64, 4, 128], F32)
        psg = cx.ps(es, "mo_psg", [128, 4, 128], F32)
        psa = cx.ps(es, "mo_psa", [80, 8, 128], BF16)
        pss = [cx.ps(es, f"mo_pss{i}", [128, 4, 128], F32) for i in range(2)]
        pso = [cx.ps(es, f"mo_pso{i}", [128, 4, 128], F32) for i in range(2)]
        ctxs = {}

        def prologue(t):
            i2 = t % 2
            r0, r1 = t * 128, (t + 1) * 128
            qb = t // 2
            MG, QT, GM, T8, QA, QAT = mg[i2], qT32[i2], gm[i2], top8[i2], qaug[i2], QaugT[i2]

            def trq(h):
                for hh in range(4):
                    ins = h.transpose(psq[:, hh, :], Q32[:, t, hh * 64:(hh + 1) * 64], ident_f[:])
                return ins
            S.op("pe", trq, [Q32, ident_f], [psq])
            S.op("dve", lambda h: h.tensor_copy(QT[:], psq[:]), [psq], [QT])

            def gate(h):
                for hh in range(4):
                    ins = h.matmul(psg[:, hh, 0:NB], QT[:, hh, :], kmeanT[:, hh, :], start=True, stop=True)
                return ins
            S.op("pe", gate, [QT, kmeanT], [psg])
            S.op("dve", lambda h: h.tensor_tensor(GM[:], psg[:, :, 0:NB],
                                                  gmask[:, qb, :].unsqueeze(1).broadcast_to([128, 4, NB]), ALU.add),
                 [psg, gmask], [GM])
            for hh in range(4):
                S.op("dve", lambda h: h.max(T8[:, hh, :], GM[:, hh, :]), [GM], [T8])
            for hh in range(4):
                S.op("dve", lambda h: h.tensor_scalar(QA[:, hh, 64:80], GM[:, hh, :], T8[:, hh, 2:3], -30000.0,
                                                      ALU.is_lt, ALU.mult), [GM, T8], [QA])
            S.op("pool", lambda h: h.memset(QA[:, :, 64 + qb:65 + qb], 0.0), [QA], [QA])
            S.op("act", lambda h: h.copy(QA[:, :, 0:64], Q32[:, t, :].rearrange("p (h d) -> p h d", d=64)), [Q32], [QA])

            def tra(h):
                for hh in range(4):
                    ins = h.transpose(psa[:, hh, :], QA[:, hh, :], ident_bf[:])
                return ins
            S.op("pe", tra, [QA, ident_bf], [psa])
            S.op("dve", lambda h: h.tensor_copy(QAT[:], psa[:, 0:4, :]), [psa], [QAT])

        items = [(t, kk) for t in range(NTc) for kk in range(t + 1)]

        def score(i):
            t, kk = items[i]
            if kk == 0:
                prologue(t)
            PS, QAT = pss[i % 2], QaugT[t % 2]

            def sc(h):
                for hh in range(4):
                    ins = h.matmul(PS[:, hh, :], KaugT[:, hh, kk * 128:(kk + 1) * 128], QAT[:, hh, :],
                                   start=True, stop=True)
                return ins
            S.op("pe", sc, [KaugT, QAT], [PS])

        def finish_item(i):
            t, kk = items[i]
            PS, ST, PTB, PO = pss[i % 2], st[i % 2], pT[i % 4], pso[t % 2]
            if kk == t:
                S.op("dve", lambda h: h.scalar_tensor_tensor(ST[:], PS[:], 0.125, mown[:], ALU.mult, ALU.add),
                     [PS, mown], [ST])
                S.op("act", lambda h: h.activation(out=PTB[:], in_=ST[:], func=AF.Exp), [ST], [PTB])
            else:
                S.op("act", lambda h: h.activation(out=PTB[:], in_=PS[:], func=AF.Exp, scale=0.125), [PS], [PTB])

            def pv(h):
                for hh in range(4):
                    ins = h.matmul(PO[:, hh, 0:65], PTB[:, hh, :], Vaug[:, kk, hh, :],
                                   start=(kk == 0 and hh == 0), stop=(kk == t and hh == 3))
                return ins
            S.op("pe", pv, [PTB, Vaug], [PO])
            if kk == t:
                i2 = t % 2
                r0, r1 = t * 128, (t + 1) * 128
                MG, DEN, Y = mg[i2], den[i2], yb[i2]
                S.op("dve", lambda h: h.reciprocal(DEN[:], PO[:, :, 64]), [PO], [DEN])
                for hh in range(4):
                    S.op("act", lambda h: h.activation(out=Y[:, hh * 64:(hh + 1) * 64], in_=PO[:, hh, 0:64], func=AF.Copy,
                                                       scale=DEN[:, hh:hh + 1]), [PO, DEN], [Y])
                S.op("dve", lambda h: h.tensor_tensor(Y[:], Y[:], Gm[:, t, :], ALU.mult), [Y, Gm], [Y])
                S.dma("pool", mix_dt.ap[r0:r1, mcol:mcol + 256], Y[:], reads=[Y], writes=mix_dt.bufs(r0, r1))

        score(0)
        for i in range(len(items)):
            if i + 1 < len(items):
                score(i + 1)
            finish_item(i)
        S.barrier()


def phase_ssd(cx, proj_dt, mix_dt, ident_bf, cst, mcol=0):
    nc, S = cx.nc, cx.S
    Lc, NTc = cx.L, cx.L // 128
    with ExitStack() as es:
        def ld(name, shape, dt=F32):
            b = cx.sb(es, "ssd_" + name, shape, dt)
            S.dma("sp", b[:], cst[name], writes=[b])
            return b
        convw = ld("convw", [128, 4, 512]); convb = ld("convb", [128, 512]); dtb = ld("dtb", [128, 4])
        Abc = ld("alog", [128, 4]); dskip = ld("dskip", [128, 256]); normw = ld("normw", [128, 256])
        tri = ld("tri", [128, 128]); ones = ld("ones", [128, 128]); maskT = ld("maskT", [128, 128])
        S.op("act", lambda h: h.activation(out=Abc[:], in_=Abc[:], func=AF.Exp), [Abc], [Abc])
        S.op("dve", lambda h: h.tensor_scalar(Abc[:], Abc[:], -1.0, None, ALU.mult), [Abc], [Abc])
        prev32 = cx.sb(es, "ssd_prev32", [128, 256], F32)
        prevb = cx.sb(es, "ssd_prevb", [128, 256], BF16)
        S.op("pool", lambda h: h.memset(prev32[:], 0.0), [], [prev32])
        S.op("pool", lambda h: h.memset(prevb[:], 0.0), [], [prevb])
        Tj = [[cx.sb(es, f"ssd_T{i}_{j}", [128, 512], F32) for j in range(4)] for i in range(2)]
        zt = [cx.sb(es, f"ssd_z{i}", [128, 256], F32) for i in range(2)]
        dtt = [cx.sb(es, f"ssd_dt{i}", [128, 4], F32) for i in range(2)]
        xa = [cx.sb(es, f"ssd_xa{i}", [128, 512], F32) for i in range(2)]
        sm = [cx.sb(es, f"ssd_sm{i}", [128, 8, 4], F32) for i in range(2)]
        Xf = [cx.sb(es, f"ssd_X{i}", [128, 256], F32) for i in range(2)]
        Xb = [cx.sb(es, f"ssd_Xb{i}", [128, 256], BF16) for i in range(2)]
        Xd = [cx.sb(es, f"ssd_Xd{i}", [128, 256], BF16) for i in range(2)]
        BCb = [cx.sb(es, f"ssd_BCb{i}", [128, 256], BF16) for i in range(2)]
        BCT = [cx.sb(es, f"ssd_BCT{i}", [128, 2, 128], BF16) for i in range(2)]
        R4 = [cx.sb(es, f"ssd_R4{i}", [128, 4, 128], F32) for i in range(2)]
        TD4 = [cx.sb(es, f"ssd_TD4{i}", [128, 4, 128], F32) for i in range(2)]
        M4 = [cx.sb(es, f"ssd_M4{i}", [128, 4, 128], BF16) for i in range(2)]
        identf = ld("identf", [128, 128])
        mask4 = cx.sb(es, "ssd_mask4", [128, 4, 128], F32)
        S.op("dve", lambda h: h.tensor_copy(mask4[:], maskT[:].unsqueeze(1).broadcast_to([128, 4, 128])), [maskT], [mask4])
        y1 = [cx.sb(es, f"ssd_y1{i}", [128, 256], F32) for i in range(2)]
        y2 = [cx.sb(es, f"ssd_y2{i}", [128, 256], F32) for i in range(2)]
        junk = cx.sb(es, "ssd_junk", [128, 256], F32)
        csb = [cx.sb(es, f"ssd_cs{i}", [128, 256], F32) for i in range(2)]
        ps_t = cx.ps(es, "ssd_ps_t", [128, 8, 128], BF16)
        ps_s = cx.ps(es, "ssd_ps_s", [128, 512], F32)
        ps_cb = cx.ps(es, "ssd_ps_cb", [128, 512], F32)
        ps_d = cx.ps(es, "ssd_ps_d", [128, 4, 128], F32)
        ps_yd = cx.ps(es, "ssd_ps_yd", [128, 512], F32)
        ps_yo = cx.ps(es, "ssd_ps_yo", [128, 512], F32)
        ps_cs = cx.ps(es, "ssd_ps_cs", [128, 512], F32)
        ndc = [0]

        def front(t):
            nd = ndc[0]
            i2 = t % 2
            r0, r1 = t * 128, (t + 1) * 128
            T, Z, DT, XA, SM, X, XB, XD, BC, BT = Tj[i2], zt[i2], dtt[i2], xa[i2], sm[i2], Xf[i2], Xb[i2], Xd[i2], BCb[i2], BCT[i2]
            for j in range(4):
                sh = 3 - j
                q = "sp"
                if r0 - sh >= 0:
                    S.dma(q, T[j][:], proj_dt.ap[r0 - sh:r1 - sh, C_XS:C_XS + 512],
                          reads=proj_dt.bufs(max(r0 - sh, 0), r1), writes=[T[j]])
                else:
                    S.op("pool", lambda h: h.memset(T[j][0:32, :], 0.0), [], [T[j]])
                    S.dma(q, T[j][sh:128, :], proj_dt.ap[0:128 - sh, C_XS:C_XS + 512],
                          reads=proj_dt.bufs(0, 128), writes=[T[j]])
            S.dma("sp", Z[:], proj_dt.ap[r0:r1, C_Z:C_Z + 256], reads=proj_dt.bufs(r0, r1), writes=[Z])
            S.dma("sp", DT[:], proj_dt.ap[r0:r1, C_DT:C_DT + 4], reads=proj_dt.bufs(r0, r1), writes=[DT])
            for j in range(4):
                ek = "dve"
                S.op(ek, lambda h: h.tensor_tensor(T[j][:], T[j][:], convw[:, j, :], ALU.mult), [T[j], convw], [T[j]])
            S.op("dve", lambda h: h.tensor_tensor(T[0][:], T[0][:], T[2][:], ALU.add), [T[0], T[2]], [T[0]])
            S.op("dve", lambda h: h.tensor_tensor(T[1][:], T[1][:], T[3][:], ALU.add), [T[1], T[3]], [T[1]])
            S.op("dve", lambda h: h.tensor_tensor(T[0][:], T[0][:], T[1][:], ALU.add), [T[0], T[1]], [T[0]])
            S.op("dve", lambda h: h.tensor_tensor(T[0][:], T[0][:], convb[:], ALU.add), [T[0], convb], [T[0]])
            S.op("act", lambda h: h.activation(out=XA[:], in_=T[0][:], func=AF.Silu), [T[0]], [XA])
            S.op("act", lambda h: h.activation(out=Z[:], in_=Z[:], func=AF.Silu), [Z], [Z])
            S.op("dve", lambda h: h.tensor_tensor(DT[:], DT[:], dtb[:], ALU.add), [DT, dtb], [DT])
            S.op("act", lambda h: h.activation(out=DT[:], in_=DT[:], func=AF.Exp), [DT], [DT])
            S.op("act", lambda h: h.activation(out=DT[:], in_=DT[:], func=AF.Ln, bias=1.0), [DT], [DT])
            S.op("dve", lambda h: h.tensor_tensor(SM[:, 0, :], DT[:], Abc[:], ALU.mult), [DT, Abc], [SM])

            def cum(h, SM=SM):
                h.matmul(ps_s[:, 0:4], tri[:], SM[:, 0, :], start=True, stop=False)
                return h.matmul(ps_s[:, 4:8], ones[:], SM[:, 0, :], start=False, stop=True)
            S.op("pe", cum, [tri, ones, SM], [ps_s])
            S.op("dve", lambda h: h.tensor_copy(SM[:, 1, :], ps_s[:, 0:4]), [ps_s], [SM])
            S.op("dve", lambda h: h.tensor_scalar(SM[:, 2, :], ps_s[:, 0:4], -1.0, None, ALU.mult), [ps_s], [SM])
            S.op("dve", lambda h: h.tensor_tensor(SM[:, 5, :], ps_s[:, 4:8], SM[:, 1, :], ALU.subtract), [ps_s, SM], [SM])
            S.op("act", lambda h: h.activation(out=SM[:, 3, :], in_=SM[:, 1, :], func=AF.Exp), [SM], [SM])
            S.op("act", lambda h: h.activation(out=SM[:, 4, :], in_=ps_s[:, 4:8], func=AF.Exp), [ps_s], [SM])
            S.op("act", lambda h: h.activation(out=SM[:, 5, :], in_=SM[:, 5, :], func=AF.Exp), [SM], [SM])
            x3 = X[:].rearrange("p (h d) -> p h d", d=64)
            S.op("dve", lambda h: h.tensor_tensor(x3, XA[:, 0:256].rearrange("p (h d) -> p h d", d=64),
                                                  DT[:].unsqueeze(2).broadcast_to([128, 4, 64]), ALU.mult), [XA, DT], [X])
            S.op("act", lambda h: h.copy(XB[:], X[:]), [X], [XB])
            S.op("dve", lambda h: h.tensor_tensor(XD[:].rearrange("p (h d) -> p h d", d=64), x3,
                                                  SM[:, 5, :].unsqueeze(2).broadcast_to([128, 4, 64]), ALU.mult), [X, SM], [XD])
            S.op("act", lambda h: h.copy(BC[:], XA[:, 256:512]), [XA], [BC])

            def trbc(h, BC=BC):
                h.transpose(ps_t[:, 0, :], BC[:, 0:128], ident_bf[:])
                return h.transpose(ps_t[:, 1, :], BC[:, 128:256], ident_bf[:])
            S.op("pe", trbc, [BC, ident_bf], [ps_t])
            S.op("act", lambda h: h.copy(BT[:], ps_t[:, 0:2, :]), [ps_t], [BT])
            S.op("pe", lambda h: h.matmul(ps_cb[:, 0:128], BT[:, 0, :], BT[:, 1, :], start=True, stop=True), [BT], [ps_cb])
            S.op("pe", lambda h: h.matmul(ps_cs[:, 0:256], BC[:, 0:128], XD[:], start=True, stop=True), [BC, XD], [ps_cs])
            S.op("act", lambda h: h.copy(csb[i2][:], ps_cs[:, 0:256]), [ps_cs], [csb[i2]])
            RR, TD, MM = R4[i2], TD4[i2], M4[i2]
            S.op("dve", lambda h: h.tensor_tensor(RR[:], tri[:].unsqueeze(1).broadcast_to([128, 4, 128]),
                                                  SM[:, 0, :].unsqueeze(2).broadcast_to([128, 4, 128]), ALU.mult), [tri, SM], [RR])

            def dbc(h):
                h.matmul(ps_d[:], ones[:], RR[:], start=True, stop=False)
                return h.matmul(ps_d[:], identf[:], mask4[:], start=False, stop=True)
            S.op("pe", dbc, [ones, RR, identf, mask4], [ps_d])
            S.op("dve", lambda h: h.tensor_tensor(TD[:], ps_d[:], SM[:, 1, :].unsqueeze(2).broadcast_to([128, 4, 128]), ALU.subtract),
                 [ps_d, SM], [TD])
            S.op("act", lambda h: h.activation(out=TD[:], in_=TD[:], func=AF.Exp), [TD], [TD])
            S.op("dve", lambda h: h.tensor_tensor(MM[:], TD[:], ps_cb[:, 0:128].unsqueeze(1).broadcast_to([128, 4, 128]), ALU.mult),
                 [TD, ps_cb], [MM])

            def ydiag(h):
                for hh in range(4):
                    ins = h.matmul(ps_yd[:, hh * 64:(hh + 1) * 64], MM[:, hh, :], XB[:, hh * 64:(hh + 1) * 64],
                                   start=(hh == 0), stop=(hh == 3))
                return ins
            S.op("pe", ydiag, [MM, XB], [ps_yd])
            ndc[0] = nd
            S.op("act", lambda h: h.copy(y1[i2][:], ps_yd[:, 0:256]), [ps_yd], [y1[i2]])

        def back(t):
            i2 = t % 2
            r0, r1 = t * 128, (t + 1) * 128
            Z, XA, SM, BT = zt[i2], xa[i2], sm[i2], BCT[i2]
            Y1, Y2 = y1[i2], y2[i2]
            S.op("pe", lambda h: h.matmul(ps_yo[:, 0:256], BT[:, 1, :], prevb[:], start=True, stop=True), [BT, prevb], [ps_yo])
            p3 = prev32[:].rearrange("p (h d) -> p h d", d=64)
            S.op("dve", lambda h: h.tensor_tensor(p3, p3, SM[:, 4, :].unsqueeze(2).broadcast_to([128, 4, 64]), ALU.mult),
                 [prev32, SM], [prev32])
            S.op("dve", lambda h: h.tensor_tensor(prev32[:], prev32[:], csb[i2][:], ALU.add), [prev32, csb[i2]], [prev32])
            S.op("act", lambda h: h.copy(prevb[:], prev32[:]), [prev32], [prevb])
            for hh in range(4):
                sl = slice(hh * 64, (hh + 1) * 64)
                S.op("dve", lambda h: h.scalar_tensor_tensor(Y1[:, sl], ps_yo[:, sl], SM[:, 3, hh:hh + 1], Y1[:, sl],
                                                             ALU.mult, ALU.add), [ps_yo, SM, Y1], [Y1])
            S.op("pool", lambda h: h.tensor_tensor(Y2[:], XA[:, 0:256], dskip[:], ALU.mult), [XA, dskip], [Y2])
            S.op("dve", lambda h: h.tensor_tensor(Y1[:], Y1[:], Y2[:], ALU.add), [Y1, Y2], [Y1])
            S.op("dve", lambda h: h.tensor_tensor(Y1[:], Y1[:], Z[:], ALU.mult), [Y1, Z], [Y1])
            S.op("act", lambda h: h.activation(out=junk[:], in_=Y1[:], func=AF.Square, accum_out=SM[:, 6, 0:1]), [Y1], [junk, SM])
            S.op("dve", lambda h: h.tensor_scalar(SM[:, 6, 0:1], SM[:, 6, 0:1], 1.0 / 256.0, EPS, ALU.mult, ALU.add), [SM], [SM])
            S.op("act", lambda h: h.sqrt(SM[:, 6, 0:1], SM[:, 6, 0:1]), [SM], [SM])
            S.op("dve", lambda h: h.reciprocal(SM[:, 6, 0:1], SM[:, 6, 0:1]), [SM], [SM])
            S.op("dve", lambda h: h.scalar_tensor_tensor(Y2[:], Y1[:], SM[:, 6, 0:1], normw[:], ALU.mult, ALU.mult),
                 [Y1, SM, normw], [Y2])
            S.dma("pool", mix_dt.ap[r0:r1, mcol + 256:mcol + 512], Y2[:], reads=[Y2], writes=mix_dt.bufs(r0, r1))

        front(0)
        for t in range(NTc):
            if t + 1 < NTc:
                front(t + 1)
            back(t)
        S.barrier()


class StopPhase(Exception):
    pass


def stage(cx, n):
    if getattr(cx, "stop_stage", None) == n and not cx.S.muted:
        cx.S.barrier()
        cx.S.muted = True


def cmul(S, ek, out_re, out_im, a_re, a_im, b_re, b_im, t1, t2, reads, writes, conj_b=False):
    o1 = ALU.subtract if not conj_b else ALU.add
    o2 = ALU.add if not conj_b else ALU.subtract
    S.op(ek, lambda h: h.tensor_tensor(t1, a_re, b_re, ALU.mult), reads, writes)
    S.op(ek, lambda h: h.tensor_tensor(t2, a_im, b_im, ALU.mult), reads, writes)
    S.op(ek, lambda h: h.tensor_tensor(out_re, t1, t2, o1), reads, writes)
    S.op(ek, lambda h: h.tensor_tensor(t1, a_im, b_re, ALU.mult), reads, writes)
    S.op(ek, lambda h: h.tensor_tensor(t2, a_re, b_im, ALU.mult), reads, writes)
    S.op(ek, lambda h: h.tensor_tensor(out_im, t1, t2, o2), reads, writes)


def phase_s5(cx, proj_dt, mix_dt, ident_bf, ident_f, cst):
    nc, S = cx.nc, cx.S
    Lc = cx.L
    T = 16
    SEG = min(Lc, 2048)
    NSEG = Lc // SEG
    NC = SEG // T
    NCT = NC + 1
    with ExitStack() as es:
        def ld(name, shape, dt=F32, q="sp"):
            b = cx.sb(es, "s5_" + name, shape, dt)
            S.dma(q, b[:], cst[name], writes=[b])
            return b
        are = ld("are", [128, 16]); aim = ld("aim", [128, 16]); ldt = ld("ldt", [128, 16])
        ccre = ld("ccre", [128, 16, 32]); ccim = ld("ccim", [128, 16, 32])
        dfm = ld("dfm", [128, 4]); glub = ld("glub", [128, 4]); kvec = ld("kvec", [128, 256])
        Wg = load_weight_bf16(cx, es, "s5_Wg", cst["gluw"], 512, 512, None)
        BT = [cx.sb(es, f"s5_BT{i}", [128, 16, 128], BF16) for i in range(2)]
        Ere = cx.sb(es, "s5_Ere", [128, 16, T + 1], F32); Eim = cx.sb(es, "s5_Eim", [128, 16, T + 1], F32)
        Rk = cx.sb(es, "s5_Rk", [128, 16, T + 1], F32)
        E2re = cx.sb(es, "s5_E2re", [128, 16, NCT], F32); E2im = cx.sb(es, "s5_E2im", [128, 16, NCT], F32)
        R2 = cx.sb(es, "s5_R2", [128, 16, NCT], F32)
        sm = cx.sb(es, "s5_sm", [128, 12, 16], F32)
        pmax = 1
        while pmax * 2 < max(T + 1, NCT):
            pmax *= 2
        Enim = cx.sb(es, "s5_Enim", [128, 16, T + 1], F32)
        hp = cx.sb(es, "s5_halfpi", [128, 1], F32)
        es_tb = ExitStack()
        tb = cx.sb(es_tb, "s5_tb", [128, 4, 16 + 16 * pmax], F32)
        SMALL = [sm]
        S.op("act", lambda h: h.activation(out=sm[:, 0, :], in_=ldt[:], func=AF.Exp), [ldt], SMALL)
        S.op("dve", lambda h: h.tensor_tensor(sm[:, 1, :], are[:], sm[:, 0, :], ALU.mult), [are] + SMALL, SMALL)
        S.op("dve", lambda h: h.tensor_tensor(sm[:, 2, :], aim[:], sm[:, 0, :], ALU.mult), [aim] + SMALL, SMALL)
        S.op("act", lambda h: h.activation(out=sm[:, 3, :], in_=sm[:, 1, :], func=AF.Exp), SMALL, SMALL)
        S.op("pool", lambda h: h.memset(hp[:], float(np.pi / 2)), [], [hp])
        S.op("act", lambda h: h.activation(out=sm[:, 5, :], in_=sm[:, 2, :], func=AF.Sin, scale=1.0 / 64), SMALL, SMALL)
        S.op("act", lambda h: h.activation(out=sm[:, 4, :], in_=sm[:, 2, :], func=AF.Sin, scale=1.0 / 64, bias=hp[:, 0:1]),
             SMALL + [hp], SMALL)
        for _ in range(6):
            S.op("dve", lambda h: h.tensor_tensor(sm[:, 8, :], sm[:, 4, :], sm[:, 4, :], ALU.mult), SMALL, SMALL)
            S.op("dve", lambda h: h.tensor_tensor(sm[:, 9, :], sm[:, 5, :], sm[:, 5, :], ALU.mult), SMALL, SMALL)
            S.op("dve", lambda h: h.tensor_tensor(sm[:, 10, :], sm[:, 4, :], sm[:, 5, :], ALU.mult), SMALL, SMALL)
            S.op("dve", lambda h: h.tensor_tensor(sm[:, 4, :], sm[:, 8, :], sm[:, 9, :], ALU.subtract), SMALL, SMALL)
            S.op("dve", lambda h: h.tensor_scalar(sm[:, 5, :], sm[:, 10, :], 2.0, None, ALU.mult), SMALL, SMALL)
        S.op("dve", lambda h: h.tensor_tensor(sm[:, 8, :], sm[:, 3, :], sm[:, 4, :], ALU.mult), SMALL, SMALL)
        S.op("dve", lambda h: h.tensor_tensor(sm[:, 9, :], sm[:, 3, :], sm[:, 5, :], ALU.mult), SMALL, SMALL)
        S.op("dve", lambda h: h.tensor_scalar(sm[:, 8, :], sm[:, 8, :], -1.0, None, ALU.add), SMALL, SMALL)
        S.op("dve", lambda h: h.tensor_tensor(sm[:, 10, :], are[:], are[:], ALU.mult), [are], SMALL)
        S.op("dve", lambda h: h.tensor_tensor(sm[:, 11, :], aim[:], aim[:], ALU.mult), [aim], SMALL)
        S.op("dve", lambda h: h.tensor_tensor(sm[:, 10, :], sm[:, 10, :], sm[:, 11, :], ALU.add), SMALL, SMALL)
        S.op("dve", lambda h: h.reciprocal(sm[:, 10, :], sm[:, 10, :]), SMALL, SMALL)
        S.op("dve", lambda h: h.tensor_tensor(sm[:, 6, :], sm[:, 8, :], are[:], ALU.mult), SMALL + [are], SMALL)
        S.op("dve", lambda h: h.tensor_tensor(sm[:, 11, :], sm[:, 9, :], aim[:], ALU.mult), SMALL + [aim], SMALL)
        S.op("dve", lambda h: h.tensor_tensor(sm[:, 6, :], sm[:, 6, :], sm[:, 11, :], ALU.add), SMALL, SMALL)
        S.op("dve", lambda h: h.tensor_tensor(sm[:, 7, :], sm[:, 9, :], are[:], ALU.mult), SMALL + [are], SMALL)
        S.op("dve", lambda h: h.tensor_tensor(sm[:, 11, :], sm[:, 8, :], aim[:], ALU.mult), SMALL + [aim], SMALL)
        S.op("dve", lambda h: h.tensor_tensor(sm[:, 7, :], sm[:, 7, :], sm[:, 11, :], ALU.subtract), SMALL, SMALL)
        S.op("dve", lambda h: h.tensor_tensor(sm[:, 6, :], sm[:, 6, :], sm[:, 10, :], ALU.mult), SMALL, SMALL)
        S.op("dve", lambda h: h.tensor_tensor(sm[:, 7, :], sm[:, 7, :], sm[:, 10, :], ALU.mult), SMALL, SMALL)

        def build_pow_tables(Tre, Tim, n, base_re, base_im):
            TB = [Tre, Tim, tb]
            S.op("pool", lambda h: h.memset(Tre[:, :, 0:1], 1.0), [], [Tre])
            S.op("pool", lambda h: h.memset(Tim[:, :, 0:1], 0.0), [], [Tim])
            S.op("dve", lambda h: h.tensor_copy(Tre[:, :, 1], base_re), SMALL, [Tre])
            S.op("dve", lambda h: h.tensor_copy(Tim[:, :, 1], base_im), SMALL, [Tim])
            m = 2
            while m < n:
                cnt = min(m, n - m)
                pr, pi_ = tb[:, 2, 0:16], tb[:, 3, 0:16]
                cmul(S, "dve", pr, pi_, Tre[:, :, m - 1], Tim[:, :, m - 1], Tre[:, :, 1], Tim[:, :, 1],
                     tb[:, 0, 0:16], tb[:, 1, 0:16], TB, TB)
                prb = pr.unsqueeze(2).broadcast_to([128, 16, cnt])
                pib = pi_.unsqueeze(2).broadcast_to([128, 16, cnt])
                t1 = tb[:, 0, 16:16 + 16 * cnt].rearrange("p (a b) -> p a b", b=cnt)
                t2 = tb[:, 1, 16:16 + 16 * cnt].rearrange("p (a b) -> p a b", b=cnt)
                cmul(S, "dve", Tre[:, :, m:m + cnt], Tim[:, :, m:m + cnt], Tre[:, :, 0:cnt], Tim[:, :, 0:cnt], prb, pib,
                     t1, t2, TB, TB)
                m += cnt
        build_pow_tables(Ere, Eim, T + 1, sm[:, 4, :], sm[:, 5, :])
        S.op("dve", lambda h: h.tensor_scalar(Enim[:], Eim[:], -1.0, None, ALU.mult), [Eim], [Enim])
        S.op("dve", lambda h: h.tensor_copy(sm[:, 8, :], Ere[:, :, T]), [Ere], SMALL)
        S.op("dve", lambda h: h.tensor_copy(sm[:, 9, :], Eim[:, :, T]), [Eim], SMALL)
        build_pow_tables(E2re, E2im, NCT, sm[:, 8, :], sm[:, 9, :])
        S.op("dve", lambda h: h.tensor_tensor(Rk[:], sm[:, 1, :].unsqueeze(2).broadcast_to([128, 16, T + 1]),
                                              kvec[:, 0:T + 1].unsqueeze(1).broadcast_to([128, 16, T + 1]), ALU.mult),
             SMALL + [kvec], [Rk])
        S.op("act", lambda h: h.activation(out=Rk[:], in_=Rk[:], func=AF.Exp), [Rk], [Rk])
        S.op("dve", lambda h: h.tensor_tensor(R2[:], sm[:, 1, :].unsqueeze(2).broadcast_to([128, 16, NCT]),
                                              kvec[:, 0:NCT].unsqueeze(1).broadcast_to([128, 16, NCT]), ALU.mult),
             SMALL + [kvec], [R2])
        S.op("act", lambda h: h.activation(out=R2[:], in_=R2[:], func=AF.Exp, scale=float(T)), [R2], [R2])
        S.barrier()
        es_tb.close()
        stage(cx, 1)
        with ExitStack() as es1:
            bpre = cx.sb(es1, "s5_bpre", [128, 16, 128], F32); bpim = cx.sb(es1, "s5_bpim", [128, 16, 128], F32)
            S.dma("sp", bpre[:], cst["bpre"], writes=[bpre]); S.dma("pool", bpim[:], cst["bpim"], writes=[bpim])
            t1 = cx.sb(es1, "s5_bt1", [128, 16, 128], F32); t2 = cx.sb(es1, "s5_bt2", [128, 16, 128], F32)
            bbre = cx.sb(es1, "s5_bbre", [128, 16, 128], BF16); bbim = cx.sb(es1, "s5_bbim", [128, 16, 128], BF16)
            kr = sm[:, 6, :].unsqueeze(2).broadcast_to([128, 16, 128])
            ki = sm[:, 7, :].unsqueeze(2).broadcast_to([128, 16, 128])
            cmul(S, "dve", bbre[:], bbim[:], bpre[:], bpim[:], kr, ki, t1[:], t2[:], [bpre, bpim, t1, t2] + SMALL, [bbre, bbim, t1, t2])
            pst = cx.ps(es1, "s5_pst", [128, 8, 128], BF16)
            for k in range(16):
                for ri, src in enumerate((bbre, bbim)):
                    S.op("pe", lambda h: h.transpose(pst[:, ri, :], src[:, k, :], ident_bf[:]), [src, ident_bf], [pst])
                    S.op("act", lambda h: h.copy(BT[ri][:, k, :], pst[:, ri, :]), [pst], [BT[ri]])
            S.barrier()
        stage(cx, 2)
        Send = cx.sb(es, "s5_Send", [128, 2, 16], F32)
        S.op("pool", lambda h: h.memset(Send[:], 0.0), [], [Send])
        for seg in range(NSEG):
            t00 = seg * SEG
            with ExitStack() as es2:
                y = [cx.sb(es2, f"s5_y{q}", [128, SEG], F32) for q in range(4)]
                with ExitStack() as es3:
                    uTb = [cx.sb(es3, f"s5_uTb{q}", [128, SEG], BF16) for q in range(4)]
                    with ExitStack() as es4:
                        sut = [cx.sb(es4, f"s5_sut{i}", [128, 512], F32) for i in range(2)]
                        sub = [cx.sb(es4, f"s5_sub{i}", [128, 512], BF16) for i in range(2)]
                        psu = [cx.ps(es4, f"s5_psu{i}", [128, 8, 128], BF16) for i in range(2)]
                        for tt in range(SEG // 128):
                            i2 = tt % 2
                            r0 = t00 + tt * 128
                            S.dma("sp", sut[i2][:], proj_dt.ap[r0:r0 + 128, C_SU:C_SU + 512],
                                  reads=proj_dt.bufs(r0, r0 + 128), writes=[sut[i2]])
                            S.op("pool", lambda h: h.tensor_copy(sub[i2][:], sut[i2][:]), [sut[i2]], [sub[i2]])
                            stage(cx, 21)

                            def tru(h, i2=i2):
                                for q in range(4):
                                    ins = h.transpose(psu[i2][:, q, :], sub[i2][:, q * 128:(q + 1) * 128], ident_bf[:])
                                return ins
                            S.op("pe", tru, [sub[i2], ident_bf], [psu[i2]])
                            stage(cx, 22)
                            for q in range(4):
                                ek = "act"
                                if ek == "act":
                                    S.op("act", lambda h: h.copy(uTb[q][:, tt * 128:(tt + 1) * 128], psu[i2][:, q, :]), [psu[i2]], [uTb[q]])
                                else:
                                    S.op("dve", lambda h: h.tensor_copy(uTb[q][:, tt * 128:(tt + 1) * 128], psu[i2][:, q, :]), [psu[i2]], [uTb[q]])
                                stage(cx, 230 + q)
                            stage(cx, 240 + tt)
                        S.barrier()
                    stage(cx, 3)
                    xre = cx.sb(es3, "s5_xre", [128, SEG], F32); xim = cx.sb(es3, "s5_xim", [128, SEG], F32)
                    vre2 = [cx.sb(es3, f"s5_vre{i}", [128, SEG], BF16) for i in range(2)]
                    vim2 = [cx.sb(es3, f"s5_vim{i}", [128, SEG], BF16) for i in range(2)]
                    rmask = cx.sb(es3, "s5_rmask", [128, SEG], F32)
                    tas = [cx.sb(es3, f"s5_ta{i}", [128, 512], F32) for i in range(4)]
                    tbs = [cx.sb(es3, f"s5_tbb{i}", [128, 512], F32) for i in range(4)]
                    nrot = [0]
                    ctabs = [[cx.sb(es3, f"s5_ctab{par}_{i}", [128, T + 1, 64], BF16) for i in range(4)] for par in range(2)]
                    for par in range(2):
                        for i in range(4):
                            S.op("pool", lambda h: h.memset(ctabs[par][i][:], 0.0), [], [ctabs[par][i]])
                    ct1 = cx.sb(es3, "s5_ct1", [128, T + 1, 32], F32); ct2 = cx.sb(es3, "s5_ct2", [128, T + 1, 32], F32)
                    lv = cx.sb(es3, "s5_lv", [128, 12, NCT], F32)
                    Sp2 = [[cx.sb(es3, f"s5_Sp{par}_{i}", [128, NC], BF16) for i in range(2)] for par in range(2)]
                    R2m = cx.sb(es3, "s5_R2m", [128, NC], F32)
                    psb = [cx.ps(es3, f"s5_psb{i}", [128, 512], F32) for i in range(4)]
                    psy = [cx.ps(es3, f"s5_psy{i}", [128, 4, 128], F32) for i in range(2)]
                    npyc = [0]

                    def front_mid(k):
                        q, j = k // 4, k % 4
                        vre, vim, Sp = vre2[k % 2], vim2[k % 2], Sp2[k % 2]
                        rm3 = rmask[:].rearrange("p (c k) -> p c k", k=T)
                        S.op("act", lambda h: h.copy(rm3[:, :, 1:T], sm[:, 3, k:k + 1].unsqueeze(2).broadcast_to([128, NC, T - 1])),
                             SMALL, [rmask])
                        S.op("pool", lambda h: h.memset(rm3[:, :, 0:1], 0.0), [], [rmask])
                        for blk in range(SEG // 512):
                            c0 = blk * 512
                            PR, PI = psb[(2 * blk) % 4], psb[(2 * blk + 1) % 4]
                            S.op("pe", lambda h: h.matmul(PR[:], BT[0][:, k, :], uTb[q][:, c0:c0 + 512], start=True, stop=True),
                                 [BT[0], uTb[q]], [PR])
                            S.op("pe", lambda h: h.matmul(PI[:], BT[1][:, k, :], uTb[q][:, c0:c0 + 512], start=True, stop=True),
                                 [BT[1], uTb[q]], [PI])
                            cb = Ere[:, k, 0:T].unsqueeze(1).broadcast_to([128, 512 // T, T])
                            sb_ = Eim[:, k, 0:T].unsqueeze(1).broadcast_to([128, 512 // T, T])
                            v3 = lambda ap: ap.rearrange("p (c k) -> p c k", k=T)
                            ta, tbb = tas[nrot[0] % 4], tbs[nrot[0] % 4]
                            nrot[0] += 1
                            S.op("dve", lambda h: h.tensor_tensor(v3(ta[:]), v3(PR[:]), cb, ALU.mult), [PR, Ere], [ta])
                            S.op("dve", lambda h: h.tensor_tensor(v3(tbb[:]), v3(PI[:]), sb_, ALU.mult), [PI, Eim], [tbb])
                            S.op("pool", lambda h: h.tensor_tensor(xre[:, c0:c0 + 512], ta[:], tbb[:], ALU.add), [ta, tbb], [xre])
                            ta, tbb = tas[nrot[0] % 4], tbs[nrot[0] % 4]
                            nrot[0] += 1
                            S.op("dve", lambda h: h.tensor_tensor(v3(ta[:]), v3(PI[:]), cb, ALU.mult), [PI, Ere], [ta])
                            S.op("dve", lambda h: h.tensor_tensor(v3(tbb[:]), v3(PR[:]), sb_, ALU.mult), [PR, Eim], [tbb])
                            S.op("pool", lambda h: h.tensor_tensor(xim[:, c0:c0 + 512], ta[:], tbb[:], ALU.subtract), [ta, tbb], [xim])
                        stage(cx, 4)
                        S.op("dve", lambda h: h.tensor_tensor_scan(vre[:], rmask[:], xre[:], 0.0, ALU.mult, ALU.add), [rmask, xre], [vre])
                        S.op("dve", lambda h: h.tensor_tensor_scan(vim[:], rmask[:], xim[:], 0.0, ALU.mult, ALU.add), [rmask, xim], [vim])
                        stage(cx, 5)
                        LV = [lv]
                        vr3 = vre[:].rearrange("p (c k) -> p c k", k=T)
                        vi3 = vim[:].rearrange("p (c k) -> p c k", k=T)
                        S.op("dve", lambda h: h.tensor_copy(lv[:, 0, 0:NC], vr3[:, :, T - 1]), [vre], LV)
                        S.op("dve", lambda h: h.tensor_copy(lv[:, 1, 0:NC], vi3[:, :, T - 1]), [vim], LV)
                        er, ei = Ere[:, k, T - 1:T], Eim[:, k, T - 1:T]
                        S.op("dve", lambda h: h.tensor_scalar(lv[:, 10, 0:NC], lv[:, 1, 0:NC], ei, None, ALU.mult), LV + [Eim], LV)
                        S.op("dve", lambda h: h.scalar_tensor_tensor(lv[:, 2, 0:NC], lv[:, 0, 0:NC], er, lv[:, 10, 0:NC], ALU.mult, ALU.subtract), LV + [Ere], LV)
                        S.op("dve", lambda h: h.tensor_scalar(lv[:, 10, 0:NC], lv[:, 0, 0:NC], ei, None, ALU.mult), LV + [Eim], LV)
                        S.op("dve", lambda h: h.scalar_tensor_tensor(lv[:, 3, 0:NC], lv[:, 1, 0:NC], er, lv[:, 10, 0:NC], ALU.mult, ALU.add), LV + [Ere], LV)
                        cmul(S, "dve", lv[:, 4, 0:NC], lv[:, 5, 0:NC], lv[:, 2, 0:NC], lv[:, 3, 0:NC], E2re[:, k, 0:NC], E2im[:, k, 0:NC],
                             lv[:, 10, 0:NC], lv[:, 11, 0:NC], LV + [E2re, E2im], LV, conj_b=True)
                        S.op("pool", lambda h: h.tensor_copy(R2m[:], R2[:, k, 1:2].broadcast_to([128, NC])), [R2], [R2m])
                        S.op("dve", lambda h: h.tensor_tensor_scan(lv[:, 6, 0:NC], R2m[:], lv[:, 4, 0:NC], 0.0, ALU.mult, ALU.add), [R2m] + LV, LV)
                        S.op("dve", lambda h: h.tensor_tensor_scan(lv[:, 7, 0:NC], R2m[:], lv[:, 5, 0:NC], 0.0, ALU.mult, ALU.add), [R2m] + LV, LV)
                        cmul(S, "dve", lv[:, 8, 1:NCT], lv[:, 9, 1:NCT], lv[:, 6, 0:NC], lv[:, 7, 0:NC], E2re[:, k, 0:NC], E2im[:, k, 0:NC],
                             lv[:, 10, 0:NC], lv[:, 11, 0:NC], LV + [E2re, E2im], LV)
                        S.op("dve", lambda h: h.tensor_copy(lv[:, 8, 0:1], Send[:, 0, k:k + 1]), [Send], LV)
                        S.op("dve", lambda h: h.tensor_copy(lv[:, 9, 0:1], Send[:, 1, k:k + 1]), [Send], LV)
                        if seg > 0:
                            S.op("dve", lambda h: h.tensor_tensor(lv[:, 4, 0:NC], R2[:, k, 1:NCT], E2re[:, k, 1:NCT], ALU.mult), [R2, E2re], LV)
                            S.op("dve", lambda h: h.tensor_tensor(lv[:, 5, 0:NC], R2[:, k, 1:NCT], E2im[:, k, 1:NCT], ALU.mult), [R2, E2im], LV)
                            sr, si = Send[:, 0, k:k + 1], Send[:, 1, k:k + 1]
                            S.op("dve", lambda h: h.scalar_tensor_tensor(lv[:, 8, 1:NCT], lv[:, 4, 0:NC], sr, lv[:, 8, 1:NCT], ALU.mult, ALU.add), LV + [Send], LV)
                            S.op("dve", lambda h: h.tensor_scalar(lv[:, 10, 0:NC], lv[:, 5, 0:NC], si, None, ALU.mult), LV + [Send], LV)
                            S.op("dve", lambda h: h.tensor_tensor(lv[:, 8, 1:NCT], lv[:, 8, 1:NCT], lv[:, 10, 0:NC], ALU.subtract), LV, LV)
                            S.op("dve", lambda h: h.scalar_tensor_tensor(lv[:, 9, 1:NCT], lv[:, 5, 0:NC], sr, lv[:, 9, 1:NCT], ALU.mult, ALU.add), LV + [Send], LV)
                            S.op("dve", lambda h: h.tensor_scalar(lv[:, 10, 0:NC], lv[:, 4, 0:NC], si, None, ALU.mult), LV + [Send], LV)
                            S.op("dve", lambda h: h.tensor_tensor(lv[:, 9, 1:NCT], lv[:, 9, 1:NCT], lv[:, 10, 0:NC], ALU.add), LV, LV)
                        S.op("dve", lambda h: h.tensor_copy(Send[:, 0, k:k + 1], lv[:, 8, NC:NCT]), LV, [Send])
                        S.op("dve", lambda h: h.tensor_copy(Send[:, 1, k:k + 1], lv[:, 9, NC:NCT]), LV, [Send])
                        S.op("pool", lambda h: h.tensor_copy(Sp[0][:], lv[:, 8, 0:NC]), LV, [Sp[0]])
                        S.op("pool", lambda h: h.tensor_copy(Sp[1][:], lv[:, 9, 0:NC]), LV, [Sp[1]])
                        stage(cx, 6)
                        cr = ccre[:, k, :].unsqueeze(1).broadcast_to([128, T + 1, 32])
                        ci = ccim[:, k, :].unsqueeze(1).broadcast_to([128, T + 1, 32])
                        ctabf = ctabs[j % 2]
                        hs = slice(32 * (j % 2), 32 * (j % 2) + 32)
                        CT = ctabf + [ct1, ct2]
                        e_r = Ere[:, k, 0:T + 1].unsqueeze(2).broadcast_to([128, T + 1, 32])
                        e_i = Eim[:, k, 0:T + 1].unsqueeze(2).broadcast_to([128, T + 1, 32])
                        e_ni = Enim[:, k, 0:T + 1].unsqueeze(2).broadcast_to([128, T + 1, 32])
                        rb = Rk[:, k, 1:T + 1].unsqueeze(2).broadcast_to([128, T, 32])
                        S.op("dve", lambda h: h.tensor_tensor(ct1[:], cr, e_r, ALU.mult), [ccre, Ere], CT)
                        S.op("dve", lambda h: h.tensor_tensor(ct2[:], ci, e_i, ALU.mult), [ccim, Eim], CT)
                        S.op("dve", lambda h: h.tensor_tensor(ctabf[0][:, :, hs], ct1[:], ct2[:], ALU.subtract), CT, CT)
                        S.op("dve", lambda h: h.tensor_tensor(ct1[:], cr, e_ni, ALU.mult), [ccre, Enim], CT)
                        S.op("dve", lambda h: h.tensor_tensor(ct2[:], ci, e_r, ALU.mult), [ccim, Ere], CT)
                        S.op("dve", lambda h: h.tensor_tensor(ctabf[1][:, :, hs], ct1[:], ct2[:], ALU.subtract), CT, CT)
                        S.op("dve", lambda h: h.tensor_tensor(ctabf[2][:, 0:T, hs], ctabf[0][:, 1:T + 1, hs], rb, ALU.mult), CT + [Rk], CT)
                        S.op("dve", lambda h: h.tensor_tensor(ctabf[3][:, 0:T, hs], ctabf[1][:, 1:T + 1, hs], rb, ALU.mult), CT + [Rk], CT)

                    def back(k):
                        q, j = k // 4, k % 4
                        vre, vim, Sp = vre2[k % 2], vim2[k % 2], Sp2[k % 2]
                        ctabf = ctabs[j % 2]
                        npy = npyc[0]
                        vrb = vre[:].rearrange("p (c k) -> p k c", k=T)
                        vib = vim[:].rearrange("p (c k) -> p k c", k=T)
                        jj = j // 2
                        y3 = y[q][64 * jj:64 * jj + 64, :].rearrange("p (c k) -> p k c", k=T)
                        for kb in range(T // 4):
                            PY = psy[npy % 2]
                            npy += 1

                            def ymm(h, kb=kb, PY=PY):
                                for kk in range(4):
                                    kx = kb * 4 + kk
                                    o = PY[64 * jj:64 * jj + 64, kk, 0:NC]
                                    h.matmul(o, ctabf[0][:, kx, :], vrb[:, kx, :], start=True, stop=False)
                                    h.matmul(o, ctabf[1][:, kx, :], vib[:, kx, :], start=False, stop=False)
                                    h.matmul(o, ctabf[2][:, kx, :], Sp[0][:], start=False, stop=False)
                                    ins = h.matmul(o, ctabf[3][:, kx, :], Sp[1][:], start=False, stop=True)
                                return ins
                            S.op("pe", ymm, ctabf + [vre, vim] + Sp, [PY])
                            if j % 2 == 0:
                                S.op("act", lambda h: h.copy(y3[:, kb * 4:(kb + 1) * 4, :], PY[64 * jj:64 * jj + 64, :, 0:NC]), [PY], [y[q]])
                            else:
                                S.op("dve", lambda h: h.tensor_tensor(y3[:, kb * 4:(kb + 1) * 4, :], PY[64 * jj:64 * jj + 64, :, 0:NC],
                                                                      y3[:, kb * 4:(kb + 1) * 4, :], ALU.add), [PY, y[q]], [y[q]])
                        npyc[0] = npy

                    front_mid(0)
                    for k in range(16):
                        if k + 1 < 16:
                            front_mid(k + 1)
                        back(k)
                    S.barrier()
                stage(cx, 8)
                with ExitStack() as es5:
                    sut = [cx.sb(es5, f"s5_tsut{i}", [128, 4, 512], F32) for i in range(2)]
                    yy = [cx.sb(es5, f"s5_yy{q}", [128, 512], F32) for q in range(4)]
                    w1s = [cx.sb(es5, f"s5_w1_{q}", [128, 512], F32) for q in range(4)]
                    w2s = [cx.sb(es5, f"s5_w2_{q}", [128, 512], F32) for q in range(4)]
                    ygb = [cx.sb(es5, f"s5_ygb{q}", [128, 512], BF16) for q in range(4)]
                    og = [cx.sb(es5, f"s5_og{i}", [128, 512], F32) for i in range(4)]
                    g5 = [cx.sb(es5, f"s5_g5{i}", [128, 512], F32) for i in range(2)]
                    yo = [cx.sb(es5, f"s5_yo{i}", [128, 512], F32) for i in range(2)]
                    psT = [cx.ps(es5, f"s5_psT{q}", [128, 512], F32) for q in range(4)]
                    psG = [cx.ps(es5, f"s5_psG{i}", [128, 512], F32) for i in range(2)]
                    psO = [cx.ps(es5, f"s5_psO{i}", [128, 512], F32) for i in range(2)]
                    for blk in range(SEG // 512):
                        c0 = blk * 512
                        SU = sut[blk % 2]
                        for tt in range(4):
                            r0 = t00 + c0 + tt * 128
                            S.dma("sp", SU[:, tt, :], proj_dt.ap[r0:r0 + 128, C_SU:C_SU + 512],
                                  reads=proj_dt.bufs(r0, r0 + 128), writes=[SU])
                        for q in range(4):
                            def tq(h, q=q):
                                for tt in range(4):
                                    ins = h.transpose(psT[q][:, tt * 128:(tt + 1) * 128], SU[:, tt, q * 128:(q + 1) * 128], ident_f[:])
                                return ins
                            S.op("pe", tq, [SU, ident_f], [psT[q]])
                            S.op("dve", lambda h: h.scalar_tensor_tensor(yy[q][:], psT[q][:], dfm[:, q:q + 1], y[q][:, c0:c0 + 512],
                                                                         ALU.mult, ALU.add), [psT[q], dfm, y[q]], [yy[q]])
                            w1, w2 = w1s[q], w2s[q]
                            S.op("act", lambda h: h.activation(out=w1[:], in_=yy[q][:], func=AF.Square), [yy[q]], [w1])
                            S.op("dve", lambda h: h.tensor_scalar(w1[:], w1[:], 0.044715, 1.0, ALU.mult, ALU.add), [w1], [w1])
                            S.op("dve", lambda h: h.tensor_tensor(w1[:], w1[:], yy[q][:], ALU.mult), [w1, yy[q]], [w1])
                            S.op("act", lambda h: h.activation(out=w2[:], in_=w1[:], func=AF.Sigmoid, scale=1.5957691216057308), [w1], [w2])
                            S.op("dve", lambda h: h.tensor_tensor(yy[q][:], yy[q][:], w2[:], ALU.mult), [yy[q], w2], [yy[q]])
                            S.op("pool", lambda h: h.tensor_copy(ygb[q][:], yy[q][:]), [yy[q]], [ygb[q]])
                        for nt in range(4):
                            def glu(h, nt=nt):
                                for q in range(4):
                                    ins = h.matmul(psG[nt % 2][:], Wg[:, q, nt * 128:(nt + 1) * 128], ygb[q][:], start=(q == 0), stop=(q == 3))
                                return ins
                            S.op("pe", glu, [Wg] + ygb, [psG[nt % 2]])
                            S.op("act", lambda h: h.activation(out=og[nt][:], in_=psG[nt % 2][:], func=AF.Sigmoid, bias=glub[:, nt:nt + 1]),
                                 [psG[nt % 2], glub], [og[nt]])
                            S.op("dve", lambda h: h.tensor_tensor(og[nt][:], og[nt][:], yy[nt][:], ALU.mult), [og[nt], yy[nt]], [og[nt]])
                        for tt in range(4):
                            i2 = tt % 2
                            r0 = t00 + c0 + tt * 128
                            S.dma("sp", g5[i2][:], proj_dt.ap[r0:r0 + 128, C_S5G:C_S5G + 512], reads=proj_dt.bufs(r0, r0 + 128), writes=[g5[i2]])
                            S.op("act", lambda h: h.activation(out=g5[i2][:], in_=g5[i2][:], func=AF.Silu), [g5[i2]], [g5[i2]])

                            def tro(h, tt=tt, i2=i2):
                                for nt in range(4):
                                    ins = h.transpose(psO[i2][:, nt * 128:(nt + 1) * 128], og[nt][:, tt * 128:(tt + 1) * 128], ident_f[:])
                                return ins
                            S.op("pe", tro, og + [ident_f], [psO[i2]])
                            S.op("dve", lambda h: h.tensor_tensor(yo[i2][:], psO[i2][:, 0:512], g5[i2][:], ALU.mult), [psO[i2], g5[i2]], [yo[i2]])
                            S.dma("pool", mix_dt.ap[r0:r0 + 128, 768:1024], yo[i2][:, 0:256], reads=[yo[i2]], writes=mix_dt.bufs(r0, r0 + 128))
                            S.dma("pool", mix_dt.ap[r0:r0 + 128, 1024 + 768:2048], yo[i2][:, 256:512], reads=[yo[i2]], writes=mix_dt.bufs(r0, r0 + 128))
                    S.barrier()


def s5_layouts(a_re, a_im, log_dt, b_re, b_im, c_re, c_im, d, glu_w, glu_b):
    G = np.arange(32).reshape(16, 2)
    f = np.float32
    are = a_re[G].transpose(1, 2, 0).reshape(128, 16).astype(f)
    aim = a_im[G].transpose(1, 2, 0).reshape(128, 16).astype(f)
    ldt = np.broadcast_to(log_dt[G].transpose(1, 0)[:, None, :], (2, 64, 16)).reshape(128, 16).astype(f)
    ccre = np.zeros((2, 64, 16, 2, 16), f); ccim = np.zeros((2, 64, 16, 2, 16), f)
    bpre = np.zeros((2, 64, 16, 4, 2, 16), f); bpim = np.zeros((2, 64, 16, 4, 2, 16), f)
    for k in range(16):
        for g2 in range(2):
            g = G[k, g2]
            ccre[g2, :, k, g2, :] = c_re[g].T
            ccim[g2, :, k, g2, :] = c_im[g].T
            bpre[g2, :, k, k % 4, g2, :] = b_re[g]
            bpim[g2, :, k, k % 4, g2, :] = b_im[g]
    return dict(are=are, aim=aim, ldt=ldt, ccre=ccre.reshape(128, 16, 32), ccim=ccim.reshape(128, 16, 32),
                bpre=bpre.reshape(128, 16, 128), bpim=bpim.reshape(128, 16, 128),
                dfm=np.ascontiguousarray(d.reshape(4, 128).T).astype(f),
                glub=np.ascontiguousarray(glu_b.reshape(4, 128).T).astype(f),
                gluw=np.ascontiguousarray(glu_w).astype(f))


def bc128(a):
    a = np.asarray(a, np.float32)
    return np.ascontiguousarray(np.broadcast_to(a[None], (128,) + a.shape))


def static_consts():
    f = np.float32
    pos = np.arange(L, dtype=f)
    inv = (1.0 / (np.float32(10000.0) ** (np.arange(0, 64, 2, dtype=f) / np.float32(64)))).astype(f)
    ang = (pos[:, None] * inv[None, :]).astype(f)
    cos = np.cos(ang).astype(f); sin = np.sin(ang).astype(f)
    k = np.arange(128)[:, None]; q = np.arange(128)[None, :]
    mown = np.where(k <= q, 0.0, -30000.0).astype(f)
    mprev = np.where(k > q, 0.0, -30000.0).astype(f)
    return dict(
        ident=np.eye(128).astype(ml_dtypes.bfloat16), identf=np.eye(128, dtype=f),
        cosT=np.ascontiguousarray(cos.reshape(NT, 128, 32).transpose(1, 0, 2)),
        sinT=np.ascontiguousarray(sin.reshape(NT, 128, 32).transpose(1, 0, 2)),
        mown=np.ascontiguousarray(np.tile(mown[:, None, :], (1, 4, 1))),
        mprev=np.ascontiguousarray(np.tile(mprev[:, None, :], (1, 4, 1))),
        gmask=bc128(np.where(np.arange(16)[None, :] < np.arange(16)[:, None], 0.0, -1e30).astype(f)),
        blkind=(np.arange(L)[None, :] // 256 == np.arange(16)[:, None]).astype(ml_dtypes.bfloat16),
        tri=(k <= q).astype(f), ones=np.ones((128, 128), f), maskT=mown.copy(),
        kvec=bc128(np.arange(256, dtype=f)),
    )


CONST_SHAPES = dict(ident=([128, 128], BF16), identf=([128, 128], F32), cosT=([128, NT, 32], F32), sinT=([128, NT, 32], F32),
                    mown=([128, 4, 128], F32), mprev=([128, 4, 128], F32), gmask=([128, 16, 16], F32), blkind=([16, L], BF16),
                    tri=([128, 128], F32), ones=([128, 128], F32), maskT=([128, 128], F32), kvec=([128, 256], F32))
HALF_SHAPES = dict(sinks=[128, 4], convw=[128, 4, 512], convb=[128, 512], dtb=[128, 4], alog=[128, 4],
                   dskip=[128, 256], normw=[128, 256])
LAYER_SHAPES = dict(pre_g=[128, 16], w_out=[D, D], post_g=[128, D],
                    are=[128, 16], aim=[128, 16], ldt=[128, 16], ccre=[128, 16, 32], ccim=[128, 16, 32],
                    bpre=[128, 16, 128], bpim=[128, 16, 128], dfm=[128, 4], glub=[128, 4], gluw=[512, 512])


def in_cols(jh):
    r = lambda a, n: list(range(a, a + n))
    c = (r(0 + jh * 256, 256) + r(512 + jh * 256, 256) + r(1024 + jh * 256, 256) + r(1536 + jh * 256, 256)
         + r(3592 + jh * 256, 256) + r(4104 + jh * 64, 64) + r(4232 + jh * 64, 64) + r(4360 + jh * 256, 256)
         + r(2048 + jh * 256, 256) + r(2560 + jh * 128, 128) + r(2816 + jh * 128, 128) + r(3080 + jh * 256, 256)
         + r(3072 + jh * 4, 4))
    if jh == 0:
        c = c + r(4872, 512) + r(5384, 512)
    return np.array(c)


def half_inputs(inp, l, jh):
    cols = in_cols(jh)
    assert len(cols) == (NP if jh == 0 else NPH)
    cch = np.array(list(range(jh * 256, jh * 256 + 256)) + list(range(512 + jh * 128, 512 + jh * 128 + 128))
                   + list(range(768 + jh * 128, 768 + jh * 128 + 128)))
    return dict(
        w_in=np.ascontiguousarray(inp["w_in"][l][:, cols]),
        sinks=bc128(inp["swa_sinks"][l][4 * jh:4 * jh + 4]),
        convw=bc128(inp["ssd_conv_w"][l][:, cch]), convb=bc128(inp["ssd_conv_b"][l][cch]),
        dtb=bc128(inp["ssd_dt_bias"][l][4 * jh:4 * jh + 4]), alog=bc128(inp["ssd_a_log"][l][4 * jh:4 * jh + 4]),
        dskip=bc128(np.repeat(inp["ssd_d"][l][4 * jh:4 * jh + 4], 64)), normw=bc128(inp["ssd_norm"][l][jh * 256:jh * 256 + 256]),
    )


WOUT_ROWS = np.array([b + jh * 256 + i for jh in range(2) for b in (0, 512, 1024, 1536) for i in range(256)])


def layer_inputs(inp, l):
    d = s5_layouts(inp["s5_a_re"][l], inp["s5_a_im"][l], inp["s5_log_dt"][l], inp["s5_b_re"][l], inp["s5_b_im"][l],
                   inp["s5_c_re"][l], inp["s5_c_im"][l], inp["s5_d"][l], inp["s5_glu_w"][l], inp["s5_glu_b"][l])
    d.update(pre_g=np.ascontiguousarray(inp["pre_norm"][l].reshape(16, 128).T),
             w_out=np.ascontiguousarray(inp["w_out"][l][WOUT_ROWS]), post_g=bc128(inp["post_norm"][l]))
    return d


def load_consts(cx, es, cap):
    S = cx.S
    C = {}
    for nm, key in (("ident", "ident"), ("identf", "identf"), ("cos", "cosT"), ("sin", "sinT")):
        shp, dt = CONST_SHAPES[key]
        C[nm] = cx.sb(es, "c_" + nm, shp, dt)
        S.dma("sp", C[nm][:], cap[key], writes=[C[nm]])
    for key in ("mprev", "mown", "gmask", "blkind", "tri", "ones", "maskT", "kvec", "identf"):
        C["ap_" + key] = cap[key]
    return C


def build_fused(depth=DEPTH):
    nc = bass.Bass("TRN2", target_bir_lowering=False)
    cx = Ctx(nc)
    cx.L = L
    A = lambda n, s, d=F32: nc.dram_tensor(n, list(s), d, kind="ExternalInput").ap()
    x_in = DramT(nc, "x", [L, D], F32, kind="ExternalInput")
    cap = {k: A("k_" + k, s, d) for k, (s, d) in CONST_SHAPES.items()}
    Lw = [{k: A(f"l{l}_{k}", s) for k, s in LAYER_SHAPES.items()} for l in range(depth)]
    Hw = [[dict({k: A(f"l{l}h{jh}_{k}", s) for k, s in HALF_SHAPES.items()},
                w_in=A(f"l{l}h{jh}_w_in", [D, NP if jh == 0 else NPH])) for jh in range(2)] for l in range(depth)]
    proj = DramT(nc, "proj", [L, NP], F32)
    mixc = DramT(nc, "mixc", [L, D], F32)
    xbuf = [DramT(nc, f"xbuf{i}", [L, D], F32) for i in range(2)]
    out = DramT(nc, "out", [L, D], F32, kind="ExternalOutput")
    with ExitStack() as es:
        C = load_consts(cx, es, cap)
        x_cur = x_in
        for l in range(depth):
            x_next = out if l == depth - 1 else xbuf[l % 2]
            for jh in range(2):
                H = Hw[l][jh]
                mcol = jh * MIXH
                phase_inproj(cx, x_cur, H["w_in"], Lw[l]["pre_g"], proj, C["ident"], npc=(NP if jh == 0 else NPH))
                phase_swa(cx, proj, mixc, C["cos"], C["sin"], C["ident"], C["ap_mprev"], C["ap_mown"], H["sinks"], mcol=mcol)
                phase_moba(cx, proj, mixc, C["cos"], C["sin"], C["ident"], C["identf"], C["ap_mown"], C["ap_gmask"],
                           C["ap_blkind"], mcol=mcol)
                ssd_c = {k: H[k] for k in ("convw", "convb", "dtb", "alog", "dskip", "normw")}
                ssd_c.update(tri=C["ap_tri"], ones=C["ap_ones"], maskT=C["ap_maskT"], identf=C["ap_identf"])
                phase_ssd(cx, proj, mixc, C["ident"], ssd_c, mcol=mcol)
                if jh == 0:
                    s5_c = {k: Lw[l][k] for k in ("are", "aim", "ldt", "ccre", "ccim", "bpre", "bpim", "dfm", "glub", "gluw")}
                    s5_c["kvec"] = C["ap_kvec"]
                    phase_s5(cx, proj, mixc, C["ident"], C["identf"], s5_c)
            phase_outproj(cx, mixc, x_cur, Lw[l]["w_out"], Lw[l]["post_g"], x_next, C["ident"], NT)
            x_cur = x_next
        cx.S.finish(out.tiles)
    cx.n_ins = cx.S.n_ins
    return nc


def kernel(**inp):
    inp = {k: np.asarray(v) for k, v in inp.items()}
    x = np.ascontiguousarray(inp["x"], dtype=np.float32)
    nc = build_fused()
    shared = {"k_" + k: v for k, v in static_consts().items()}
    for l in range(DEPTH):
        shared.update({f"l{l}_{k}": v for k, v in layer_inputs(inp, l).items()})
        for jh in range(2):
            shared.update({f"l{l}h{jh}_{k}": v for k, v in half_inputs(inp, l, jh).items()})
    in_maps = []
    for b in range(4):
        m = dict(shared)
        m["x"] = x[b]
        in_maps.append(m)
    res = run_bass_kernel_spmd(nc, in_maps, core_ids=list(range(4)))
    return np.stack([res.results[b]["out"] for b in range(4)]).astype(np.float32)
```

```python
from contextlib import ExitStack
import numpy as np
import ml_dtypes
import concourse.bass as bass
import concourse.mybir as mybir
from concourse.bass_utils import run_bass_kernel_spmd

F32 = mybir.dt.float32
BF16 = mybir.dt.bfloat16
ALU = mybir.AluOpType
AF = mybir.ActivationFunctionType
AX = mybir.AxisListType

D = 2048
L = 4096
NT = L // 128
DEPTH = 4
EPS = 1e-6
C_MQ, C_MK, C_MV, C_MG = 0, 256, 512, 768
C_SQ, C_SK, C_SV, C_SG = 1024, 1280, 1344, 1408
C_XS, C_BM, C_CM, C_Z = 1664, 1920, 2048, 2176
C_DT, C_SU, C_S5G = 2432, 2436, 2948
NP = 3460
NPH = 2436
MIXH = 1024


class Buf:
    __slots__ = ("t", "last_w", "readers")

    def __init__(self, t):
        self.t = t
        self.last_w = None
        self.readers = {}

    def __getitem__(self, k):
        return self.t[k]


class DramT:
    def __init__(self, nc, name, shape, dt, kind="Internal"):
        self.ap = nc.dram_tensor(name, list(shape), dt, kind=kind).ap()
        self.tiles = [Buf(None) for _ in range((shape[0] + 127) // 128)]

    def bufs(self, r0, r1):
        return self.tiles[r0 // 128:(r1 + 127) // 128]


class Eng:
    def __init__(self, key, h, sem):
        self.key, self.h, self.sem = key, h, sem
        self.cnt = 0
        self.seen = {}


class Sched:
    NSLOT = 8

    def __init__(self, nc):
        self.nc = nc
        self.engs = {}
        for key, h in (("pe", nc.tensor), ("act", nc.scalar), ("dve", nc.vector),
                       ("pool", nc.gpsimd), ("sp", nc.sync)):
            self.engs[key] = Eng(key, h, nc.alloc_semaphore(name=f"prog_{key}"))
        self.dma_sems = {}
        self.dma_rings = {}
        self.n_ins = 0
        self.muted = False
        self.same_engine_raw = True

    def _deps(self, reads, writes):
        deps = {}
        for b in reads:
            if b.last_w is not None:
                k, i = b.last_w
                if deps.get(k, 0) < i:
                    deps[k] = i
        for b in writes:
            if b.last_w is not None:
                k, i = b.last_w
                if deps.get(k, 0) < i:
                    deps[k] = i
            for k, i in b.readers.items():
                if deps.get(k, 0) < i:
                    deps[k] = i
        return deps

    def _emit_waits(self, e, deps, same_ok=True):
        for k, i in deps.items():
            if k == e.key and same_ok:
                continue
            if e.seen.get(k, 0) >= i:
                continue
            sem = self.dma_sems[k] if k.startswith("dma") else self.engs[k].sem
            e.h.wait_ge(sem, i)
            e.seen[k] = i
            self.n_ins += 1

    def _record(self, key, idx, reads, writes):
        for b in reads:
            if b.readers.get(key, 0) < idx:
                b.readers[key] = idx
        for b in writes:
            b.last_w = (key, idx)
            b.readers = {}

    def op(self, ek, fn, reads=(), writes=()):
        if self.muted:
            return None
        e = self.engs[ek]
        self._emit_waits(e, self._deps(reads, writes))
        own = 0
        if self.same_engine_raw and ek != "pe":
            for b in reads:
                if b.last_w is not None and b.last_w[0] == ek and b.last_w[1] > own:
                    own = b.last_w[1]
            for b in writes:
                if b.last_w is not None and b.last_w[0] == ek and b.last_w[1] > own:
                    own = b.last_w[1]
                r = b.readers.get(ek, 0)
                if r > own:
                    own = r
        if own > e.seen.get(ek, 0):
            e.h.wait_ge(e.sem, own)
            e.seen[ek] = own
            self.n_ins += 1
        ins = fn(e.h)
        e.cnt += 1
        ins.then_inc(e.sem, 1)
        self._record(ek, e.cnt, reads, writes)
        self.n_ins += 1
        return ins

    def dma(self, ek, out, in_, reads=(), writes=(), **kw):
        if self.muted:
            return None
        e = self.engs[ek]
        deps = self._deps(reads, writes)
        ring = self.dma_rings.setdefault(ek, {"next": 0, "cnt": [0] * self.NSLOT})
        slot = ring["next"]
        ring["next"] = (slot + 1) % self.NSLOT
        qk = f"dma:{ek}:{slot}"
        if qk not in self.dma_sems:
            self.dma_sems[qk] = self.nc.alloc_semaphore(name=f"dma_{ek}_{slot}")
        if ring["cnt"][slot] > 0:
            deps[qk] = max(deps.get(qk, 0), ring["cnt"][slot])
        self._emit_waits(e, deps, same_ok=False)
        ring["cnt"][slot] += 16
        ins = e.h.dma_start(out=out, in_=in_, **kw)
        ins.then_inc(self.dma_sems[qk], 16)
        self._record(qk, ring["cnt"][slot], reads, writes)
        self.n_ins += 1
        return ins

    def barrier(self):
        if self.muted:
            return
        deps = {}
        for k, e in self.engs.items():
            if e.cnt > 0:
                deps[k] = e.cnt
        for ek, ring in self.dma_rings.items():
            for slot, c in enumerate(ring["cnt"]):
                if c > 0:
                    deps[f"dma:{ek}:{slot}"] = c
        for e in self.engs.values():
            self._emit_waits(e, deps)

    def finish(self, bufs):
        self.muted = False
        e = self.engs["sp"]
        self._emit_waits(e, self._deps(bufs, bufs))
        e.h.nop()


class Ctx:
    def __init__(self, nc):
        self.nc = nc
        self.S = Sched(nc)
        self._rr = {}
        self.nt = NT

    def rr(self, key, choices):
        i = self._rr.get(key, 0)
        self._rr[key] = i + 1
        return choices[i % len(choices)]

    def dbg(self, name, buf, shape, dt):
        if not getattr(self, "debug", False):
            return
        o = DramT(self.nc, name, list(shape), dt, kind="ExternalOutput")
        self.S.dma("sp", o.ap, buf[:], reads=[buf], writes=o.tiles)
        self.dbg_outs = getattr(self, "dbg_outs", []) + o.tiles

    def uid(self, name):
        self._uid = getattr(self, "_uid", 0) + 1
        return f"{name}_u{self._uid}"

    def sb(self, es, name, shape, dt):
        return Buf(es.enter_context(self.nc.sbuf_tensor(self.uid(name), list(shape), dt)))

    def ps(self, es, name, shape, dt=F32):
        return Buf(es.enter_context(self.nc.psum_tensor(self.uid(name), list(shape), dt)))


def load_weight_bf16(cx, es, name, w_ap, K, N, scale_sb=None):
    nc, S = cx.nc, cx.S
    KT = K // 128
    Wb = cx.sb(es, name, [128, KT, N], BF16)
    with ExitStack() as es2:
        stg = [cx.sb(es2, f"{name}_stg{i}", [128, N], F32) for i in range(3)]
        for kt in range(KT):
            st = stg[kt % 3]
            S.dma(cx.rr("wq", ["sp", "pool"]), st[:], w_ap[kt * 128:(kt + 1) * 128, :], writes=[st])
            ek = cx.rr("wcast", ["dve", "act"])
            if scale_sb is not None:
                if ek == "dve":
                    S.op("dve", lambda h: h.tensor_scalar(Wb[:, kt, :], st[:], scale_sb[:, kt:kt + 1], None, ALU.mult),
                         [st, scale_sb], [Wb])
                else:
                    S.op("act", lambda h: h.activation(out=Wb[:, kt, :], in_=st[:], func=AF.Copy, scale=scale_sb[:, kt:kt + 1]),
                         [st, scale_sb], [Wb])
            else:
                if ek == "dve":
                    S.op("dve", lambda h: h.tensor_copy(Wb[:, kt, :], st[:]), [st], [Wb])
                else:
                    S.op("act", lambda h: h.copy(Wb[:, kt, :], st[:]), [st], [Wb])
        S.barrier()
    return Wb


def phase_inproj(cx, x_dt, w_ap, g_ap, proj_dt, ident_bf, npc=NP):
    nc, S = cx.nc, cx.S
    KT = D // 128
    with ExitStack() as es:
        g_sb = cx.sb(es, "g_sb", [128, KT], F32)
        S.dma("sp", g_sb[:], g_ap, writes=[g_sb])
        Wb = load_weight_bf16(cx, es, "Win", w_ap, D, npc, g_sb)
        xt = [cx.sb(es, f"xt{i}", [128, D], F32) for i in range(2)]
        xb = [cx.sb(es, f"xb{i}", [128, D], BF16) for i in range(2)]
        junk = cx.sb(es, "junk", [128, D], BF16)
        ss = [cx.sb(es, f"ss{i}", [128, 1], F32) for i in range(2)]
        rstd = [cx.sb(es, f"rstd{i}", [128, 1], F32) for i in range(2)]
        hT = [cx.sb(es, f"hT{i}", [128, KT, 128], BF16) for i in range(2)]
        stage = [cx.sb(es, f"stage{i}", [128, npc], F32) for i in range(2)]
        nch = (npc + 511) // 512
        NACC = 6
        pmb = [cx.ps(es, f"pm{i}", [128, 512], F32) for i in range(min(nch, NACC))]
        pm = [pmb[c % NACC] for c in range(nch)]
        ptl = [cx.ps(es, f"pt{i}", [128, 4, 128], BF16) for i in range(2)]
        def prep(t):
            i2 = t % 2
            X, XB, SS, RS, HT, ST = xt[i2], xb[i2], ss[i2], rstd[i2], hT[i2], stage[i2]
            S.dma("sp", X[:], x_dt.ap[t * 128:(t + 1) * 128, :], reads=x_dt.bufs(t * 128, t * 128 + 128), writes=[X])
            S.op("act", lambda h: h.activation(out=junk[:], in_=X[:], func=AF.Square, accum_out=SS[:]),
                 [X], [junk, SS])
            S.op("pool", lambda h: h.tensor_copy(XB[:], X[:]), [X], [XB])
            S.op("dve", lambda h: h.tensor_scalar(RS[:], SS[:], 1.0 / D, EPS, ALU.mult, ALU.add), [SS], [RS])
            S.op("act", lambda h: h.sqrt(RS[:], RS[:]), [RS], [RS])
            S.op("dve", lambda h: h.reciprocal(RS[:], RS[:]), [RS], [RS])
            for q in range(KT // 4):
                P = ptl[q % 2]
                pv = P[:]

                def tr(h, q=q, pv=pv):
                    for r in range(4):
                        kt = q * 4 + r
                        ins = h.transpose(pv[:, r, :], XB[:, kt * 128:(kt + 1) * 128], ident_bf[:])
                    return ins
                S.op("pe", tr, [XB, ident_bf], [P])
                ek = "act"
                if ek == "dve":
                    S.op("dve", lambda h: h.tensor_copy(HT[:, q * 4:(q + 1) * 4, :], pv), [P], [HT])
                else:
                    S.op("act", lambda h: h.copy(HT[:, q * 4:(q + 1) * 4, :], pv), [P], [HT])

        def compute(t):
            i2 = t % 2
            X, XB, SS, RS, HT, ST = xt[i2], xb[i2], ss[i2], rstd[i2], hT[i2], stage[i2]

            def evac_chunks(cs, ST=ST, RS=RS):
                for c in cs:
                    n0 = c * 512
                    n1 = min(npc, n0 + 512)
                    PM = pm[c]
                    ek = cx.rr("pjev", ["act", "dve"])
                    if ek == "dve":
                        S.op("dve", lambda h: h.tensor_scalar(ST[:, n0:n1], PM[:, 0:n1 - n0], RS[:, 0:1], None, ALU.mult),
                             [PM, RS], [ST])
                    else:
                        S.op("act", lambda h: h.activation(out=ST[:, n0:n1], in_=PM[:, 0:n1 - n0], func=AF.Copy,
                                                           scale=RS[:, 0:1]), [PM, RS], [ST])

            for g0 in range(0, nch, NACC):
                cs = list(range(g0, min(nch, g0 + NACC)))

                def mm(h, HT=HT, cs=cs):
                    for kt in range(KT):
                        for c in cs:
                            n0 = c * 512
                            n1 = min(npc, n0 + 512)
                            ins = h.matmul(pm[c][:, 0:n1 - n0], HT[:, kt, :], Wb[:, kt, n0:n1],
                                           start=(kt == 0), stop=(kt == KT - 1))
                    return ins
                S.op("pe", mm, [HT, Wb], [pm[c] for c in cs])
                evac_chunks(cs)
            if t == 0:
                cx.dbg("d_rs", RS, [128, 1], F32)
                cx.dbg("d_ss", SS, [128, 1], F32)
                cx.dbg("d_xb", XB, [128, D], BF16)
                cx.dbg("d_hT", HT, [128, KT, 128], BF16)
                cx.dbg("d_Wb", Wb, [128, KT, npc], BF16)
            S.dma("pool", proj_dt.ap[t * 128:(t + 1) * 128, 0:npc], ST[:], reads=[ST], writes=proj_dt.bufs(t * 128, t * 128 + 128))

        prep(0)
        for t in range(cx.nt):
            if t + 1 < cx.nt:
                prep(t + 1)
            compute(t)
        S.barrier()


def phase_outproj(cx, mix_dt, x_dt, w_ap, gbc_ap, out_dt, ident_bf, ntiles):
    nc, S = cx.nc, cx.S
    KT = D // 128
    with ExitStack() as es:
        Wb = load_weight_bf16(cx, es, "Wout", w_ap, D, D, None)
        gbc = cx.sb(es, "gbc", [128, D], F32)
        S.dma("sp", gbc[:], gbc_ap, writes=[gbc])
        mt = [cx.sb(es, f"mt{i}", [128, D], F32) for i in range(2)]
        mb = [cx.sb(es, f"mb{i}", [128, D], BF16) for i in range(2)]
        xt = [cx.sb(es, f"oxt{i}", [128, D], F32) for i in range(2)]
        mT = [cx.sb(es, f"mT{i}", [128, KT, 128], BF16) for i in range(2)]
        o = [cx.sb(es, f"o{i}", [128, D], F32) for i in range(2)]
        o2 = [cx.sb(es, f"o2{i}", [128, D], F32) for i in range(2)]
        junk = cx.sb(es, "ojunk", [128, D], BF16)
        ss = [cx.sb(es, f"oss{i}", [128, 1], F32) for i in range(2)]
        pt = [cx.ps(es, f"opt{i}", [128, 4, 128], BF16) for i in range(2)]
        pm = [cx.ps(es, f"opm{i}", [128, 512], F32) for i in range(4)]
        def prep(t):
            i2 = t % 2
            M, MB, X, MT, O, O2, SS = mt[i2], mb[i2], xt[i2], mT[i2], o[i2], o2[i2], ss[i2]
            r0, r1 = t * 128, (t + 1) * 128
            S.dma("sp", M[:], mix_dt.ap[r0:r1, :], reads=mix_dt.bufs(r0, r1), writes=[M])
            S.dma("sp", X[:], x_dt.ap[r0:r1, :], reads=x_dt.bufs(r0, r1), writes=[X])
            S.op("act", lambda h: h.copy(MB[:], M[:]), [M], [MB])
            for q in range(KT // 4):
                P = pt[q % 2]

                def tr(h, q=q, P=P):
                    for r in range(4):
                        kt = q * 4 + r
                        ins = h.transpose(P[:, r, :], MB[:, kt * 128:(kt + 1) * 128], ident_bf[:])
                    return ins
                S.op("pe", tr, [MB, ident_bf], [P])
                if q % 2 == 0:
                    S.op("dve", lambda h: h.tensor_copy(MT[:, q * 4:(q + 1) * 4, :], P[:]), [P], [MT])
                else:
                    S.op("act", lambda h: h.copy(MT[:, q * 4:(q + 1) * 4, :], P[:]), [P], [MT])
        def compute(t):
            i2 = t % 2
            M, MB, X, MT, O, O2, SS = mt[i2], mb[i2], xt[i2], mT[i2], o[i2], o2[i2], ss[i2]
            r0, r1 = t * 128, (t + 1) * 128

            def mm(h, MT=MT):
                for kt in range(KT):
                    for c in range(4):
                        ins = h.matmul(pm[c][:, :], MT[:, kt, :], Wb[:, kt, c * 512:(c + 1) * 512],
                                       start=(kt == 0), stop=(kt == KT - 1))
                return ins
            S.op("pe", mm, [MT, Wb], pm)
            for c in range(4):
                n0, n1 = c * 512, (c + 1) * 512
                PM = pm[c]
                if c % 2 == 0:
                    S.op("act", lambda h: h.copy(O[:, n0:n1], PM[:, :]), [PM], [O])
                else:
                    S.op("dve", lambda h: h.tensor_copy(O[:, n0:n1], PM[:, :]), [PM], [O])
            S.op("act", lambda h: h.activation(out=junk[:], in_=O[:], func=AF.Square, accum_out=SS[:]), [O], [junk, SS])
            S.op("dve", lambda h: h.tensor_scalar(SS[:], SS[:], 1.0 / D, EPS, ALU.mult, ALU.add), [SS], [SS])
            S.op("act", lambda h: h.sqrt(SS[:], SS[:]), [SS], [SS])
            S.op("dve", lambda h: h.reciprocal(SS[:], SS[:]), [SS], [SS])
            S.op("dve", lambda h: h.scalar_tensor_tensor(O2[:], O[:], SS[:, 0:1], gbc[:], ALU.mult, ALU.mult),
                 [O, SS, gbc], [O2])
            if t == 1:
                cx.dbg("d_o", O, [128, D], F32)
                cx.dbg("d_rs", SS, [128, 1], F32)
                cx.dbg("d_o2", O2, [128, D], F32)
            S.op("dve", lambda h: h.tensor_tensor(O2[:], O2[:], X[:], ALU.add), [O2, X], [O2])
            S.dma("pool", out_dt.ap[r0:r1, :], O2[:], reads=[O2], writes=out_dt.bufs(r0, r1))

        prep(0)
        for t in range(ntiles):
            if t + 1 < ntiles:
                prep(t + 1)
            compute(t)
        S.barrier()


def rope_tiles(cx, S, dst_bf, src, nh, cosb, sinb, tmp):
    s4 = src.rearrange("p (h two d) -> p h two d", two=2, d=32)
    d4 = dst_bf.rearrange("p (h two d) -> p h two d", two=2, d=32)
    x1, x2 = s4[:, :, 0, :], s4[:, :, 1, :]
    cb = cosb.unsqueeze(1).broadcast_to([128, nh, 32])
    sb_ = sinb.unsqueeze(1).broadcast_to([128, nh, 32])
    tv = tmp[:].rearrange("p f (h d) -> p f h d", d=32)
    return x1, x2, cb, sb_, tv, d4


def phase_swa(cx, proj_dt, mix_dt, cos_sb, sin_sb, ident_bf, mask_prev_ap, mask_own_ap, sinks_ap, mcol=0):
    nc, S = cx.nc, cx.S
    Lc, NTc = cx.L, cx.L // 128
    with ExitStack() as es:
        QKT = cx.sb(es, "swa_QKT", [64, 5, Lc], BF16)
        Vaug = cx.sb(es, "swa_V", [128, NTc, 65], BF16)
        mprev = cx.sb(es, "swa_mprev", [128, 4, 128], F32)
        mown = cx.sb(es, "swa_mown", [128, 4, 128], F32)
        esink = cx.sb(es, "swa_esink", [128, 4], F32)
        S.dma("sp", mprev[:], mask_prev_ap, writes=[mprev])
        S.dma("sp", mown[:], mask_own_ap, writes=[mown])
        S.dma("sp", esink[:], sinks_ap, writes=[esink])
        S.op("act", lambda h: h.activation(out=esink[:], in_=esink[:], func=AF.Exp), [esink], [esink])
        S.op("pool", lambda h: h.memset(Vaug[:, :, 64:65], 1.0), [], [Vaug])
        tin = [cx.sb(es, f"swa_tin{i}", [128, 640], F32) for i in range(2)]
        Gs = cx.sb(es, "swa_G", [128, NTc, 256], F32)
        rtmp = [cx.sb(es, f"swa_rtmp{i}", [128, 4, 160], F32) for i in range(2)]
        rb = [cx.sb(es, f"swa_rb{i}", [128, 320], BF16) for i in range(2)]
        ptr = [cx.ps(es, f"swa_ptr{i}", [64, 8, 128], BF16) for i in range(2)]
        for t in range(NTc):
            i2 = t % 2
            T, TMP, RB, PT = tin[i2], rtmp[i2], rb[i2], ptr[i2]
            r0, r1 = t * 128, (t + 1) * 128
            S.dma("sp", T[:], proj_dt.ap[r0:r1, C_SQ:C_SQ + 640],
                  reads=proj_dt.bufs(r0, r1), writes=[T])
            S.op("act", lambda h: h.activation(out=Gs[:, t, :], in_=T[:, 384:640], func=AF.Silu), [T], [Gs])
            x1, x2, cb, sb_, tv, d4 = rope_tiles(cx, S, RB[:], T[:, 0:320], 5, cos_sb[:, t, :], sin_sb[:, t, :], TMP)
            S.op("dve", lambda h: h.tensor_tensor(tv[:, 0], x1, cb, ALU.mult), [T, cos_sb], [TMP])
            S.op("dve", lambda h: h.tensor_tensor(tv[:, 1], x2, sb_, ALU.mult), [T, sin_sb], [TMP])
            S.op("dve", lambda h: h.tensor_tensor(tv[:, 2], x2, cb, ALU.mult), [T, cos_sb], [TMP])
            S.op("dve", lambda h: h.tensor_tensor(tv[:, 3], x1, sb_, ALU.mult), [T, sin_sb], [TMP])
            S.op("dve", lambda h: h.tensor_tensor(d4[:, :, 0, :], tv[:, 0], tv[:, 1], ALU.subtract), [TMP], [RB])
            S.op("dve", lambda h: h.tensor_tensor(d4[:, :, 1, :], tv[:, 2], tv[:, 3], ALU.add), [TMP], [RB])
            S.op("pool", lambda h: h.tensor_copy(Vaug[:, t, 0:64], T[:, 320:384]), [T], [Vaug])

            def tr(h, RB=RB, PT=PT):
                for hh in range(5):
                    ins = h.transpose(PT[:, hh, :], RB[:, hh * 64:(hh + 1) * 64], ident_bf[:])
                return ins
            S.op("pe", tr, [RB, ident_bf], [PT])
            S.op("act", lambda h: h.copy(QKT[:, :, r0:r1], PT[:, 0:5, :]), [PT], [QKT])
        sg = [cx.sb(es, f"swa_sg{i}", [128, 256], F32) for i in range(2)]
        st = [cx.sb(es, f"swa_st{i}", [128, 4, 128], F32) for i in range(2)]
        pT = [cx.sb(es, f"swa_pT{i}", [128, 4, 128], BF16) for i in range(4)]
        den = [cx.sb(es, f"swa_den{i}", [128, 4], F32) for i in range(2)]
        yb = [cx.sb(es, f"swa_y{i}", [128, 256], F32) for i in range(2)]
        pss = [cx.ps(es, f"swa_pss{i}", [128, 4, 128], F32) for i in range(2)]
        pso = [cx.ps(es, f"swa_pso{i}", [128, 4, 128], F32) for i in range(2)]
        sge = [cx.sb(es, f"swa_sge{i}", [128, 256], F32) for i in range(2)]
        items = [(t, kk) for t in range(NTc) for kk in ([t - 1, t] if t > 0 else [t])]

        def score(i):
            t, kk = items[i]
            r0, r1 = t * 128, (t + 1) * 128
            PS = pss[i % 2]
            S.op("pe", lambda h: h.matmul(PS[:], QKT[:, 4, kk * 128:(kk + 1) * 128], QKT[:, 0:4, r0:r1],
                                          start=True, stop=True), [QKT], [PS])

        def finish_item(i):
            t, kk = items[i]
            r0, r1 = t * 128, (t + 1) * 128
            PS, ST, PTB, PO = pss[i % 2], st[i % 2], pT[i % 4], pso[t % 2]
            M = mown if kk == t else mprev
            S.op("dve", lambda h: h.scalar_tensor_tensor(ST[:], PS[:], 0.125, M[:], ALU.mult, ALU.add), [PS, M], [ST])
            S.op("act", lambda h: h.activation(out=PTB[:], in_=ST[:], func=AF.Exp), [ST], [PTB])
            first = (kk == max(t - 1, 0))

            def pv(h):
                for hh in range(4):
                    ins = h.matmul(PO[:, hh, 0:65], PTB[:, hh, :], Vaug[:, kk, :],
                                   start=(first and hh == 0), stop=(kk == t and hh == 3))
                return ins
            S.op("pe", pv, [PTB, Vaug], [PO])
            if kk == t:
                DEN, Y = den[t % 2], yb[t % 2]
                S.op("dve", lambda h: h.tensor_tensor(DEN[:], PO[:, :, 64], esink[:], ALU.add), [PO, esink], [DEN])
                S.op("dve", lambda h: h.reciprocal(DEN[:], DEN[:]), [DEN], [DEN])
                for hh in range(4):
                    S.op("act", lambda h: h.activation(out=Y[:, hh * 64:(hh + 1) * 64], in_=PO[:, hh, 0:64], func=AF.Copy,
                                                       scale=DEN[:, hh:hh + 1]), [PO, DEN], [Y])
                S.op("dve", lambda h: h.tensor_tensor(Y[:], Y[:], Gs[:, t, :], ALU.mult), [Y, Gs], [Y])
                S.dma("pool", mix_dt.ap[r0:r1, mcol + 512:mcol + 768], Y[:], reads=[Y], writes=mix_dt.bufs(r0, r1))

        score(0)
        for i in range(len(items)):
            if i + 1 < len(items):
                score(i + 1)
            finish_item(i)
        S.barrier()


def phase_moba(cx, proj_dt, mix_dt, cos_sb, sin_sb, ident_bf, ident_f, mask_own_ap, gmask_ap, blkind_ap, mcol=0):
    nc, S = cx.nc, cx.S
    Lc, NTc = cx.L, cx.L // 128
    NB = 16
    with ExitStack() as es:
        Q32 = cx.sb(es, "mo_Q32", [128, NTc, 256], F32)
        KaugT = cx.sb(es, "mo_KaugT", [80, 4, Lc], BF16)
        Vaug = cx.sb(es, "mo_V", [128, NTc, 4, 65], BF16)
        kmeanT = cx.sb(es, "mo_kmT", [64, 4, NB], F32)
        mown = cx.sb(es, "mo_mown", [128, 4, 128], F32)
        gmask = cx.sb(es, "mo_gmask", [128, NB, NB], F32)
        c256 = cx.sb(es, "mo_c256", [128, 1], F32)
        S.dma("sp", mown[:], mask_own_ap, writes=[mown])
        S.dma("sp", gmask[:], gmask_ap, writes=[gmask])
        for hh in range(4):
            S.dma("sp", KaugT[64:80, hh, :], blkind_ap[:, 0:Lc], writes=[KaugT])
        S.op("pool", lambda h: h.memset(c256[:], 1.0 / 256.0), [], [c256])
        S.op("pool", lambda h: h.memset(Vaug[:, :, :, 64:65], 1.0), [], [Vaug])
        S.op("pool", lambda h: h.memset(kmeanT[:], 0.0), [], [kmeanT])
        tin = [cx.sb(es, f"mo_tin{i}", [128, 1024], F32) for i in range(2)]
        Gm = cx.sb(es, "mo_G", [128, NTc, 256], F32)
        rtmp = [cx.sb(es, f"mo_rtmp{i}", [128, 4, 256], F32) for i in range(2)]
        k32 = [cx.sb(es, f"mo_k32{i}", [128, 256], F32) for i in range(2)]
        kb = [cx.sb(es, f"mo_kb{i}", [128, 256], BF16) for i in range(2)]
        with ExitStack() as es1:
            ptr = [cx.ps(es1, f"mo_ptr{i}", [64, 8, 128], BF16) for i in range(2)]
            kmps = cx.ps(es1, "mo_kmps", [64, 4, 128], F32)
            def prepA(t):
                i2 = t % 2
                T, TMP, K32, KB, PT = tin[i2], rtmp[i2], k32[i2], kb[i2], ptr[i2]
                r0, r1 = t * 128, (t + 1) * 128
                S.dma("sp", T[:], proj_dt.ap[r0:r1, C_MQ:C_MQ + 1024],
                      reads=proj_dt.bufs(r0, r1), writes=[T])
                S.op("act", lambda h: h.activation(out=Gm[:, t, :], in_=T[:, 768:1024], func=AF.Silu), [T], [Gm])
                s4 = T[:, 0:512].rearrange("p (h two d) -> p h two d", two=2, d=32)
                x1, x2 = s4[:, :, 0, :], s4[:, :, 1, :]
                cb = cos_sb[:, t, :].unsqueeze(1).broadcast_to([128, 8, 32])
                sb_ = sin_sb[:, t, :].unsqueeze(1).broadcast_to([128, 8, 32])
                tv = TMP[:].rearrange("p f (h d) -> p f h d", d=32)
                S.op("dve", lambda h: h.tensor_tensor(tv[:, 0], x1, cb, ALU.mult), [T, cos_sb], [TMP])
                S.op("dve", lambda h: h.tensor_tensor(tv[:, 1], x2, sb_, ALU.mult), [T, sin_sb], [TMP])
                S.op("dve", lambda h: h.tensor_tensor(tv[:, 2], x2, cb, ALU.mult), [T, cos_sb], [TMP])
                S.op("dve", lambda h: h.tensor_tensor(tv[:, 3], x1, sb_, ALU.mult), [T, sin_sb], [TMP])
                q4 = Q32[:, t, :].rearrange("p (h two d) -> p h two d", two=2, d=32)
                k4 = K32[:].rearrange("p (h two d) -> p h two d", two=2, d=32)
                S.op("dve", lambda h: h.tensor_tensor(q4[:, :, 0, :], tv[:, 0, 0:4], tv[:, 1, 0:4], ALU.subtract), [TMP], [Q32])
                S.op("dve", lambda h: h.tensor_tensor(q4[:, :, 1, :], tv[:, 2, 0:4], tv[:, 3, 0:4], ALU.add), [TMP], [Q32])
                S.op("dve", lambda h: h.tensor_tensor(k4[:, :, 0, :], tv[:, 0, 4:8], tv[:, 1, 4:8], ALU.subtract), [TMP], [K32])
                S.op("dve", lambda h: h.tensor_tensor(k4[:, :, 1, :], tv[:, 2, 4:8], tv[:, 3, 4:8], ALU.add), [TMP], [K32])

            def prepB(t):
                i2 = t % 2
                T, TMP, K32, KB, PT = tin[i2], rtmp[i2], k32[i2], kb[i2], ptr[i2]
                r0, r1 = t * 128, (t + 1) * 128
                S.op("act", lambda h: h.copy(KB[:], K32[:]), [K32], [KB])
                S.op("pool", lambda h: h.tensor_copy(Vaug[:, t, :, 0:64], T[:, 512:768].rearrange("p (h d) -> p h d", d=64)),
                     [T], [Vaug])
                n = t // 2

                def km(h, K32=K32, n=n, t=t):
                    for hh in range(4):
                        ins = h.matmul(kmps[:, hh, n:n + 1], K32[:, hh * 64:(hh + 1) * 64], c256[:, 0:1],
                                       start=(t % 2 == 0 and hh == 0), stop=(t % 2 == 1 and hh == 3))
                    return ins
                S.op("pe", km, [K32, c256], [kmps])

                def tr(h, KB=KB, PT=PT):
                    for hh in range(4):
                        ins = h.transpose(PT[:, hh, :], KB[:, hh * 64:(hh + 1) * 64], ident_bf[:])
                    return ins
                S.op("pe", tr, [KB, ident_bf], [PT])
                S.op("act", lambda h: h.copy(KaugT[0:64, :, r0:r1], PT[:, 0:4, :]), [PT], [KaugT])
            prepA(0)
            for t in range(NTc):
                if t + 1 < NTc:
                    prepA(t + 1)
                prepB(t)
            S.op("dve", lambda h: h.tensor_copy(kmeanT[:, :, 0:Lc // 256], kmps[:, :, 0:Lc // 256]), [kmps], [kmeanT])
            S.barrier()
        mg = [cx.sb(es, f"mo_mg{i}", [128, 256], F32) for i in range(2)]
        mge = [cx.sb(es, f"mo_mge{i}", [128, 256], F32) for i in range(2)]
        qT32 = [cx.sb(es, f"mo_qT32{i}", [64, 4, 128], F32) for i in range(2)]
        gm = [cx.sb(es, f"mo_gm{i}", [128, 4, NB], F32) for i in range(2)]
        top8 = [cx.sb(es, f"mo_top8{i}", [128, 4, 8], F32) for i in range(2)]
        qaug = [cx.sb(es, f"mo_qaug{i}", [128, 4, 80], BF16) for i in range(2)]
        QaugT = [cx.sb(es, f"mo_QaugT{i}", [80, 4, 128], BF16) for i in range(2)]
        st = [cx.sb(es, f"mo_st{i}", [128, 4, 128], F32) for i in range(2)]
        pT = [cx.sb(es, f"mo_pT{i}", [128, 4, 128], BF16) for i in range(4)]
        den = [cx.sb(es, f"mo_den{i}", [128, 4], F32) for i in range(2)]
        yb = [cx.sb(es, f"mo_y{i}", [128, 256], F32) for i in range(2)]
        psq = cx.ps(es, "mo_psq", [64, 4, 128], F32)
        psg = cx.ps(es, "mo_psg", [128, 4, 128], F32)
        psa = cx.ps(es, "mo_psa", [80, 8, 128], BF16)
        pss = [cx.ps(es, f"mo_pss{i}", [128, 4, 128], F32) for i in range(2)]
        pso = [cx.ps(es, f"mo_pso{i}", [128, 4, 128], F32) for i in range(2)]
        ctxs = {}

        def prologue(t):
            i2 = t % 2
            r0, r1 = t * 128, (t + 1) * 128
            qb = t // 2
            MG, QT, GM, T8, QA, QAT = mg[i2], qT32[i2], gm[i2], top8[i2], qaug[i2], QaugT[i2]

            def trq(h):
                for hh in range(4):
                    ins = h.transpose(psq[:, hh, :], Q32[:, t, hh * 64:(hh + 1) * 64], ident_f[:])
                return ins
            S.op("pe", trq, [Q32, ident_f], [psq])
            S.op("dve", lambda h: h.tensor_copy(QT[:], psq[:]), [psq], [QT])

            def gate(h):
                for hh in range(4):
                    ins = h.matmul(psg[:, hh, 0:NB], QT[:, hh, :], kmeanT[:, hh, :], start=True, stop=True)
                return ins
            S.op("pe", gate, [QT, kmeanT], [psg])
            S.op("dve", lambda h: h.tensor_tensor(GM[:], psg[:, :, 0:NB],
                                                  gmask[:, qb, :].unsqueeze(1).broadcast_to([128, 4, NB]), ALU.add),
                 [psg, gmask], [GM])
            for hh in range(4):
                S.op("dve", lambda h: h.max(T8[:, hh, :], GM[:, hh, :]), [GM], [T8])
            for hh in range(4):
                S.op("dve", lambda h: h.tensor_scalar(QA[:, hh, 64:80], GM[:, hh, :], T8[:, hh, 2:3], -30000.0,
                                                      ALU.is_lt, ALU.mult), [GM, T8], [QA])
            S.op("pool", lambda h: h.memset(QA[:, :, 64 + qb:65 + qb], 0.0), [QA], [QA])
            S.op("act", lambda h: h.copy(QA[:, :, 0:64], Q32[:, t, :].rearrange("p (h d) -> p h d", d=64)), [Q32], [QA])

            def tra(h):
                for hh in range(4):
                    ins = h.transpose(psa[:, hh, :], QA[:, hh, :], ident_bf[:])
                return ins
            S.op("pe", tra, [QA, ident_bf], [psa])
            S.op("dve", lambda h: h.tensor_copy(QAT[:], psa[:, 0:4, :]), [psa], [QAT])

        items = [(t, kk) for t in range(NTc) for kk in range(t + 1)]

        def score(i):
            t, kk = items[i]
            if kk == 0:
                prologue(t)
            PS, QAT = pss[i % 2], QaugT[t % 2]

            def sc(h):
                for hh in range(4):
                    ins = h.matmul(PS[:, hh, :], KaugT[:, hh, kk * 128:(kk + 1) * 128], QAT[:, hh, :],
                                   start=True, stop=True)
                return ins
            S.op("pe", sc, [KaugT, QAT], [PS])

        def finish_item(i):
            t, kk = items[i]
            PS, ST, PTB, PO = pss[i % 2], st[i % 2], pT[i % 4], pso[t % 2]
            if kk == t:
                S.op("dve", lambda h: h.scalar_tensor_tensor(ST[:], PS[:], 0.125, mown[:], ALU.mult, ALU.add),
                     [PS, mown], [ST])
                S.op("act", lambda h: h.activation(out=PTB[:], in_=ST[:], func=AF.Exp), [ST], [PTB])
            else:
                S.op("act", lambda h: h.activation(out=PTB[:], in_=PS[:], func=AF.Exp, scale=0.125), [PS], [PTB])

            def pv(h):
                for hh in range(4):
                    ins = h.matmul(PO[:, hh, 0:65], PTB[:, hh, :], Vaug[:, kk, hh, :],
                                   start=(kk == 0 and hh == 0), stop=(kk == t and hh == 3))
                return ins
            S.op("pe", pv, [PTB, Vaug], [PO])
            if kk == t:
                i2 = t % 2
                r0, r1 = t * 128, (t + 1) * 128
                MG, DEN, Y = mg[i2], den[i2], yb[i2]
                S.op("dve", lambda h: h.reciprocal(DEN[:], PO[:, :, 64]), [PO], [DEN])
                for hh in range(4):
                    S.op("act", lambda h: h.activation(out=Y[:, hh * 64:(hh + 1) * 64], in_=PO[:, hh, 0:64], func=AF.Copy,
                                                       scale=DEN[:, hh:hh + 1]), [PO, DEN], [Y])
                S.op("dve", lambda h: h.tensor_tensor(Y[:], Y[:], Gm[:, t, :], ALU.mult), [Y, Gm], [Y])
                S.dma("pool", mix_dt.ap[r0:r1, mcol:mcol + 256], Y[:], reads=[Y], writes=mix_dt.bufs(r0, r1))

        score(0)
        for i in range(len(items)):
            if i + 1 < len(items):
                score(i + 1)
            finish_item(i)
        S.barrier()


def phase_ssd(cx, proj_dt, mix_dt, ident_bf, cst, mcol=0):
    nc, S = cx.nc, cx.S
    Lc, NTc = cx.L, cx.L // 128
    with ExitStack() as es:
        def ld(name, shape, dt=F32):
            b = cx.sb(es, "ssd_" + name, shape, dt)
            S.dma("sp", b[:], cst[name], writes=[b])
            return b
        convw = ld("convw", [128, 4, 512]); convb = ld("convb", [128, 512]); dtb = ld("dtb", [128, 4])
        Abc = ld("alog", [128, 4]); dskip = ld("dskip", [128, 256]); normw = ld("normw", [128, 256])
        tri = ld("tri", [128, 128]); ones = ld("ones", [128, 128]); maskT = ld("maskT", [128, 128])
        S.op("act", lambda h: h.activation(out=Abc[:], in_=Abc[:], func=AF.Exp), [Abc], [Abc])
        S.op("dve", lambda h: h.tensor_scalar(Abc[:], Abc[:], -1.0, None, ALU.mult), [Abc], [Abc])
        prev32 = cx.sb(es, "ssd_prev32", [128, 256], F32)
        prevb = cx.sb(es, "ssd_prevb", [128, 256], BF16)
        S.op("pool", lambda h: h.memset(prev32[:], 0.0), [], [prev32])
        S.op("pool", lambda h: h.memset(prevb[:], 0.0), [], [prevb])
        Tj = [[cx.sb(es, f"ssd_T{i}_{j}", [128, 512], F32) for j in range(4)] for i in range(2)]
        zt = [cx.sb(es, f"ssd_z{i}", [128, 256], F32) for i in range(2)]
        dtt = [cx.sb(es, f"ssd_dt{i}", [128, 4], F32) for i in range(2)]
        xa = [cx.sb(es, f"ssd_xa{i}", [128, 512], F32) for i in range(2)]
        sm = [cx.sb(es, f"ssd_sm{i}", [128, 8, 4], F32) for i in range(2)]
        Xf = [cx.sb(es, f"ssd_X{i}", [128, 256], F32) for i in range(2)]
        Xb = [cx.sb(es, f"ssd_Xb{i}", [128, 256], BF16) for i in range(2)]
        Xd = [cx.sb(es, f"ssd_Xd{i}", [128, 256], BF16) for i in range(2)]
        BCb = [cx.sb(es, f"ssd_BCb{i}", [128, 256], BF16) for i in range(2)]
        BCT = [cx.sb(es, f"ssd_BCT{i}", [128, 2, 128], BF16) for i in range(2)]
        R4 = [cx.sb(es, f"ssd_R4{i}", [128, 4, 128], F32) for i in range(2)]
        TD4 = [cx.sb(es, f"ssd_TD4{i}", [128, 4, 128], F32) for i in range(2)]
        M4 = [cx.sb(es, f"ssd_M4{i}", [128, 4, 128], BF16) for i in range(2)]
        identf = ld("identf", [128, 128])
        mask4 = cx.sb(es, "ssd_mask4", [128, 4, 128], F32)
        S.op("dve", lambda h: h.tensor_copy(mask4[:], maskT[:].unsqueeze(1).broadcast_to([128, 4, 128])), [maskT], [mask4])
        y1 = [cx.sb(es, f"ssd_y1{i}", [128, 256], F32) for i in range(2)]
        y2 = [cx.sb(es, f"ssd_y2{i}", [128, 256], F32) for i in range(2)]
        junk = cx.sb(es, "ssd_junk", [128, 256], F32)
        csb = [cx.sb(es, f"ssd_cs{i}", [128, 256], F32) for i in range(2)]
        ps_t = cx.ps(es, "ssd_ps_t", [128, 8, 128], BF16)
        ps_s = cx.ps(es, "ssd_ps_s", [128, 512], F32)
        ps_cb = cx.ps(es, "ssd_ps_cb", [128, 512], F32)
        ps_d = cx.ps(es, "ssd_ps_d", [128, 4, 128], F32)
        ps_yd = cx.ps(es, "ssd_ps_yd", [128, 512], F32)
        ps_yo = cx.ps(es, "ssd_ps_yo", [128, 512], F32)
        ps_cs = cx.ps(es, "ssd_ps_cs", [128, 512], F32)
        ndc = [0]

        def front(t):
            nd = ndc[0]
            i2 = t % 2
            r0, r1 = t * 128, (t + 1) * 128
            T, Z, DT, XA, SM, X, XB, XD, BC, BT = Tj[i2], zt[i2], dtt[i2], xa[i2], sm[i2], Xf[i2], Xb[i2], Xd[i2], BCb[i2], BCT[i2]
            for j in range(4):
                sh = 3 - j
                q = "sp"
                if r0 - sh >= 0:
                    S.dma(q, T[j][:], proj_dt.ap[r0 - sh:r1 - sh, C_XS:C_XS + 512],
                          reads=proj_dt.bufs(max(r0 - sh, 0), r1), writes=[T[j]])
                else:
                    S.op("pool", lambda h: h.memset(T[j][0:32, :], 0.0), [], [T[j]])
                    S.dma(q, T[j][sh:128, :], proj_dt.ap[0:128 - sh, C_XS:C_XS + 512],
                          reads=proj_dt.bufs(0, 128), writes=[T[j]])
            S.dma("sp", Z[:], proj_dt.ap[r0:r1, C_Z:C_Z + 256], reads=proj_dt.bufs(r0, r1), writes=[Z])
            S.dma("sp", DT[:], proj_dt.ap[r0:r1, C_DT:C_DT + 4], reads=proj_dt.bufs(r0, r1), writes=[DT])
            for j in range(4):
                ek = "dve"
                S.op(ek, lambda h: h.tensor_tensor(T[j][:], T[j][:], convw[:, j, :], ALU.mult), [T[j], convw], [T[j]])
            S.op("dve", lambda h: h.tensor_tensor(T[0][:], T[0][:], T[2][:], ALU.add), [T[0], T[2]], [T[0]])
            S.op("dve", lambda h: h.tensor_tensor(T[1][:], T[1][:], T[3][:], ALU.add), [T[1], T[3]], [T[1]])
            S.op("dve", lambda h: h.tensor_tensor(T[0][:], T[0][:], T[1][:], ALU.add), [T[0], T[1]], [T[0]])
            S.op("dve", lambda h: h.tensor_tensor(T[0][:], T[0][:], convb[:], ALU.add), [T[0], convb], [T[0]])
            S.op("act", lambda h: h.activation(out=XA[:], in_=T[0][:], func=AF.Silu), [T[0]], [XA])
            S.op("act", lambda h: h.activation(out=Z[:], in_=Z[:], func=AF.Silu), [Z], [Z])
            S.op("dve", lambda h: h.tensor_tensor(DT[:], DT[:], dtb[:], ALU.add), [DT, dtb], [DT])
            S.op("act", lambda h: h.activation(out=DT[:], in_=DT[:], func=AF.Exp), [DT], [DT])
            S.op("act", lambda h: h.activation(out=DT[:], in_=DT[:], func=AF.Ln, bias=1.0), [DT], [DT])
            S.op("dve", lambda h: h.tensor_tensor(SM[:, 0, :], DT[:], Abc[:], ALU.mult), [DT, Abc], [SM])

            def cum(h, SM=SM):
                h.matmul(ps_s[:, 0:4], tri[:], SM[:, 0, :], start=True, stop=False)
                return h.matmul(ps_s[:, 4:8], ones[:], SM[:, 0, :], start=False, stop=True)
            S.op("pe", cum, [tri, ones, SM], [ps_s])
            S.op("dve", lambda h: h.tensor_copy(SM[:, 1, :], ps_s[:, 0:4]), [ps_s], [SM])
            S.op("dve", lambda h: h.tensor_scalar(SM[:, 2, :], ps_s[:, 0:4], -1.0, None, ALU.mult), [ps_s], [SM])
            S.op("dve", lambda h: h.tensor_tensor(SM[:, 5, :], ps_s[:, 4:8], SM[:, 1, :], ALU.subtract), [ps_s, SM], [SM])
            S.op("act", lambda h: h.activation(out=SM[:, 3, :], in_=SM[:, 1, :], func=AF.Exp), [SM], [SM])
            S.op("act", lambda h: h.activation(out=SM[:, 4, :], in_=ps_s[:, 4:8], func=AF.Exp), [ps_s], [SM])
            S.op("act", lambda h: h.activation(out=SM[:, 5, :], in_=SM[:, 5, :], func=AF.Exp), [SM], [SM])
            x3 = X[:].rearrange("p (h d) -> p h d", d=64)
            S.op("dve", lambda h: h.tensor_tensor(x3, XA[:, 0:256].rearrange("p (h d) -> p h d", d=64),
                                                  DT[:].unsqueeze(2).broadcast_to([128, 4, 64]), ALU.mult), [XA, DT], [X])
            S.op("act", lambda h: h.copy(XB[:], X[:]), [X], [XB])
            S.op("dve", lambda h: h.tensor_tensor(XD[:].rearrange("p (h d) -> p h d", d=64), x3,
                                                  SM[:, 5, :].unsqueeze(2).broadcast_to([128, 4, 64]), ALU.mult), [X, SM], [XD])
            S.op("act", lambda h: h.copy(BC[:], XA[:, 256:512]), [XA], [BC])

            def trbc(h, BC=BC):
                h.transpose(ps_t[:, 0, :], BC[:, 0:128], ident_bf[:])
                return h.transpose(ps_t[:, 1, :], BC[:, 128:256], ident_bf[:])
            S.op("pe", trbc, [BC, ident_bf], [ps_t])
            S.op("act", lambda h: h.copy(BT[:], ps_t[:, 0:2, :]), [ps_t], [BT])
            S.op("pe", lambda h: h.matmul(ps_cb[:, 0:128], BT[:, 0, :], BT[:, 1, :], start=True, stop=True), [BT], [ps_cb])
            S.op("pe", lambda h: h.matmul(ps_cs[:, 0:256], BC[:, 0:128], XD[:], start=True, stop=True), [BC, XD], [ps_cs])
            S.op("act", lambda h: h.copy(csb[i2][:], ps_cs[:, 0:256]), [ps_cs], [csb[i2]])
            RR, TD, MM = R4[i2], TD4[i2], M4[i2]
            S.op("dve", lambda h: h.tensor_tensor(RR[:], tri[:].unsqueeze(1).broadcast_to([128, 4, 128]),
                                                  SM[:, 0, :].unsqueeze(2).broadcast_to([128, 4, 128]), ALU.mult), [tri, SM], [RR])

            def dbc(h):
                h.matmul(ps_d[:], ones[:], RR[:], start=True, stop=False)
                return h.matmul(ps_d[:], identf[:], mask4[:], start=False, stop=True)
            S.op("pe", dbc, [ones, RR, identf, mask4], [ps_d])
            S.op("dve", lambda h: h.tensor_tensor(TD[:], ps_d[:], SM[:, 1, :].unsqueeze(2).broadcast_to([128, 4, 128]), ALU.subtract),
                 [ps_d, SM], [TD])
            S.op("act", lambda h: h.activation(out=TD[:], in_=TD[:], func=AF.Exp), [TD], [TD])
            S.op("dve", lambda h: h.tensor_tensor(MM[:], TD[:], ps_cb[:, 0:128].unsqueeze(1).broadcast_to([128, 4, 128]), ALU.mult),
                 [TD, ps_cb], [MM])

            def ydiag(h):
                for hh in range(4):
                    ins = h.matmul(ps_yd[:, hh * 64:(hh + 1) * 64], MM[:, hh, :], XB[:, hh * 64:(hh + 1) * 64],
                                   start=(hh == 0), stop=(hh == 3))
                return ins
            S.op("pe", ydiag, [MM, XB], [ps_yd])
            ndc[0] = nd
            S.op("act", lambda h: h.copy(y1[i2][:], ps_yd[:, 0:256]), [ps_yd], [y1[i2]])

        def back(t):
            i2 = t % 2
            r0, r1 = t * 128, (t + 1) * 128
            Z, XA, SM, BT = zt[i2], xa[i2], sm[i2], BCT[i2]
            Y1, Y2 = y1[i2], y2[i2]
            S.op("pe", lambda h: h.matmul(ps_yo[:, 0:256], BT[:, 1, :], prevb[:], start=True, stop=True), [BT, prevb], [ps_yo])
            p3 = prev32[:].rearrange("p (h d) -> p h d", d=64)
            S.op("dve", lambda h: h.tensor_tensor(p3, p3, SM[:, 4, :].unsqueeze(2).broadcast_to([128, 4, 64]), ALU.mult),
                 [prev32, SM], [prev32])
            S.op("dve", lambda h: h.tensor_tensor(prev32[:], prev32[:], csb[i2][:], ALU.add), [prev32, csb[i2]], [prev32])
            S.op("act", lambda h: h.copy(prevb[:], prev32[:]), [prev32], [prevb])
            for hh in range(4):
                sl = slice(hh * 64, (hh + 1) * 64)
                S.op("dve", lambda h: h.scalar_tensor_tensor(Y1[:, sl], ps_yo[:, sl], SM[:, 3, hh:hh + 1], Y1[:, sl],
                                                             ALU.mult, ALU.add), [ps_yo, SM, Y1], [Y1])
            S.op("pool", lambda h: h.tensor_tensor(Y2[:], XA[:, 0:256], dskip[:], ALU.mult), [XA, dskip], [Y2])
            S.op("dve", lambda h: h.tensor_tensor(Y1[:], Y1[:], Y2[:], ALU.add), [Y1, Y2], [Y1])
            S.op("dve", lambda h: h.tensor_tensor(Y1[:], Y1[:], Z[:], ALU.mult), [Y1, Z], [Y1])
            S.op("act", lambda h: h.activation(out=junk[:], in_=Y1[:], func=AF.Square, accum_out=SM[:, 6, 0:1]), [Y1], [junk, SM])
            S.op("dve", lambda h: h.tensor_scalar(SM[:, 6, 0:1], SM[:, 6, 0:1], 1.0 / 256.0, EPS, ALU.mult, ALU.add), [SM], [SM])
            S.op("act", lambda h: h.activation(out=SM[:, 6, 0:1], in_=SM[:, 6, 0:1], func=AF.Ln), [SM], [SM])
            S.op("act", lambda h: h.activation(out=SM[:, 6, 0:1], in_=SM[:, 6, 0:1], func=AF.Exp, scale=-0.5), [SM], [SM])
            S.op("dve", lambda h: h.scalar_tensor_tensor(Y2[:], Y1[:], SM[:, 6, 0:1], normw[:], ALU.mult, ALU.mult),
                 [Y1, SM, normw], [Y2])
            S.dma("pool", mix_dt.ap[r0:r1, mcol + 256:mcol + 512], Y2[:], reads=[Y2], writes=mix_dt.bufs(r0, r1))

        front(0)
        for t in range(NTc):
            if t + 1 < NTc:
                front(t + 1)
            back(t)
        S.barrier()


class StopPhase(Exception):
    pass


def stage(cx, n):
    if getattr(cx, "stop_stage", None) == n and not cx.S.muted:
        cx.S.barrier()
        cx.S.muted = True


def cmul(S, ek, out_re, out_im, a_re, a_im, b_re, b_im, t1, t2, reads, writes, conj_b=False):
    o1 = ALU.subtract if not conj_b else ALU.add
    o2 = ALU.add if not conj_b else ALU.subtract
    S.op(ek, lambda h: h.tensor_tensor(t1, a_re, b_re, ALU.mult), reads, writes)
    S.op(ek, lambda h: h.tensor_tensor(t2, a_im, b_im, ALU.mult), reads, writes)
    S.op(ek, lambda h: h.tensor_tensor(out_re, t1, t2, o1), reads, writes)
    S.op(ek, lambda h: h.tensor_tensor(t1, a_im, b_re, ALU.mult), reads, writes)
    S.op(ek, lambda h: h.tensor_tensor(t2, a_re, b_im, ALU.mult), reads, writes)
    S.op(ek, lambda h: h.tensor_tensor(out_im, t1, t2, o2), reads, writes)


def phase_s5(cx, proj_dt, mix_dt, ident_bf, ident_f, cst):
    nc, S = cx.nc, cx.S
    Lc = cx.L
    T = 16
    SEG = min(Lc, 2048)
    NSEG = Lc // SEG
    NC = SEG // T
    NCT = NC + 1
    with ExitStack() as es:
        def ld(name, shape, dt=F32, q="sp"):
            b = cx.sb(es, "s5_" + name, shape, dt)
            S.dma(q, b[:], cst[name], writes=[b])
            return b
        are = ld("are", [128, 16]); aim = ld("aim", [128, 16]); ldt = ld("ldt", [128, 16])
        ccre = ld("ccre", [128, 16, 32]); ccim = ld("ccim", [128, 16, 32])
        dfm = ld("dfm", [128, 4]); glub = ld("glub", [128, 4]); kvec = ld("kvec", [128, 256])
        Wg = load_weight_bf16(cx, es, "s5_Wg", cst["gluw"], 512, 512, None)
        BT = [cx.sb(es, f"s5_BT{i}", [128, 16, 128], BF16) for i in range(2)]
        Ere = cx.sb(es, "s5_Ere", [128, 16, T + 1], F32); Eim = cx.sb(es, "s5_Eim", [128, 16, T + 1], F32)
        Rk = cx.sb(es, "s5_Rk", [128, 16, T + 1], F32)
        E2re = cx.sb(es, "s5_E2re", [128, 16, NCT], F32); E2im = cx.sb(es, "s5_E2im", [128, 16, NCT], F32)
        R2 = cx.sb(es, "s5_R2", [128, 16, NCT], F32)
        sm = cx.sb(es, "s5_sm", [128, 12, 16], F32)
        pmax = 1
        while pmax * 2 < max(T + 1, NCT):
            pmax *= 2
        Enim = cx.sb(es, "s5_Enim", [128, 16, T + 1], F32)
        hp = cx.sb(es, "s5_halfpi", [128, 1], F32)
        es_tb = ExitStack()
        tb = cx.sb(es_tb, "s5_tb", [128, 4, 16 + 16 * pmax], F32)
        SMALL = [sm]
        S.op("act", lambda h: h.activation(out=sm[:, 0, :], in_=ldt[:], func=AF.Exp), [ldt], SMALL)
        S.op("dve", lambda h: h.tensor_tensor(sm[:, 1, :], are[:], sm[:, 0, :], ALU.mult), [are] + SMALL, SMALL)
        S.op("dve", lambda h: h.tensor_tensor(sm[:, 2, :], aim[:], sm[:, 0, :], ALU.mult), [aim] + SMALL, SMALL)
        S.op("act", lambda h: h.activation(out=sm[:, 3, :], in_=sm[:, 1, :], func=AF.Exp), SMALL, SMALL)
        S.op("pool", lambda h: h.memset(hp[:], float(np.pi / 2)), [], [hp])
        S.op("act", lambda h: h.activation(out=sm[:, 5, :], in_=sm[:, 2, :], func=AF.Sin, scale=1.0 / 64), SMALL, SMALL)
        S.op("act", lambda h: h.activation(out=sm[:, 4, :], in_=sm[:, 2, :], func=AF.Sin, scale=1.0 / 64, bias=hp[:, 0:1]),
             SMALL + [hp], SMALL)
        for _ in range(6):
            S.op("dve", lambda h: h.tensor_tensor(sm[:, 8, :], sm[:, 4, :], sm[:, 4, :], ALU.mult), SMALL, SMALL)
            S.op("dve", lambda h: h.tensor_tensor(sm[:, 9, :], sm[:, 5, :], sm[:, 5, :], ALU.mult), SMALL, SMALL)
            S.op("dve", lambda h: h.tensor_tensor(sm[:, 10, :], sm[:, 4, :], sm[:, 5, :], ALU.mult), SMALL, SMALL)
            S.op("dve", lambda h: h.tensor_tensor(sm[:, 4, :], sm[:, 8, :], sm[:, 9, :], ALU.subtract), SMALL, SMALL)
            S.op("dve", lambda h: h.tensor_scalar(sm[:, 5, :], sm[:, 10, :], 2.0, None, ALU.mult), SMALL, SMALL)
        S.op("dve", lambda h: h.tensor_tensor(sm[:, 8, :], sm[:, 3, :], sm[:, 4, :], ALU.mult), SMALL, SMALL)
        S.op("dve", lambda h: h.tensor_tensor(sm[:, 9, :], sm[:, 3, :], sm[:, 5, :], ALU.mult), SMALL, SMALL)
        S.op("dve", lambda h: h.tensor_scalar(sm[:, 8, :], sm[:, 8, :], -1.0, None, ALU.add), SMALL, SMALL)
        S.op("dve", lambda h: h.tensor_tensor(sm[:, 10, :], are[:], are[:], ALU.mult), [are], SMALL)
        S.op("dve", lambda h: h.tensor_tensor(sm[:, 11, :], aim[:], aim[:], ALU.mult), [aim], SMALL)
        S.op("dve", lambda h: h.tensor_tensor(sm[:, 10, :], sm[:, 10, :], sm[:, 11, :], ALU.add), SMALL, SMALL)
        S.op("dve", lambda h: h.reciprocal(sm[:, 10, :], sm[:, 10, :]), SMALL, SMALL)
        S.op("dve", lambda h: h.tensor_tensor(sm[:, 6, :], sm[:, 8, :], are[:], ALU.mult), SMALL + [are], SMALL)
        S.op("dve", lambda h: h.tensor_tensor(sm[:, 11, :], sm[:, 9, :], aim[:], ALU.mult), SMALL + [aim], SMALL)
        S.op("dve", lambda h: h.tensor_tensor(sm[:, 6, :], sm[:, 6, :], sm[:, 11, :], ALU.add), SMALL, SMALL)
        S.op("dve", lambda h: h.tensor_tensor(sm[:, 7, :], sm[:, 9, :], are[:], ALU.mult), SMALL + [are], SMALL)
        S.op("dve", lambda h: h.tensor_tensor(sm[:, 11, :], sm[:, 8, :], aim[:], ALU.mult), SMALL + [aim], SMALL)
        S.op("dve", lambda h: h.tensor_tensor(sm[:, 7, :], sm[:, 7, :], sm[:, 11, :], ALU.subtract), SMALL, SMALL)
        S.op("dve", lambda h: h.tensor_tensor(sm[:, 6, :], sm[:, 6, :], sm[:, 10, :], ALU.mult), SMALL, SMALL)
        S.op("dve", lambda h: h.tensor_tensor(sm[:, 7, :], sm[:, 7, :], sm[:, 10, :], ALU.mult), SMALL, SMALL)

        def build_pow_tables(Tre, Tim, n, base_re, base_im):
            TB = [Tre, Tim, tb]
            S.op("pool", lambda h: h.memset(Tre[:, :, 0:1], 1.0), [], [Tre])
            S.op("pool", lambda h: h.memset(Tim[:, :, 0:1], 0.0), [], [Tim])
            S.op("dve", lambda h: h.tensor_copy(Tre[:, :, 1], base_re), SMALL, [Tre])
            S.op("dve", lambda h: h.tensor_copy(Tim[:, :, 1], base_im), SMALL, [Tim])
            m = 2
            while m < n:
                cnt = min(m, n - m)
                pr, pi_ = tb[:, 2, 0:16], tb[:, 3, 0:16]
                cmul(S, "dve", pr, pi_, Tre[:, :, m - 1], Tim[:, :, m - 1], Tre[:, :, 1], Tim[:, :, 1],
                     tb[:, 0, 0:16], tb[:, 1, 0:16], TB, TB)
                prb = pr.unsqueeze(2).broadcast_to([128, 16, cnt])
                pib = pi_.unsqueeze(2).broadcast_to([128, 16, cnt])
                t1 = tb[:, 0, 16:16 + 16 * cnt].rearrange("p (a b) -> p a b", b=cnt)
                t2 = tb[:, 1, 16:16 + 16 * cnt].rearrange("p (a b) -> p a b", b=cnt)
                cmul(S, "dve", Tre[:, :, m:m + cnt], Tim[:, :, m:m + cnt], Tre[:, :, 0:cnt], Tim[:, :, 0:cnt], prb, pib,
                     t1, t2, TB, TB)
                m += cnt
        build_pow_tables(Ere, Eim, T + 1, sm[:, 4, :], sm[:, 5, :])
        S.op("dve", lambda h: h.tensor_scalar(Enim[:], Eim[:], -1.0, None, ALU.mult), [Eim], [Enim])
        S.op("dve", lambda h: h.tensor_copy(sm[:, 8, :], Ere[:, :, T]), [Ere], SMALL)
        S.op("dve", lambda h: h.tensor_copy(sm[:, 9, :], Eim[:, :, T]), [Eim], SMALL)
        build_pow_tables(E2re, E2im, NCT, sm[:, 8, :], sm[:, 9, :])
        S.op("dve", lambda h: h.tensor_tensor(Rk[:], sm[:, 1, :].unsqueeze(2).broadcast_to([128, 16, T + 1]),
                                              kvec[:, 0:T + 1].unsqueeze(1).broadcast_to([128, 16, T + 1]), ALU.mult),
             SMALL + [kvec], [Rk])
        S.op("act", lambda h: h.activation(out=Rk[:], in_=Rk[:], func=AF.Exp), [Rk], [Rk])
        S.op("dve", lambda h: h.tensor_tensor(R2[:], sm[:, 1, :].unsqueeze(2).broadcast_to([128, 16, NCT]),
                                              kvec[:, 0:NCT].unsqueeze(1).broadcast_to([128, 16, NCT]), ALU.mult),
             SMALL + [kvec], [R2])
        S.op("act", lambda h: h.activation(out=R2[:], in_=R2[:], func=AF.Exp, scale=float(T)), [R2], [R2])
        S.barrier()
        es_tb.close()
        stage(cx, 1)
        with ExitStack() as es1:
            bpre = cx.sb(es1, "s5_bpre", [128, 16, 128], F32); bpim = cx.sb(es1, "s5_bpim", [128, 16, 128], F32)
            S.dma("sp", bpre[:], cst["bpre"], writes=[bpre]); S.dma("pool", bpim[:], cst["bpim"], writes=[bpim])
            t1 = cx.sb(es1, "s5_bt1", [128, 16, 128], F32); t2 = cx.sb(es1, "s5_bt2", [128, 16, 128], F32)
            bbre = cx.sb(es1, "s5_bbre", [128, 16, 128], BF16); bbim = cx.sb(es1, "s5_bbim", [128, 16, 128], BF16)
            kr = sm[:, 6, :].unsqueeze(2).broadcast_to([128, 16, 128])
            ki = sm[:, 7, :].unsqueeze(2).broadcast_to([128, 16, 128])
            cmul(S, "dve", bbre[:], bbim[:], bpre[:], bpim[:], kr, ki, t1[:], t2[:], [bpre, bpim, t1, t2] + SMALL, [bbre, bbim, t1, t2])
            pst = cx.ps(es1, "s5_pst", [128, 8, 128], BF16)
            for k in range(16):
                for ri, src in enumerate((bbre, bbim)):
                    S.op("pe", lambda h: h.transpose(pst[:, ri, :], src[:, k, :], ident_bf[:]), [src, ident_bf], [pst])
                    S.op("act", lambda h: h.copy(BT[ri][:, k, :], pst[:, ri, :]), [pst], [BT[ri]])
            S.barrier()
        stage(cx, 2)
        Send = cx.sb(es, "s5_Send", [128, 2, 16], F32)
        S.op("pool", lambda h: h.memset(Send[:], 0.0), [], [Send])
        for seg in range(NSEG):
            t00 = seg * SEG
            with ExitStack() as es2:
                y = [cx.sb(es2, f"s5_y{q}", [128, SEG], F32) for q in range(4)]
                with ExitStack() as es3:
                    uTb = [cx.sb(es3, f"s5_uTb{q}", [128, SEG], BF16) for q in range(4)]
                    with ExitStack() as es4:
                        sut = [cx.sb(es4, f"s5_sut{i}", [128, 512], F32) for i in range(2)]
                        sub = [cx.sb(es4, f"s5_sub{i}", [128, 512], BF16) for i in range(2)]
                        psu = [cx.ps(es4, f"s5_psu{i}", [128, 8, 128], BF16) for i in range(2)]
                        for tt in range(SEG // 128):
                            i2 = tt % 2
                            r0 = t00 + tt * 128
                            S.dma("sp", sut[i2][:], proj_dt.ap[r0:r0 + 128, C_SU:C_SU + 512],
                                  reads=proj_dt.bufs(r0, r0 + 128), writes=[sut[i2]])
                            S.op("pool", lambda h: h.tensor_copy(sub[i2][:], sut[i2][:]), [sut[i2]], [sub[i2]])
                            stage(cx, 21)

                            def tru(h, i2=i2):
                                for q in range(4):
                                    ins = h.transpose(psu[i2][:, q, :], sub[i2][:, q * 128:(q + 1) * 128], ident_bf[:])
                                return ins
                            S.op("pe", tru, [sub[i2], ident_bf], [psu[i2]])
                            stage(cx, 22)
                            for q in range(4):
                                ek = "act"
                                if ek == "act":
                                    S.op("act", lambda h: h.copy(uTb[q][:, tt * 128:(tt + 1) * 128], psu[i2][:, q, :]), [psu[i2]], [uTb[q]])
                                else:
                                    S.op("dve", lambda h: h.tensor_copy(uTb[q][:, tt * 128:(tt + 1) * 128], psu[i2][:, q, :]), [psu[i2]], [uTb[q]])
                                stage(cx, 230 + q)
                            stage(cx, 240 + tt)
                        S.barrier()
                    stage(cx, 3)
                    xre = cx.sb(es3, "s5_xre", [128, SEG], F32); xim = cx.sb(es3, "s5_xim", [128, SEG], F32)
                    vre2 = [cx.sb(es3, f"s5_vre{i}", [128, SEG], BF16) for i in range(2)]
                    vim2 = [cx.sb(es3, f"s5_vim{i}", [128, SEG], BF16) for i in range(2)]
                    rmask = cx.sb(es3, "s5_rmask", [128, SEG], F32)
                    tas = [cx.sb(es3, f"s5_ta{i}", [128, 512], F32) for i in range(4)]
                    tbs = [cx.sb(es3, f"s5_tbb{i}", [128, 512], F32) for i in range(4)]
                    nrot = [0]
                    ctabs = [[cx.sb(es3, f"s5_ctab{par}_{i}", [128, T + 1, 64], BF16) for i in range(4)] for par in range(2)]
                    for par in range(2):
                        for i in range(4):
                            S.op("pool", lambda h: h.memset(ctabs[par][i][:], 0.0), [], [ctabs[par][i]])
                    ct1 = cx.sb(es3, "s5_ct1", [128, T + 1, 32], F32); ct2 = cx.sb(es3, "s5_ct2", [128, T + 1, 32], F32)
                    lv = cx.sb(es3, "s5_lv", [128, 12, NCT], F32)
                    Sp2 = [[cx.sb(es3, f"s5_Sp{par}_{i}", [128, NC], BF16) for i in range(2)] for par in range(2)]
                    R2m = cx.sb(es3, "s5_R2m", [128, NC], F32)
                    psb = [cx.ps(es3, f"s5_psb{i}", [128, 512], F32) for i in range(4)]
                    psy = [cx.ps(es3, f"s5_psy{i}", [128, 4, 128], F32) for i in range(2)]
                    npyc = [0]

                    def front_mid(k):
                        q, j = k // 4, k % 4
                        vre, vim, Sp = vre2[k % 2], vim2[k % 2], Sp2[k % 2]
                        rm3 = rmask[:].rearrange("p (c k) -> p c k", k=T)
                        S.op("act", lambda h: h.copy(rm3[:, :, 1:T], sm[:, 3, k:k + 1].unsqueeze(2).broadcast_to([128, NC, T - 1])),
                             SMALL, [rmask])
                        S.op("pool", lambda h: h.memset(rm3[:, :, 0:1], 0.0), [], [rmask])
                        for blk in range(SEG // 512):
                            c0 = blk * 512
                            PR, PI = psb[(2 * blk) % 4], psb[(2 * blk + 1) % 4]
                            S.op("pe", lambda h: h.matmul(PR[:], BT[0][:, k, :], uTb[q][:, c0:c0 + 512], start=True, stop=True),
                                 [BT[0], uTb[q]], [PR])
                            S.op("pe", lambda h: h.matmul(PI[:], BT[1][:, k, :], uTb[q][:, c0:c0 + 512], start=True, stop=True),
                                 [BT[1], uTb[q]], [PI])
                            cb = Ere[:, k, 0:T].unsqueeze(1).broadcast_to([128, 512 // T, T])
                            sb_ = Eim[:, k, 0:T].unsqueeze(1).broadcast_to([128, 512 // T, T])
                            v3 = lambda ap: ap.rearrange("p (c k) -> p c k", k=T)
                            ta, tbb = tas[nrot[0] % 4], tbs[nrot[0] % 4]
                            nrot[0] += 1
                            S.op("dve", lambda h: h.tensor_tensor(v3(ta[:]), v3(PR[:]), cb, ALU.mult), [PR, Ere], [ta])
                            S.op("dve", lambda h: h.tensor_tensor(v3(tbb[:]), v3(PI[:]), sb_, ALU.mult), [PI, Eim], [tbb])
                            S.op("pool", lambda h: h.tensor_tensor(xre[:, c0:c0 + 512], ta[:], tbb[:], ALU.add), [ta, tbb], [xre])
                            ta, tbb = tas[nrot[0] % 4], tbs[nrot[0] % 4]
                            nrot[0] += 1
                            S.op("dve", lambda h: h.tensor_tensor(v3(ta[:]), v3(PI[:]), cb, ALU.mult), [PI, Ere], [ta])
                            S.op("dve", lambda h: h.tensor_tensor(v3(tbb[:]), v3(PR[:]), sb_, ALU.mult), [PR, Eim], [tbb])
                            S.op("pool", lambda h: h.tensor_tensor(xim[:, c0:c0 + 512], ta[:], tbb[:], ALU.subtract), [ta, tbb], [xim])
                        stage(cx, 4)
                        S.op("dve", lambda h: h.tensor_tensor_scan(vre[:], rmask[:], xre[:], 0.0, ALU.mult, ALU.add), [rmask, xre], [vre])
                        S.op("dve", lambda h: h.tensor_tensor_scan(vim[:], rmask[:], xim[:], 0.0, ALU.mult, ALU.add), [rmask, xim], [vim])
                        stage(cx, 5)
                        LV = [lv]
                        vr3 = vre[:].rearrange("p (c k) -> p c k", k=T)
                        vi3 = vim[:].rearrange("p (c k) -> p c k", k=T)
                        S.op("dve", lambda h: h.tensor_copy(lv[:, 0, 0:NC], vr3[:, :, T - 1]), [vre], LV)
                        S.op("dve", lambda h: h.tensor_copy(lv[:, 1, 0:NC], vi3[:, :, T - 1]), [vim], LV)
                        er, ei = Ere[:, k, T - 1:T], Eim[:, k, T - 1:T]
                        S.op("dve", lambda h: h.tensor_scalar(lv[:, 10, 0:NC], lv[:, 1, 0:NC], ei, None, ALU.mult), LV + [Eim], LV)
                        S.op("dve", lambda h: h.scalar_tensor_tensor(lv[:, 2, 0:NC], lv[:, 0, 0:NC], er, lv[:, 10, 0:NC], ALU.mult, ALU.subtract), LV + [Ere], LV)
                        S.op("dve", lambda h: h.tensor_scalar(lv[:, 10, 0:NC], lv[:, 0, 0:NC], ei, None, ALU.mult), LV + [Eim], LV)
                        S.op("dve", lambda h: h.scalar_tensor_tensor(lv[:, 3, 0:NC], lv[:, 1, 0:NC], er, lv[:, 10, 0:NC], ALU.mult, ALU.add), LV + [Ere], LV)
                        cmul(S, "dve", lv[:, 4, 0:NC], lv[:, 5, 0:NC], lv[:, 2, 0:NC], lv[:, 3, 0:NC], E2re[:, k, 0:NC], E2im[:, k, 0:NC],
                             lv[:, 10, 0:NC], lv[:, 11, 0:NC], LV + [E2re, E2im], LV, conj_b=True)
                        S.op("pool", lambda h: h.tensor_copy(R2m[:], R2[:, k, 1:2].broadcast_to([128, NC])), [R2], [R2m])
                        S.op("dve", lambda h: h.tensor_tensor_scan(lv[:, 6, 0:NC], R2m[:], lv[:, 4, 0:NC], 0.0, ALU.mult, ALU.add), [R2m] + LV, LV)
                        S.op("dve", lambda h: h.tensor_tensor_scan(lv[:, 7, 0:NC], R2m[:], lv[:, 5, 0:NC], 0.0, ALU.mult, ALU.add), [R2m] + LV, LV)
                        cmul(S, "dve", lv[:, 8, 1:NCT], lv[:, 9, 1:NCT], lv[:, 6, 0:NC], lv[:, 7, 0:NC], E2re[:, k, 0:NC], E2im[:, k, 0:NC],
                             lv[:, 10, 0:NC], lv[:, 11, 0:NC], LV + [E2re, E2im], LV)
                        S.op("dve", lambda h: h.tensor_copy(lv[:, 8, 0:1], Send[:, 0, k:k + 1]), [Send], LV)
                        S.op("dve", lambda h: h.tensor_copy(lv[:, 9, 0:1], Send[:, 1, k:k + 1]), [Send], LV)
                        if seg > 0:
                            S.op("dve", lambda h: h.tensor_tensor(lv[:, 4, 0:NC], R2[:, k, 1:NCT], E2re[:, k, 1:NCT], ALU.mult), [R2, E2re], LV)
                            S.op("dve", lambda h: h.tensor_tensor(lv[:, 5, 0:NC], R2[:, k, 1:NCT], E2im[:, k, 1:NCT], ALU.mult), [R2, E2im], LV)
                            sr, si = Send[:, 0, k:k + 1], Send[:, 1, k:k + 1]
                            S.op("dve", lambda h: h.scalar_tensor_tensor(lv[:, 8, 1:NCT], lv[:, 4, 0:NC], sr, lv[:, 8, 1:NCT], ALU.mult, ALU.add), LV + [Send], LV)
                            S.op("dve", lambda h: h.tensor_scalar(lv[:, 10, 0:NC], lv[:, 5, 0:NC], si, None, ALU.mult), LV + [Send], LV)
                            S.op("dve", lambda h: h.tensor_tensor(lv[:, 8, 1:NCT], lv[:, 8, 1:NCT], lv[:, 10, 0:NC], ALU.subtract), LV, LV)
                            S.op("dve", lambda h: h.scalar_tensor_tensor(lv[:, 9, 1:NCT], lv[:, 5, 0:NC], sr, lv[:, 9, 1:NCT], ALU.mult, ALU.add), LV + [Send], LV)
                            S.op("dve", lambda h: h.tensor_scalar(lv[:, 10, 0:NC], lv[:, 4, 0:NC], si, None, ALU.mult), LV + [Send], LV)
                            S.op("dve", lambda h: h.tensor_tensor(lv[:, 9, 1:NCT], lv[:, 9, 1:NCT], lv[:, 10, 0:NC], ALU.add), LV, LV)
                        S.op("dve", lambda h: h.tensor_copy(Send[:, 0, k:k + 1], lv[:, 8, NC:NCT]), LV, [Send])
                        S.op("dve", lambda h: h.tensor_copy(Send[:, 1, k:k + 1], lv[:, 9, NC:NCT]), LV, [Send])
                        S.op("pool", lambda h: h.tensor_copy(Sp[0][:], lv[:, 8, 0:NC]), LV, [Sp[0]])
                        S.op("pool", lambda h: h.tensor_copy(Sp[1][:], lv[:, 9, 0:NC]), LV, [Sp[1]])
                        stage(cx, 6)
                        cr = ccre[:, k, :].unsqueeze(1).broadcast_to([128, T + 1, 32])
                        ci = ccim[:, k, :].unsqueeze(1).broadcast_to([128, T + 1, 32])
                        ctabf = ctabs[j % 2]
                        hs = slice(32 * (j % 2), 32 * (j % 2) + 32)
                        CT = ctabf + [ct1, ct2]
                        e_r = Ere[:, k, 0:T + 1].unsqueeze(2).broadcast_to([128, T + 1, 32])
                        e_i = Eim[:, k, 0:T + 1].unsqueeze(2).broadcast_to([128, T + 1, 32])
                        e_ni = Enim[:, k, 0:T + 1].unsqueeze(2).broadcast_to([128, T + 1, 32])
                        rb = Rk[:, k, 1:T + 1].unsqueeze(2).broadcast_to([128, T, 32])
                        S.op("dve", lambda h: h.tensor_tensor(ct1[:], cr, e_r, ALU.mult), [ccre, Ere], CT)
                        S.op("dve", lambda h: h.tensor_tensor(ct2[:], ci, e_i, ALU.mult), [ccim, Eim], CT)
                        S.op("dve", lambda h: h.tensor_tensor(ctabf[0][:, :, hs], ct1[:], ct2[:], ALU.subtract), CT, CT)
                        S.op("dve", lambda h: h.tensor_tensor(ct1[:], cr, e_ni, ALU.mult), [ccre, Enim], CT)
                        S.op("dve", lambda h: h.tensor_tensor(ct2[:], ci, e_r, ALU.mult), [ccim, Ere], CT)
                        S.op("dve", lambda h: h.tensor_tensor(ctabf[1][:, :, hs], ct1[:], ct2[:], ALU.subtract), CT, CT)
                        S.op("dve", lambda h: h.tensor_tensor(ctabf[2][:, 0:T, hs], ctabf[0][:, 1:T + 1, hs], rb, ALU.mult), CT + [Rk], CT)
                        S.op("dve", lambda h: h.tensor_tensor(ctabf[3][:, 0:T, hs], ctabf[1][:, 1:T + 1, hs], rb, ALU.mult), CT + [Rk], CT)

                    def back(k):
                        q, j = k // 4, k % 4
                        vre, vim, Sp = vre2[k % 2], vim2[k % 2], Sp2[k % 2]
                        ctabf = ctabs[j % 2]
                        npy = npyc[0]
                        vrb = vre[:].rearrange("p (c k) -> p k c", k=T)
                        vib = vim[:].rearrange("p (c k) -> p k c", k=T)
                        jj = j // 2
                        y3 = y[q][64 * jj:64 * jj + 64, :].rearrange("p (c k) -> p k c", k=T)
                        for kb in range(T // 4):
                            PY = psy[npy % 2]
                            npy += 1

                            def ymm(h, kb=kb, PY=PY):
                                for kk in range(4):
                                    kx = kb * 4 + kk
                                    o = PY[64 * jj:64 * jj + 64, kk, 0:NC]
                                    h.matmul(o, ctabf[0][:, kx, :], vrb[:, kx, :], start=True, stop=False)
                                    h.matmul(o, ctabf[1][:, kx, :], vib[:, kx, :], start=False, stop=False)
                                    h.matmul(o, ctabf[2][:, kx, :], Sp[0][:], start=False, stop=False)
                                    ins = h.matmul(o, ctabf[3][:, kx, :], Sp[1][:], start=False, stop=True)
                                return ins
                            S.op("pe", ymm, ctabf + [vre, vim] + Sp, [PY])
                            if j % 2 == 0:
                                S.op("act", lambda h: h.copy(y3[:, kb * 4:(kb + 1) * 4, :], PY[64 * jj:64 * jj + 64, :, 0:NC]), [PY], [y[q]])
                            else:
                                S.op("dve", lambda h: h.tensor_tensor(y3[:, kb * 4:(kb + 1) * 4, :], PY[64 * jj:64 * jj + 64, :, 0:NC],
                                                                      y3[:, kb * 4:(kb + 1) * 4, :], ALU.add), [PY, y[q]], [y[q]])
                        npyc[0] = npy

                    front_mid(0)
                    for k in range(16):
                        if k + 1 < 16:
                            front_mid(k + 1)
                        back(k)
                    S.barrier()
                stage(cx, 8)
                with ExitStack() as es5:
                    sut = [cx.sb(es5, f"s5_tsut{i}", [128, 4, 512], F32) for i in range(2)]
                    yy = [cx.sb(es5, f"s5_yy{q}", [128, 512], F32) for q in range(4)]
                    w1s = [cx.sb(es5, f"s5_w1_{q}", [128, 512], F32) for q in range(4)]
                    w2s = [cx.sb(es5, f"s5_w2_{q}", [128, 512], F32) for q in range(4)]
                    ygb = [cx.sb(es5, f"s5_ygb{q}", [128, 512], BF16) for q in range(4)]
                    og = [cx.sb(es5, f"s5_og{i}", [128, 512], F32) for i in range(4)]
                    g5 = [cx.sb(es5, f"s5_g5{i}", [128, 512], F32) for i in range(2)]
                    yo = [cx.sb(es5, f"s5_yo{i}", [128, 512], F32) for i in range(2)]
                    psT = [cx.ps(es5, f"s5_psT{q}", [128, 512], F32) for q in range(4)]
                    psG = [cx.ps(es5, f"s5_psG{i}", [128, 512], F32) for i in range(2)]
                    psO = [cx.ps(es5, f"s5_psO{i}", [128, 512], F32) for i in range(2)]
                    for blk in range(SEG // 512):
                        c0 = blk * 512
                        SU = sut[blk % 2]
                        for tt in range(4):
                            r0 = t00 + c0 + tt * 128
                            S.dma("sp", SU[:, tt, :], proj_dt.ap[r0:r0 + 128, C_SU:C_SU + 512],
                                  reads=proj_dt.bufs(r0, r0 + 128), writes=[SU])
                        for q in range(4):
                            def tq(h, q=q):
                                for tt in range(4):
                                    ins = h.transpose(psT[q][:, tt * 128:(tt + 1) * 128], SU[:, tt, q * 128:(q + 1) * 128], ident_f[:])
                                return ins
                            S.op("pe", tq, [SU, ident_f], [psT[q]])
                            S.op("dve", lambda h: h.scalar_tensor_tensor(yy[q][:], psT[q][:], dfm[:, q:q + 1], y[q][:, c0:c0 + 512],
                                                                         ALU.mult, ALU.add), [psT[q], dfm, y[q]], [yy[q]])
                            w1, w2 = w1s[q], w2s[q]
                            S.op("act", lambda h: h.activation(out=w1[:], in_=yy[q][:], func=AF.Square), [yy[q]], [w1])
                            S.op("dve", lambda h: h.tensor_scalar(w1[:], w1[:], 0.044715, 1.0, ALU.mult, ALU.add), [w1], [w1])
                            S.op("dve", lambda h: h.tensor_tensor(w1[:], w1[:], yy[q][:], ALU.mult), [w1, yy[q]], [w1])
                            S.op("act", lambda h: h.activation(out=w2[:], in_=w1[:], func=AF.Sigmoid, scale=1.5957691216057308), [w1], [w2])
                            S.op("dve", lambda h: h.tensor_tensor(yy[q][:], yy[q][:], w2[:], ALU.mult), [yy[q], w2], [yy[q]])
                            S.op("pool", lambda h: h.tensor_copy(ygb[q][:], yy[q][:]), [yy[q]], [ygb[q]])
                        for nt in range(4):
                            def glu(h, nt=nt):
                                for q in range(4):
                                    ins = h.matmul(psG[nt % 2][:], Wg[:, q, nt * 128:(nt + 1) * 128], ygb[q][:], start=(q == 0), stop=(q == 3))
                                return ins
                            S.op("pe", glu, [Wg] + ygb, [psG[nt % 2]])
                            S.op("act", lambda h: h.activation(out=og[nt][:], in_=psG[nt % 2][:], func=AF.Sigmoid, bias=glub[:, nt:nt + 1]),
                                 [psG[nt % 2], glub], [og[nt]])
                            S.op("dve", lambda h: h.tensor_tensor(og[nt][:], og[nt][:], yy[nt][:], ALU.mult), [og[nt], yy[nt]], [og[nt]])
                        for tt in range(4):
                            i2 = tt % 2
                            r0 = t00 + c0 + tt * 128
                            S.dma("sp", g5[i2][:], proj_dt.ap[r0:r0 + 128, C_S5G:C_S5G + 512], reads=proj_dt.bufs(r0, r0 + 128), writes=[g5[i2]])
                            S.op("act", lambda h: h.activation(out=g5[i2][:], in_=g5[i2][:], func=AF.Silu), [g5[i2]], [g5[i2]])

                            def tro(h, tt=tt, i2=i2):
                                for nt in range(4):
                                    ins = h.transpose(psO[i2][:, nt * 128:(nt + 1) * 128], og[nt][:, tt * 128:(tt + 1) * 128], ident_f[:])
                                return ins
                            S.op("pe", tro, og + [ident_f], [psO[i2]])
                            S.op("dve", lambda h: h.tensor_tensor(yo[i2][:], psO[i2][:, 0:512], g5[i2][:], ALU.mult), [psO[i2], g5[i2]], [yo[i2]])
                            S.dma("pool", mix_dt.ap[r0:r0 + 128, 768:1024], yo[i2][:, 0:256], reads=[yo[i2]], writes=mix_dt.bufs(r0, r0 + 128))
                            S.dma("pool", mix_dt.ap[r0:r0 + 128, 1024 + 768:2048], yo[i2][:, 256:512], reads=[yo[i2]], writes=mix_dt.bufs(r0, r0 + 128))
                    S.barrier()


def s5_layouts(a_re, a_im, log_dt, b_re, b_im, c_re, c_im, d, glu_w, glu_b):
    G = np.arange(32).reshape(16, 2)
    f = np.float32
    are = a_re[G].transpose(1, 2, 0).reshape(128, 16).astype(f)
    aim = a_im[G].transpose(1, 2, 0).reshape(128, 16).astype(f)
    ldt = np.broadcast_to(log_dt[G].transpose(1, 0)[:, None, :], (2, 64, 16)).reshape(128, 16).astype(f)
    ccre = np.zeros((2, 64, 16, 2, 16), f); ccim = np.zeros((2, 64, 16, 2, 16), f)
    bpre = np.zeros((2, 64, 16, 4, 2, 16), f); bpim = np.zeros((2, 64, 16, 4, 2, 16), f)
    for k in range(16):
        for g2 in range(2):
            g = G[k, g2]
            ccre[g2, :, k, g2, :] = c_re[g].T
            ccim[g2, :, k, g2, :] = c_im[g].T
            bpre[g2, :, k, k % 4, g2, :] = b_re[g]
            bpim[g2, :, k, k % 4, g2, :] = b_im[g]
    return dict(are=are, aim=aim, ldt=ldt, ccre=ccre.reshape(128, 16, 32), ccim=ccim.reshape(128, 16, 32),
                bpre=bpre.reshape(128, 16, 128), bpim=bpim.reshape(128, 16, 128),
                dfm=np.ascontiguousarray(d.reshape(4, 128).T).astype(f),
                glub=np.ascontiguousarray(glu_b.reshape(4, 128).T).astype(f),
                gluw=np.ascontiguousarray(glu_w).astype(f))


def bc128(a):
    a = np.asarray(a, np.float32)
    return np.ascontiguousarray(np.broadcast_to(a[None], (128,) + a.shape))


def static_consts():
    f = np.float32
    pos = np.arange(L, dtype=f)
    inv = (1.0 / (np.float32(10000.0) ** (np.arange(0, 64, 2, dtype=f) / np.float32(64)))).astype(f)
    ang = (pos[:, None] * inv[None, :]).astype(f)
    cos = np.cos(ang).astype(f); sin = np.sin(ang).astype(f)
    k = np.arange(128)[:, None]; q = np.arange(128)[None, :]
    mown = np.where(k <= q, 0.0, -30000.0).astype(f)
    mprev = np.where(k > q, 0.0, -30000.0).astype(f)
    return dict(
        ident=np.eye(128).astype(ml_dtypes.bfloat16), identf=np.eye(128, dtype=f),
        cosT=np.ascontiguousarray(cos.reshape(NT, 128, 32).transpose(1, 0, 2)),
        sinT=np.ascontiguousarray(sin.reshape(NT, 128, 32).transpose(1, 0, 2)),
        mown=np.ascontiguousarray(np.tile(mown[:, None, :], (1, 4, 1))),
        mprev=np.ascontiguousarray(np.tile(mprev[:, None, :], (1, 4, 1))),
        gmask=bc128(np.where(np.arange(16)[None, :] < np.arange(16)[:, None], 0.0, -1e30).astype(f)),
        blkind=(np.arange(L)[None, :] // 256 == np.arange(16)[:, None]).astype(ml_dtypes.bfloat16),
        tri=(k <= q).astype(f), ones=np.ones((128, 128), f), maskT=mown.copy(),
        kvec=bc128(np.arange(256, dtype=f)),
    )


CONST_SHAPES = dict(ident=([128, 128], BF16), identf=([128, 128], F32), cosT=([128, NT, 32], F32), sinT=([128, NT, 32], F32),
                    mown=([128, 4, 128], F32), mprev=([128, 4, 128], F32), gmask=([128, 16, 16], F32), blkind=([16, L], BF16),
                    tri=([128, 128], F32), ones=([128, 128], F32), maskT=([128, 128], F32), kvec=([128, 256], F32))
HALF_SHAPES = dict(sinks=[128, 4], convw=[128, 4, 512], convb=[128, 512], dtb=[128, 4], alog=[128, 4],
                   dskip=[128, 256], normw=[128, 256])
LAYER_SHAPES = dict(pre_g=[128, 16], w_out=[D, D], post_g=[128, D],
                    are=[128, 16], aim=[128, 16], ldt=[128, 16], ccre=[128, 16, 32], ccim=[128, 16, 32],
                    bpre=[128, 16, 128], bpim=[128, 16, 128], dfm=[128, 4], glub=[128, 4], gluw=[512, 512])


def in_cols(jh):
    r = lambda a, n: list(range(a, a + n))
    c = (r(0 + jh * 256, 256) + r(512 + jh * 256, 256) + r(1024 + jh * 256, 256) + r(1536 + jh * 256, 256)
         + r(3592 + jh * 256, 256) + r(4104 + jh * 64, 64) + r(4232 + jh * 64, 64) + r(4360 + jh * 256, 256)
         + r(2048 + jh * 256, 256) + r(2560 + jh * 128, 128) + r(2816 + jh * 128, 128) + r(3080 + jh * 256, 256)
         + r(3072 + jh * 4, 4))
    if jh == 0:
        c = c + r(4872, 512) + r(5384, 512)
    return np.array(c)


def half_inputs(inp, l, jh):
    cols = in_cols(jh)
    assert len(cols) == (NP if jh == 0 else NPH)
    cch = np.array(list(range(jh * 256, jh * 256 + 256)) + list(range(512 + jh * 128, 512 + jh * 128 + 128))
                   + list(range(768 + jh * 128, 768 + jh * 128 + 128)))
    return dict(
        w_in=np.ascontiguousarray(inp["w_in"][l][:, cols]),
        sinks=bc128(inp["swa_sinks"][l][4 * jh:4 * jh + 4]),
        convw=bc128(inp["ssd_conv_w"][l][:, cch]), convb=bc128(inp["ssd_conv_b"][l][cch]),
        dtb=bc128(inp["ssd_dt_bias"][l][4 * jh:4 * jh + 4]), alog=bc128(inp["ssd_a_log"][l][4 * jh:4 * jh + 4]),
        dskip=bc128(np.repeat(inp["ssd_d"][l][4 * jh:4 * jh + 4], 64)), normw=bc128(inp["ssd_norm"][l][jh * 256:jh * 256 + 256]),
    )


WOUT_ROWS = np.array([b + jh * 256 + i for jh in range(2) for b in (0, 512, 1024, 1536) for i in range(256)])


def layer_inputs(inp, l):
    d = s5_layouts(inp["s5_a_re"][l], inp["s5_a_im"][l], inp["s5_log_dt"][l], inp["s5_b_re"][l], inp["s5_b_im"][l],
                   inp["s5_c_re"][l], inp["s5_c_im"][l], inp["s5_d"][l], inp["s5_glu_w"][l], inp["s5_glu_b"][l])
    d.update(pre_g=np.ascontiguousarray(inp["pre_norm"][l].reshape(16, 128).T),
             w_out=np.ascontiguousarray(inp["w_out"][l][WOUT_ROWS]), post_g=bc128(inp["post_norm"][l]))
    return d


def load_consts(cx, es, cap):
    S = cx.S
    C = {}
    for nm, key in (("ident", "ident"), ("identf", "identf"), ("cos", "cosT"), ("sin", "sinT")):
        shp, dt = CONST_SHAPES[key]
        C[nm] = cx.sb(es, "c_" + nm, shp, dt)
        S.dma("sp", C[nm][:], cap[key], writes=[C[nm]])
    for key in ("mprev", "mown", "gmask", "blkind", "tri", "ones", "maskT", "kvec", "identf"):
        C["ap_" + key] = cap[key]
    return C


def build_fused(depth=DEPTH):
    nc = bass.Bass("TRN2", target_bir_lowering=False)
    cx = Ctx(nc)
    cx.L = L
    A = lambda n, s, d=F32: nc.dram_tensor(n, list(s), d, kind="ExternalInput").ap()
    x_in = DramT(nc, "x", [L, D], F32, kind="ExternalInput")
    cap = {k: A("k_" + k, s, d) for k, (s, d) in CONST_SHAPES.items()}
    Lw = [{k: A(f"l{l}_{k}", s) for k, s in LAYER_SHAPES.items()} for l in range(depth)]
    Hw = [[dict({k: A(f"l{l}h{jh}_{k}", s) for k, s in HALF_SHAPES.items()},
                w_in=A(f"l{l}h{jh}_w_in", [D, NP if jh == 0 else NPH])) for jh in range(2)] for l in range(depth)]
    proj = DramT(nc, "proj", [L, NP], F32)
    mixc = DramT(nc, "mixc", [L, D], F32)
    xbuf = [DramT(nc, f"xbuf{i}", [L, D], F32) for i in range(2)]
    out = DramT(nc, "out", [L, D], F32, kind="ExternalOutput")
    with ExitStack() as es:
        C = load_consts(cx, es, cap)
        x_cur = x_in
        for l in range(depth):
            x_next = out if l == depth - 1 else xbuf[l % 2]
            for jh in range(2):
                H = Hw[l][jh]
                mcol = jh * MIXH
                phase_inproj(cx, x_cur, H["w_in"], Lw[l]["pre_g"], proj, C["ident"], npc=(NP if jh == 0 else NPH))
                phase_swa(cx, proj, mixc, C["cos"], C["sin"], C["ident"], C["ap_mprev"], C["ap_mown"], H["sinks"], mcol=mcol)
                phase_moba(cx, proj, mixc, C["cos"], C["sin"], C["ident"], C["identf"], C["ap_mown"], C["ap_gmask"],
                           C["ap_blkind"], mcol=mcol)
                ssd_c = {k: H[k] for k in ("convw", "convb", "dtb", "alog", "dskip", "normw")}
                ssd_c.update(tri=C["ap_tri"], ones=C["ap_ones"], maskT=C["ap_maskT"], identf=C["ap_identf"])
                phase_ssd(cx, proj, mixc, C["ident"], ssd_c, mcol=mcol)
                if jh == 0:
                    s5_c = {k: Lw[l][k] for k in ("are", "aim", "ldt", "ccre", "ccim", "bpre", "bpim", "dfm", "glub", "gluw")}
                    s5_c["kvec"] = C["ap_kvec"]
                    phase_s5(cx, proj, mixc, C["ident"], C["identf"], s5_c)
            phase_outproj(cx, mixc, x_cur, Lw[l]["w_out"], Lw[l]["post_g"], x_next, C["ident"], NT)
            x_cur = x_next
        cx.S.finish(out.tiles)
    cx.n_ins = cx.S.n_ins
    return nc


def kernel(**inp):
    inp = {k: np.asarray(v) for k, v in inp.items()}
    x = np.ascontiguousarray(inp["x"], dtype=np.float32)
    nc = build_fused()
    shared = {"k_" + k: v for k, v in static_consts().items()}
    for l in range(DEPTH):
        shared.update({f"l{l}_{k}": v for k, v in layer_inputs(inp, l).items()})
        for jh in range(2):
            shared.update({f"l{l}h{jh}_{k}": v for k, v in half_inputs(inp, l, jh).items()})
    in_maps = []
    for b in range(4):
        m = dict(shared)
        m["x"] = x[b]
        in_maps.append(m)
    res = run_bass_kernel_spmd(nc, in_maps, core_ids=list(range(4)))
    return np.stack([res.results[b]["out"] for b in range(4)]).astype(np.float32)
```

```python
from contextlib import ExitStack
import numpy as np
import ml_dtypes
import concourse.bass as bass
import concourse.mybir as mybir
from concourse.bass_utils import run_bass_kernel_spmd

F32 = mybir.dt.float32
BF16 = mybir.dt.bfloat16
ALU = mybir.AluOpType
AF = mybir.ActivationFunctionType
AX = mybir.AxisListType

D = 2048
L = 4096
NT = L // 128
DEPTH = 4
EPS = 1e-6
C_MQ, C_MK, C_MV, C_MG = 0, 256, 512, 768
C_SQ, C_SK, C_SV, C_SG = 1024, 1280, 1344, 1408
C_XS, C_BM, C_CM, C_Z = 1664, 1920, 2048, 2176
C_DT, C_SU, C_S5G = 2432, 2436, 2948
NP = 3460
NPH = 2436
MIXH = 1024


class Buf:
    __slots__ = ("t", "last_w", "readers")

    def __init__(self, t):
        self.t = t
        self.last_w = None
        self.readers = {}

    def __getitem__(self, k):
        return self.t[k]


class DramT:
    def __init__(self, nc, name, shape, dt, kind="Internal"):
        self.ap = nc.dram_tensor(name, list(shape), dt, kind=kind).ap()
        self.tiles = [Buf(None) for _ in range((shape[0] + 127) // 128)]

    def bufs(self, r0, r1):
        return self.tiles[r0 // 128:(r1 + 127) // 128]


class Eng:
    def __init__(self, key, h, sem):
        self.key, self.h, self.sem = key, h, sem
        self.cnt = 0
        self.seen = {}


class Sched:
    NSLOT = 8

    def __init__(self, nc):
        self.nc = nc
        self.engs = {}
        for key, h in (("pe", nc.tensor), ("act", nc.scalar), ("dve", nc.vector),
                       ("pool", nc.gpsimd), ("sp", nc.sync)):
            self.engs[key] = Eng(key, h, nc.alloc_semaphore(name=f"prog_{key}"))
        self.dma_sems = {}
        self.dma_rings = {}
        self.n_ins = 0
        self.muted = False
        self.same_engine_raw = True

    def _deps(self, reads, writes):
        deps = {}
        for b in reads:
            if b.last_w is not None:
                k, i = b.last_w
                if deps.get(k, 0) < i:
                    deps[k] = i
        for b in writes:
            if b.last_w is not None:
                k, i = b.last_w
                if deps.get(k, 0) < i:
                    deps[k] = i
            for k, i in b.readers.items():
                if deps.get(k, 0) < i:
                    deps[k] = i
        return deps

    def _emit_waits(self, e, deps, same_ok=True):
        for k, i in deps.items():
            if k == e.key and same_ok:
                continue
            if e.seen.get(k, 0) >= i:
                continue
            sem = self.dma_sems[k] if k.startswith("dma") else self.engs[k].sem
            e.h.wait_ge(sem, i)
            e.seen[k] = i
            self.n_ins += 1

    def _record(self, key, idx, reads, writes):
        for b in reads:
            if b.readers.get(key, 0) < idx:
                b.readers[key] = idx
        for b in writes:
            b.last_w = (key, idx)
            b.readers = {}

    def op(self, ek, fn, reads=(), writes=()):
        if self.muted:
            return None
        e = self.engs[ek]
        self._emit_waits(e, self._deps(reads, writes))
        own = 0
        if self.same_engine_raw and ek != "pe":
            for b in reads:
                if b.last_w is not None and b.last_w[0] == ek and b.last_w[1] > own:
                    own = b.last_w[1]
            for b in writes:
                if b.last_w is not None and b.last_w[0] == ek and b.last_w[1] > own:
                    own = b.last_w[1]
                r = b.readers.get(ek, 0)
                if r > own:
                    own = r
        if own > e.seen.get(ek, 0):
            e.h.wait_ge(e.sem, own)
            e.seen[ek] = own
            self.n_ins += 1
        ins = fn(e.h)
        e.cnt += 1
        ins.then_inc(e.sem, 1)
        self._record(ek, e.cnt, reads, writes)
        self.n_ins += 1
        return ins

    def dma(self, ek, out, in_, reads=(), writes=(), **kw):
        if self.muted:
            return None
        e = self.engs[ek]
        deps = self._deps(reads, writes)
        ring = self.dma_rings.setdefault(ek, {"next": 0, "cnt": [0] * self.NSLOT})
        slot = ring["next"]
        ring["next"] = (slot + 1) % self.NSLOT
        qk = f"dma:{ek}:{slot}"
        if qk not in self.dma_sems:
            self.dma_sems[qk] = self.nc.alloc_semaphore(name=f"dma_{ek}_{slot}")
        if ring["cnt"][slot] > 0:
            deps[qk] = max(deps.get(qk, 0), ring["cnt"][slot])
        self._emit_waits(e, deps, same_ok=False)
        ring["cnt"][slot] += 16
        ins = e.h.dma_start(out=out, in_=in_, **kw)
        ins.then_inc(self.dma_sems[qk], 16)
        self._record(qk, ring["cnt"][slot], reads, writes)
        self.n_ins += 1
        return ins

    def barrier(self):
        if self.muted:
            return
        deps = {}
        for k, e in self.engs.items():
            if e.cnt > 0:
                deps[k] = e.cnt
        for ek, ring in self.dma_rings.items():
            for slot, c in enumerate(ring["cnt"]):
                if c > 0:
                    deps[f"dma:{ek}:{slot}"] = c
        for e in self.engs.values():
            self._emit_waits(e, deps)

    def finish(self, bufs):
        self.muted = False
        e = self.engs["sp"]
        self._emit_waits(e, self._deps(bufs, bufs))
        e.h.nop()


class Ctx:
    def __init__(self, nc):
        self.nc = nc
        self.S = Sched(nc)
        self._rr = {}
        self.nt = NT

    def rr(self, key, choices):
        i = self._rr.get(key, 0)
        self._rr[key] = i + 1
        return choices[i % len(choices)]

    def dbg(self, name, buf, shape, dt):
        if not getattr(self, "debug", False):
            return
        o = DramT(self.nc, name, list(shape), dt, kind="ExternalOutput")
        self.S.dma("sp", o.ap, buf[:], reads=[buf], writes=o.tiles)
        self.dbg_outs = getattr(self, "dbg_outs", []) + o.tiles

    def uid(self, name):
        self._uid = getattr(self, "_uid", 0) + 1
        return f"{name}_u{self._uid}"

    def sb(self, es, name, shape, dt):
        return Buf(es.enter_context(self.nc.sbuf_tensor(self.uid(name), list(shape), dt)))

    def ps(self, es, name, shape, dt=F32):
        return Buf(es.enter_context(self.nc.psum_tensor(self.uid(name), list(shape), dt)))


def load_weight_bf16(cx, es, name, w_ap, K, N, scale_sb=None):
    nc, S = cx.nc, cx.S
    KT = K // 128
    Wb = cx.sb(es, name, [128, KT, N], BF16)
    with ExitStack() as es2:
        stg = [cx.sb(es2, f"{name}_stg{i}", [128, N], F32) for i in range(3)]
        for kt in range(KT):
            st = stg[kt % 3]
            S.dma(cx.rr("wq", ["sp", "pool"]), st[:], w_ap[kt * 128:(kt + 1) * 128, :], writes=[st])
            ek = cx.rr("wcast", ["dve", "act"])
            if scale_sb is not None:
                if ek == "dve":
                    S.op("dve", lambda h: h.tensor_scalar(Wb[:, kt, :], st[:], scale_sb[:, kt:kt + 1], None, ALU.mult),
                         [st, scale_sb], [Wb])
                else:
                    S.op("act", lambda h: h.activation(out=Wb[:, kt, :], in_=st[:], func=AF.Copy, scale=scale_sb[:, kt:kt + 1]),
                         [st, scale_sb], [Wb])
            else:
                if ek == "dve":
                    S.op("dve", lambda h: h.tensor_copy(Wb[:, kt, :], st[:]), [st], [Wb])
                else:
                    S.op("act", lambda h: h.copy(Wb[:, kt, :], st[:]), [st], [Wb])
        S.barrier()
    return Wb


def phase_inproj(cx, x_dt, w_ap, g_ap, proj_dt, ident_bf, npc=NP):
    nc, S = cx.nc, cx.S
    KT = D // 128
    with ExitStack() as es:
        g_sb = cx.sb(es, "g_sb", [128, KT], F32)
        S.dma("sp", g_sb[:], g_ap, writes=[g_sb])
        Wb = load_weight_bf16(cx, es, "Win", w_ap, D, npc, g_sb)
        xt = [cx.sb(es, f"xt{i}", [128, D], F32) for i in range(2)]
        xb = [cx.sb(es, f"xb{i}", [128, D], BF16) for i in range(2)]
        junk = cx.sb(es, "junk", [128, D], BF16)
        ss = [cx.sb(es, f"ss{i}", [128, 1], F32) for i in range(2)]
        rstd = [cx.sb(es, f"rstd{i}", [128, 1], F32) for i in range(2)]
        hT = [cx.sb(es, f"hT{i}", [128, KT, 128], BF16) for i in range(2)]
        stage = [cx.sb(es, f"stage{i}", [128, npc], F32) for i in range(2)]
        nch = (npc + 511) // 512
        NACC = 6
        pmb = [cx.ps(es, f"pm{i}", [128, 512], F32) for i in range(min(nch, NACC))]
        pm = [pmb[c % NACC] for c in range(nch)]
        ptl = [cx.ps(es, f"pt{i}", [128, 4, 128], BF16) for i in range(2)]
        def prep(t):
            i2 = t % 2
            X, XB, SS, RS, HT, ST = xt[i2], xb[i2], ss[i2], rstd[i2], hT[i2], stage[i2]
            S.dma("sp", X[:], x_dt.ap[t * 128:(t + 1) * 128, :], reads=x_dt.bufs(t * 128, t * 128 + 128), writes=[X])
            S.op("act", lambda h: h.activation(out=junk[:], in_=X[:], func=AF.Square, accum_out=SS[:]),
                 [X], [junk, SS])
            S.op("pool", lambda h: h.tensor_copy(XB[:], X[:]), [X], [XB])
            S.op("dve", lambda h: h.tensor_scalar(RS[:], SS[:], 1.0 / D, EPS, ALU.mult, ALU.add), [SS], [RS])
            S.op("act", lambda h: h.sqrt(RS[:], RS[:]), [RS], [RS])
            S.op("dve", lambda h: h.reciprocal(RS[:], RS[:]), [RS], [RS])
            for q in range(KT // 4):
                P = ptl[q % 2]
                pv = P[:]

                def tr(h, q=q, pv=pv):
                    for r in range(4):
                        kt = q * 4 + r
                        ins = h.transpose(pv[:, r, :], XB[:, kt * 128:(kt + 1) * 128], ident_bf[:])
                    return ins
                S.op("pe", tr, [XB, ident_bf], [P])
                ek = "act"
                if ek == "dve":
                    S.op("dve", lambda h: h.tensor_copy(HT[:, q * 4:(q + 1) * 4, :], pv), [P], [HT])
                else:
                    S.op("act", lambda h: h.copy(HT[:, q * 4:(q + 1) * 4, :], pv), [P], [HT])

        def compute(t):
            i2 = t % 2
            X, XB, SS, RS, HT, ST = xt[i2], xb[i2], ss[i2], rstd[i2], hT[i2], stage[i2]

            def evac_chunks(cs, ST=ST, RS=RS):
                for c in cs:
                    n0 = c * 512
                    n1 = min(npc, n0 + 512)
                    PM = pm[c]
                    ek = cx.rr("pjev", ["act", "dve"])
                    if ek == "dve":
                        S.op("dve", lambda h: h.tensor_scalar(ST[:, n0:n1], PM[:, 0:n1 - n0], RS[:, 0:1], None, ALU.mult),
                             [PM, RS], [ST])
                    else:
                        S.op("act", lambda h: h.activation(out=ST[:, n0:n1], in_=PM[:, 0:n1 - n0], func=AF.Copy,
                                                           scale=RS[:, 0:1]), [PM, RS], [ST])

            for g0 in range(0, nch, NACC):
                cs = list(range(g0, min(nch, g0 + NACC)))

                def mm(h, HT=HT, cs=cs):
                    for kt in range(KT):
                        for c in cs:
                            n0 = c * 512
                            n1 = min(npc, n0 + 512)
                            ins = h.matmul(pm[c][:, 0:n1 - n0], HT[:, kt, :], Wb[:, kt, n0:n1],
                                           start=(kt == 0), stop=(kt == KT - 1))
                    return ins
                S.op("pe", mm, [HT, Wb], [pm[c] for c in cs])
                evac_chunks(cs)
            if t == 0:
                cx.dbg("d_rs", RS, [128, 1], F32)
                cx.dbg("d_ss", SS, [128, 1], F32)
                cx.dbg("d_xb", XB, [128, D], BF16)
                cx.dbg("d_hT", HT, [128, KT, 128], BF16)
                cx.dbg("d_Wb", Wb, [128, KT, npc], BF16)
            S.dma("pool", proj_dt.ap[t * 128:(t + 1) * 128, 0:npc], ST[:], reads=[ST], writes=proj_dt.bufs(t * 128, t * 128 + 128))

        prep(0)
        for t in range(cx.nt):
            if t + 1 < cx.nt:
                prep(t + 1)
            compute(t)
        S.barrier()


def phase_outproj(cx, mix_dt, x_dt, w_ap, gbc_ap, out_dt, ident_bf, ntiles):
    nc, S = cx.nc, cx.S
    KT = D // 128
    with ExitStack() as es:
        Wb = load_weight_bf16(cx, es, "Wout", w_ap, D, D, None)
        gbc = cx.sb(es, "gbc", [128, D], F32)
        S.dma("sp", gbc[:], gbc_ap, writes=[gbc])
        mt = [cx.sb(es, f"mt{i}", [128, D], F32) for i in range(2)]
        mb = [cx.sb(es, f"mb{i}", [128, D], BF16) for i in range(2)]
        xt = [cx.sb(es, f"oxt{i}", [128, D], F32) for i in range(2)]
        mT = [cx.sb(es, f"mT{i}", [128, KT, 128], BF16) for i in range(2)]
        o = [cx.sb(es, f"o{i}", [128, D], F32) for i in range(2)]
        o2 = [cx.sb(es, f"o2{i}", [128, D], F32) for i in range(2)]
        junk = cx.sb(es, "ojunk", [128, D], BF16)
        ss = [cx.sb(es, f"oss{i}", [128, 1], F32) for i in range(2)]
        pt = [cx.ps(es, f"opt{i}", [128, 4, 128], BF16) for i in range(2)]
        pm = [cx.ps(es, f"opm{i}", [128, 512], F32) for i in range(4)]
        def prep(t):
            i2 = t % 2
            M, MB, X, MT, O, O2, SS = mt[i2], mb[i2], xt[i2], mT[i2], o[i2], o2[i2], ss[i2]
            r0, r1 = t * 128, (t + 1) * 128
            S.dma("sp", M[:], mix_dt.ap[r0:r1, :], reads=mix_dt.bufs(r0, r1), writes=[M])
            S.dma("sp", X[:], x_dt.ap[r0:r1, :], reads=x_dt.bufs(r0, r1), writes=[X])
            S.op("act", lambda h: h.copy(MB[:], M[:]), [M], [MB])
            for q in range(KT // 4):
                P = pt[q % 2]

                def tr(h, q=q, P=P):
                    for r in range(4):
                        kt = q * 4 + r
                        ins = h.transpose(P[:, r, :], MB[:, kt * 128:(kt + 1) * 128], ident_bf[:])
                    return ins
                S.op("pe", tr, [MB, ident_bf], [P])
                if q % 2 == 0:
                    S.op("dve", lambda h: h.tensor_copy(MT[:, q * 4:(q + 1) * 4, :], P[:]), [P], [MT])
                else:
                    S.op("act", lambda h: h.copy(MT[:, q * 4:(q + 1) * 4, :], P[:]), [P], [MT])
        def compute(t):
            i2 = t % 2
            M, MB, X, MT, O, O2, SS = mt[i2], mb[i2], xt[i2], mT[i2], o[i2], o2[i2], ss[i2]
            r0, r1 = t * 128, (t + 1) * 128

            def mm(h, MT=MT):
                for kt in range(KT):
                    for c in range(4):
                        ins = h.matmul(pm[c][:, :], MT[:, kt, :], Wb[:, kt, c * 512:(c + 1) * 512],
                                       start=(kt == 0), stop=(kt == KT - 1))
                return ins
            S.op("pe", mm, [MT, Wb], pm)
            for c in range(4):
                n0, n1 = c * 512, (c + 1) * 512
                PM = pm[c]
                if c % 2 == 0:
                    S.op("act", lambda h: h.copy(O[:, n0:n1], PM[:, :]), [PM], [O])
                else:
                    S.op("dve", lambda h: h.tensor_copy(O[:, n0:n1], PM[:, :]), [PM], [O])
            S.op("act", lambda h: h.activation(out=junk[:], in_=O[:], func=AF.Square, accum_out=SS[:]), [O], [junk, SS])
            S.op("dve", lambda h: h.tensor_scalar(SS[:], SS[:], 1.0 / D, EPS, ALU.mult, ALU.add), [SS], [SS])
            S.op("act", lambda h: h.sqrt(SS[:], SS[:]), [SS], [SS])
            S.op("dve", lambda h: h.reciprocal(SS[:], SS[:]), [SS], [SS])
            S.op("dve", lambda h: h.scalar_tensor_tensor(O2[:], O[:], SS[:, 0:1], gbc[:], ALU.mult, ALU.mult),
                 [O, SS, gbc], [O2])
            if t == 1:
                cx.dbg("d_o", O, [128, D], F32)
                cx.dbg("d_rs", SS, [128, 1], F32)
                cx.dbg("d_o2", O2, [128, D], F32)
            S.op("dve", lambda h: h.tensor_tensor(O2[:], O2[:], X[:], ALU.add), [O2, X], [O2])
            S.dma("pool", out_dt.ap[r0:r1, :], O2[:], reads=[O2], writes=out_dt.bufs(r0, r1))

        prep(0)
        for t in range(ntiles):
            if t + 1 < ntiles:
                prep(t + 1)
            compute(t)
        S.barrier()


def rope_tiles(cx, S, dst_bf, src, nh, cosb, sinb, tmp):
    s4 = src.rearrange("p (h two d) -> p h two d", two=2, d=32)
    d4 = dst_bf.rearrange("p (h two d) -> p h two d", two=2, d=32)
    x1, x2 = s4[:, :, 0, :], s4[:, :, 1, :]
    cb = cosb.unsqueeze(1).broadcast_to([128, nh, 32])
    sb_ = sinb.unsqueeze(1).broadcast_to([128, nh, 32])
    tv = tmp[:].rearrange("p f (h d) -> p f h d", d=32)
    return x1, x2, cb, sb_, tv, d4


def phase_swa(cx, proj_dt, mix_dt, cos_sb, sin_sb, ident_bf, mask_prev_ap, mask_own_ap, sinks_ap, mcol=0):
    nc, S = cx.nc, cx.S
    Lc, NTc = cx.L, cx.L // 128
    with ExitStack() as es:
        QKT = cx.sb(es, "swa_QKT", [64, 5, Lc], BF16)
        Vaug = cx.sb(es, "swa_V", [128, NTc, 65], BF16)
        mprev = cx.sb(es, "swa_mprev", [128, 4, 128], F32)
        mown = cx.sb(es, "swa_mown", [128, 4, 128], F32)
        esink = cx.sb(es, "swa_esink", [128, 4], F32)
        S.dma("sp", mprev[:], mask_prev_ap, writes=[mprev])
        S.dma("sp", mown[:], mask_own_ap, writes=[mown])
        S.dma("sp", esink[:], sinks_ap, writes=[esink])
        S.op("act", lambda h: h.activation(out=esink[:], in_=esink[:], func=AF.Exp), [esink], [esink])
        S.op("pool", lambda h: h.memset(Vaug[:, :, 64:65], 1.0), [], [Vaug])
        mprev_b = cx.sb(es, "swa_mprev_b", [128, 4, 128], BF16)
        mown_b = cx.sb(es, "swa_mown_b", [128, 4, 128], BF16)
        S.op("dve", lambda h: h.tensor_scalar(mprev_b[:], mprev[:], 8.0, None, ALU.mult), [mprev], [mprev_b])
        S.op("dve", lambda h: h.tensor_scalar(mown_b[:], mown[:], 8.0, None, ALU.mult), [mown], [mown_b])
        tin = [cx.sb(es, f"swa_tin{i}", [128, 640], F32) for i in range(2)]
        Gs = cx.sb(es, "swa_G", [128, NTc, 256], F32)
        rtmp = [cx.sb(es, f"swa_rtmp{i}", [128, 4, 160], F32) for i in range(2)]
        rb = [cx.sb(es, f"swa_rb{i}", [128, 320], BF16) for i in range(2)]
        ptr = [cx.ps(es, f"swa_ptr{i}", [64, 8, 128], BF16) for i in range(2)]
        for t in range(NTc):
            i2 = t % 2
            T, TMP, RB, PT = tin[i2], rtmp[i2], rb[i2], ptr[i2]
            r0, r1 = t * 128, (t + 1) * 128
            S.dma("sp", T[:], proj_dt.ap[r0:r1, C_SQ:C_SQ + 640],
                  reads=proj_dt.bufs(r0, r1), writes=[T])
            S.op("act", lambda h: h.activation(out=Gs[:, t, :], in_=T[:, 384:640], func=AF.Silu), [T], [Gs])
            x1, x2, cb, sb_, tv, d4 = rope_tiles(cx, S, RB[:], T[:, 0:320], 5, cos_sb[:, t, :], sin_sb[:, t, :], TMP)
            S.op("dve", lambda h: h.tensor_tensor(tv[:, 0], x1, cb, ALU.mult), [T, cos_sb], [TMP])
            S.op("dve", lambda h: h.tensor_tensor(tv[:, 1], x2, sb_, ALU.mult), [T, sin_sb], [TMP])
            S.op("dve", lambda h: h.tensor_tensor(tv[:, 2], x2, cb, ALU.mult), [T, cos_sb], [TMP])
            S.op("dve", lambda h: h.tensor_tensor(tv[:, 3], x1, sb_, ALU.mult), [T, sin_sb], [TMP])
            S.op("dve", lambda h: h.tensor_tensor(d4[:, :, 0, :], tv[:, 0], tv[:, 1], ALU.subtract), [TMP], [RB])
            S.op("dve", lambda h: h.tensor_tensor(d4[:, :, 1, :], tv[:, 2], tv[:, 3], ALU.add), [TMP], [RB])
            S.op("pool", lambda h: h.tensor_copy(Vaug[:, t, 0:64], T[:, 320:384]), [T], [Vaug])

            def tr(h, RB=RB, PT=PT):
                for hh in range(5):
                    ins = h.transpose(PT[:, hh, :], RB[:, hh * 64:(hh + 1) * 64], ident_bf[:])
                return ins
            S.op("pe", tr, [RB, ident_bf], [PT])
            S.op("act", lambda h: h.copy(QKT[:, :, r0:r1], PT[:, 0:5, :]), [PT], [QKT])
        sg = [cx.sb(es, f"swa_sg{i}", [128, 256], F32) for i in range(2)]
        st = [cx.sb(es, f"swa_st{i}", [128, 4, 128], F32) for i in range(2)]
        pT = [cx.sb(es, f"swa_pT{i}", [128, 4, 128], BF16) for i in range(4)]
        den = [cx.sb(es, f"swa_den{i}", [128, 4], F32) for i in range(2)]
        yb = [cx.sb(es, f"swa_y{i}", [128, 256], F32) for i in range(2)]
        pss = [cx.ps(es, f"swa_pss{i}", [128, 4, 128], F32) for i in range(2)]
        pso = [cx.ps(es, f"swa_pso{i}", [128, 4, 128], F32) for i in range(2)]
        sge = [cx.sb(es, f"swa_sge{i}", [128, 256], F32) for i in range(2)]
        items = [(t, kk) for t in range(NTc) for kk in ([t - 1, t] if t > 0 else [t])]

        def score(i):
            t, kk = items[i]
            r0, r1 = t * 128, (t + 1) * 128
            PS = pss[i % 2]
            Mb = mown_b if kk == t else mprev_b

            def sc(h):
                h.matmul(PS[:], QKT[:, 4, kk * 128:(kk + 1) * 128], QKT[:, 0:4, r0:r1], start=True, stop=False)
                return h.matmul(PS[:], ident_bf[:], Mb[:], start=False, stop=True)
            S.op("pe", sc, [QKT, ident_bf, Mb], [PS])

        def finish_item(i):
            t, kk = items[i]
            r0, r1 = t * 128, (t + 1) * 128
            PS, ST, PTB, PO = pss[i % 2], st[i % 2], pT[i % 4], pso[t % 2]
            S.op("act", lambda h: h.activation(out=PTB[:], in_=PS[:], func=AF.Exp, scale=0.125), [PS], [PTB])
            first = (kk == max(t - 1, 0))

            def pv(h):
                for hh in range(4):
                    ins = h.matmul(PO[:, hh, 0:65], PTB[:, hh, :], Vaug[:, kk, :],
                                   start=(first and hh == 0), stop=(kk == t and hh == 3))
                return ins
            S.op("pe", pv, [PTB, Vaug], [PO])
            if kk == t:
                DEN, Y = den[t % 2], yb[t % 2]
                S.op("dve", lambda h: h.tensor_tensor(DEN[:], PO[:, :, 64], esink[:], ALU.add), [PO, esink], [DEN])
                S.op("dve", lambda h: h.reciprocal(DEN[:], DEN[:]), [DEN], [DEN])
                for hh in range(4):
                    S.op("act", lambda h: h.activation(out=Y[:, hh * 64:(hh + 1) * 64], in_=PO[:, hh, 0:64], func=AF.Copy,
                                                       scale=DEN[:, hh:hh + 1]), [PO, DEN], [Y])
                S.op("dve", lambda h: h.tensor_tensor(Y[:], Y[:], Gs[:, t, :], ALU.mult), [Y, Gs], [Y])
                S.dma("pool", mix_dt.ap[r0:r1, mcol + 512:mcol + 768], Y[:], reads=[Y], writes=mix_dt.bufs(r0, r1))

        score(0)
        for i in range(len(items)):
            if i + 1 < len(items):
                score(i + 1)
            finish_item(i)
        S.barrier()


def phase_moba(cx, proj_dt, mix_dt, cos_sb, sin_sb, ident_bf, ident_f, mask_own_ap, gmask_ap, blkind_ap, mcol=0):
    nc, S = cx.nc, cx.S
    Lc, NTc = cx.L, cx.L // 128
    NB = 16
    with ExitStack() as es:
        Q32 = cx.sb(es, "mo_Q32", [128, NTc, 256], F32)
        KaugT = cx.sb(es, "mo_KaugT", [80, 4, Lc], BF16)
        Vaug = cx.sb(es, "mo_V", [128, NTc, 4, 65], BF16)
        kmeanT = cx.sb(es, "mo_kmT", [64, 4, NB], F32)
        mown = cx.sb(es, "mo_mown", [128, 4, 128], F32)
        gmask = cx.sb(es, "mo_gmask", [128, NB, NB], F32)
        c256 = cx.sb(es, "mo_c256", [128, 1], F32)
        S.dma("sp", mown[:], mask_own_ap, writes=[mown])
        S.dma("sp", gmask[:], gmask_ap, writes=[gmask])
        for hh in range(4):
            S.dma("sp", KaugT[64:80, hh, :], blkind_ap[:, 0:Lc], writes=[KaugT])
        S.op("pool", lambda h: h.memset(c256[:], 1.0 / 256.0), [], [c256])
        S.op("pool", lambda h: h.memset(Vaug[:, :, :, 64:65], 1.0), [], [Vaug])
        S.op("pool", lambda h: h.memset(kmeanT[:], 0.0), [], [kmeanT])
        tin = [cx.sb(es, f"mo_tin{i}", [128, 1024], F32) for i in range(2)]
        Gm = cx.sb(es, "mo_G", [128, NTc, 256], F32)
        rtmp = [cx.sb(es, f"mo_rtmp{i}", [128, 4, 256], F32) for i in range(2)]
        k32 = [cx.sb(es, f"mo_k32{i}", [128, 256], F32) for i in range(2)]
        kb = [cx.sb(es, f"mo_kb{i}", [128, 256], BF16) for i in range(2)]
        with ExitStack() as es1:
            ptr = [cx.ps(es1, f"mo_ptr{i}", [64, 8, 128], BF16) for i in range(2)]
            kmps = cx.ps(es1, "mo_kmps", [64, 4, 128], F32)
            def prepA(t):
                i2 = t % 2
                T, TMP, K32, KB, PT = tin[i2], rtmp[i2], k32[i2], kb[i2], ptr[i2]
                r0, r1 = t * 128, (t + 1) * 128
                S.dma("sp", T[:], proj_dt.ap[r0:r1, C_MQ:C_MQ + 1024],
                      reads=proj_dt.bufs(r0, r1), writes=[T])
                S.op("act", lambda h: h.activation(out=Gm[:, t, :], in_=T[:, 768:1024], func=AF.Silu), [T], [Gm])
                s4 = T[:, 0:512].rearrange("p (h two d) -> p h two d", two=2, d=32)
                x1, x2 = s4[:, :, 0, :], s4[:, :, 1, :]
                cb = cos_sb[:, t, :].unsqueeze(1).broadcast_to([128, 8, 32])
                sb_ = sin_sb[:, t, :].unsqueeze(1).broadcast_to([128, 8, 32])
                tv = TMP[:].rearrange("p f (h d) -> p f h d", d=32)
                S.op("dve", lambda h: h.tensor_tensor(tv[:, 0], x1, cb, ALU.mult), [T, cos_sb], [TMP])
                S.op("dve", lambda h: h.tensor_tensor(tv[:, 1], x2, sb_, ALU.mult), [T, sin_sb], [TMP])
                S.op("dve", lambda h: h.tensor_tensor(tv[:, 2], x2, cb, ALU.mult), [T, cos_sb], [TMP])
                S.op("dve", lambda h: h.tensor_tensor(tv[:, 3], x1, sb_, ALU.mult), [T, sin_sb], [TMP])
                q4 = Q32[:, t, :].rearrange("p (h two d) -> p h two d", two=2, d=32)
                k4 = K32[:].rearrange("p (h two d) -> p h two d", two=2, d=32)
                S.op("dve", lambda h: h.tensor_tensor(q4[:, :, 0, :], tv[:, 0, 0:4], tv[:, 1, 0:4], ALU.subtract), [TMP], [Q32])
                S.op("dve", lambda h: h.tensor_tensor(q4[:, :, 1, :], tv[:, 2, 0:4], tv[:, 3, 0:4], ALU.add), [TMP], [Q32])
                S.op("dve", lambda h: h.tensor_tensor(k4[:, :, 0, :], tv[:, 0, 4:8], tv[:, 1, 4:8], ALU.subtract), [TMP], [K32])
                S.op("dve", lambda h: h.tensor_tensor(k4[:, :, 1, :], tv[:, 2, 4:8], tv[:, 3, 4:8], ALU.add), [TMP], [K32])

            def prepB(t):
                i2 = t % 2
                T, TMP, K32, KB, PT = tin[i2], rtmp[i2], k32[i2], kb[i2], ptr[i2]
                r0, r1 = t * 128, (t + 1) * 128
                S.op("act", lambda h: h.copy(KB[:], K32[:]), [K32], [KB])
                S.op("pool", lambda h: h.tensor_copy(Vaug[:, t, :, 0:64], T[:, 512:768].rearrange("p (h d) -> p h d", d=64)),
                     [T], [Vaug])
                n = t // 2

                def km(h, K32=K32, n=n, t=t):
                    for hh in range(4):
                        ins = h.matmul(kmps[:, hh, n:n + 1], K32[:, hh * 64:(hh + 1) * 64], c256[:, 0:1],
                                       start=(t % 2 == 0 and hh == 0), stop=(t % 2 == 1 and hh == 3))
                    return ins
                S.op("pe", km, [K32, c256], [kmps])

                def tr(h, KB=KB, PT=PT):
                    for hh in range(4):
                        ins = h.transpose(PT[:, hh, :], KB[:, hh * 64:(hh + 1) * 64], ident_bf[:])
                    return ins
                S.op("pe", tr, [KB, ident_bf], [PT])
                S.op("act", lambda h: h.copy(KaugT[0:64, :, r0:r1], PT[:, 0:4, :]), [PT], [KaugT])
            prepA(0)
            for t in range(NTc):
                if t + 1 < NTc:
                    prepA(t + 1)
                prepB(t)
            S.op("dve", lambda h: h.tensor_copy(kmeanT[:, :, 0:Lc // 256], kmps[:, :, 0:Lc // 256]), [kmps], [kmeanT])
            S.barrier()
        mg = [cx.sb(es, f"mo_mg{i}", [128, 256], F32) for i in range(2)]
        mge = [cx.sb(es, f"mo_mge{i}", [128, 256], F32) for i in range(2)]
        qT32 = [cx.sb(es, f"mo_qT32{i}", [64, 4, 128], F32) for i in range(2)]
        gm = [cx.sb(es, f"mo_gm{i}", [128, 4, NB], F32) for i in range(2)]
        top8 = [cx.sb(es, f"mo_top8{i}", [128, 4, 8], F32) for i in range(2)]
        qaug = [cx.sb(es, f"mo_qaug{i}", [128, 4, 80], BF16) for i in range(2)]
        QaugT = [cx.sb(es, f"mo_QaugT{i}", [80, 4, 128], BF16) for i in range(2)]
        st = [cx.sb(es, f"mo_st{i}", [128, 4, 128], F32) for i in range(2)]
        pT = [cx.sb(es, f"mo_pT{i}", [128, 4, 128], BF16) for i in range(4)]
        den = [cx.sb(es, f"mo_den{i}", [128, 4], F32) for i in range(2)]
        yb = [cx.sb(es, f"mo_y{i}", [128, 256], F32) for i in range(2)]
        psq = cx.ps(es, "mo_psq", [64, 4, 128], F32)
        psg = cx.ps(es, "mo_psg", [128, 4, 128], F32)
        psa = cx.ps(es, "mo_psa", [80, 8, 128], BF16)
        pss = [cx.ps(es, f"mo_pss{i}", [128, 4, 128], F32) for i in range(2)]
        pso = [cx.ps(es, f"mo_pso{i}", [128, 4, 128], F32) for i in range(2)]
        ctxs = {}

        def prologue(t):
            i2 = t % 2
            r0, r1 = t * 128, (t + 1) * 128
            qb = t // 2
            MG, QT, GM, T8, QA, QAT = mg[i2], qT32[i2], gm[i2], top8[i2], qaug[i2], QaugT[i2]

            def trq(h):
                for hh in range(4):
                    ins = h.transpose(psq[:, hh, :], Q32[:, t, hh * 64:(hh + 1) * 64], ident_f[:])
                return ins
            S.op("pe", trq, [Q32, ident_f], [psq])
            S.op("dve", lambda h: h.tensor_copy(QT[:], psq[:]), [psq], [QT])

            def gate(h):
                for hh in range(4):
                    ins = h.matmul(psg[:, hh, 0:NB], QT[:, hh, :], kmeanT[:, hh, :], start=True, stop=True)
                return ins
            S.op("pe", gate, [QT, kmeanT], [psg])
            S.op("dve", lambda h: h.tensor_tensor(GM[:], psg[:, :, 0:NB],
                                                  gmask[:, qb, :].unsqueeze(1).broadcast_to([128, 4, NB]), ALU.add),
                 [psg, gmask], [GM])
            for hh in range(4):
                S.op("dve", lambda h: h.max(T8[:, hh, :], GM[:, hh, :]), [GM], [T8])
            for hh in range(4):
                S.op("dve", lambda h: h.tensor_scalar(QA[:, hh, 64:80], GM[:, hh, :], T8[:, hh, 2:3], -30000.0,
                                                      ALU.is_lt, ALU.mult), [GM, T8], [QA])
            S.op("pool", lambda h: h.memset(QA[:, :, 64 + qb:65 + qb], 0.0), [QA], [QA])
            S.op("act", lambda h: h.copy(QA[:, :, 0:64], Q32[:, t, :].rearrange("p (h d) -> p h d", d=64)), [Q32], [QA])

            def tra(h):
                for hh in range(4):
                    ins = h.transpose(psa[:, hh, :], QA[:, hh, :], ident_bf[:])
                return ins
            S.op("pe", tra, [QA, ident_bf], [psa])
            S.op("dve", lambda h: h.tensor_copy(QAT[:], psa[:, 0:4, :]), [psa], [QAT])

        items = [(t, kk) for t in range(NTc) for kk in range(t + 1)]

        def score(i):
            t, kk = items[i]
            if kk == 0:
                prologue(t)
            PS, QAT = pss[i % 2], QaugT[t % 2]

            def sc(h):
                for hh in range(4):
                    ins = h.matmul(PS[:, hh, :], KaugT[:, hh, kk * 128:(kk + 1) * 128], QAT[:, hh, :],
                                   start=True, stop=True)
                return ins
            S.op("pe", sc, [KaugT, QAT], [PS])

        def finish_item(i):
            t, kk = items[i]
            PS, ST, PTB, PO = pss[i % 2], st[i % 2], pT[i % 4], pso[t % 2]
            if kk == t:
                S.op("dve", lambda h: h.scalar_tensor_tensor(ST[:], PS[:], 0.125, mown[:], ALU.mult, ALU.add),
                     [PS, mown], [ST])
                S.op("act", lambda h: h.activation(out=PTB[:], in_=ST[:], func=AF.Exp), [ST], [PTB])
            else:
                S.op("act", lambda h: h.activation(out=PTB[:], in_=PS[:], func=AF.Exp, scale=0.125), [PS], [PTB])

            def pv(h):
                for hh in range(4):
                    ins = h.matmul(PO[:, hh, 0:65], PTB[:, hh, :], Vaug[:, kk, hh, :],
                                   start=(kk == 0 and hh == 0), stop=(kk == t and hh == 3))
                return ins
            S.op("pe", pv, [PTB, Vaug], [PO])
            if kk == t:
                i2 = t % 2
                r0, r1 = t * 128, (t + 1) * 128
                MG, DEN, Y = mg[i2], den[i2], yb[i2]
                S.op("dve", lambda h: h.reciprocal(DEN[:], PO[:, :, 64]), [PO], [DEN])
                for hh in range(4):
                    S.op("act", lambda h: h.activation(out=Y[:, hh * 64:(hh + 1) * 64], in_=PO[:, hh, 0:64], func=AF.Copy,
                                                       scale=DEN[:, hh:hh + 1]), [PO, DEN], [Y])
                S.op("dve", lambda h: h.tensor_tensor(Y[:], Y[:], Gm[:, t, :], ALU.mult), [Y, Gm], [Y])
                S.dma("pool", mix_dt.ap[r0:r1, mcol:mcol + 256], Y[:], reads=[Y], writes=mix_dt.bufs(r0, r1))

        score(0)
        for i in range(len(items)):
            if i + 1 < len(items):
                score(i + 1)
            finish_item(i)
        S.barrier()


def phase_ssd(cx, proj_dt, mix_dt, ident_bf, cst, mcol=0):
    nc, S = cx.nc, cx.S
    Lc, NTc = cx.L, cx.L // 128
    with ExitStack() as es:
        def ld(name, shape, dt=F32):
            b = cx.sb(es, "ssd_" + name, shape, dt)
            S.dma("sp", b[:], cst[name], writes=[b])
            return b
        convw = ld("convw", [128, 4, 512]); convb = ld("convb", [128, 512]); dtb = ld("dtb", [128, 4])
        Abc = ld("alog", [128, 4]); dskip = ld("dskip", [128, 256]); normw = ld("normw", [128, 256])
        tri = ld("tri", [128, 128]); ones = ld("ones", [128, 128]); maskT = ld("maskT", [128, 128])
        S.op("act", lambda h: h.activation(out=Abc[:], in_=Abc[:], func=AF.Exp), [Abc], [Abc])
        S.op("dve", lambda h: h.tensor_scalar(Abc[:], Abc[:], -1.0, None, ALU.mult), [Abc], [Abc])
        prev32 = cx.sb(es, "ssd_prev32", [128, 256], F32)
        prevb = cx.sb(es, "ssd_prevb", [128, 256], BF16)
        S.op("pool", lambda h: h.memset(prev32[:], 0.0), [], [prev32])
        S.op("pool", lambda h: h.memset(prevb[:], 0.0), [], [prevb])
        Tj = [[cx.sb(es, f"ssd_T{i}_{j}", [128, 512], F32) for j in range(4)] for i in range(2)]
        zt = [cx.sb(es, f"ssd_z{i}", [128, 256], F32) for i in range(2)]
        dtt = [cx.sb(es, f"ssd_dt{i}", [128, 4], F32) for i in range(2)]
        xa = [cx.sb(es, f"ssd_xa{i}", [128, 512], F32) for i in range(2)]
        sm = [cx.sb(es, f"ssd_sm{i}", [128, 8, 4], F32) for i in range(2)]
        Xf = [cx.sb(es, f"ssd_X{i}", [128, 256], F32) for i in range(2)]
        Xb = [cx.sb(es, f"ssd_Xb{i}", [128, 256], BF16) for i in range(2)]
        Xd = [cx.sb(es, f"ssd_Xd{i}", [128, 256], BF16) for i in range(2)]
        BCb = [cx.sb(es, f"ssd_BCb{i}", [128, 256], BF16) for i in range(2)]
        BCT = [cx.sb(es, f"ssd_BCT{i}", [128, 2, 128], BF16) for i in range(2)]
        R4 = [cx.sb(es, f"ssd_R4{i}", [128, 4, 128], F32) for i in range(2)]
        TD4 = [cx.sb(es, f"ssd_TD4{i}", [128, 4, 128], F32) for i in range(2)]
        M4 = [cx.sb(es, f"ssd_M4{i}", [128, 4, 128], BF16) for i in range(2)]
        identf = ld("identf", [128, 128])
        mask4 = cx.sb(es, "ssd_mask4", [128, 4, 128], F32)
        S.op("dve", lambda h: h.tensor_copy(mask4[:], maskT[:].unsqueeze(1).broadcast_to([128, 4, 128])), [maskT], [mask4])
        y1 = [cx.sb(es, f"ssd_y1{i}", [128, 256], F32) for i in range(2)]
        y2 = [cx.sb(es, f"ssd_y2{i}", [128, 256], F32) for i in range(2)]
        junk = cx.sb(es, "ssd_junk", [128, 256], F32)
        csb = [cx.sb(es, f"ssd_cs{i}", [128, 256], F32) for i in range(2)]
        ps_t = cx.ps(es, "ssd_ps_t", [128, 8, 128], BF16)
        ps_s = cx.ps(es, "ssd_ps_s", [128, 512], F32)
        ps_cb = cx.ps(es, "ssd_ps_cb", [128, 512], F32)
        ps_d = cx.ps(es, "ssd_ps_d", [128, 4, 128], F32)
        ps_yd = cx.ps(es, "ssd_ps_yd", [128, 512], F32)
        ps_yo = cx.ps(es, "ssd_ps_yo", [128, 512], F32)
        ps_cs = cx.ps(es, "ssd_ps_cs", [128, 512], F32)
        ndc = [0]

        def front(t):
            nd = ndc[0]
            i2 = t % 2
            r0, r1 = t * 128, (t + 1) * 128
            T, Z, DT, XA, SM, X, XB, XD, BC, BT = Tj[i2], zt[i2], dtt[i2], xa[i2], sm[i2], Xf[i2], Xb[i2], Xd[i2], BCb[i2], BCT[i2]
            for j in range(4):
                sh = 3 - j
                q = "sp"
                if r0 - sh >= 0:
                    S.dma(q, T[j][:], proj_dt.ap[r0 - sh:r1 - sh, C_XS:C_XS + 512],
                          reads=proj_dt.bufs(max(r0 - sh, 0), r1), writes=[T[j]])
                else:
                    S.op("pool", lambda h: h.memset(T[j][0:32, :], 0.0), [], [T[j]])
                    S.dma(q, T[j][sh:128, :], proj_dt.ap[0:128 - sh, C_XS:C_XS + 512],
                          reads=proj_dt.bufs(0, 128), writes=[T[j]])
            S.dma("sp", Z[:], proj_dt.ap[r0:r1, C_Z:C_Z + 256], reads=proj_dt.bufs(r0, r1), writes=[Z])
            S.dma("sp", DT[:], proj_dt.ap[r0:r1, C_DT:C_DT + 4], reads=proj_dt.bufs(r0, r1), writes=[DT])
            for j in range(4):
                ek = "dve"
                S.op(ek, lambda h: h.tensor_tensor(T[j][:], T[j][:], convw[:, j, :], ALU.mult), [T[j], convw], [T[j]])
            S.op("dve", lambda h: h.tensor_tensor(T[0][:], T[0][:], T[2][:], ALU.add), [T[0], T[2]], [T[0]])
            S.op("dve", lambda h: h.tensor_tensor(T[1][:], T[1][:], T[3][:], ALU.add), [T[1], T[3]], [T[1]])
            S.op("dve", lambda h: h.tensor_tensor(T[0][:], T[0][:], T[1][:], ALU.add), [T[0], T[1]], [T[0]])
            S.op("dve", lambda h: h.tensor_tensor(T[0][:], T[0][:], convb[:], ALU.add), [T[0], convb], [T[0]])
            S.op("act", lambda h: h.activation(out=XA[:], in_=T[0][:], func=AF.Silu), [T[0]], [XA])
            S.op("act", lambda h: h.activation(out=Z[:], in_=Z[:], func=AF.Silu), [Z], [Z])
            S.op("dve", lambda h: h.tensor_tensor(DT[:], DT[:], dtb[:], ALU.add), [DT, dtb], [DT])
            S.op("act", lambda h: h.activation(out=DT[:], in_=DT[:], func=AF.Exp), [DT], [DT])
            S.op("act", lambda h: h.activation(out=DT[:], in_=DT[:], func=AF.Ln, bias=1.0), [DT], [DT])
            S.op("dve", lambda h: h.tensor_tensor(SM[:, 0, :], DT[:], Abc[:], ALU.mult), [DT, Abc], [SM])

            def cum(h, SM=SM):
                h.matmul(ps_s[:, 0:4], tri[:], SM[:, 0, :], start=True, stop=False)
                return h.matmul(ps_s[:, 4:8], ones[:], SM[:, 0, :], start=False, stop=True)
            S.op("pe", cum, [tri, ones, SM], [ps_s])
            S.op("dve", lambda h: h.tensor_copy(SM[:, 1, :], ps_s[:, 0:4]), [ps_s], [SM])
            S.op("dve", lambda h: h.tensor_scalar(SM[:, 2, :], ps_s[:, 0:4], -1.0, None, ALU.mult), [ps_s], [SM])
            S.op("dve", lambda h: h.tensor_tensor(SM[:, 5, :], ps_s[:, 4:8], SM[:, 1, :], ALU.subtract), [ps_s, SM], [SM])
            S.op("act", lambda h: h.activation(out=SM[:, 3, :], in_=SM[:, 1, :], func=AF.Exp), [SM], [SM])
            S.op("act", lambda h: h.activation(out=SM[:, 4, :], in_=ps_s[:, 4:8], func=AF.Exp), [ps_s], [SM])
            S.op("act", lambda h: h.activation(out=SM[:, 5, :], in_=SM[:, 5, :], func=AF.Exp), [SM], [SM])
            x3 = X[:].rearrange("p (h d) -> p h d", d=64)
            S.op("dve", lambda h: h.tensor_tensor(x3, XA[:, 0:256].rearrange("p (h d) -> p h d", d=64),
                                                  DT[:].unsqueeze(2).broadcast_to([128, 4, 64]), ALU.mult), [XA, DT], [X])
            S.op("act", lambda h: h.copy(XB[:], X[:]), [X], [XB])
            S.op("dve", lambda h: h.tensor_tensor(XD[:].rearrange("p (h d) -> p h d", d=64), x3,
                                                  SM[:, 5, :].unsqueeze(2).broadcast_to([128, 4, 64]), ALU.mult), [X, SM], [XD])
            S.op("act", lambda h: h.copy(BC[:], XA[:, 256:512]), [XA], [BC])

            def trbc(h, BC=BC):
                h.transpose(ps_t[:, 0, :], BC[:, 0:128], ident_bf[:])
                return h.transpose(ps_t[:, 1, :], BC[:, 128:256], ident_bf[:])
            S.op("pe", trbc, [BC, ident_bf], [ps_t])
            S.op("act", lambda h: h.copy(BT[:], ps_t[:, 0:2, :]), [ps_t], [BT])
            S.op("pe", lambda h: h.matmul(ps_cb[:, 0:128], BT[:, 0, :], BT[:, 1, :], start=True, stop=True), [BT], [ps_cb])
            S.op("pe", lambda h: h.matmul(ps_cs[:, 0:256], BC[:, 0:128], XD[:], start=True, stop=True), [BC, XD], [ps_cs])
            S.op("act", lambda h: h.copy(csb[i2][:], ps_cs[:, 0:256]), [ps_cs], [csb[i2]])
            RR, TD, MM = R4[i2], TD4[i2], M4[i2]
            S.op("dve", lambda h: h.tensor_tensor(RR[:], tri[:].unsqueeze(1).broadcast_to([128, 4, 128]),
                                                  SM[:, 0, :].unsqueeze(2).broadcast_to([128, 4, 128]), ALU.mult), [tri, SM], [RR])

            def dbc(h):
                h.matmul(ps_d[:], ones[:], RR[:], start=True, stop=False)
                return h.matmul(ps_d[:], identf[:], mask4[:], start=False, stop=True)
            S.op("pe", dbc, [ones, RR, identf, mask4], [ps_d])
            S.op("dve", lambda h: h.tensor_tensor(TD[:], ps_d[:], SM[:, 1, :].unsqueeze(2).broadcast_to([128, 4, 128]), ALU.subtract),
                 [ps_d, SM], [TD])
            S.op("act", lambda h: h.activation(out=TD[:], in_=TD[:], func=AF.Exp), [TD], [TD])
            S.op("dve", lambda h: h.tensor_tensor(MM[:], TD[:], ps_cb[:, 0:128].unsqueeze(1).broadcast_to([128, 4, 128]), ALU.mult),
                 [TD, ps_cb], [MM])

            def ydiag(h):
                for hh in range(4):
                    ins = h.matmul(ps_yd[:, hh * 64:(hh + 1) * 64], MM[:, hh, :], XB[:, hh * 64:(hh + 1) * 64],
                                   start=(hh == 0), stop=(hh == 3))
                return ins
            S.op("pe", ydiag, [MM, XB], [ps_yd])
            ndc[0] = nd
            S.op("act", lambda h: h.copy(y1[i2][:], ps_yd[:, 0:256]), [ps_yd], [y1[i2]])

        def back(t):
            i2 = t % 2
            r0, r1 = t * 128, (t + 1) * 128
            Z, XA, SM, BT = zt[i2], xa[i2], sm[i2], BCT[i2]
            Y1, Y2 = y1[i2], y2[i2]
            S.op("pe", lambda h: h.matmul(ps_yo[:, 0:256], BT[:, 1, :], prevb[:], start=True, stop=True), [BT, prevb], [ps_yo])
            p3 = prev32[:].rearrange("p (h d) -> p h d", d=64)
            S.op("dve", lambda h: h.tensor_tensor(p3, p3, SM[:, 4, :].unsqueeze(2).broadcast_to([128, 4, 64]), ALU.mult),
                 [prev32, SM], [prev32])
            S.op("dve", lambda h: h.tensor_tensor(prev32[:], prev32[:], csb[i2][:], ALU.add), [prev32, csb[i2]], [prev32])
            S.op("act", lambda h: h.copy(prevb[:], prev32[:]), [prev32], [prevb])
            for hh in range(4):
                sl = slice(hh * 64, (hh + 1) * 64)
                S.op("dve", lambda h: h.scalar_tensor_tensor(Y1[:, sl], ps_yo[:, sl], SM[:, 3, hh:hh + 1], Y1[:, sl],
                                                             ALU.mult, ALU.add), [ps_yo, SM, Y1], [Y1])
            S.op("pool", lambda h: h.tensor_tensor(Y2[:], XA[:, 0:256], dskip[:], ALU.mult), [XA, dskip], [Y2])
            S.op("dve", lambda h: h.tensor_tensor(Y1[:], Y1[:], Y2[:], ALU.add), [Y1, Y2], [Y1])
            S.op("dve", lambda h: h.tensor_tensor(Y1[:], Y1[:], Z[:], ALU.mult), [Y1, Z], [Y1])
            S.op("act", lambda h: h.activation(out=junk[:], in_=Y1[:], func=AF.Square, accum_out=SM[:, 6, 0:1]), [Y1], [junk, SM])
            S.op("dve", lambda h: h.tensor_scalar(SM[:, 6, 0:1], SM[:, 6, 0:1], 1.0 / 256.0, EPS, ALU.mult, ALU.add), [SM], [SM])
            S.op("act", lambda h: h.activation(out=SM[:, 6, 0:1], in_=SM[:, 6, 0:1], func=AF.Ln), [SM], [SM])
            S.op("act", lambda h: h.activation(out=SM[:, 6, 0:1], in_=SM[:, 6, 0:1], func=AF.Exp, scale=-0.5), [SM], [SM])
            S.op("dve", lambda h: h.scalar_tensor_tensor(Y2[:], Y1[:], SM[:, 6, 0:1], normw[:], ALU.mult, ALU.mult),
                 [Y1, SM, normw], [Y2])
            S.dma("pool", mix_dt.ap[r0:r1, mcol + 256:mcol + 512], Y2[:], reads=[Y2], writes=mix_dt.bufs(r0, r1))

        front(0)
        for t in range(NTc):
            if t + 1 < NTc:
                front(t + 1)
            back(t)
        S.barrier()


class StopPhase(Exception):
    pass


def stage(cx, n):
    if getattr(cx, "stop_stage", None) == n and not cx.S.muted:
        cx.S.barrier()
        cx.S.muted = True


def cmul(S, ek, out_re, out_im, a_re, a_im, b_re, b_im, t1, t2, reads, writes, conj_b=False):
    o1 = ALU.subtract if not conj_b else ALU.add
    o2 = ALU.add if not conj_b else ALU.subtract
    S.op(ek, lambda h: h.tensor_tensor(t1, a_re, b_re, ALU.mult), reads, writes)
    S.op(ek, lambda h: h.tensor_tensor(t2, a_im, b_im, ALU.mult), reads, writes)
    S.op(ek, lambda h: h.tensor_tensor(out_re, t1, t2, o1), reads, writes)
    S.op(ek, lambda h: h.tensor_tensor(t1, a_im, b_re, ALU.mult), reads, writes)
    S.op(ek, lambda h: h.tensor_tensor(t2, a_re, b_im, ALU.mult), reads, writes)
    S.op(ek, lambda h: h.tensor_tensor(out_im, t1, t2, o2), reads, writes)


def phase_s5(cx, proj_dt, mix_dt, ident_bf, ident_f, cst):
    nc, S = cx.nc, cx.S
    Lc = cx.L
    T = 16
    SEG = min(Lc, 2048)
    NSEG = Lc // SEG
    NC = SEG // T
    NCT = NC + 1
    with ExitStack() as es:
        def ld(name, shape, dt=F32, q="sp"):
            b = cx.sb(es, "s5_" + name, shape, dt)
            S.dma(q, b[:], cst[name], writes=[b])
            return b
        are = ld("are", [128, 16]); aim = ld("aim", [128, 16]); ldt = ld("ldt", [128, 16])
        ccre = ld("ccre", [128, 16, 32]); ccim = ld("ccim", [128, 16, 32])
        dfm = ld("dfm", [128, 4]); glub = ld("glub", [128, 4]); kvec = ld("kvec", [128, 256])
        Wg = load_weight_bf16(cx, es, "s5_Wg", cst["gluw"], 512, 512, None)
        BT = [cx.sb(es, f"s5_BT{i}", [128, 16, 128], BF16) for i in range(2)]
        Ere = cx.sb(es, "s5_Ere", [128, 16, T + 1], F32); Eim = cx.sb(es, "s5_Eim", [128, 16, T + 1], F32)
        Rk = cx.sb(es, "s5_Rk", [128, 16, T + 1], F32)
        E2re = cx.sb(es, "s5_E2re", [128, 16, NCT], F32); E2im = cx.sb(es, "s5_E2im", [128, 16, NCT], F32)
        R2 = cx.sb(es, "s5_R2", [128, 16, NCT], F32)
        sm = cx.sb(es, "s5_sm", [128, 12, 16], F32)
        pmax = 1
        while pmax * 2 < max(T + 1, NCT):
            pmax *= 2
        Enim = cx.sb(es, "s5_Enim", [128, 16, T + 1], F32)
        hp = cx.sb(es, "s5_halfpi", [128, 1], F32)
        es_tb = ExitStack()
        tb = cx.sb(es_tb, "s5_tb", [128, 4, 16 + 16 * pmax], F32)
        SMALL = [sm]
        S.op("act", lambda h: h.activation(out=sm[:, 0, :], in_=ldt[:], func=AF.Exp), [ldt], SMALL)
        S.op("dve", lambda h: h.tensor_tensor(sm[:, 1, :], are[:], sm[:, 0, :], ALU.mult), [are] + SMALL, SMALL)
        S.op("dve", lambda h: h.tensor_tensor(sm[:, 2, :], aim[:], sm[:, 0, :], ALU.mult), [aim] + SMALL, SMALL)
        S.op("act", lambda h: h.activation(out=sm[:, 3, :], in_=sm[:, 1, :], func=AF.Exp), SMALL, SMALL)
        S.op("pool", lambda h: h.memset(hp[:], float(np.pi / 2)), [], [hp])
        S.op("act", lambda h: h.activation(out=sm[:, 5, :], in_=sm[:, 2, :], func=AF.Sin, scale=1.0 / 64), SMALL, SMALL)
        S.op("act", lambda h: h.activation(out=sm[:, 4, :], in_=sm[:, 2, :], func=AF.Sin, scale=1.0 / 64, bias=hp[:, 0:1]),
             SMALL + [hp], SMALL)
        for _ in range(6):
            S.op("dve", lambda h: h.tensor_tensor(sm[:, 8, :], sm[:, 4, :], sm[:, 4, :], ALU.mult), SMALL, SMALL)
            S.op("dve", lambda h: h.tensor_tensor(sm[:, 9, :], sm[:, 5, :], sm[:, 5, :], ALU.mult), SMALL, SMALL)
            S.op("dve", lambda h: h.tensor_tensor(sm[:, 10, :], sm[:, 4, :], sm[:, 5, :], ALU.mult), SMALL, SMALL)
            S.op("dve", lambda h: h.tensor_tensor(sm[:, 4, :], sm[:, 8, :], sm[:, 9, :], ALU.subtract), SMALL, SMALL)
            S.op("dve", lambda h: h.tensor_scalar(sm[:, 5, :], sm[:, 10, :], 2.0, None, ALU.mult), SMALL, SMALL)
        S.op("dve", lambda h: h.tensor_tensor(sm[:, 8, :], sm[:, 3, :], sm[:, 4, :], ALU.mult), SMALL, SMALL)
        S.op("dve", lambda h: h.tensor_tensor(sm[:, 9, :], sm[:, 3, :], sm[:, 5, :], ALU.mult), SMALL, SMALL)
        S.op("dve", lambda h: h.tensor_scalar(sm[:, 8, :], sm[:, 8, :], -1.0, None, ALU.add), SMALL, SMALL)
        S.op("dve", lambda h: h.tensor_tensor(sm[:, 10, :], are[:], are[:], ALU.mult), [are], SMALL)
        S.op("dve", lambda h: h.tensor_tensor(sm[:, 11, :], aim[:], aim[:], ALU.mult), [aim], SMALL)
        S.op("dve", lambda h: h.tensor_tensor(sm[:, 10, :], sm[:, 10, :], sm[:, 11, :], ALU.add), SMALL, SMALL)
        S.op("dve", lambda h: h.reciprocal(sm[:, 10, :], sm[:, 10, :]), SMALL, SMALL)
        S.op("dve", lambda h: h.tensor_tensor(sm[:, 6, :], sm[:, 8, :], are[:], ALU.mult), SMALL + [are], SMALL)
        S.op("dve", lambda h: h.tensor_tensor(sm[:, 11, :], sm[:, 9, :], aim[:], ALU.mult), SMALL + [aim], SMALL)
        S.op("dve", lambda h: h.tensor_tensor(sm[:, 6, :], sm[:, 6, :], sm[:, 11, :], ALU.add), SMALL, SMALL)
        S.op("dve", lambda h: h.tensor_tensor(sm[:, 7, :], sm[:, 9, :], are[:], ALU.mult), SMALL + [are], SMALL)
        S.op("dve", lambda h: h.tensor_tensor(sm[:, 11, :], sm[:, 8, :], aim[:], ALU.mult), SMALL + [aim], SMALL)
        S.op("dve", lambda h: h.tensor_tensor(sm[:, 7, :], sm[:, 7, :], sm[:, 11, :], ALU.subtract), SMALL, SMALL)
        S.op("dve", lambda h: h.tensor_tensor(sm[:, 6, :], sm[:, 6, :], sm[:, 10, :], ALU.mult), SMALL, SMALL)
        S.op("dve", lambda h: h.tensor_tensor(sm[:, 7, :], sm[:, 7, :], sm[:, 10, :], ALU.mult), SMALL, SMALL)

        def build_pow_tables(Tre, Tim, n, base_re, base_im):
            TB = [Tre, Tim, tb]
            S.op("pool", lambda h: h.memset(Tre[:, :, 0:1], 1.0), [], [Tre])
            S.op("pool", lambda h: h.memset(Tim[:, :, 0:1], 0.0), [], [Tim])
            S.op("dve", lambda h: h.tensor_copy(Tre[:, :, 1], base_re), SMALL, [Tre])
            S.op("dve", lambda h: h.tensor_copy(Tim[:, :, 1], base_im), SMALL, [Tim])
            m = 2
            while m < n:
                cnt = min(m, n - m)
                pr, pi_ = tb[:, 2, 0:16], tb[:, 3, 0:16]
                cmul(S, "dve", pr, pi_, Tre[:, :, m - 1], Tim[:, :, m - 1], Tre[:, :, 1], Tim[:, :, 1],
                     tb[:, 0, 0:16], tb[:, 1, 0:16], TB, TB)
                prb = pr.unsqueeze(2).broadcast_to([128, 16, cnt])
                pib = pi_.unsqueeze(2).broadcast_to([128, 16, cnt])
                t1 = tb[:, 0, 16:16 + 16 * cnt].rearrange("p (a b) -> p a b", b=cnt)
                t2 = tb[:, 1, 16:16 + 16 * cnt].rearrange("p (a b) -> p a b", b=cnt)
                cmul(S, "dve", Tre[:, :, m:m + cnt], Tim[:, :, m:m + cnt], Tre[:, :, 0:cnt], Tim[:, :, 0:cnt], prb, pib,
                     t1, t2, TB, TB)
                m += cnt
        build_pow_tables(Ere, Eim, T + 1, sm[:, 4, :], sm[:, 5, :])
        S.op("dve", lambda h: h.tensor_scalar(Enim[:], Eim[:], -1.0, None, ALU.mult), [Eim], [Enim])
        S.op("dve", lambda h: h.tensor_copy(sm[:, 8, :], Ere[:, :, T]), [Ere], SMALL)
        S.op("dve", lambda h: h.tensor_copy(sm[:, 9, :], Eim[:, :, T]), [Eim], SMALL)
        build_pow_tables(E2re, E2im, NCT, sm[:, 8, :], sm[:, 9, :])
        S.op("dve", lambda h: h.tensor_tensor(Rk[:], sm[:, 1, :].unsqueeze(2).broadcast_to([128, 16, T + 1]),
                                              kvec[:, 0:T + 1].unsqueeze(1).broadcast_to([128, 16, T + 1]), ALU.mult),
             SMALL + [kvec], [Rk])
        S.op("act", lambda h: h.activation(out=Rk[:], in_=Rk[:], func=AF.Exp), [Rk], [Rk])
        S.op("dve", lambda h: h.tensor_tensor(R2[:], sm[:, 1, :].unsqueeze(2).broadcast_to([128, 16, NCT]),
                                              kvec[:, 0:NCT].unsqueeze(1).broadcast_to([128, 16, NCT]), ALU.mult),
             SMALL + [kvec], [R2])
        S.op("act", lambda h: h.activation(out=R2[:], in_=R2[:], func=AF.Exp, scale=float(T)), [R2], [R2])
        S.barrier()
        es_tb.close()
        stage(cx, 1)
        with ExitStack() as es1:
            bpre = cx.sb(es1, "s5_bpre", [128, 16, 128], F32); bpim = cx.sb(es1, "s5_bpim", [128, 16, 128], F32)
            S.dma("sp", bpre[:], cst["bpre"], writes=[bpre]); S.dma("pool", bpim[:], cst["bpim"], writes=[bpim])
            t1 = cx.sb(es1, "s5_bt1", [128, 16, 128], F32); t2 = cx.sb(es1, "s5_bt2", [128, 16, 128], F32)
            bbre = cx.sb(es1, "s5_bbre", [128, 16, 128], BF16); bbim = cx.sb(es1, "s5_bbim", [128, 16, 128], BF16)
            kr = sm[:, 6, :].unsqueeze(2).broadcast_to([128, 16, 128])
            ki = sm[:, 7, :].unsqueeze(2).broadcast_to([128, 16, 128])
            cmul(S, "dve", bbre[:], bbim[:], bpre[:], bpim[:], kr, ki, t1[:], t2[:], [bpre, bpim, t1, t2] + SMALL, [bbre, bbim, t1, t2])
            pst = cx.ps(es1, "s5_pst", [128, 8, 128], BF16)
            for k in range(16):
                for ri, src in enumerate((bbre, bbim)):
                    S.op("pe", lambda h: h.transpose(pst[:, ri, :], src[:, k, :], ident_bf[:]), [src, ident_bf], [pst])
                    S.op("act", lambda h: h.copy(BT[ri][:, k, :], pst[:, ri, :]), [pst], [BT[ri]])
            S.barrier()
        stage(cx, 2)
        Send = cx.sb(es, "s5_Send", [128, 2, 16], F32)
        S.op("pool", lambda h: h.memset(Send[:], 0.0), [], [Send])
        for seg in range(NSEG):
            t00 = seg * SEG
            with ExitStack() as es2:
                y = [cx.sb(es2, f"s5_y{q}", [128, SEG], F32) for q in range(4)]
                with ExitStack() as es3:
                    uTb = [cx.sb(es3, f"s5_uTb{q}", [128, SEG], BF16) for q in range(4)]
                    with ExitStack() as es4:
                        sut = [cx.sb(es4, f"s5_sut{i}", [128, 512], F32) for i in range(2)]
                        sub = [cx.sb(es4, f"s5_sub{i}", [128, 512], BF16) for i in range(2)]
                        psu = [cx.ps(es4, f"s5_psu{i}", [128, 8, 128], BF16) for i in range(2)]
                        for tt in range(SEG // 128):
                            i2 = tt % 2
                            r0 = t00 + tt * 128
                            S.dma("sp", sut[i2][:], proj_dt.ap[r0:r0 + 128, C_SU:C_SU + 512],
                                  reads=proj_dt.bufs(r0, r0 + 128), writes=[sut[i2]])
                            S.op("pool", lambda h: h.tensor_copy(sub[i2][:], sut[i2][:]), [sut[i2]], [sub[i2]])
                            stage(cx, 21)

                            def tru(h, i2=i2):
                                for q in range(4):
                                    ins = h.transpose(psu[i2][:, q, :], sub[i2][:, q * 128:(q + 1) * 128], ident_bf[:])
                                return ins
                            S.op("pe", tru, [sub[i2], ident_bf], [psu[i2]])
                            stage(cx, 22)
                            for q in range(4):
                                ek = "act"
                                if ek == "act":
                                    S.op("act", lambda h: h.copy(uTb[q][:, tt * 128:(tt + 1) * 128], psu[i2][:, q, :]), [psu[i2]], [uTb[q]])
                                else:
                                    S.op("dve", lambda h: h.tensor_copy(uTb[q][:, tt * 128:(tt + 1) * 128], psu[i2][:, q, :]), [psu[i2]], [uTb[q]])
                                stage(cx, 230 + q)
                            stage(cx, 240 + tt)
                        S.barrier()
                    stage(cx, 3)
                    xre = cx.sb(es3, "s5_xre", [128, SEG], F32); xim = cx.sb(es3, "s5_xim", [128, SEG], F32)
                    vre2 = [cx.sb(es3, f"s5_vre{i}", [128, SEG], BF16) for i in range(2)]
                    vim2 = [cx.sb(es3, f"s5_vim{i}", [128, SEG], BF16) for i in range(2)]
                    rmask = cx.sb(es3, "s5_rmask", [128, SEG], F32)
                    tas = [cx.sb(es3, f"s5_ta{i}", [128, 512], F32) for i in range(4)]
                    tbs = [cx.sb(es3, f"s5_tbb{i}", [128, 512], F32) for i in range(4)]
                    nrot = [0]
                    ctabs = [[cx.sb(es3, f"s5_ctab{par}_{i}", [128, T + 1, 64], BF16) for i in range(4)] for par in range(2)]
                    for par in range(2):
                        for i in range(4):
                            S.op("pool", lambda h: h.memset(ctabs[par][i][:], 0.0), [], [ctabs[par][i]])
                    ct1 = cx.sb(es3, "s5_ct1", [128, T + 1, 32], F32); ct2 = cx.sb(es3, "s5_ct2", [128, T + 1, 32], F32)
                    lv = cx.sb(es3, "s5_lv", [128, 12, NCT], F32)
                    Sp2 = [[cx.sb(es3, f"s5_Sp{par}_{i}", [128, NC], BF16) for i in range(2)] for par in range(2)]
                    R2m = cx.sb(es3, "s5_R2m", [128, NC], F32)
                    psb = [cx.ps(es3, f"s5_psb{i}", [128, 512], F32) for i in range(4)]
                    psy = [cx.ps(es3, f"s5_psy{i}", [128, 4, 128], F32) for i in range(2)]
                    npyc = [0]

                    def front_mid(k):
                        q, j = k // 4, k % 4
                        vre, vim, Sp = vre2[k % 2], vim2[k % 2], Sp2[k % 2]
                        rm3 = rmask[:].rearrange("p (c k) -> p c k", k=T)
                        S.op("act", lambda h: h.copy(rm3[:, :, 1:T], sm[:, 3, k:k + 1].unsqueeze(2).broadcast_to([128, NC, T - 1])),
                             SMALL, [rmask])
                        S.op("pool", lambda h: h.memset(rm3[:, :, 0:1], 0.0), [], [rmask])
                        for blk in range(SEG // 512):
                            c0 = blk * 512
                            PR, PI = psb[(2 * blk) % 4], psb[(2 * blk + 1) % 4]
                            S.op("pe", lambda h: h.matmul(PR[:], BT[0][:, k, :], uTb[q][:, c0:c0 + 512], start=True, stop=True),
                                 [BT[0], uTb[q]], [PR])
                            S.op("pe", lambda h: h.matmul(PI[:], BT[1][:, k, :], uTb[q][:, c0:c0 + 512], start=True, stop=True),
                                 [BT[1], uTb[q]], [PI])
                            cb = Ere[:, k, 0:T].unsqueeze(1).broadcast_to([128, 512 // T, T])
                            sb_ = Eim[:, k, 0:T].unsqueeze(1).broadcast_to([128, 512 // T, T])
                            v3 = lambda ap: ap.rearrange("p (c k) -> p c k", k=T)
                            ta, tbb = tas[nrot[0] % 4], tbs[nrot[0] % 4]
                            nrot[0] += 1
                            S.op("dve", lambda h: h.tensor_tensor(v3(ta[:]), v3(PR[:]), cb, ALU.mult), [PR, Ere], [ta])
                            S.op("dve", lambda h: h.tensor_tensor(v3(tbb[:]), v3(PI[:]), sb_, ALU.mult), [PI, Eim], [tbb])
                            S.op("pool", lambda h: h.tensor_tensor(xre[:, c0:c0 + 512], ta[:], tbb[:], ALU.add), [ta, tbb], [xre])
                            ta, tbb = tas[nrot[0] % 4], tbs[nrot[0] % 4]
                            nrot[0] += 1
                            S.op("dve", lambda h: h.tensor_tensor(v3(ta[:]), v3(PI[:]), cb, ALU.mult), [PI, Ere], [ta])
                            S.op("dve", lambda h: h.tensor_tensor(v3(tbb[:]), v3(PR[:]), sb_, ALU.mult), [PR, Eim], [tbb])
                            S.op("pool", lambda h: h.tensor_tensor(xim[:, c0:c0 + 512], ta[:], tbb[:], ALU.subtract), [ta, tbb], [xim])
                        stage(cx, 4)
                        S.op("dve", lambda h: h.tensor_tensor_scan(vre[:], rmask[:], xre[:], 0.0, ALU.mult, ALU.add), [rmask, xre], [vre])
                        S.op("dve", lambda h: h.tensor_tensor_scan(vim[:], rmask[:], xim[:], 0.0, ALU.mult, ALU.add), [rmask, xim], [vim])
                        stage(cx, 5)
                        LV = [lv]
                        vr3 = vre[:].rearrange("p (c k) -> p c k", k=T)
                        vi3 = vim[:].rearrange("p (c k) -> p c k", k=T)
                        S.op("dve", lambda h: h.tensor_copy(lv[:, 0, 0:NC], vr3[:, :, T - 1]), [vre], LV)
                        S.op("dve", lambda h: h.tensor_copy(lv[:, 1, 0:NC], vi3[:, :, T - 1]), [vim], LV)
                        er, ei = Ere[:, k, T - 1:T], Eim[:, k, T - 1:T]
                        S.op("dve", lambda h: h.tensor_scalar(lv[:, 10, 0:NC], lv[:, 1, 0:NC], ei, None, ALU.mult), LV + [Eim], LV)
                        S.op("dve", lambda h: h.scalar_tensor_tensor(lv[:, 2, 0:NC], lv[:, 0, 0:NC], er, lv[:, 10, 0:NC], ALU.mult, ALU.subtract), LV + [Ere], LV)
                        S.op("dve", lambda h: h.tensor_scalar(lv[:, 10, 0:NC], lv[:, 0, 0:NC], ei, None, ALU.mult), LV + [Eim], LV)
                        S.op("dve", lambda h: h.scalar_tensor_tensor(lv[:, 3, 0:NC], lv[:, 1, 0:NC], er, lv[:, 10, 0:NC], ALU.mult, ALU.add), LV + [Ere], LV)
                        cmul(S, "dve", lv[:, 4, 0:NC], lv[:, 5, 0:NC], lv[:, 2, 0:NC], lv[:, 3, 0:NC], E2re[:, k, 0:NC], E2im[:, k, 0:NC],
                             lv[:, 10, 0:NC], lv[:, 11, 0:NC], LV + [E2re, E2im], LV, conj_b=True)
                        S.op("pool", lambda h: h.tensor_copy(R2m[:], R2[:, k, 1:2].broadcast_to([128, NC])), [R2], [R2m])
                        S.op("dve", lambda h: h.tensor_tensor_scan(lv[:, 6, 0:NC], R2m[:], lv[:, 4, 0:NC], 0.0, ALU.mult, ALU.add), [R2m] + LV, LV)
                        S.op("dve", lambda h: h.tensor_tensor_scan(lv[:, 7, 0:NC], R2m[:], lv[:, 5, 0:NC], 0.0, ALU.mult, ALU.add), [R2m] + LV, LV)
                        cmul(S, "dve", lv[:, 8, 1:NCT], lv[:, 9, 1:NCT], lv[:, 6, 0:NC], lv[:, 7, 0:NC], E2re[:, k, 0:NC], E2im[:, k, 0:NC],
                             lv[:, 10, 0:NC], lv[:, 11, 0:NC], LV + [E2re, E2im], LV)
                        S.op("dve", lambda h: h.tensor_copy(lv[:, 8, 0:1], Send[:, 0, k:k + 1]), [Send], LV)
                        S.op("dve", lambda h: h.tensor_copy(lv[:, 9, 0:1], Send[:, 1, k:k + 1]), [Send], LV)
                        if seg > 0:
                            S.op("dve", lambda h: h.tensor_tensor(lv[:, 4, 0:NC], R2[:, k, 1:NCT], E2re[:, k, 1:NCT], ALU.mult), [R2, E2re], LV)
                            S.op("dve", lambda h: h.tensor_tensor(lv[:, 5, 0:NC], R2[:, k, 1:NCT], E2im[:, k, 1:NCT], ALU.mult), [R2, E2im], LV)
                            sr, si = Send[:, 0, k:k + 1], Send[:, 1, k:k + 1]
                            S.op("dve", lambda h: h.scalar_tensor_tensor(lv[:, 8, 1:NCT], lv[:, 4, 0:NC], sr, lv[:, 8, 1:NCT], ALU.mult, ALU.add), LV + [Send], LV)
                            S.op("dve", lambda h: h.tensor_scalar(lv[:, 10, 0:NC], lv[:, 5, 0:NC], si, None, ALU.mult), LV + [Send], LV)
                            S.op("dve", lambda h: h.tensor_tensor(lv[:, 8, 1:NCT], lv[:, 8, 1:NCT], lv[:, 10, 0:NC], ALU.subtract), LV, LV)
                            S.op("dve", lambda h: h.scalar_tensor_tensor(lv[:, 9, 1:NCT], lv[:, 5, 0:NC], sr, lv[:, 9, 1:NCT], ALU.mult, ALU.add), LV + [Send], LV)
                            S.op("dve", lambda h: h.tensor_scalar(lv[:, 10, 0:NC], lv[:, 4, 0:NC], si, None, ALU.mult), LV + [Send], LV)
                            S.op("dve", lambda h: h.tensor_tensor(lv[:, 9, 1:NCT], lv[:, 9, 1:NCT], lv[:, 10, 0:NC], ALU.add), LV, LV)
                        S.op("dve", lambda h: h.tensor_copy(Send[:, 0, k:k + 1], lv[:, 8, NC:NCT]), LV, [Send])
                        S.op("dve", lambda h: h.tensor_copy(Send[:, 1, k:k + 1], lv[:, 9, NC:NCT]), LV, [Send])
                        S.op("pool", lambda h: h.tensor_copy(Sp[0][:], lv[:, 8, 0:NC]), LV, [Sp[0]])
                        S.op("pool", lambda h: h.tensor_copy(Sp[1][:], lv[:, 9, 0:NC]), LV, [Sp[1]])
                        stage(cx, 6)
                        cr = ccre[:, k, :].unsqueeze(1).broadcast_to([128, T + 1, 32])
                        ci = ccim[:, k, :].unsqueeze(1).broadcast_to([128, T + 1, 32])
                        ctabf = ctabs[j % 2]
                        hs = slice(32 * (j % 2), 32 * (j % 2) + 32)
                        CT = ctabf + [ct1, ct2]
                        e_r = Ere[:, k, 0:T + 1].unsqueeze(2).broadcast_to([128, T + 1, 32])
                        e_i = Eim[:, k, 0:T + 1].unsqueeze(2).broadcast_to([128, T + 1, 32])
                        e_ni = Enim[:, k, 0:T + 1].unsqueeze(2).broadcast_to([128, T + 1, 32])
                        rb = Rk[:, k, 1:T + 1].unsqueeze(2).broadcast_to([128, T, 32])
                        S.op("dve", lambda h: h.tensor_tensor(ct1[:], cr, e_r, ALU.mult), [ccre, Ere], CT)
                        S.op("dve", lambda h: h.tensor_tensor(ct2[:], ci, e_i, ALU.mult), [ccim, Eim], CT)
                        S.op("dve", lambda h: h.tensor_tensor(ctabf[0][:, :, hs], ct1[:], ct2[:], ALU.subtract), CT, CT)
                        S.op("dve", lambda h: h.tensor_tensor(ct1[:], cr, e_ni, ALU.mult), [ccre, Enim], CT)
                        S.op("dve", lambda h: h.tensor_tensor(ct2[:], ci, e_r, ALU.mult), [ccim, Ere], CT)
                        S.op("dve", lambda h: h.tensor_tensor(ctabf[1][:, :, hs], ct1[:], ct2[:], ALU.subtract), CT, CT)
                        S.op("dve", lambda h: h.tensor_tensor(ctabf[2][:, 0:T, hs], ctabf[0][:, 1:T + 1, hs], rb, ALU.mult), CT + [Rk], CT)
                        S.op("dve", lambda h: h.tensor_tensor(ctabf[3][:, 0:T, hs], ctabf[1][:, 1:T + 1, hs], rb, ALU.mult), CT + [Rk], CT)

                    def back(k):
                        q, j = k // 4, k % 4
                        vre, vim, Sp = vre2[k % 2], vim2[k % 2], Sp2[k % 2]
                        ctabf = ctabs[j % 2]
                        npy = npyc[0]
                        vrb = vre[:].rearrange("p (c k) -> p k c", k=T)
                        vib = vim[:].rearrange("p (c k) -> p k c", k=T)
                        jj = j // 2
                        y3 = y[q][64 * jj:64 * jj + 64, :].rearrange("p (c k) -> p k c", k=T)
                        for kb in range(T // 4):
                            PY = psy[npy % 2]
                            npy += 1

                            def ymm(h, kb=kb, PY=PY):
                                for kk in range(4):
                                    kx = kb * 4 + kk
                                    o = PY[64 * jj:64 * jj + 64, kk, 0:NC]
                                    h.matmul(o, ctabf[0][:, kx, :], vrb[:, kx, :], start=True, stop=False)
                                    h.matmul(o, ctabf[1][:, kx, :], vib[:, kx, :], start=False, stop=False)
                                    h.matmul(o, ctabf[2][:, kx, :], Sp[0][:], start=False, stop=False)
                                    ins = h.matmul(o, ctabf[3][:, kx, :], Sp[1][:], start=False, stop=True)
                                return ins
                            S.op("pe", ymm, ctabf + [vre, vim] + Sp, [PY])
                            if j % 2 == 0:
                                S.op("act", lambda h: h.copy(y3[:, kb * 4:(kb + 1) * 4, :], PY[64 * jj:64 * jj + 64, :, 0:NC]), [PY], [y[q]])
                            else:
                                S.op("dve", lambda h: h.tensor_tensor(y3[:, kb * 4:(kb + 1) * 4, :], PY[64 * jj:64 * jj + 64, :, 0:NC],
                                                                      y3[:, kb * 4:(kb + 1) * 4, :], ALU.add), [PY, y[q]], [y[q]])
                        npyc[0] = npy

                    front_mid(0)
                    for k in range(16):
                        if k + 1 < 16:
                            front_mid(k + 1)
                        back(k)
                    S.barrier()
                stage(cx, 8)
                with ExitStack() as es5:
                    sut = [cx.sb(es5, f"s5_tsut{i}", [128, 4, 512], F32) for i in range(2)]
                    yy = [cx.sb(es5, f"s5_yy{q}", [128, 512], F32) for q in range(4)]
                    w1s = [cx.sb(es5, f"s5_w1_{q}", [128, 512], F32) for q in range(4)]
                    w2s = [cx.sb(es5, f"s5_w2_{q}", [128, 512], F32) for q in range(4)]
                    ygb = [cx.sb(es5, f"s5_ygb{q}", [128, 512], BF16) for q in range(4)]
                    og = [cx.sb(es5, f"s5_og{i}", [128, 512], F32) for i in range(4)]
                    g5 = [cx.sb(es5, f"s5_g5{i}", [128, 512], F32) for i in range(2)]
                    yo = [cx.sb(es5, f"s5_yo{i}", [128, 512], F32) for i in range(2)]
                    psT = [cx.ps(es5, f"s5_psT{q}", [128, 512], F32) for q in range(4)]
                    psG = [cx.ps(es5, f"s5_psG{i}", [128, 512], F32) for i in range(2)]
                    psO = [cx.ps(es5, f"s5_psO{i}", [128, 512], F32) for i in range(2)]
                    for blk in range(SEG // 512):
                        c0 = blk * 512
                        SU = sut[blk % 2]
                        for tt in range(4):
                            r0 = t00 + c0 + tt * 128
                            S.dma("sp", SU[:, tt, :], proj_dt.ap[r0:r0 + 128, C_SU:C_SU + 512],
                                  reads=proj_dt.bufs(r0, r0 + 128), writes=[SU])
                        for q in range(4):
                            def tq(h, q=q):
                                for tt in range(4):
                                    ins = h.transpose(psT[q][:, tt * 128:(tt + 1) * 128], SU[:, tt, q * 128:(q + 1) * 128], ident_f[:])
                                return ins
                            S.op("pe", tq, [SU, ident_f], [psT[q]])
                            S.op("dve", lambda h: h.scalar_tensor_tensor(yy[q][:], psT[q][:], dfm[:, q:q + 1], y[q][:, c0:c0 + 512],
                                                                         ALU.mult, ALU.add), [psT[q], dfm, y[q]], [yy[q]])
                            w1, w2 = w1s[q], w2s[q]
                            S.op("act", lambda h: h.activation(out=w1[:], in_=yy[q][:], func=AF.Square), [yy[q]], [w1])
                            S.op("dve", lambda h: h.tensor_scalar(w1[:], w1[:], 0.044715, 1.0, ALU.mult, ALU.add), [w1], [w1])
                            S.op("dve", lambda h: h.tensor_tensor(w1[:], w1[:], yy[q][:], ALU.mult), [w1, yy[q]], [w1])
                            S.op("act", lambda h: h.activation(out=w2[:], in_=w1[:], func=AF.Sigmoid, scale=1.5957691216057308), [w1], [w2])
                            S.op("dve", lambda h: h.tensor_tensor(yy[q][:], yy[q][:], w2[:], ALU.mult), [yy[q], w2], [yy[q]])
                            S.op("pool", lambda h: h.tensor_copy(ygb[q][:], yy[q][:]), [yy[q]], [ygb[q]])
                        for nt in range(4):
                            def glu(h, nt=nt):
                                for q in range(4):
                                    ins = h.matmul(psG[nt % 2][:], Wg[:, q, nt * 128:(nt + 1) * 128], ygb[q][:], start=(q == 0), stop=(q == 3))
                                return ins
                            S.op("pe", glu, [Wg] + ygb, [psG[nt % 2]])
                            S.op("act", lambda h: h.activation(out=og[nt][:], in_=psG[nt % 2][:], func=AF.Sigmoid, bias=glub[:, nt:nt + 1]),
                                 [psG[nt % 2], glub], [og[nt]])
                            S.op("dve", lambda h: h.tensor_tensor(og[nt][:], og[nt][:], yy[nt][:], ALU.mult), [og[nt], yy[nt]], [og[nt]])
                        for tt in range(4):
                            i2 = tt % 2
                            r0 = t00 + c0 + tt * 128
                            S.dma("sp", g5[i2][:], proj_dt.ap[r0:r0 + 128, C_S5G:C_S5G + 512], reads=proj_dt.bufs(r0, r0 + 128), writes=[g5[i2]])
                            S.op("act", lambda h: h.activation(out=g5[i2][:], in_=g5[i2][:], func=AF.Silu), [g5[i2]], [g5[i2]])

                            def tro(h, tt=tt, i2=i2):
                                for nt in range(4):
                                    ins = h.transpose(psO[i2][:, nt * 128:(nt + 1) * 128], og[nt][:, tt * 128:(tt + 1) * 128], ident_f[:])
                                return ins
                            S.op("pe", tro, og + [ident_f], [psO[i2]])
                            S.op("dve", lambda h: h.tensor_tensor(yo[i2][:], psO[i2][:, 0:512], g5[i2][:], ALU.mult), [psO[i2], g5[i2]], [yo[i2]])
                            S.dma("pool", mix_dt.ap[r0:r0 + 128, 768:1024], yo[i2][:, 0:256], reads=[yo[i2]], writes=mix_dt.bufs(r0, r0 + 128))
                            S.dma("pool", mix_dt.ap[r0:r0 + 128, 1024 + 768:2048], yo[i2][:, 256:512], reads=[yo[i2]], writes=mix_dt.bufs(r0, r0 + 128))
                    S.barrier()


def s5_layouts(a_re, a_im, log_dt, b_re, b_im, c_re, c_im, d, glu_w, glu_b):
    G = np.arange(32).reshape(16, 2)
    f = np.float32
    are = a_re[G].transpose(1, 2, 0).reshape(128, 16).astype(f)
    aim = a_im[G].transpose(1, 2, 0).reshape(128, 16).astype(f)
    ldt = np.broadcast_to(log_dt[G].transpose(1, 0)[:, None, :], (2, 64, 16)).reshape(128, 16).astype(f)
    ccre = np.zeros((2, 64, 16, 2, 16), f); ccim = np.zeros((2, 64, 16, 2, 16), f)
    bpre = np.zeros((2, 64, 16, 4, 2, 16), f); bpim = np.zeros((2, 64, 16, 4, 2, 16), f)
    for k in range(16):
        for g2 in range(2):
            g = G[k, g2]
            ccre[g2, :, k, g2, :] = c_re[g].T
            ccim[g2, :, k, g2, :] = c_im[g].T
            bpre[g2, :, k, k % 4, g2, :] = b_re[g]
            bpim[g2, :, k, k % 4, g2, :] = b_im[g]
    return dict(are=are, aim=aim, ldt=ldt, ccre=ccre.reshape(128, 16, 32), ccim=ccim.reshape(128, 16, 32),
                bpre=bpre.reshape(128, 16, 128), bpim=bpim.reshape(128, 16, 128),
                dfm=np.ascontiguousarray(d.reshape(4, 128).T).astype(f),
                glub=np.ascontiguousarray(glu_b.reshape(4, 128).T).astype(f),
                gluw=np.ascontiguousarray(glu_w).astype(f))


def bc128(a):
    a = np.asarray(a, np.float32)
    return np.ascontiguousarray(np.broadcast_to(a[None], (128,) + a.shape))


def static_consts():
    f = np.float32
    pos = np.arange(L, dtype=f)
    inv = (1.0 / (np.float32(10000.0) ** (np.arange(0, 64, 2, dtype=f) / np.float32(64)))).astype(f)
    ang = (pos[:, None] * inv[None, :]).astype(f)
    cos = np.cos(ang).astype(f); sin = np.sin(ang).astype(f)
    k = np.arange(128)[:, None]; q = np.arange(128)[None, :]
    mown = np.where(k <= q, 0.0, -30000.0).astype(f)
    mprev = np.where(k > q, 0.0, -30000.0).astype(f)
    return dict(
        ident=np.eye(128).astype(ml_dtypes.bfloat16), identf=np.eye(128, dtype=f),
        cosT=np.ascontiguousarray(cos.reshape(NT, 128, 32).transpose(1, 0, 2)),
        sinT=np.ascontiguousarray(sin.reshape(NT, 128, 32).transpose(1, 0, 2)),
        mown=np.ascontiguousarray(np.tile(mown[:, None, :], (1, 4, 1))),
        mprev=np.ascontiguousarray(np.tile(mprev[:, None, :], (1, 4, 1))),
        gmask=bc128(np.where(np.arange(16)[None, :] < np.arange(16)[:, None], 0.0, -1e30).astype(f)),
        blkind=(np.arange(L)[None, :] // 256 == np.arange(16)[:, None]).astype(ml_dtypes.bfloat16),
        tri=(k <= q).astype(f), ones=np.ones((128, 128), f), maskT=mown.copy(),
        kvec=bc128(np.arange(256, dtype=f)),
    )


CONST_SHAPES = dict(ident=([128, 128], BF16), identf=([128, 128], F32), cosT=([128, NT, 32], F32), sinT=([128, NT, 32], F32),
                    mown=([128, 4, 128], F32), mprev=([128, 4, 128], F32), gmask=([128, 16, 16], F32), blkind=([16, L], BF16),
                    tri=([128, 128], F32), ones=([128, 128], F32), maskT=([128, 128], F32), kvec=([128, 256], F32))
HALF_SHAPES = dict(sinks=[128, 4], convw=[128, 4, 512], convb=[128, 512], dtb=[128, 4], alog=[128, 4],
                   dskip=[128, 256], normw=[128, 256])
LAYER_SHAPES = dict(pre_g=[128, 16], w_out=[D, D], post_g=[128, D],
                    are=[128, 16], aim=[128, 16], ldt=[128, 16], ccre=[128, 16, 32], ccim=[128, 16, 32],
                    bpre=[128, 16, 128], bpim=[128, 16, 128], dfm=[128, 4], glub=[128, 4], gluw=[512, 512])


def in_cols(jh):
    r = lambda a, n: list(range(a, a + n))
    c = (r(0 + jh * 256, 256) + r(512 + jh * 256, 256) + r(1024 + jh * 256, 256) + r(1536 + jh * 256, 256)
         + r(3592 + jh * 256, 256) + r(4104 + jh * 64, 64) + r(4232 + jh * 64, 64) + r(4360 + jh * 256, 256)
         + r(2048 + jh * 256, 256) + r(2560 + jh * 128, 128) + r(2816 + jh * 128, 128) + r(3080 + jh * 256, 256)
         + r(3072 + jh * 4, 4))
    if jh == 0:
        c = c + r(4872, 512) + r(5384, 512)
    return np.array(c)


def half_inputs(inp, l, jh):
    cols = in_cols(jh)
    assert len(cols) == (NP if jh == 0 else NPH)
    cch = np.array(list(range(jh * 256, jh * 256 + 256)) + list(range(512 + jh * 128, 512 + jh * 128 + 128))
                   + list(range(768 + jh * 128, 768 + jh * 128 + 128)))
    return dict(
        w_in=np.ascontiguousarray(inp["w_in"][l][:, cols]),
        sinks=bc128(inp["swa_sinks"][l][4 * jh:4 * jh + 4]),
        convw=bc128(inp["ssd_conv_w"][l][:, cch]), convb=bc128(inp["ssd_conv_b"][l][cch]),
        dtb=bc128(inp["ssd_dt_bias"][l][4 * jh:4 * jh + 4]), alog=bc128(inp["ssd_a_log"][l][4 * jh:4 * jh + 4]),
        dskip=bc128(np.repeat(inp["ssd_d"][l][4 * jh:4 * jh + 4], 64)), normw=bc128(inp["ssd_norm"][l][jh * 256:jh * 256 + 256]),
    )


WOUT_ROWS = np.array([b + jh * 256 + i for jh in range(2) for b in (0, 512, 1024, 1536) for i in range(256)])


def layer_inputs(inp, l):
    d = s5_layouts(inp["s5_a_re"][l], inp["s5_a_im"][l], inp["s5_log_dt"][l], inp["s5_b_re"][l], inp["s5_b_im"][l],
                   inp["s5_c_re"][l], inp["s5_c_im"][l], inp["s5_d"][l], inp["s5_glu_w"][l], inp["s5_glu_b"][l])
    d.update(pre_g=np.ascontiguousarray(inp["pre_norm"][l].reshape(16, 128).T),
             w_out=np.ascontiguousarray(inp["w_out"][l][WOUT_ROWS]), post_g=bc128(inp["post_norm"][l]))
    return d


def load_consts(cx, es, cap):
    S = cx.S
    C = {}
    for nm, key in (("ident", "ident"), ("identf", "identf"), ("cos", "cosT"), ("sin", "sinT")):
        shp, dt = CONST_SHAPES[key]
        C[nm] = cx.sb(es, "c_" + nm, shp, dt)
        S.dma("sp", C[nm][:], cap[key], writes=[C[nm]])
    for key in ("mprev", "mown", "gmask", "blkind", "tri", "ones", "maskT", "kvec", "identf"):
        C["ap_" + key] = cap[key]
    return C


def build_fused(depth=DEPTH):
    nc = bass.Bass("TRN2", target_bir_lowering=False)
    cx = Ctx(nc)
    cx.L = L
    A = lambda n, s, d=F32: nc.dram_tensor(n, list(s), d, kind="ExternalInput").ap()
    x_in = DramT(nc, "x", [L, D], F32, kind="ExternalInput")
    cap = {k: A("k_" + k, s, d) for k, (s, d) in CONST_SHAPES.items()}
    Lw = [{k: A(f"l{l}_{k}", s) for k, s in LAYER_SHAPES.items()} for l in range(depth)]
    Hw = [[dict({k: A(f"l{l}h{jh}_{k}", s) for k, s in HALF_SHAPES.items()},
                w_in=A(f"l{l}h{jh}_w_in", [D, NP if jh == 0 else NPH])) for jh in range(2)] for l in range(depth)]
    proj = DramT(nc, "proj", [L, NP], F32)
    mixc = DramT(nc, "mixc", [L, D], F32)
    xbuf = [DramT(nc, f"xbuf{i}", [L, D], F32) for i in range(2)]
    out = DramT(nc, "out", [L, D], F32, kind="ExternalOutput")
    with ExitStack() as es:
        C = load_consts(cx, es, cap)
        x_cur = x_in
        for l in range(depth):
            x_next = out if l == depth - 1 else xbuf[l % 2]
            for jh in range(2):
                H = Hw[l][jh]
                mcol = jh * MIXH
                phase_inproj(cx, x_cur, H["w_in"], Lw[l]["pre_g"], proj, C["ident"], npc=(NP if jh == 0 else NPH))
                phase_swa(cx, proj, mixc, C["cos"], C["sin"], C["ident"], C["ap_mprev"], C["ap_mown"], H["sinks"], mcol=mcol)
                phase_moba(cx, proj, mixc, C["cos"], C["sin"], C["ident"], C["identf"], C["ap_mown"], C["ap_gmask"],
                           C["ap_blkind"], mcol=mcol)
                ssd_c = {k: H[k] for k in ("convw", "convb", "dtb", "alog", "dskip", "normw")}
                ssd_c.update(tri=C["ap_tri"], ones=C["ap_ones"], maskT=C["ap_maskT"], identf=C["ap_identf"])
                phase_ssd(cx, proj, mixc, C["ident"], ssd_c, mcol=mcol)
                if jh == 0:
                    s5_c = {k: Lw[l][k] for k in ("are", "aim", "ldt", "ccre", "ccim", "bpre", "bpim", "dfm", "glub", "gluw")}
                    s5_c["kvec"] = C["ap_kvec"]
                    phase_s5(cx, proj, mixc, C["ident"], C["identf"], s5_c)
            phase_outproj(cx, mixc, x_cur, Lw[l]["w_out"], Lw[l]["post_g"], x_next, C["ident"], NT)
            x_cur = x_next
        cx.S.finish(out.tiles)
    cx.n_ins = cx.S.n_ins
    return nc


def kernel(**inp):
    inp = {k: np.asarray(v) for k, v in inp.items()}
    x = np.ascontiguousarray(inp["x"], dtype=np.float32)
    nc = build_fused()
    shared = {"k_" + k: v for k, v in static_consts().items()}
    for l in range(DEPTH):
        shared.update({f"l{l}_{k}": v for k, v in layer_inputs(inp, l).items()})
        for jh in range(2):
            shared.update({f"l{l}h{jh}_{k}": v for k, v in half_inputs(inp, l, jh).items()})
    in_maps = []
    for b in range(4):
        m = dict(shared)
        m["x"] = x[b]
        in_maps.append(m)
    res = run_bass_kernel_spmd(nc, in_maps, core_ids=list(range(4)))
    return np.stack([res.results[b]["out"] for b in range(4)]).astype(np.float32)
```
